# Optimizing a Trainium2 kernel written in Bass

```python
import math
import jax, jax.numpy as jnp
from jax import lax
import numpy as np

D_MODEL = 1024
BATCH = 8
SEQ = 4096
DEPTH = 1

GRID_W = 64
CTX_LEN = 256
EPS = 1e-6
N_MOD = 6

HG_WIDTH = 512
HG_HEADS = 4
HG_DK = HG_WIDTH // HG_HEADS
HG_DV = HG_WIDTH // HG_HEADS
CHUNK = 64

HY_WIDTH = 512
HY_SHORT = 3
HY_EMB = 33
HY_BANDS = (HY_EMB - 1) // 2
HY_FILTER_HIDDEN = 64
HY_DECAY_TARGET = 1e-2
HY_FAST_PCT = 0.3
HY_SLOW_PCT = 1.5

D_FF = 4 * D_MODEL

HY_OFF = 5 * HG_WIDTH
GATE_OFF = HY_OFF + 3 * HY_WIDTH
IN_COLS = GATE_OFF + 2 * D_MODEL

kernel_name = 'hgrn2_hyena_gated_hybrid_prefix_dit'


def rmsnorm(x, w):
    xf = x.astype(jnp.float32)
    r = xf * lax.rsqrt(jnp.mean(xf * xf, axis=-1, keepdims=True) + EPS)
    return (r * w.astype(jnp.float32)).astype(x.dtype)


def modulate(h, shift, scale):
    return h * (1 + scale) + shift


def to_heads(t):
    b, l, _ = t.shape
    return t.reshape(b, l, HG_HEADS, -1).transpose(0, 2, 1, 3)


def flip_seq(t):
    return jnp.flip(t, axis=2)


def hgrn_gates(z, lb):
    f = lb + (1.0 - lb) * jax.nn.sigmoid(z.astype(jnp.float32))
    return to_heads(jnp.log(f)), to_heads(1.0 - f)


def _chunk(t):
    b, h, l, d = t.shape
    return t.reshape(b, h, l // CHUNK, CHUNK, d)


def _chunk_states(s0, kc, vc, a):
    a_last = a[..., -1:, :]
    ds = jnp.einsum('bhnsd,bhnsv->bhndv', kc * jnp.exp(a_last - a), vc)
    decay = jnp.exp(a_last[..., 0, :])

    def step(s, inp):
        d, dsn = inp
        return d[..., None] * s + dsn, s

    s_fin, s_starts = lax.scan(step, s0, (jnp.moveaxis(decay, 2, 0), jnp.moveaxis(ds, 2, 0)))
    return s_fin, jnp.moveaxis(s_starts, 0, 2)


def gla_scan(q, k, v, logf, s0):
    b, h, l, _ = q.shape
    qc, kc, vc = _chunk(q), _chunk(k), _chunk(v)
    a = jnp.cumsum(_chunk(logf), axis=3)
    s_fin, s_starts = _chunk_states(s0, kc, vc, a)
    qe = qc * jnp.exp(a)
    scores = jnp.einsum('bhncd,bhnsd->bhncs', qe, kc * jnp.exp(-a))
    mask = jnp.tril(jnp.ones((CHUNK, CHUNK), dtype=bool))
    scores = jnp.where(mask, scores, 0.0)
    o = jnp.einsum('bhncs,bhnsv->bhncv', scores, vc) + jnp.einsum('bhncd,bhndv->bhncv', qe, s_starts)
    return o.reshape(b, h, l, -1), s_fin


def gla_final_state(k, v, logf, s0):
    s_fin, _ = _chunk_states(s0, _chunk(k), _chunk(v), jnp.cumsum(_chunk(logf), axis=3))
    return s_fin


def hgrn_context_states(i, zf, zb, lb, s0):
    vh = to_heads(i.astype(jnp.float32))
    logf_f, k_f = hgrn_gates(zf, lb)
    logf_b, k_b = hgrn_gates(zb, lb)
    s_f = gla_final_state(k_f, vh, logf_f, s0)
    s_b = gla_final_state(flip_seq(k_b), flip_seq(vh), flip_seq(logf_b), s0)
    return s_f, s_b


def hgrn_branch(i, zf, zb, q, g, lb, g_norm, s0f, s0b):
    b, l, _ = i.shape
    qh = to_heads(jax.nn.silu(q.astype(jnp.float32)))
    vh = to_heads(i.astype(jnp.float32))
    logf_f, k_f = hgrn_gates(zf, lb)
    logf_b, k_b = hgrn_gates(zb, lb)
    o_f, s_f = gla_scan(qh, k_f, vh, logf_f, s0f)
    o_b, s_b = gla_scan(flip_seq(qh), flip_seq(k_b), flip_seq(vh), flip_seq(logf_b), s0b)
    o = o_f + flip_seq(o_b)
    o = o * lax.rsqrt(jnp.mean(o * o, axis=-1, keepdims=True) + EPS) * g_norm.astype(jnp.float32)
    o = o.transpose(0, 2, 1, 3).reshape(b, l, HG_WIDTH) * jax.nn.silu(g.astype(jnp.float32))
    return o.astype(i.dtype), s_f, s_b


def short_conv(u, w, bias):
    l = u.shape[-2]
    up = jnp.pad(u, [(0, 0)] * (u.ndim - 2) + [(1, 1), (0, 0)])
    return up[..., 0:l, :] * w[0] + up[..., 1:l + 1, :] * w[1] + up[..., 2:l + 2, :] * w[2] + bias


def hyena_filters(length, f_w1, f_b1, f_w2, f_b2, f_w3, f_freq):
    f32 = jnp.float32
    pos = jnp.arange(length, dtype=f32)
    t = pos / max(length - 1, 1)
    w = 2.0 * math.pi * pos / length
    bands = jnp.linspace(1e-4, HY_BANDS - 1, HY_BANDS, dtype=f32)
    ang = w[:, None] * bands[None, :]
    z = jnp.concatenate([t[:, None], jnp.cos(ang), -jnp.sin(ang)], axis=-1)
    fr = f_freq.astype(f32)
    hdn = jnp.sin(fr * (z @ f_w1.astype(f32) + f_b1.astype(f32)))
    hdn = jnp.sin(fr * (hdn @ f_w2.astype(f32) + f_b2.astype(f32)))
    h = (hdn @ f_w3.astype(f32)).reshape(length, 2, HY_WIDTH)
    min_decay = math.log(HY_DECAY_TARGET) / HY_SLOW_PCT
    max_decay = math.log(HY_DECAY_TARGET) / HY_FAST_PCT
    deltas = jnp.abs(jnp.linspace(min_decay, max_decay, HY_WIDTH, dtype=f32))
    h = h * jnp.exp(-t[:, None] * deltas[None, :])[:, None, :]
    return h[:, 0], h[:, 1]


def long_conv_bidir(u, h_fwd, h_bwd, d_skip):
    l = u.shape[1]
    n = 2 * l
    kern = jnp.concatenate([h_fwd, jnp.zeros((1, HY_WIDTH), jnp.float32), h_bwd[:0:-1]], axis=0)
    uf32 = u.astype(jnp.float32)
    y = jnp.fft.irfft(jnp.fft.rfft(uf32, n=n, axis=1) * jnp.fft.rfft(kern, n=n, axis=0)[None], n=n, axis=1)[:, :l]
    return (y + uf32 * d_skip.astype(jnp.float32)).astype(u.dtype)


def hyena_branch(u, grid, hy_short_w, hy_short_b, f_w1, f_b1, f_w2, f_b2, f_w3, f_freq, hy_d):
    b, l, cdim = u.shape
    if grid:
        rows = l // GRID_W
        uc = short_conv(u.reshape(b, rows, GRID_W, cdim), hy_short_w, hy_short_b).reshape(b, l, cdim)
    else:
        uc = short_conv(u, hy_short_w, hy_short_b)
    x0, x1, v = uc[..., :HY_WIDTH], uc[..., HY_WIDTH:2 * HY_WIDTH], uc[..., 2 * HY_WIDTH:]
    h_fwd, h_bwd = hyena_filters(l, f_w1, f_b1, f_w2, f_b2, f_w3, f_freq)
    v = long_conv_bidir(v * x1, h_fwd, h_bwd, hy_d)
    return v * x0


def mixer(p, lb, s0f, s0b, grid, g_norm, hy_short_w, hy_short_b, f_w1, f_b1, f_w2, f_b2, f_w3,
          f_freq, hy_d, w_pa, w_pb, w_out):
    i, zf, zb, q, g = (p[..., k * HG_WIDTH:(k + 1) * HG_WIDTH] for k in range(5))
    u = p[..., HY_OFF:GATE_OFF]
    gates = jax.nn.sigmoid(p[..., GATE_OFF:])
    a, s_f, s_b = hgrn_branch(i, zf, zb, q, g, lb, g_norm, s0f, s0b)
    y = hyena_branch(u, grid, hy_short_w, hy_short_b, f_w1, f_b1, f_w2, f_b2, f_w3, f_freq, hy_d)
    merged = gates[..., :D_MODEL] * (a @ w_pa) + gates[..., D_MODEL:] * (y @ w_pb)
    return merged @ w_out, s_f, s_b


def sq_relu_mlp(h, w1, w2):
    return jnp.square(jax.nn.relu(h @ w1)) @ w2


def setup_inputs(seed: int = 0) -> dict:
    key = jax.random.key(seed)
    ks = jax.random.split(key, 26)
    f32 = jnp.float32
    nrm = lambda k, shape, s: (jax.random.normal(k, shape, f32) * s)
    d = D_MODEL
    return {
        'x': nrm(ks[0], (BATCH, SEQ, d), 1.0),
        'c': nrm(ks[1], (BATCH, d), 1.0),
        'ctx': nrm(ks[2], (BATCH, CTX_LEN, d), 1.0),
        'c_ctx': nrm(ks[3], (d,), 1.0),
        'w_ada': nrm(ks[4], (DEPTH, d, N_MOD * d), 0.5 * d ** -0.5),
        'b_ada': nrm(ks[5], (DEPTH, N_MOD * d), 0.02),
        'norm_w': 1.0 + nrm(ks[6], (DEPTH, 4, d), 0.02),
        'w_in': nrm(ks[7], (DEPTH, d, IN_COLS), d ** -0.5),
        'lb_param': nrm(ks[8], (DEPTH + 1, HG_WIDTH), 0.1),
        'g_norm': 1.0 + nrm(ks[9], (DEPTH, HG_DV), 0.02),
        'hy_short_w': nrm(ks[10], (DEPTH, HY_SHORT, 3 * HY_WIDTH), HY_SHORT ** -0.5),
        'hy_short_b': nrm(ks[11], (DEPTH, 3 * HY_WIDTH), 0.02),
        'f_w1': nrm(ks[12], (DEPTH, HY_EMB, HY_FILTER_HIDDEN), HY_EMB ** -0.5),
        'f_b1': nrm(ks[13], (DEPTH, HY_FILTER_HIDDEN), 0.1),
        'f_w2': nrm(ks[14], (DEPTH, HY_FILTER_HIDDEN, HY_FILTER_HIDDEN), HY_FILTER_HIDDEN ** -0.5),
        'f_b2': nrm(ks[15], (DEPTH, HY_FILTER_HIDDEN), 0.1),
        'f_w3': nrm(ks[16], (DEPTH, HY_FILTER_HIDDEN, 2 * HY_WIDTH), 0.05 * HY_FILTER_HIDDEN ** -0.5),
        'f_freq': 1.0 + nrm(ks[17], (DEPTH, HY_FILTER_HIDDEN), 0.01),
        'hy_d': nrm(ks[18], (DEPTH, HY_WIDTH), 0.5),
        'w_pa': nrm(ks[19], (DEPTH, HG_WIDTH, d), HG_WIDTH ** -0.5),
        'w_pb': nrm(ks[20], (DEPTH, HY_WIDTH, d), HY_WIDTH ** -0.5),
        'w_out': nrm(ks[21], (DEPTH, d, d), d ** -0.5),
        'w_mlp1': nrm(ks[22], (DEPTH, d, D_FF), d ** -0.5),
        'w_mlp2': nrm(ks[23], (DEPTH, D_FF, d), D_FF ** -0.5),
    }


def reference(x, c, ctx, c_ctx, w_ada, b_ada, norm_w, w_in, lb_param, g_norm, hy_short_w, hy_short_b,
              f_w1, f_b1, f_w2, f_b2, f_w3, f_freq, hy_d, w_pa, w_pb, w_out, w_mlp1, w_mlp2):
    b_, _, _ = x.shape
    lb_all = jnp.cumsum(jax.nn.softmax(lb_param.astype(jnp.float32), axis=0), axis=0)
    s0 = jnp.zeros((b_, HG_HEADS, HG_DK, HG_DV), jnp.float32)
    x_lat, x_ctx = x, ctx
    for layer in range(DEPTH):
        last = layer == DEPTH - 1
        lb = lb_all[layer]
        nw = norm_w[layer]
        mod = (jax.nn.silu(c) @ w_ada[layer] + b_ada[layer]).reshape(b_, N_MOD, 1, D_MODEL)
        mod_c = (jax.nn.silu(c_ctx) @ w_ada[layer] + b_ada[layer]).reshape(N_MOD, D_MODEL)
        mix_args = (g_norm[layer], hy_short_w[layer], hy_short_b[layer], f_w1[layer], f_b1[layer],
                    f_w2[layer], f_b2[layer], f_w3[layer], f_freq[layer], hy_d[layer],
                    w_pa[layer], w_pb[layer], w_out[layer])
        h_lat = modulate(rmsnorm(x_lat, nw[0]), mod[:, 1], mod[:, 0])
        h_ctx = modulate(rmsnorm(x_ctx, nw[0]), mod_c[1], mod_c[0])
        if last:
            p_ctx = h_ctx @ w_in[layer][:, :3 * HG_WIDTH]
            s_cf, s_cb = hgrn_context_states(p_ctx[..., :HG_WIDTH], p_ctx[..., HG_WIDTH:2 * HG_WIDTH],
                                             p_ctx[..., 2 * HG_WIDTH:], lb, s0)
        else:
            p_ctx = h_ctx @ w_in[layer]
            mix_ctx, s_cf, s_cb = mixer(p_ctx, lb, s0, s0, False, *mix_args)
            x_ctx = x_ctx + mod_c[2] * rmsnorm(mix_ctx, nw[1])
            hc = modulate(rmsnorm(x_ctx, nw[2]), mod_c[4], mod_c[3])
            x_ctx = x_ctx + mod_c[5] * rmsnorm(sq_relu_mlp(hc, w_mlp1[layer], w_mlp2[layer]), nw[3])
        p_lat = h_lat @ w_in[layer]
        mix_lat, _, _ = mixer(p_lat, lb, s_cf, s_cb, True, *mix_args)
        x_lat = x_lat + mod[:, 2] * rmsnorm(mix_lat, nw[1])
        hl = modulate(rmsnorm(x_lat, nw[2]), mod[:, 4], mod[:, 3])
        x_lat = x_lat + mod[:, 5] * rmsnorm(sq_relu_mlp(hl, w_mlp1[layer], w_mlp2[layer]), nw[3])
    return x_lat
```

```python
import contextlib
import numpy as np
import ml_dtypes
import concourse.bass as bass
import concourse.mybir as mybir
from concourse.bass_utils import run_bass_kernel_spmd

F32 = mybir.dt.float32
BF16 = mybir.dt.bfloat16
AF = mybir.ActivationFunctionType
ALU = mybir.AluOpType

D = 1024
L = 4096
CTX = 256
NT = 32
NTA = 34
EPS = 1e-6
NFFT = 8192
PI = float(np.pi)


class Sched:
    ENG = ('pe', 'act', 'dve', 'pool', 'sp')

    def __init__(self, nc, ndma=16):
        self.nc = nc
        self.ops = {e: [] for e in self.ENG}
        self.cnt = {e: 0 for e in self.ENG}
        self.known = {e: {} for e in self.ENG}
        self.lastw = {}
        self.readers = {}
        self.ndma = ndma
        self.dma_n = 0
        self.dma_cnt = [0] * ndma
        self.out_toks = []

    def _need(self, eng, tok, waits, same_ok):
        if tok is None:
            return
        sem, val = tok
        if sem == eng and same_ok:
            return
        if self.known[eng].get(sem, 0) >= val:
            return
        if sem == eng:
            assert val <= self.cnt[eng], "same-engine wait on un-inc'd op"
        self.known[eng][sem] = val
        waits.append((sem, val))

    def _deps(self, eng, reads, writes):
        waits = []
        for k in reads:
            self._need(eng, self.lastw.get(k), waits, False)
        for k in writes:
            self._need(eng, self.lastw.get(k), waits, True)
            for s, v in self.readers.get(k, {}).items():
                self._need(eng, (s, v), waits, True)
        return waits

    def _commit(self, tok, reads, writes):
        for k in reads:
            d = self.readers.setdefault(k, {})
            if d.get(tok[0], 0) < tok[1]:
                d[tok[0]] = tok[1]
        for k in writes:
            self.lastw[k] = tok
            self.readers[k] = {}

    def op(self, eng, fn, reads=(), writes=(), inc=True):
        waits = self._deps(eng, reads, writes)
        if inc:
            self.cnt[eng] += 1
            tok = (eng, self.cnt[eng])
        else:
            tok = (eng, self.cnt[eng] + 1)
        self._commit(tok, reads, writes)
        self.ops[eng].append((waits, fn, eng if inc else None, 1))
        return tok

    def dma(self, q, out, in_, reads=(), writes=(), is_out=False, **kw):
        slot = self.dma_n % self.ndma
        self.dma_n += 1
        sem = 'dma%d' % slot
        waits = self._deps(q, reads, writes)
        prev = self.dma_cnt[slot]
        if prev > 0:
            self._need(q, (sem, prev), waits, False)
        self.dma_cnt[slot] += 16
        tok = (sem, self.dma_cnt[slot])
        self._commit(tok, reads, writes)
        self.ops[q].append((waits, lambda e: e.dma_start(out=out, in_=in_, **kw), sem, 16))
        if is_out:
            self.out_toks.append(tok)
        return tok

    def barrier(self):
        for e in self.ENG:
            waits = []
            for f in self.ENG[:4]:
                if f != e and self.cnt[f] > 0:
                    self._need(e, (f, self.cnt[f]), waits, False)
            for s in range(self.ndma):
                if self.dma_cnt[s] > 0:
                    self._need(e, ('dma%d' % s, self.dma_cnt[s]), waits, False)
            self.ops[e].append((waits, None, None, 0))

    def emit(self):
        nc = self.nc
        waits = []
        for tok in self.out_toks:
            self._need('sp', tok, waits, False)
        self.ops['sp'].append((waits, None, None, 0))
        for e in ('pe', 'act', 'dve', 'pool'):
            if self.ops[e]:
                last = [o for o in self.ops[e] if o[1] is not None][-1]
                assert last[2] is not None, "last op on %s must inc" % e
        names = list(self.ENG[:4]) + ['dma%d' % i for i in range(self.ndma)]
        with contextlib.ExitStack() as st:
            sems = {n: st.enter_context(nc.semaphore(n)) for n in names}
            block = st.enter_context(nc.Block())

            def run(ename):
                def body(e):
                    for waits, fn, incsem, incv in self.ops[ename]:
                        for s, v in waits:
                            e.wait_ge(sems[s], v)
                        if fn is not None:
                            ins = fn(e)
                            if incsem is not None:
                                ins.then_inc(sems[incsem], incv)
                return body
            block.tensor(run('pe'))
            block.scalar(run('act'))
            block.vector(run('dve'))
            block.gpsimd(run('pool'))
            block.sync(run('sp'))


def host_consts():
    p = np.arange(128)
    s = p[:, None]
    t = p[None, :]
    c = {}
    c['idf'] = (s == t)
    c['ones'] = np.ones((128, 128))
    c['Mincl_f'] = (s <= t)
    c['Mb_f'] = (s <= t).astype(np.float64) - (s <= 63)
    c['nMb_f'] = -c['Mb_f']
    c['Mkd_f'] = (s > t)
    c['Mincl_b'] = (s >= t)
    c['Mb_b'] = (s >= t).astype(np.float64) - (s >= 64)
    c['nMb_b'] = -c['Mb_b']
    c['Mkd_b'] = (s < t)
    c['mask_f'] = np.tile((s <= t), (1, 4))
    c['mask_b'] = np.tile((s >= t), (1, 4))
    deltas = np.abs(np.linspace(np.log(1e-2) / 1.5, np.log(1e-2) / 0.3, 512, dtype=np.float32)).astype(np.float64)
    c['delta'] = np.tile(deltas[None, :], (128, 1))
    tt = (np.arange(L).reshape(32, 128).T).astype(np.float64)
    c['ntn'] = -(tt / (L - 1))
    c['alt'] = np.tile(((-1.0) ** p)[:, None], (1, 2))
    c['altrow'] = np.tile(((-1.0) ** np.arange(512))[None, :], (128, 1))
    offs = {}
    cols = []
    o = 0
    for k, v in c.items():
        v = np.asarray(v, dtype=np.float32)
        offs[k] = (o, v.shape[1])
        cols.append(v)
        o += v.shape[1]
    arr = np.concatenate(cols, axis=1).astype(np.float32)
    pos = np.arange(L, dtype=np.float32)
    tn = pos / np.float32(L - 1)
    w = np.float32(2.0 * np.pi) * pos / np.float32(L)
    bands = np.linspace(1e-4, 15, 16, dtype=np.float32)
    ang = w[:, None] * bands[None, :]
    z = np.concatenate([tn[:, None], np.cos(ang), -np.sin(ang)], axis=-1).astype(np.float32)
    zT = np.ascontiguousarray(z.T)
    a = (np.arange(L, dtype=np.int64).reshape(32, 128).T)[None, :, :, None]
    b = np.arange(L, dtype=np.int64).reshape(8, 1, 1, 512)
    m = (a * b) % NFFT
    angm = (2.0 * np.pi / NFFT) * m
    ctab = np.cos(angm).astype(ml_dtypes.bfloat16)
    stab = np.sin(angm).astype(ml_dtypes.bfloat16)
    return arr, offs, zT, ctab, stab


_HC = None


def get_hc():
    global _HC
    if _HC is None:
        _HC = host_consts()
    return _HC


def build_program(debug=False, stop_after=None):
    carr, coffs, _, _, _ = get_hc()
    NCC = carr.shape[1]
    nc = bass.Bass("TRN2", target_bir_lowering=False)
    S = Sched(nc)

    def din(name, shape, dt=F32):
        return nc.dram_tensor(name, list(shape), dt, kind="ExternalInput").ap()

    def dscr(name, shape, dt):
        return nc.dram_tensor(name, list(shape), dt, kind=("ExternalOutput" if debug else "Internal")).ap()

    x = din("x", [L, D]); ctx = din("ctx", [CTX, D]); cvec = din("cvec", [2, D])
    w_ada = din("w_ada", [D, 6 * D]); b_ada = din("b_ada", [6 * D]); norm_w = din("norm_w", [4, D])
    w_in = din("w_in", [D, 6144]); lb_param = din("lb_param", [2, 512]); g_norm = din("g_norm", [128])
    hy_short_w = din("hy_short_w", [3, 1536]); hy_short_b = din("hy_short_b", [1536])
    f_w1 = din("f_w1", [33, 64]); f_b1 = din("f_b1", [64]); f_w2 = din("f_w2", [64, 64]); f_b2 = din("f_b2", [64])
    f_w3 = din("f_w3", [64, 1024]); f_freq = din("f_freq", [64]); hy_d = din("hy_d", [512])
    w_pa = din("w_pa", [512, D]); w_pb = din("w_pb", [512, D]); w_out = din("w_out", [D, D])
    w_mlp1 = din("w_mlp1", [D, 4 * D]); w_mlp2 = din("w_mlp2", [4 * D, D])
    consts_d = din("consts", [128, NCC]); zT_d = din("zT", [33, L])
    ctab_d = din("ctab", [8, 128, 32, 512], BF16); stab_d = din("stab", [8, 128, 32, 512], BF16)
    out = nc.dram_tensor("out", [L, D], F32, kind="ExternalOutput").ap()

    sc_tm = dscr("sc_tm", [NTA, 128, 1536], F32)
    sc_q = dscr("sc_q", [4, 128, L], BF16)
    sc_g = dscr("sc_g", [4, 128, L], BF16)
    sc_x0 = dscr("sc_x0", [4, 128, L], BF16)
    sc_u = dscr("sc_u", [128, 32, 512], BF16)
    sc_gate = dscr("sc_gate", [16, 128, L], BF16)
    sc_Y = dscr("sc_Y", [128, 64, 512], BF16)
    sc_yx0 = dscr("sc_yx0", [4, 128, L], BF16)
    sc_a = dscr("sc_a", [4, 128, L], BF16)
    sc_xlat = dscr("sc_xlat", [L, D], F32)
    sc_hl = dscr("sc_hl", [8, 128, L], BF16)

    ARENA = 102 * 1024
    arena = nc.alloc_sbuf_tensor("arena", [128, ARENA], BF16)
    top = [0]

    def sb(shape, dt, name=None):
        rows = shape[0]
        n = int(np.prod(shape[1:]))
        n16 = n * (2 if dt == F32 else 1)
        off = top[0]
        top[0] += (n16 + 15) // 16 * 16
        assert top[0] <= ARENA, ("SBUF arena overflow", name, top[0])
        ap = arena[0:rows, off:off + n16]
        if dt == F32:
            ap = ap.bitcast(F32)
        if len(shape) == 3:
            ap = ap.rearrange("p (a b) -> p a b", b=shape[2])
        return ap

    ps = [nc.alloc_psum_tensor("ps%d" % i, [128, 512], F32) for i in range(8)]
    psi = [0]

    def nps():
        i = psi[0] % 8
        psi[0] += 1
        return i

    def mm(o, lhsT, rhs, start, stop, reads, writes, inc=None):
        if inc is None:
            inc = stop
        S.op('pe', lambda e: e.matmul(o, lhsT, rhs, start=start, stop=stop), reads, writes, inc)

    def tr(o, i, ident, reads, writes, inc=True):
        S.op('pe', lambda e: e.transpose(o, i, ident), reads, writes, inc)

    def act(o, i, func, reads, writes, **kw):
        S.op('act', lambda e: e.activation(out=o, in_=i, func=func, **kw), reads, writes)

    def tt(eng, o, a, b, op, reads, writes):
        S.op(eng, lambda e: e.tensor_tensor(out=o, in0=a, in1=b, op=op), reads, writes)

    def ts(eng, o, a, s1, s2, op0, op1, reads, writes):
        if s2 is None:
            S.op(eng, lambda e: e.tensor_scalar(out=o, in0=a, scalar1=s1, scalar2=None, op0=op0), reads, writes)
        else:
            S.op(eng, lambda e: e.tensor_scalar(out=o, in0=a, scalar1=s1, scalar2=s2, op0=op0, op1=op1), reads, writes)

    def stt(eng, o, a, sc, b, op0, op1, reads, writes):
        S.op(eng, lambda e: e.scalar_tensor_tensor(out=o, in0=a, scalar=sc, in1=b, op0=op0, op1=op1), reads, writes)

    def cp(eng, o, i, reads, writes):
        if eng == 'act':
            act(o, i, AF.Copy, reads, writes)
        else:
            S.op(eng, lambda e: e.tensor_copy(o, i), reads, writes)

    cst = sb([128, NCC], F32, "cst")
    S.dma('sp', cst[:], consts_d, writes=['cst'])

    def C(name, rows=128):
        o, n = coffs[name]
        return cst[0:rows, o:o + n]
    idb = sb([128, 128], BF16, "idb"); onesb = sb([128, 128], BF16, "onesb")
    maskf = sb([128, 512], BF16, "maskf"); maskb = sb([128, 512], BF16, "maskb")
    altb = sb([128, 2], BF16, "altb"); altrowb = sb([1, 512], BF16, "altrowb")
    cp('dve', idb[:], C('idf'), ['cst'], ['idb'])
    cp('dve', onesb[:], C('ones'), ['cst'], ['onesb'])
    cp('dve', maskf[:], C('mask_f'), ['cst'], ['maskf'])
    cp('dve', maskb[:], C('mask_b'), ['cst'], ['maskb'])
    cp('dve', altb[:], C('alt'), ['cst'], ['altb'])
    cp('dve', altrowb[:], C('altrow', 1), ['cst'], ['altrowb'])
    ident = C('idf')

    NSL = dict(allow_slow_non_contiguous=True)
    sm = sb([128, 160], F32, "sm")
    so = {}
    o = [0]

    def smcol(name, n):
        so[name] = o[0]
        o[0] += n
        return sm[:, so[name]:so[name] + n]
    S.dma('sp', smcol('bada', 48), b_ada.rearrange("(t p) -> p t", p=128), writes=['sm'], **NSL)
    S.dma('sp', smcol('nw', 32), norm_w.rearrange("r (t p) -> p (r t)", p=128), writes=['sm'], **NSL)
    S.dma('sp', smcol('c', 8), cvec[0, :].rearrange("(t p) -> p t", p=128), writes=['sm'], **NSL)
    S.dma('sp', smcol('cc', 8), cvec[1, :].rearrange("(t p) -> p t", p=128), writes=['sm'], **NSL)
    S.dma('sp', smcol('lb', 8), lb_param.rearrange("r (t p) -> p (r t)", p=128), writes=['sm'], **NSL)
    S.dma('sp', smcol('gn', 1), g_norm.rearrange("(t p) -> p t", p=128), writes=['sm'], **NSL)
    S.dma('sp', smcol('hsw', 36), hy_short_w.rearrange("r (t p) -> p (r t)", p=128), writes=['sm'], **NSL)
    S.dma('sp', smcol('hsb', 12), hy_short_b.rearrange("(t p) -> p t", p=128), writes=['sm'], **NSL)
    S.dma('sp', smcol('hyd', 4), hy_d.rearrange("(t p) -> p t", p=128), writes=['sm'], **NSL)
    sm64 = sb([64, 4], F32, "sm64")
    S.dma('sp', sm64[:, 0:1], f_b1.rearrange("(p t) -> p t", t=1), writes=['sm64'], **NSL)
    S.dma('sp', sm64[:, 1:2], f_b2.rearrange("(p t) -> p t", t=1), writes=['sm64'], **NSL)
    S.dma('sp', sm64[:, 2:3], f_freq.rearrange("(p t) -> p t", t=1), writes=['sm64'], **NSL)

    def SM(name, i=0, n=1):
        return sm[:, so[name] + i: so[name] + i + n]

    cc2 = sb([128, 8, 2], F32, "cc2")
    cp('dve', cc2[:, :, 0], SM('c', 0, 8), ['sm'], ['cc2'])
    cp('dve', cc2[:, :, 1], SM('cc', 0, 8), ['sm'], ['cc2'])
    e2 = sb([128, 16], F32, "e2")
    cc2f = cc2[:].rearrange("p a b -> p (a b)")
    act(e2[:], cc2f, AF.Exp, ['cc2'], ['e2'], scale=-1.0)
    ts('dve', e2[:], e2[:], 1.0, None, ALU.add, None, ['e2'], ['e2'])
    S.op('dve', lambda e: e.reciprocal(e2[:], e2[:]), ['e2'], ['e2'])
    scb = sb([128, 8, 2], BF16, "scb")
    tt('dve', scb[:].rearrange("p a b -> p (a b)"), cc2f, e2[:], ALU.mult, ['cc2', 'e2'], ['scb'])
    mod = sb([128, 48, 2], F32, "mod")
    fv = sb([128, 8, 8], F32, "fv")
    c1f = sb([128, 4], F32, "c1f")
    c1bc = sb([128, 512], F32, "c1bc")
    hydbc = sb([128, 512], F32, "hydbc")
    diag = [sb([128, 128], F32, "diag") for _ in range(2)]
    ynq = sb([1, 512], BF16, "ynq")
    MARK0 = top[0]
    wad = [sb([128, 8, 512], BF16, "wad") for _ in range(2)]
    modp = nps()
    for pc in range(12):
        wb = wad[pc % 2]
        S.dma('pool', wb[:], w_ada[:, pc * 512:(pc + 1) * 512].rearrange("(k p) n -> p k n", p=128), writes=[('wad', pc % 2)])
        for j in range(4):
            ft = pc * 4 + j
            for k in range(8):
                mm(ps[modp][:, 2 * ft:2 * ft + 2], wb[:, k, j * 128:(j + 1) * 128], scb[:, k, :], k == 0, k == 7,
                   [('wad', pc % 2), 'scb'], [('ps', modp)], inc=(k == 7))
    cp('dve', mod[:].rearrange("p a b -> p (a b)"), ps[modp][:, 0:96], [('ps', modp)], ['mod'])
    for j in range(2):
        tt('dve', mod[:, :, j], mod[:, :, j], SM('bada', 0, 48), ALU.add, ['mod', 'sm'], ['mod'])
    def nwv(r):
        return SM('nw', 8 * r, 8)
    stt('dve', fv[:, 0, :], mod[:, 0:8, 0], 1.0, nwv(0), ALU.add, ALU.mult, ['mod', 'sm'], ['fv'])
    cp('dve', fv[:, 1, :], mod[:, 8:16, 0], ['mod'], ['fv'])
    stt('dve', fv[:, 2, :], mod[:, 0:8, 1], 1.0, nwv(0), ALU.add, ALU.mult, ['mod', 'sm'], ['fv'])
    cp('dve', fv[:, 3, :], mod[:, 8:16, 1], ['mod'], ['fv'])
    tt('dve', fv[:, 4, :], mod[:, 16:24, 0], nwv(1), ALU.mult, ['mod', 'sm'], ['fv'])
    stt('dve', fv[:, 5, :], mod[:, 24:32, 0], 1.0, nwv(2), ALU.add, ALU.mult, ['mod', 'sm'], ['fv'])
    cp('dve', fv[:, 6, :], mod[:, 32:40, 0], ['mod'], ['fv'])
    tt('dve', fv[:, 7, :], mod[:, 40:48, 0], nwv(3), ALU.mult, ['mod', 'sm'], ['fv'])
    tt('dve', c1f[:], SM('lb', 0, 4), SM('lb', 4, 4), ALU.subtract, ['sm'], ['c1f'])
    act(c1f[:], c1f[:], AF.Exp, ['c1f'], ['c1f'])
    act(c1f[:], c1f[:], AF.Ln, ['c1f'], ['c1f'], bias=1.0)
    ts('dve', c1f[:], c1f[:], -1.0, None, ALU.mult, None, ['c1f'], ['c1f'])
    onesf = C('ones')
    di = [0]

    def bcast(dst, col, dkey, rkey):
        d = diag[di[0] % 2]
        dk = ('diag', di[0] % 2)
        di[0] += 1
        ts('dve', d[:], ident, col, None, ALU.mult, None, ['cst', rkey], [dk])
        b = nps()
        mm(ps[b][:, 0:128], onesf, d[:], True, True, ['cst', dk], [('ps', b)])
        cp('act', dst, ps[b][:, 0:128], [('ps', b)], [dkey])
    for k in range(4):
        bcast(c1bc[:, k * 128:(k + 1) * 128], c1f[:, k:k + 1], 'c1bc', 'c1f')
        bcast(hydbc[:, k * 128:(k + 1) * 128], SM('hyd', k, 1), 'hydbc', 'sm')
    S.barrier()
    top[0] = MARK0

    hT = sb([128, 8, NTA * 128], BF16, "hT")
    MARK1 = top[0]
    xb = [sb([128, 1024], F32, "xb") for _ in range(2)]
    junk = sb([128, 1024], BF16, "junk")
    xn = [sb([128, 1024], BF16, "xn") for _ in range(2)]
    st = [sb([128, 4], F32, "st") for _ in range(2)]
    for i in range(NTA):
        s2 = i % 2
        src = ctx[i * 128:(i + 1) * 128, :] if i < 2 else x[(i - 2) * 128:(i - 1) * 128, :]
        S.dma('sp', xb[s2][:], src, writes=[('xb', s2)])
        S.op('pool', lambda e, t_=st[s2]: e.memset(t_[:, 0:1], 0.0), [], [('st', s2)])
        act(junk[:], xb[s2][:], AF.Square, [('xb', s2), ('st', s2)], ['junk', ('st', s2)], accum_out=st[s2][:, 0:1])
        act(st[s2][:, 1:2], st[s2][:, 0:1], AF.Ln, [('st', s2)], [('st', s2)], scale=1.0 / D, bias=EPS)
        act(st[s2][:, 2:3], st[s2][:, 1:2], AF.Exp, [('st', s2)], [('st', s2)], scale=-0.5)
        ts('dve', xn[s2][:], xb[s2][:], st[s2][:, 2:3], None, ALU.mult, None, [('xb', s2), ('st', s2)], [('xn', s2)])
        b = nps()
        pb = ps[b][:].bitcast(BF16)
        for k in range(8):
            tr(pb[:, k * 128:(k + 1) * 128], xn[s2][:, k * 128:(k + 1) * 128], idb[:], [('xn', s2), 'idb'], [('ps', b)], inc=(k == 7))
        ai = 0 if i >= 2 else 2
        for k in range(8):
            dst = hT[:, k, i * 128:(i + 1) * 128]
            if k % 2 == 0:
                act(dst, pb[:, k * 128:(k + 1) * 128], AF.Identity, [('ps', b), 'fv'], [('hT', i)],
                    scale=fv[:, ai, k:k + 1], bias=fv[:, ai + 1, k:k + 1])
            else:
                ts('dve', dst, pb[:, k * 128:(k + 1) * 128], fv[:, ai, k:k + 1], fv[:, ai + 1, k:k + 1], ALU.mult, ALU.add,
                   [('ps', b), 'fv'], [('hT', i)])

    wtm = sb([128, 8, 1536], BF16, "wtm")
    for k3 in range(3):
        S.dma('pool', wtm[:, :, k3 * 512:(k3 + 1) * 512], w_in[:, k3 * 512:(k3 + 1) * 512].rearrange("(k p) n -> p k n", p=128), writes=['wtm'])
    tms = [sb([128, 1536], F32, "tms") for _ in range(2)]
    for i in range(NTA):
        s2 = i % 2
        for k3 in range(3):
            b = nps()
            for k in range(8):
                mm(ps[b][:], hT[:, k, i * 128:(i + 1) * 128], wtm[:, k, k3 * 512:(k3 + 1) * 512], k == 0, k == 7,
                   [('hT', i), 'wtm'], [('ps', b)])
            cp('act' if k3 != 1 else 'dve', tms[s2][:, k3 * 512:(k3 + 1) * 512], ps[b][:], [('ps', b)], [('tms', s2)])
        S.dma('sp', sc_tm[i], tms[s2][:], reads=[('tms', s2)], writes=[('sc_tm', i)])
    if stop_after == 'A':
        for i in range(NTA):
            S.out_toks.append(S.lastw[('sc_tm', i)])
        S.emit()
        return nc

    S.barrier()
    top[0] = MARK1
    wfm = sb([128, 8, 4608], BF16, "wfm")
    for k9 in range(9):
        S.dma('pool', wfm[:, :, k9 * 512:(k9 + 1) * 512], w_in[:, 1536 + k9 * 512:1536 + (k9 + 1) * 512].rearrange("(k p) n -> p k n", p=128), writes=['wfm'])
    stg = [sb([128, 512], BF16, "stg") for _ in range(4)]
    sgi = [0]
    cvt = [sb([128, 512], F32, "cvt") for _ in range(4)]
    ustage = [sb([128, 4, 512], BF16, "ustage") for _ in range(2)]
    ub = [sb([128, 512], BF16, "ub") for _ in range(2)]

    def proj(ft_col, tb):
        b = nps()
        for k in range(8):
            mm(ps[b][:], wfm[:, k, ft_col:ft_col + 128], hT[:, k, 256 + tb * 512:256 + (tb + 1) * 512], k == 0, k == 7,
               ['wfm'] + [('hT', 2 + tb * 4 + j) for j in range(4)], [('ps', b)])
        return b

    def conv(b, ft, dst, dkey, s4):
        cv = cvt[s4]
        ck = ('cvt', s4)
        p3 = ps[b][:].rearrange("p (r c) -> p r c", c=64)
        c3 = cv[:].rearrange("p (r c) -> p r c", c=64)
        act(cv[:], ps[b][:], AF.Identity, [('ps', b), 'sm'], [ck], scale=SM('hsw', 12 + ft, 1), bias=SM('hsb', ft, 1))
        stt('dve', c3[:, :, 1:64], p3[:, :, 0:63], SM('hsw', ft, 1), c3[:, :, 1:64], ALU.mult, ALU.add, [('ps', b), 'sm', ck], [ck])
        d3 = dst.rearrange("p (r c) -> p r c", c=64)
        stt('dve', d3[:, :, 0:63], p3[:, :, 1:64], SM('hsw', 24 + ft, 1), c3[:, :, 0:63], ALU.mult, ALU.add, [('ps', b), 'sm', ck], [dkey])
        cp('pool', d3[:, :, 63:64], c3[:, :, 63:64], [ck], [dkey])

    for tb in range(8):
        tsl = slice(tb * 512, (tb + 1) * 512)
        for ft in range(8):
            b = proj((ft) * 128, tb)
            sg = sgi[0] % 4
            sgi[0] += 1
            act(stg[sg][:], ps[b][:], AF.Silu, [('ps', b)], [('stg', sg)])
            dstd = (sc_q if ft < 4 else sc_g)[ft % 4, :, tsl]
            S.dma('sp', dstd, stg[sg][:], reads=[('stg', sg)], writes=[('sc_qg', ft, tb)])
        for ft in range(16):
            b = proj(2560 + ft * 128, tb)
            sg = sgi[0] % 4
            sgi[0] += 1
            act(stg[sg][:], ps[b][:], AF.Sigmoid, [('ps', b)], [('stg', sg)])
            S.dma('sp', sc_gate[ft, :, tsl], stg[sg][:], reads=[('stg', sg)], writes=[('sc_gate', ft, tb)])
        us = ustage[tb % 2]
        for j in range(4):
            b = proj(1024 + j * 128, tb)
            sg = sgi[0] % 4
            sgi[0] += 1
            conv(b, j, stg[sg][:], ('stg', sg), 0)
            S.dma('sp', sc_x0[j, :, tsl], stg[sg][:], reads=[('stg', sg)], writes=[('sc_x0', j, tb)])
            b1 = proj(1024 + (4 + j) * 128, tb)
            conv(b1, 4 + j, cvt[2][:], ('cvt', 2), 1)
            b2 = proj(1024 + (8 + j) * 128, tb)
            conv(b2, 8 + j, cvt[3][:], ('cvt', 3), 1)
            u2 = ub[j % 2]
            tt('pool', u2[:], cvt[2][:], cvt[3][:], ALU.mult, [('cvt', 2), ('cvt', 3)], [('ub', j % 2)])
            bt = nps()
            pbt = ps[bt][:].bitcast(BF16)
            for q4 in range(4):
                tr(pbt[:, q4 * 128:(q4 + 1) * 128], u2[:, q4 * 128:(q4 + 1) * 128], idb[:], [('ub', j % 2), 'idb'], [('ps', bt)], inc=(q4 == 3))
            cp('act', us[:, :, j * 128:(j + 1) * 128], pbt[:, 0:512].rearrange("p (a b) -> p a b", b=128), [('ps', bt)], [('ustage', tb % 2)])
        S.dma('sp', sc_u[:, tb * 4:(tb + 1) * 4, :], us[:], reads=[('ustage', tb % 2)], writes=[('sc_u', tb)])
    if stop_after == 'P':
        for k_, v_ in list(S.lastw.items()):
            if isinstance(k_, tuple) and str(k_[0]).startswith('sc_'):
                S.out_toks.append(v_)
        S.emit()
        return nc
    S.barrier()
    top[0] = MARK0
    hcat = sb([128, 32, 1024], BF16, "hcat")
    MARKH = top[0]
    zt = sb([33, L], F32, "zt"); w1s = sb([33, 64], F32, "w1s"); w2s = sb([64, 64], F32, "w2s"); w3s = sb([64, 1024], F32, "w3s")
    h1 = sb([64, L], F32, "h1"); h2 = sb([64, L], F32, "h2"); wtmp = sb([64, 512], F32, "wtmp")
    S.dma('sp', zt[:], zT_d, writes=['zt'])
    S.dma('sp', w1s[:], f_w1, writes=['fw'])
    S.dma('sp', w2s[:], f_w2, writes=['fw'])
    S.dma('sp', w3s[:], f_w3, writes=['fw'])

    def sinlayer(src, w, kdim, bcol, dst, skey, dkey):
        for blk in range(8):
            sl = slice(blk * 512, (blk + 1) * 512)
            b = nps()
            mm(ps[b][0:64, :], w[0:kdim, :], src[0:kdim, sl], True, True, [skey, 'fw'], [('ps', b)])
            ts('dve', dst[:, sl], ps[b][0:64, :], sm64[:, bcol:bcol + 1], sm64[:, 2:3], ALU.add, ALU.mult, [('ps', b), 'sm64'], [dkey])
            for _ in range(2):
                S.op('dve', lambda e, d_=dst[:, sl]: e.tensor_single_scalar(out=wtmp[:], in_=d_, scalar=PI, op=ALU.is_gt), [dkey], ['wtmp'])
                stt('dve', dst[:, sl], wtmp[:], -2 * PI, dst[:, sl], ALU.mult, ALU.add, ['wtmp', dkey], [dkey])
                S.op('dve', lambda e, d_=dst[:, sl]: e.tensor_single_scalar(out=wtmp[:], in_=d_, scalar=-PI, op=ALU.is_lt), [dkey], ['wtmp'])
                stt('dve', dst[:, sl], wtmp[:], 2 * PI, dst[:, sl], ALU.mult, ALU.add, ['wtmp', dkey], [dkey])
            act(dst[:, sl], dst[:, sl], AF.Sin, [dkey], [dkey])
    sinlayer(zt, w1s, 33, 0, h1, 'zt', 'h1')
    sinlayer(h1, w2s, 64, 1, h2, 'h1', 'h2')
    dect = [sb([128, 512], F32, "dect") for _ in range(2)]
    ftmp = sb([128, 512], F32, "ftmp")
    ntn = C('ntn')
    for chunk in range(32):
        dt_ = dect[chunk % 2]
        dk = ('dect', chunk % 2)
        act(dt_[:], C('delta'), AF.Exp, ['cst'], [dk], scale=ntn[:, chunk:chunk + 1])
        for half in range(2):
            b = nps()
            mm(ps[b][:], h2[:, chunk * 128:(chunk + 1) * 128], w3s[:, half * 512:(half + 1) * 512], True, True, ['h2', 'fw'], [('ps', b)])
            dst = hcat[:, chunk, half * 512:(half + 1) * 512]
            if chunk == 0:
                tt('dve', ftmp[:], ps[b][:], dt_[:], ALU.mult, [('ps', b), dk], ['ftmp'])
                if half == 0:
                    tt('dve', ftmp[0:1, :], ftmp[0:1, :], hydbc[0:1, :], ALU.add, ['ftmp', 'hydbc'], ['ftmp'])
                else:
                    S.op('dve', lambda e: e.memset(ftmp[0:1, :], 0.0), ['ftmp'], ['ftmp'])
                cp('dve', dst, ftmp[:], ['ftmp'], ['hcat'])
            else:
                tt('dve', dst, ps[b][:], dt_[:], ALU.mult, [('ps', b), dk], ['hcat'])

    S.barrier()
    top[0] = MARKH
    u_tm = sb([128, 32, 512], BF16, "u_tm")
    for q8 in range(8):
        S.dma('sp', u_tm[:, q8 * 4:(q8 + 1) * 4, :], sc_u[:, q8 * 4:(q8 + 1) * 4, :], reads=[('sc_u', q8)], writes=['u_tm'])
    tabs = [sb([128, 32, 256], BF16, "tabf") for _ in range(2)]
    Ast = sb([128, 2, 512], F32, "Ast"); Pst = sb([128, 2, 512], F32, "Pst")
    tq = [sb([128, 512], F32, "tq") for _ in range(6)]
    yst = [sb([128, 512], BF16, "yst") for _ in range(4)]
    SC = 2.0 / NFFT
    ysi = [0]

    def ynext():
        i_ = ysi[0] % 4
        ysi[0] += 1
        return yst[i_], ('yst', i_)
    for fb in range(16):
        tbi, hf_ = fb // 2, fb % 2
        for cs in range(2):
            tab = tabs[cs]
            tk = ('tab', cs)
            S.dma('sp', tab[:], (ctab_d if cs == 0 else stab_d)[tbi][:, :, hf_ * 256:(hf_ + 1) * 256], writes=[tk])
            for m in range(2):
                bb = [nps(), nps(), nps()]
                for chunk in range(32):
                    lhsT = tab[:, chunk, m * 128:(m + 1) * 128]
                    mm(ps[bb[0]][:], lhsT, u_tm[:, chunk, :], chunk == 0, chunk == 31, [tk, 'u_tm'], [('ps', bb[0])])
                    mm(ps[bb[1]][:], lhsT, hcat[:, chunk, 0:512], chunk == 0, chunk == 31, [tk, 'hcat'], [('ps', bb[1])])
                    mm(ps[bb[2]][:], lhsT, hcat[:, chunk, 512:1024], chunk == 0, chunk == 31, [tk, 'hcat'], [('ps', bb[2])])
                ft = fb * 2 + m
                if cs == 0:
                    cp('act', Ast[:, m, :], ps[bb[0]][:], [('ps', bb[0])], [('Ast', m)])
                    cp('act', tq[0][:], ps[bb[2]][:], [('ps', bb[2])], [('tq', 0)])
                    tt('dve', Pst[:, m, :], ps[bb[1]][:], tq[0][:], ALU.add, [('ps', bb[1]), ('tq', 0)], [('Pst', m)])
                else:
                    cp('act', tq[1][:], ps[bb[2]][:], [('ps', bb[2])], [('tq', 1)])
                    tt('dve', tq[2][:], ps[bb[1]][:], tq[1][:], ALU.subtract, [('ps', bb[1]), ('tq', 1)], [('tq', 2)])
                    tt('pool', tq[3][:], Ast[:, m, :], Pst[:, m, :], ALU.mult, [('Ast', m), ('Pst', m)], [('tq', 3)])
                    stt('dve', tq[4][:], ps[bb[0]][:], SC, tq[2][:], ALU.mult, ALU.mult, [('ps', bb[0]), ('tq', 2)], [('tq', 4)])
                    yr, yrk = ynext()
                    stt('dve', yr[:], tq[3][:], SC, tq[4][:], ALU.mult, ALU.subtract, [('tq', 3), ('tq', 4)], [yrk])
                    if ft == 0:
                        ts('dve', yr[0:1, :], yr[0:1, :], 0.5, None, ALU.mult, None, [yrk], [yrk])
                    S.dma('sp', sc_Y[:, ft, :], yr[:], reads=[yrk], writes=[('sc_Y', ft)])
                    tt('pool', tq[3][:], Ast[:, m, :], tq[2][:], ALU.mult, [('Ast', m), ('tq', 2)], [('tq', 3)])
                    stt('dve', tq[5][:], ps[bb[0]][:], SC, Pst[:, m, :], ALU.mult, ALU.mult, [('ps', bb[0]), ('Pst', m)], [('tq', 5)])
                    yq, yqk = ynext()
                    stt('dve', yq[:], tq[3][:], SC, tq[5][:], ALU.mult, ALU.add, [('tq', 3), ('tq', 5)], [yqk])
                    S.dma('sp', sc_Y[:, 32 + ft, :], yq[:], reads=[yqk], writes=[('sc_Y', 32 + ft)])
    bn = [nps(), nps(), nps()]
    for chunk in range(32):
        mm(ps[bn[0]][0:1, :], altb[:, 0:1], u_tm[:, chunk, :], chunk == 0, chunk == 31, ['altb', 'u_tm'], [('ps', bn[0])])
        mm(ps[bn[1]][0:1, :], altb[:, 0:1], hcat[:, chunk, 0:512], chunk == 0, chunk == 31, ['altb', 'hcat'], [('ps', bn[1])])
        mm(ps[bn[2]][0:1, :], altb[:, 0:1], hcat[:, chunk, 512:1024], chunk == 0, chunk == 31, ['altb', 'hcat'], [('ps', bn[2])])
    cp('act', tq[0][0:1, :], ps[bn[2]][0:1, :], [('ps', bn[2])], [('tq', 0)])
    tt('dve', tq[1][0:1, :], ps[bn[1]][0:1, :], tq[0][0:1, :], ALU.add, [('ps', bn[1]), ('tq', 0)], [('tq', 1)])
    stt('dve', ynq[:], ps[bn[0]][0:1, :], 1.0 / NFFT, tq[1][0:1, :], ALU.mult, ALU.mult, [('ps', bn[0]), ('tq', 1)], ['ynq'])

    S.barrier()
    top[0] = MARK0
    Yt = sb([128, 64, 512], BF16, "Yt")
    for q8 in range(8):
        S.dma('sp', Yt[:, q8 * 8:(q8 + 1) * 8, :], sc_Y[:, q8 * 8:(q8 + 1) * 8, :],
              reads=[('sc_Y', f_) for f_ in range(q8 * 8, q8 * 8 + 8)], writes=[('Yt', q8)])
    tabi = [sb([128, 32, 512], BF16, "tabi") for _ in range(2)]
    x0s = [sb([128, 512], BF16, "x0s") for _ in range(2)]
    yo = [sb([128, 512], BF16, "yo") for _ in range(2)]
    for tb in range(8):
        tsl = slice(tb * 512, (tb + 1) * 512)
        S.dma('sp', tabi[0][:], ctab_d[tb], writes=[('tabi', 0)])
        S.dma('sp', tabi[1][:], stab_d[tb], writes=[('tabi', 1)])
        banks = [nps() for _ in range(4)]
        for cs in range(2):
            for ct in range(4):
                for chunk in range(32):
                    mm(ps[banks[ct]][:], Yt[:, cs * 32 + chunk, ct * 128:(ct + 1) * 128], tabi[cs][:, chunk, :],
                       cs == 0 and chunk == 0, False, [('Yt', (cs * 32 + chunk) // 8), ('tabi', cs)], [('ps', banks[ct])], inc=False)
        for ct in range(4):
            mm(ps[banks[ct]][:], ynq[0:1, ct * 128:(ct + 1) * 128], altrowb[0:1, :], False, True, ['ynq', 'altrowb'], [('ps', banks[ct])], inc=True)
            s2 = ct % 2
            S.dma('sp', x0s[s2][:], sc_x0[ct, :, tsl], reads=[('sc_x0', ct, tb)], writes=[('x0s', s2)])
            tt('dve', yo[s2][:], ps[banks[ct]][:], x0s[s2][:], ALU.mult, [('ps', banks[ct]), ('x0s', s2)], [('yo', s2)])
            S.dma('sp', sc_yx0[ct, :, tsl], yo[s2][:], reads=[('yo', s2)], writes=[('sc_yx0', ct, tb)])
    if stop_after == 'H':
        for k_, v_ in list(S.lastw.items()):
            if isinstance(k_, tuple) and str(k_[0]).startswith('sc_'):
                S.out_toks.append(v_)
        S.emit()
        return nc

    S.barrier()
    top[0] = MARK0
    tmb = [sb([128, 1536], F32, "tmb") for _ in range(2)]
    vb = [sb([128, 512], BF16, "vb") for _ in range(2)]
    ebuf = sb([128, 512], F32, "ebuf"); lp = sb([128, 512], F32, "lp")
    lnk = [sb([128, 512], F32, "lnk") for _ in range(2)]
    logf = [sb([128, 512], F32, "logf") for _ in range(2)]
    Ea = [sb([128, 4, 128], BF16, "Ea") for _ in range(2)]
    Eb = [sb([128, 4, 128], BF16, "Eb") for _ in range(2)]
    keT = [sb([128, 512], BF16, "keT") for _ in range(2)]
    kd = [sb([128, 512], BF16, "kd") for _ in range(2)]
    dec = [sb([128, 4], F32, "dec") for _ in range(2)]
    Sf = [sb([128, 4, 128], F32, "Sf") for _ in range(2)]
    Sb16 = [sb([128, 4, 128], BF16, "Sb16") for _ in range(2)]
    sbs = sb([128, 32, 512], BF16, "sbs")
    qtb = [sb([128, 4, 128], BF16, "qtb") for _ in range(2)]
    gtb = [sb([128, 4, 128], BF16, "gtb") for _ in range(2)]
    qa = [sb([128, 4, 128], BF16, "qa") for _ in range(2)]
    qe = [sb([128, 4, 128], BF16, "qe") for _ in range(2)]
    scT = [sb([128, 512], BF16, "scT") for _ in range(2)]
    sqb = sb([128, 512], BF16, "sqb"); rs = sb([128, 512], F32, "rs"); otf = sb([128, 512], F32, "otf")
    atb = [sb([128, 512], BF16, "atb") for _ in range(2)]
    flat = lambda a_: a_.rearrange("p a b -> p (a b)")
    for d in range(2):
        S.op('pool', lambda e, t_=Sf[d]: e.memset(flat(t_), 0.0), [], [('Sf', d)])
        S.op('pool', lambda e, t_=Sb16[d]: e.memset(flat(t_), 0.0), [], [('Sb', d)])

    def gate(d, tm, tmk):
        zz = tm[:, 512 * (1 + d):512 * (2 + d)]
        act(ebuf[:], zz, AF.Exp, [tmk], ['ebuf'], scale=-1.0)
        act(lp[:], ebuf[:], AF.Ln, ['ebuf'], ['lp'], bias=1.0)
        stt('dve', lnk[d][:], zz, -1.0, c1bc[:], ALU.mult, ALU.add, [tmk, 'c1bc'], [('lnk', d)])
        tt('dve', lnk[d][:], lnk[d][:], lp[:], ALU.subtract, [('lnk', d), 'lp'], [('lnk', d)])
        act(ebuf[:], lnk[d][:], AF.Exp, [('lnk', d)], ['ebuf'])
        act(logf[d][:], ebuf[:], AF.Ln, ['ebuf'], [('logf', d)], scale=-1.0, bias=1.0)
        sfx = 'f' if d == 0 else 'b'
        oi = coffs['Mincl_' + sfx][0]
        Mcat = cst[:, oi:oi + 256]
        nMb = C('nMb_' + sfx)
        Mkd = C('Mkd_' + sfx)
        bab = [nps(), nps()]
        for h in range(4):
            mm(ps[bab[h // 2]][:, (h % 2) * 256:(h % 2) * 256 + 256], logf[d][:, h * 128:(h + 1) * 128], Mcat, True, True,
               [('logf', d), 'cst'], [('ps', bab[h // 2])], inc=(h % 2 == 1))
        bke = nps()
        for h in range(4):
            mm(ps[bke][:, h * 128:(h + 1) * 128], lnk[d][:, h * 128:(h + 1) * 128], ident, True, False, [('lnk', d), 'cst'], [('ps', bke)], inc=False)
            mm(ps[bke][:, h * 128:(h + 1) * 128], logf[d][:, h * 128:(h + 1) * 128], nMb, False, True, [('logf', d), 'cst'], [('ps', bke)], inc=(h == 3))
        bkd = nps()
        mm(ps[bkd][:], Mkd, logf[d][:], True, False, [('logf', d), 'cst'], [('ps', bkd)], inc=False)
        mm(ps[bkd][:], ident, lnk[d][:], False, True, [('lnk', d), 'cst'], [('ps', bkd)], inc=True)
        col = 127 if d == 0 else 0
        for j in range(2):
            v3 = ps[bab[j]][:].rearrange("p (h x) -> p h x", x=256)
            act(Ea[d][:, 2 * j:2 * j + 2, :], v3[:, :, 0:128], AF.Exp, [('ps', bab[j])], [('Ea', d)])
            act(Eb[d][:, 2 * j:2 * j + 2, :], v3[:, :, 128:256], AF.Exp, [('ps', bab[j])], [('Eb', d)])
            for hh in range(2):
                act(dec[d][:, 2 * j + hh:2 * j + hh + 1], ps[bab[j]][:, hh * 256 + col:hh * 256 + col + 1], AF.Exp, [('ps', bab[j])], [('dec', d)])
        act(keT[d][:], ps[bke][:], AF.Exp, [('ps', bke)], [('keT', d)])
        act(kd[d][:], ps[bkd][:], AF.Exp, [('ps', bkd)], [('kd', d)])

    def update(d, vt, vk):
        b = nps()
        for h in range(4):
            mm(ps[b][:, h * 128:(h + 1) * 128], kd[d][:, h * 128:(h + 1) * 128], vt[:, h * 128:(h + 1) * 128], True, True,
               [('kd', d), vk], [('ps', b)], inc=(h == 3))
        for h in range(4):
            stt('dve', Sf[d][:, h, :], Sf[d][:, h, :], dec[d][:, h:h + 1], ps[b][:, h * 128:(h + 1) * 128], ALU.mult, ALU.add,
                [('Sf', d), ('dec', d), ('ps', b)], [('Sf', d)])
        cp('pool', flat(Sb16[d]), flat(Sf[d]), [('Sf', d)], [('Sb', d)])

    def load_tm(i):
        s2 = i % 2
        S.dma('sp', tmb[s2][:], sc_tm[i], reads=[('sc_tm', i)], writes=[('tmb', s2)])
        cp('pool', vb[s2][:], tmb[s2][:, 0:512], [('tmb', s2)], [('vb', s2)])
        return tmb[s2], ('tmb', s2), vb[s2], ('vb', s2)
    for i in (0, 1):
        tm, tmk, vt, vk = load_tm(i)
        gate(0, tm, tmk)
        update(0, vt, vk)
    for i in (1, 0):
        tm, tmk, vt, vk = load_tm(i)
        gate(1, tm, tmk)
        update(1, vt, vk)
    for n in range(31, -1, -1):
        cp('pool', sbs[:, n, :], flat(Sb16[1]), [('Sb', 1)], [('sbs', n)])
        tm, tmk, vt, vk = load_tm(n + 2)
        gate(1, tm, tmk)
        update(1, vt, vk)
    for n in range(32):
        tm, tmk, vt, vk = load_tm(n + 2)
        q2 = n % 2
        tsl = slice(n * 128, (n + 1) * 128)
        S.dma('sp', qtb[q2][:], sc_q[:, :, tsl].rearrange("h p t -> p h t"), reads=[('sc_qg', h_, n // 4) for h_ in range(4)], writes=[('qtb', q2)])
        S.dma('sp', gtb[q2][:], sc_g[:, :, tsl].rearrange("h p t -> p h t"), reads=[('sc_qg', 4 + h_, n // 4) for h_ in range(4)], writes=[('gtb', q2)])
        for d in range(2):
            gate(d, tm, tmk)
            en = 'dve' if d == 0 else 'pool'
            tt(en, flat(qa[d]), flat(qtb[q2]), flat(Ea[d]), ALU.mult, [('qtb', q2), ('Ea', d)], [('qa', d)])
            tt(en, flat(qe[d]), flat(qtb[q2]), flat(Eb[d]), ALU.mult, [('qtb', q2), ('Eb', d)], [('qe', d)])
            bs = nps()
            for h in range(4):
                mm(ps[bs][:, h * 128:(h + 1) * 128], keT[d][:, h * 128:(h + 1) * 128], qe[d][:, h, :], True, True,
                   [('keT', d), ('qe', d)], [('ps', bs)], inc=(h == 3))
            mk = maskf if d == 0 else maskb
            tt('dve', scT[d][:], ps[bs][:], mk[:], ALU.mult, [('ps', bs), 'maskf', 'maskb'], [('scT', d)])
        bo = nps()
        for h in range(4):
            o_ = ps[bo][:, h * 128:(h + 1) * 128]
            hs = slice(h * 128, (h + 1) * 128)
            mm(o_, vt[:, hs], scT[0][:, hs], True, False, [vk, ('scT', 0)], [('ps', bo)], inc=False)
            mm(o_, vt[:, hs], scT[1][:, hs], False, False, [vk, ('scT', 1)], [('ps', bo)], inc=False)
            mm(o_, Sb16[0][:, h, :], qa[0][:, h, :], False, False, [('Sb', 0), ('qa', 0)], [('ps', bo)], inc=False)
            mm(o_, sbs[:, n, hs], qa[1][:, h, :], False, True, [('sbs', n), ('qa', 1)], [('ps', bo)], inc=(h == 3))
        act(sqb[:], ps[bo][:], AF.Square, [('ps', bo)], ['sqb'])
        bss = nps()
        mm(ps[bss][:], onesb[:], sqb[:], True, True, ['onesb', 'sqb'], [('ps', bss)])
        act(rs[:], ps[bss][:], AF.Ln, [('ps', bss)], ['rs'], scale=1.0 / 128, bias=EPS)
        act(rs[:], rs[:], AF.Exp, ['rs'], ['rs'], scale=-0.5)
        tt('dve', otf[:], ps[bo][:], rs[:], ALU.mult, [('ps', bo), 'rs'], ['otf'])
        stt('dve', atb[q2][:], otf[:], SM('gn'), flat(gtb[q2]), ALU.mult, ALU.mult, ['otf', 'sm', ('gtb', q2)], [('atb', q2)])
        S.dma('sp', sc_a[:, :, tsl].rearrange("h p t -> p h t"), atb[q2][:].rearrange("p (h t) -> p h t", t=128),
              reads=[('atb', q2)], writes=[('sc_a', n)])
        update(0, vt, vk)
    if stop_after == 'G':
        for k_, v_ in list(S.lastw.items()):
            if isinstance(k_, tuple) and str(k_[0]).startswith('sc_'):
                S.out_toks.append(v_)
        S.emit()
        return nc

    S.barrier()
    top[0] = MARK0
    bcv = sb([128, 4, 1024], F32, "bcv")
    for vi in range(4):
        for k in range(8):
            bcast(bcv[:, vi, k * 128:(k + 1) * 128], fv[:, 4 + vi, k:k + 1], 'bcv', 'fv')
    wpa = sb([128, 4, 1024], BF16, "wpa"); wpb = sb([128, 4, 1024], BF16, "wpb"); wout = sb([128, 8, 1024], BF16, "wout")
    S.dma('pool', wpa[:], w_pa.rearrange("(k p) n -> p k n", p=128), writes=['wpa'])
    S.dma('pool', wpb[:], w_pb.rearrange("(k p) n -> p k n", p=128), writes=['wpb'])
    for k2 in range(2):
        S.dma('pool', wout[:, :, k2 * 512:(k2 + 1) * 512], w_out[:, k2 * 512:(k2 + 1) * 512].rearrange("(k p) n -> p k n", p=128), writes=['wout'])
    aTb = [sb([128, 4, 512], BF16, "aTb") for _ in range(2)]
    yxb = [sb([128, 4, 512], BF16, "yxb") for _ in range(2)]
    ggb = [sb([128, 16, 512], BF16, "ggb") for _ in range(2)]
    merged = sb([128, 8, 512], BF16, "merged")
    m1 = [sb([128, 512], F32, "m1") for _ in range(2)]
    xt = [sb([128, 1024], F32, "xt") for _ in range(2)]
    xl = [sb([128, 1024], F32, "xl") for _ in range(2)]
    tmpf = sb([128, 1024], F32, "tmpf")
    hlb = sb([128, 1024], BF16, "hlb")
    hls = [sb([128, 8, 512], BF16, "hls") for _ in range(2)]
    junk2 = sb([128, 1024], BF16, "junk2")
    st2 = [sb([128, 8], F32, "st2") for _ in range(2)]
    for tb in range(8):
        tsl = slice(tb * 512, (tb + 1) * 512)
        b2 = tb % 2
        S.dma('sp', aTb[b2][:], sc_a[:, :, tsl].rearrange("h p t -> p h t"), reads=[('sc_a', tb * 4 + j_) for j_ in range(4)], writes=[('aTb', b2)])
        S.dma('sp', yxb[b2][:], sc_yx0[:, :, tsl].rearrange("h p t -> p h t"), reads=[('sc_yx0', c_, tb) for c_ in range(4)], writes=[('yxb', b2)])
        S.dma('sp', ggb[b2][:], sc_gate[:, :, tsl].rearrange("f p t -> p f t"), reads=[('sc_gate', f_, tb) for f_ in range(16)], writes=[('ggb', b2)])
        for dm in range(8):
            bA = nps()
            for k in range(4):
                mm(ps[bA][:], wpa[:, k, dm * 128:(dm + 1) * 128], aTb[b2][:, k, :], k == 0, k == 3, ['wpa', ('aTb', b2)], [('ps', bA)])
            bB = nps()
            for k in range(4):
                mm(ps[bB][:], wpb[:, k, dm * 128:(dm + 1) * 128], yxb[b2][:, k, :], k == 0, k == 3, ['wpb', ('yxb', b2)], [('ps', bB)])
            tt('dve', m1[0][:], ps[bA][:], ggb[b2][:, dm, :], ALU.mult, [('ps', bA), ('ggb', b2)], [('m1', 0)])
            tt('dve', m1[1][:], ps[bB][:], ggb[b2][:, 8 + dm, :], ALU.mult, [('ps', bB), ('ggb', b2)], [('m1', 1)])
            tt('pool', merged[:, dm, :], m1[0][:], m1[1][:], ALU.add, [('m1', 0), ('m1', 1)], [('merged', dm)])
        hs_ = hls[b2]
        for t4 in range(4):
            n = tb * 4 + t4
            s2 = n % 2
            S.dma('sp', xt[s2][:], x[n * 128:(n + 1) * 128, :], writes=[('xt', s2)])
            S.op('pool', lambda e, t_=st2[s2]: e.memset(t_[:, 0:2], 0.0), [], [('st2', s2)])
            bh = [nps(), nps()]
            for half in range(2):
                for k in range(8):
                    mm(ps[bh[half]][:], merged[:, k, t4 * 128:(t4 + 1) * 128], wout[:, k, half * 512:(half + 1) * 512], k == 0, k == 7,
                       [('merged', k), 'wout'], [('ps', bh[half])])
                act(junk2[:, 0:512], ps[bh[half]][:], AF.Square, [('ps', bh[half]), ('st2', s2)], ['junk2', ('st2', s2)], accum_out=st2[s2][:, half:half + 1])
            tt('dve', st2[s2][:, 2:3], st2[s2][:, 0:1], st2[s2][:, 1:2], ALU.add, [('st2', s2)], [('st2', s2)])
            act(st2[s2][:, 3:4], st2[s2][:, 2:3], AF.Ln, [('st2', s2)], [('st2', s2)], scale=1.0 / D, bias=EPS)
            act(st2[s2][:, 4:5], st2[s2][:, 3:4], AF.Exp, [('st2', s2)], [('st2', s2)], scale=-0.5)
            for half in range(2):
                hsl = slice(half * 512, (half + 1) * 512)
                stt('dve', tmpf[:, hsl], ps[bh[half]][:], st2[s2][:, 4:5], bcv[:, 0, hsl], ALU.mult, ALU.mult, [('ps', bh[half]), ('st2', s2), 'bcv'], ['tmpf'])
            tt('pool', xl[s2][:], tmpf[:], xt[s2][:], ALU.add, ['tmpf', ('xt', s2)], [('xl', s2)])
            S.dma('sp', sc_xlat[n * 128:(n + 1) * 128, :], xl[s2][:], reads=[('xl', s2)], writes=[('sc_xlat', n)])
            S.op('pool', lambda e, t_=st2[s2]: e.memset(t_[:, 5:6], 0.0), [], [('st2', s2)])
            act(junk2[:], xl[s2][:], AF.Square, [('xl', s2), ('st2', s2)], ['junk2', ('st2', s2)], accum_out=st2[s2][:, 5:6])
            act(st2[s2][:, 6:7], st2[s2][:, 5:6], AF.Ln, [('st2', s2)], [('st2', s2)], scale=1.0 / D, bias=EPS)
            act(st2[s2][:, 7:8], st2[s2][:, 6:7], AF.Exp, [('st2', s2)], [('st2', s2)], scale=-0.5)
            stt('dve', tmpf[:], xl[s2][:], st2[s2][:, 7:8], bcv[:, 1, :], ALU.mult, ALU.mult, [('xl', s2), ('st2', s2), 'bcv'], ['tmpf'])
            tt('pool', hlb[:], tmpf[:], bcv[:, 2, :], ALU.add, ['tmpf', 'bcv'], ['hlb'])
            bt = nps()
            pbt = ps[bt][:].bitcast(BF16)
            for k in range(8):
                tr(pbt[:, k * 128:(k + 1) * 128], hlb[:, k * 128:(k + 1) * 128], idb[:], ['hlb', 'idb'], [('ps', bt)], inc=(k == 7))
            cp('act', hs_[:, :, t4 * 128:(t4 + 1) * 128], pbt[:, 0:1024].rearrange("p (a b) -> p a b", b=128), [('ps', bt)], [('hls', b2)])
        S.dma('sp', sc_hl[:, :, tsl].rearrange("k p t -> p k t"), hs_[:], reads=[('hls', b2)], writes=[('sc_hl', tb)])

    S.barrier()
    top[0] = MARK0
    g5bc = sb([128, 1024], F32, "g5bc")
    for k in range(8):
        bcast(g5bc[:, k * 128:(k + 1) * 128], fv[:, 7, k:k + 1], 'g5bc', 'fv')
    w1 = sb([128, 8, 4096], BF16, "w1"); w2 = sb([128, 32, 1024], BF16, "w2")
    for k8 in range(8):
        S.dma('pool', w1[:, :, k8 * 512:(k8 + 1) * 512], w_mlp1[:, k8 * 512:(k8 + 1) * 512].rearrange("(k p) n -> p k n", p=128), writes=[('w1', k8)])
    for k8 in range(8):
        S.dma('pool', w2[:, k8 * 4:(k8 + 1) * 4, :], w_mlp2[k8 * 512:(k8 + 1) * 512, :].rearrange("(k p) n -> p k n", p=128), writes=[('w2', k8)])
    hlt = [sb([128, 8, 256], BF16, "hlt") for _ in range(2)]
    hid = sb([128, 32, 256], BF16, "hid")
    rr = [sb([128, 256], F32, "rr") for _ in range(2)]
    xl2 = [sb([128, 1024], F32, "xl2") for _ in range(2)]
    tmp2 = sb([128, 1024], F32, "tmp2")
    ot = [sb([128, 1024], F32, "ot") for _ in range(2)]
    junk3 = sb([128, 512], BF16, "junk3")
    st3 = [sb([128, 8], F32, "st3") for _ in range(2)]
    for tb in range(16):
        b2 = tb % 2
        tsl = slice(tb * 256, (tb + 1) * 256)
        S.dma('sp', hlt[b2][:], sc_hl[:, :, tsl].rearrange("k p t -> p k t"), reads=[('sc_hl', tb // 2)], writes=[('hlt', b2)])
        for ff in range(32):
            b = nps()
            for k in range(8):
                mm(ps[b][:, 0:256], w1[:, k, ff * 128:(ff + 1) * 128], hlt[b2][:, k, :], k == 0, k == 7, [('w1', ff // 4), ('hlt', b2)], [('ps', b)])
            r2 = ff % 2
            act(rr[r2][:], ps[b][:, 0:256], AF.Relu, [('ps', b)], [('rr', r2)])
            tt('pool' if ff % 4 else 'dve', hid[:, ff, :], rr[r2][:], rr[r2][:], ALU.mult, [('rr', r2)], [('hid', ff)])
        for t2 in range(2):
            n = tb * 2 + t2
            s2 = n % 2
            S.dma('sp', xl2[s2][:], sc_xlat[n * 128:(n + 1) * 128, :], reads=[('sc_xlat', n)], writes=[('xl2', s2)])
            S.op('pool', lambda e, t_=st3[s2]: e.memset(t_[:, 0:2], 0.0), [], [('st3', s2)])
            bh = [nps(), nps()]
            for half in range(2):
                for k in range(32):
                    mm(ps[bh[half]][:], hid[:, k, t2 * 128:(t2 + 1) * 128], w2[:, k, half * 512:(half + 1) * 512], k == 0, k == 31,
                       [('hid', k), ('w2', k // 4)], [('ps', bh[half])])
                act(junk3[:], ps[bh[half]][:], AF.Square, [('ps', bh[half]), ('st3', s2)], ['junk3', ('st3', s2)], accum_out=st3[s2][:, half:half + 1])
            tt('dve', st3[s2][:, 2:3], st3[s2][:, 0:1], st3[s2][:, 1:2], ALU.add, [('st3', s2)], [('st3', s2)])
            act(st3[s2][:, 3:4], st3[s2][:, 2:3], AF.Ln, [('st3', s2)], [('st3', s2)], scale=1.0 / D, bias=EPS)
            act(st3[s2][:, 4:5], st3[s2][:, 3:4], AF.Exp, [('st3', s2)], [('st3', s2)], scale=-0.5)
            for half in range(2):
                hsl = slice(half * 512, (half + 1) * 512)
                stt('dve', tmp2[:, hsl], ps[bh[half]][:], st3[s2][:, 4:5], g5bc[:, hsl], ALU.mult, ALU.mult, [('ps', bh[half]), ('st3', s2), 'g5bc'], ['tmp2'])
            tt('pool', ot[s2][:], tmp2[:], xl2[s2][:], ALU.add, ['tmp2', ('xl2', s2)], [('ot', s2)])
            S.dma('sp', out[n * 128:(n + 1) * 128, :], ot[s2][:], reads=[('ot', s2)], is_out=True)
    S.emit()
    return nc


def make_in_maps(inputs):
    carr, coffs, zT, ctab, stab = get_hc()
    g = lambda k: np.ascontiguousarray(np.asarray(inputs[k], dtype=np.float32))
    maps = []
    for b in range(8):
        m = {
            "x": g('x')[b], "ctx": g('ctx')[b],
            "cvec": np.ascontiguousarray(np.stack([g('c')[b], g('c_ctx')], axis=0)),
            "w_ada": g('w_ada')[0], "b_ada": g('b_ada')[0], "norm_w": g('norm_w')[0], "w_in": g('w_in')[0],
            "lb_param": g('lb_param'), "g_norm": g('g_norm')[0], "hy_short_w": g('hy_short_w')[0],
            "hy_short_b": g('hy_short_b')[0], "f_w1": g('f_w1')[0], "f_b1": g('f_b1')[0], "f_w2": g('f_w2')[0],
            "f_b2": g('f_b2')[0], "f_w3": g('f_w3')[0], "f_freq": g('f_freq')[0], "hy_d": g('hy_d')[0],
            "w_pa": g('w_pa')[0], "w_pb": g('w_pb')[0], "w_out": g('w_out')[0], "w_mlp1": g('w_mlp1')[0],
            "w_mlp2": g('w_mlp2')[0], "consts": carr, "zT": zT, "ctab": ctab, "stab": stab,
        }
        maps.append(m)
    return maps


def kernel(**inputs):
    nc = build_program()
    maps = make_in_maps(inputs)
    res = run_bass_kernel_spmd(nc, maps, core_ids=list(range(8)))
    return np.stack([np.asarray(r["out"], dtype=np.float32) for r in res.results], axis=0)
```

```python
import contextlib
import numpy as np
import ml_dtypes
import concourse.bass as bass
import concourse.mybir as mybir
from concourse.bass_utils import run_bass_kernel_spmd

F32 = mybir.dt.float32
BF16 = mybir.dt.bfloat16
AF = mybir.ActivationFunctionType
ALU = mybir.AluOpType

D = 1024
L = 4096
CTX = 256
NT = 32
NTA = 34
EPS = 1e-6
NFFT = 8192
PI = float(np.pi)


class Sched:
    ENG = ('pe', 'act', 'dve', 'pool', 'sp')

    def __init__(self, nc, ndma=12, nswd=6):
        self.nc = nc
        self.ops = {e: [] for e in self.ENG}
        self.cnt = {e: 0 for e in self.ENG}
        self.known = {e: {} for e in self.ENG}
        self.lastw = {}
        self.readers = {}
        self.ndma = ndma + nswd
        self.nhw = ndma
        self.nswd = nswd
        self.dma_n = 0
        self.swd_n = 0
        self.dma_cnt = [0] * (ndma + nswd)
        self.out_toks = []

    def _need(self, eng, tok, waits, same_ok):
        if tok is None:
            return
        sem, val = tok
        if sem == eng and same_ok:
            return
        if self.known[eng].get(sem, 0) >= val:
            return
        if sem == eng:
            assert val <= self.cnt[eng], "same-engine wait on un-inc'd op"
        self.known[eng][sem] = val
        waits.append((sem, val))

    def _deps(self, eng, reads, writes):
        waits = []
        for k in reads:
            self._need(eng, self.lastw.get(k), waits, False)
        for k in writes:
            self._need(eng, self.lastw.get(k), waits, True)
            for s, v in self.readers.get(k, {}).items():
                self._need(eng, (s, v), waits, True)
        return waits

    def _commit(self, tok, reads, writes):
        for k in reads:
            d = self.readers.setdefault(k, {})
            if d.get(tok[0], 0) < tok[1]:
                d[tok[0]] = tok[1]
        for k in writes:
            self.lastw[k] = tok
            self.readers[k] = {}

    def op(self, eng, fn, reads=(), writes=(), inc=True):
        waits = self._deps(eng, reads, writes)
        if inc:
            self.cnt[eng] += 1
            tok = (eng, self.cnt[eng])
        else:
            tok = (eng, self.cnt[eng] + 1)
        self._commit(tok, reads, writes)
        self.ops[eng].append((waits, fn, eng if inc else None, 1))
        return tok

    def dma(self, q, out, in_, reads=(), writes=(), is_out=False, **kw):
        if q == 'pool':
            slot = self.nhw + self.swd_n % self.nswd
            self.swd_n += 1
        else:
            slot = self.dma_n % self.nhw
            self.dma_n += 1
        sem = 'dma%d' % slot
        waits = self._deps(q, reads, writes)
        prev = self.dma_cnt[slot]
        if prev > 0:
            self._need(q, (sem, prev), waits, False)
        self.dma_cnt[slot] += 16
        tok = (sem, self.dma_cnt[slot])
        self._commit(tok, reads, writes)
        self.ops[q].append((waits, lambda e: e.dma_start(out=out, in_=in_, **kw), sem, 16))
        if is_out:
            self.out_toks.append(tok)
        return tok

    def barrier(self):
        for e in self.ENG:
            waits = []
            for f in self.ENG[:4]:
                if f != e and self.cnt[f] > 0:
                    self._need(e, (f, self.cnt[f]), waits, False)
            for s in range(self.ndma):
                if self.dma_cnt[s] > 0:
                    self._need(e, ('dma%d' % s, self.dma_cnt[s]), waits, False)
            self.ops[e].append((waits, None, None, 0))

    def emit(self):
        nc = self.nc
        waits = []
        for tok in self.out_toks:
            self._need('sp', tok, waits, False)
        self.ops['sp'].append((waits, None, None, 0))
        for e in ('pe', 'act', 'dve', 'pool'):
            if self.ops[e]:
                last = [o for o in self.ops[e] if o[1] is not None][-1]
                assert last[2] is not None, "last op on %s must inc" % e
        names = list(self.ENG[:4]) + ['dma%d' % i for i in range(self.ndma)]
        with contextlib.ExitStack() as st:
            sems = {n: st.enter_context(nc.semaphore(n)) for n in names}
            block = st.enter_context(nc.Block())

            def run(ename):
                def body(e):
                    for waits, fn, incsem, incv in self.ops[ename]:
                        for s, v in waits:
                            e.wait_ge(sems[s], v)
                        if fn is not None:
                            ins = fn(e)
                            if incsem is not None:
                                ins.then_inc(sems[incsem], incv)
                return body
            block.tensor(run('pe'))
            block.scalar(run('act'))
            block.vector(run('dve'))
            block.gpsimd(run('pool'))
            block.sync(run('sp'))


def host_consts():
    p = np.arange(128)
    s = p[:, None]
    t = p[None, :]
    c = {}
    c['idf'] = (s == t)
    c['ones'] = np.ones((128, 128))
    c['Mincl_f'] = (s <= t)
    c['Mb_f'] = (s <= t).astype(np.float64) - (s <= 63)
    c['nMb_f'] = -c['Mb_f']
    c['Mkd_f'] = (s > t)
    c['Mincl_b'] = (s >= t)
    c['Mb_b'] = (s >= t).astype(np.float64) - (s >= 64)
    c['nMb_b'] = -c['Mb_b']
    c['Mkd_b'] = (s < t)
    c['mask_f'] = np.tile((s <= t), (1, 4))
    c['mask_b'] = np.tile((s >= t), (1, 4))
    deltas = np.abs(np.linspace(np.log(1e-2) / 1.5, np.log(1e-2) / 0.3, 512, dtype=np.float32)).astype(np.float64)
    c['delta'] = np.tile(deltas[None, :], (128, 1))
    tt = (np.arange(L).reshape(32, 128).T).astype(np.float64)
    c['ntn'] = -(tt / (L - 1))
    c['alt'] = np.tile(((-1.0) ** p)[:, None], (1, 2))
    c['altrow'] = np.tile(((-1.0) ** np.arange(512))[None, :], (128, 1))
    offs = {}
    cols = []
    o = 0
    for k, v in c.items():
        v = np.asarray(v, dtype=np.float32)
        offs[k] = (o, v.shape[1])
        cols.append(v)
        o += v.shape[1]
    arr = np.concatenate(cols, axis=1).astype(np.float32)
    pos = np.arange(L, dtype=np.float32)
    tn = pos / np.float32(L - 1)
    w = np.float32(2.0 * np.pi) * pos / np.float32(L)
    bands = np.linspace(1e-4, 15, 16, dtype=np.float32)
    ang = w[:, None] * bands[None, :]
    z = np.concatenate([tn[:, None], np.cos(ang), -np.sin(ang)], axis=-1).astype(np.float32)
    zT = np.ascontiguousarray(z.T)
    a = (np.arange(L, dtype=np.int64).reshape(32, 128).T)[None, :, :, None]
    b = np.arange(L, dtype=np.int64).reshape(8, 1, 1, 512)
    m = (a * b) % NFFT
    angm = (2.0 * np.pi / NFFT) * m
    ctab = np.cos(angm).astype(ml_dtypes.bfloat16)
    stab = np.sin(angm).astype(ml_dtypes.bfloat16)
    return arr, offs, zT, ctab, stab


_HC = None


def get_hc():
    global _HC
    if _HC is None:
        _HC = host_consts()
    return _HC


def build_program(debug=False, stop_after=None):
    carr, coffs, _, _, _ = get_hc()
    NCC = carr.shape[1]
    nc = bass.Bass("TRN2", target_bir_lowering=False)
    S = Sched(nc)

    def din(name, shape, dt=F32):
        return nc.dram_tensor(name, list(shape), dt, kind="ExternalInput").ap()

    def dscr(name, shape, dt):
        return nc.dram_tensor(name, list(shape), dt, kind=("ExternalOutput" if debug else "Internal")).ap()

    x = din("x", [L, D]); ctx = din("ctx", [CTX, D]); cvec = din("cvec", [2, D])
    w_ada = din("w_ada", [D, 6 * D]); b_ada = din("b_ada", [6 * D]); norm_w = din("norm_w", [4, D])
    w_in = din("w_in", [D, 6144]); lb_param = din("lb_param", [2, 512]); g_norm = din("g_norm", [128])
    hy_short_w = din("hy_short_w", [3, 1536]); hy_short_b = din("hy_short_b", [1536])
    f_w1 = din("f_w1", [33, 64]); f_b1 = din("f_b1", [64]); f_w2 = din("f_w2", [64, 64]); f_b2 = din("f_b2", [64])
    f_w3 = din("f_w3", [64, 1024]); f_freq = din("f_freq", [64]); hy_d = din("hy_d", [512])
    w_pa = din("w_pa", [512, D]); w_pb = din("w_pb", [512, D]); w_out = din("w_out", [D, D])
    w_mlp1 = din("w_mlp1", [D, 4 * D]); w_mlp2 = din("w_mlp2", [4 * D, D])
    consts_d = din("consts", [128, NCC]); zT_d = din("zT", [33, L])
    ctab_d = din("ctab", [8, 128, 32, 512], BF16); stab_d = din("stab", [8, 128, 32, 512], BF16)
    out = nc.dram_tensor("out", [L, D], F32, kind="ExternalOutput").ap()

    sc_tm = dscr("sc_tm", [NTA, 128, 1536], F32)
    sc_q = dscr("sc_q", [4, 128, L], BF16)
    sc_g = dscr("sc_g", [4, 128, L], BF16)
    sc_x0 = dscr("sc_x0", [4, 128, L], BF16)
    sc_u = dscr("sc_u", [128, 32, 512], BF16)
    sc_gate = dscr("sc_gate", [16, 128, L], BF16)
    sc_Y = dscr("sc_Y", [128, 64, 512], BF16)
    sc_yx0 = dscr("sc_yx0", [4, 128, L], BF16)
    sc_a = dscr("sc_a", [4, 128, L], BF16)
    sc_xlat = dscr("sc_xlat", [L, D], F32)
    sc_hl = dscr("sc_hl", [8, 128, L], BF16)

    ARENA = 102 * 1024
    arena = nc.alloc_sbuf_tensor("arena", [128, ARENA], BF16)
    top = [0]

    def sb(shape, dt, name=None):
        rows = shape[0]
        n = int(np.prod(shape[1:]))
        n16 = n * (2 if dt == F32 else 1)
        off = top[0]
        top[0] += (n16 + 15) // 16 * 16
        assert top[0] <= ARENA, ("SBUF arena overflow", name, top[0])
        ap = arena[0:rows, off:off + n16]
        if dt == F32:
            ap = ap.bitcast(F32)
        if len(shape) == 3:
            ap = ap.rearrange("p (a b) -> p a b", b=shape[2])
        return ap

    ps = [nc.alloc_psum_tensor("ps%d" % i, [128, 512], F32) for i in range(8)]
    psi = [0]

    def nps():
        i = psi[0] % 8
        psi[0] += 1
        return i

    def mm(o, lhsT, rhs, start, stop, reads, writes, inc=None):
        if inc is None:
            inc = stop
        S.op('pe', lambda e: e.matmul(o, lhsT, rhs, start=start, stop=stop), reads, writes, inc)

    def tr(o, i, ident, reads, writes, inc=True):
        S.op('pe', lambda e: e.transpose(o, i, ident), reads, writes, inc)

    def act(o, i, func, reads, writes, **kw):
        S.op('act', lambda e: e.activation(out=o, in_=i, func=func, **kw), reads, writes)

    def tt(eng, o, a, b, op, reads, writes):
        S.op(eng, lambda e: e.tensor_tensor(out=o, in0=a, in1=b, op=op), reads, writes)

    def ts(eng, o, a, s1, s2, op0, op1, reads, writes):
        if s2 is None:
            S.op(eng, lambda e: e.tensor_scalar(out=o, in0=a, scalar1=s1, scalar2=None, op0=op0), reads, writes)
        else:
            S.op(eng, lambda e: e.tensor_scalar(out=o, in0=a, scalar1=s1, scalar2=s2, op0=op0, op1=op1), reads, writes)

    def stt(eng, o, a, sc, b, op0, op1, reads, writes):
        S.op(eng, lambda e: e.scalar_tensor_tensor(out=o, in0=a, scalar=sc, in1=b, op0=op0, op1=op1), reads, writes)

    def cp(eng, o, i, reads, writes):
        if eng == 'act':
            act(o, i, AF.Copy, reads, writes)
        else:
            S.op(eng, lambda e: e.tensor_copy(o, i), reads, writes)

    cst = sb([128, NCC], F32, "cst")
    S.dma('sp', cst[:], consts_d, writes=['cst'])

    def C(name, rows=128):
        o, n = coffs[name]
        return cst[0:rows, o:o + n]
    idb = sb([128, 128], BF16, "idb"); onesb = sb([128, 128], BF16, "onesb")
    maskf = sb([128, 512], BF16, "maskf"); maskb = sb([128, 512], BF16, "maskb")
    altb = sb([128, 2], BF16, "altb"); altrowb = sb([1, 512], BF16, "altrowb")
    cp('dve', idb[:], C('idf'), ['cst'], ['idb'])
    cp('dve', onesb[:], C('ones'), ['cst'], ['onesb'])
    cp('dve', maskf[:], C('mask_f'), ['cst'], ['maskf'])
    cp('dve', maskb[:], C('mask_b'), ['cst'], ['maskb'])
    cp('dve', altb[:], C('alt'), ['cst'], ['altb'])
    cp('dve', altrowb[:], C('altrow', 1), ['cst'], ['altrowb'])
    ident = C('idf')

    NSL = dict(allow_slow_non_contiguous=True)
    sm = sb([128, 160], F32, "sm")
    so = {}
    o = [0]

    def smcol(name, n):
        so[name] = o[0]
        o[0] += n
        return sm[:, so[name]:so[name] + n]
    S.dma('sp', smcol('bada', 48), b_ada.rearrange("(t p) -> p t", p=128), writes=['sm'], **NSL)
    S.dma('sp', smcol('nw', 32), norm_w.rearrange("r (t p) -> p (r t)", p=128), writes=['sm'], **NSL)
    S.dma('sp', smcol('c', 8), cvec[0, :].rearrange("(t p) -> p t", p=128), writes=['sm'], **NSL)
    S.dma('sp', smcol('cc', 8), cvec[1, :].rearrange("(t p) -> p t", p=128), writes=['sm'], **NSL)
    S.dma('sp', smcol('lb', 8), lb_param.rearrange("r (t p) -> p (r t)", p=128), writes=['sm'], **NSL)
    S.dma('sp', smcol('gn', 1), g_norm.rearrange("(t p) -> p t", p=128), writes=['sm'], **NSL)
    S.dma('sp', smcol('hsw', 36), hy_short_w.rearrange("r (t p) -> p (r t)", p=128), writes=['sm'], **NSL)
    S.dma('sp', smcol('hsb', 12), hy_short_b.rearrange("(t p) -> p t", p=128), writes=['sm'], **NSL)
    S.dma('sp', smcol('hyd', 4), hy_d.rearrange("(t p) -> p t", p=128), writes=['sm'], **NSL)
    sm64 = sb([64, 4], F32, "sm64")
    S.dma('sp', sm64[:, 0:1], f_b1.rearrange("(p t) -> p t", t=1), writes=['sm64'], **NSL)
    S.dma('sp', sm64[:, 1:2], f_b2.rearrange("(p t) -> p t", t=1), writes=['sm64'], **NSL)
    S.dma('sp', sm64[:, 2:3], f_freq.rearrange("(p t) -> p t", t=1), writes=['sm64'], **NSL)

    def SM(name, i=0, n=1):
        return sm[:, so[name] + i: so[name] + i + n]

    cc2 = sb([128, 8, 2], F32, "cc2")
    cp('dve', cc2[:, :, 0], SM('c', 0, 8), ['sm'], ['cc2'])
    cp('dve', cc2[:, :, 1], SM('cc', 0, 8), ['sm'], ['cc2'])
    e2 = sb([128, 16], F32, "e2")
    cc2f = cc2[:].rearrange("p a b -> p (a b)")
    act(e2[:], cc2f, AF.Exp, ['cc2'], ['e2'], scale=-1.0)
    ts('dve', e2[:], e2[:], 1.0, None, ALU.add, None, ['e2'], ['e2'])
    S.op('dve', lambda e: e.reciprocal(e2[:], e2[:]), ['e2'], ['e2'])
    scb = sb([128, 8, 2], BF16, "scb")
    tt('dve', scb[:].rearrange("p a b -> p (a b)"), cc2f, e2[:], ALU.mult, ['cc2', 'e2'], ['scb'])
    mod = sb([128, 48, 2], F32, "mod")
    fv = sb([128, 8, 8], F32, "fv")
    c1f = sb([128, 4], F32, "c1f")
    c1bc = sb([128, 512], F32, "c1bc")
    hydbc = sb([128, 512], F32, "hydbc")
    diag = [sb([128, 128], F32, "diag") for _ in range(2)]
    ynq = sb([1, 512], BF16, "ynq")
    MARK0 = top[0]
    wad = [sb([128, 8, 512], BF16, "wad") for _ in range(2)]
    modp = nps()
    for pc in range(12):
        wb = wad[pc % 2]
        S.dma('pool', wb[:], w_ada[:, pc * 512:(pc + 1) * 512].rearrange("(k p) n -> p k n", p=128), writes=[('wad', pc % 2)])
        for j in range(4):
            ft = pc * 4 + j
            for k in range(8):
                mm(ps[modp][:, 2 * ft:2 * ft + 2], wb[:, k, j * 128:(j + 1) * 128], scb[:, k, :], k == 0, k == 7,
                   [('wad', pc % 2), 'scb'], [('ps', modp)], inc=(k == 7))
    cp('dve', mod[:].rearrange("p a b -> p (a b)"), ps[modp][:, 0:96], [('ps', modp)], ['mod'])
    for j in range(2):
        tt('dve', mod[:, :, j], mod[:, :, j], SM('bada', 0, 48), ALU.add, ['mod', 'sm'], ['mod'])
    def nwv(r):
        return SM('nw', 8 * r, 8)
    stt('dve', fv[:, 0, :], mod[:, 0:8, 0], 1.0, nwv(0), ALU.add, ALU.mult, ['mod', 'sm'], ['fv'])
    cp('dve', fv[:, 1, :], mod[:, 8:16, 0], ['mod'], ['fv'])
    stt('dve', fv[:, 2, :], mod[:, 0:8, 1], 1.0, nwv(0), ALU.add, ALU.mult, ['mod', 'sm'], ['fv'])
    cp('dve', fv[:, 3, :], mod[:, 8:16, 1], ['mod'], ['fv'])
    tt('dve', fv[:, 4, :], mod[:, 16:24, 0], nwv(1), ALU.mult, ['mod', 'sm'], ['fv'])
    stt('dve', fv[:, 5, :], mod[:, 24:32, 0], 1.0, nwv(2), ALU.add, ALU.mult, ['mod', 'sm'], ['fv'])
    cp('dve', fv[:, 6, :], mod[:, 32:40, 0], ['mod'], ['fv'])
    tt('dve', fv[:, 7, :], mod[:, 40:48, 0], nwv(3), ALU.mult, ['mod', 'sm'], ['fv'])
    tt('dve', c1f[:], SM('lb', 0, 4), SM('lb', 4, 4), ALU.subtract, ['sm'], ['c1f'])
    act(c1f[:], c1f[:], AF.Exp, ['c1f'], ['c1f'])
    act(c1f[:], c1f[:], AF.Ln, ['c1f'], ['c1f'], bias=1.0)
    ts('dve', c1f[:], c1f[:], -1.0, None, ALU.mult, None, ['c1f'], ['c1f'])
    onesf = C('ones')
    di = [0]

    def bcast(dst, col, dkey, rkey):
        d = diag[di[0] % 2]
        dk = ('diag', di[0] % 2)
        di[0] += 1
        ts('dve', d[:], ident, col, None, ALU.mult, None, ['cst', rkey], [dk])
        b = nps()
        mm(ps[b][:, 0:128], onesf, d[:], True, True, ['cst', dk], [('ps', b)])
        cp('act', dst, ps[b][:, 0:128], [('ps', b)], [dkey])
    for k in range(4):
        bcast(c1bc[:, k * 128:(k + 1) * 128], c1f[:, k:k + 1], 'c1bc', 'c1f')
        bcast(hydbc[:, k * 128:(k + 1) * 128], SM('hyd', k, 1), 'hydbc', 'sm')
    hT = sb([128, 8, NTA * 128], BF16, "hT")
    MARK1 = top[0]
    xb = [sb([128, 1024], F32, "xb") for _ in range(2)]
    junk = sb([128, 1024], BF16, "junk")
    xn = [sb([128, 1024], BF16, "xn") for _ in range(2)]
    st = [sb([128, 4], F32, "st") for _ in range(2)]
    wtm = sb([128, 8, 1536], BF16, "wtm")
    for k3 in range(3):
        S.dma('pool', wtm[:, :, k3 * 512:(k3 + 1) * 512], w_in[:, k3 * 512:(k3 + 1) * 512].rearrange("(k p) n -> p k n", p=128), writes=['wtm'])
    tms = [sb([128, 1536], F32, "tms") for _ in range(2)]

    def p1_tile(i):
        s2 = i % 2
        for k3 in range(3):
            b = nps()
            for k in range(8):
                mm(ps[b][:], hT[:, k, i * 128:(i + 1) * 128], wtm[:, k, k3 * 512:(k3 + 1) * 512], k == 0, k == 7,
                   [('hT', i), 'wtm'], [('ps', b)])
            cp('act' if k3 != 1 else 'dve', tms[s2][:, k3 * 512:(k3 + 1) * 512], ps[b][:], [('ps', b)], [('tms', s2)])
        S.dma('sp', sc_tm[i], tms[s2][:], reads=[('tms', s2)], writes=[('sc_tm', i)])
    for i in range(NTA):
        s2 = i % 2
        src = ctx[i * 128:(i + 1) * 128, :] if i < 2 else x[(i - 2) * 128:(i - 1) * 128, :]
        S.dma('sp', xb[s2][:], src, writes=[('xb', s2)])
        S.op('pool', lambda e, t_=st[s2]: e.memset(t_[:, 0:1], 0.0), [], [('st', s2)])
        act(junk[:], xb[s2][:], AF.Square, [('xb', s2), ('st', s2)], ['junk', ('st', s2)], accum_out=st[s2][:, 0:1])
        act(st[s2][:, 1:2], st[s2][:, 0:1], AF.Ln, [('st', s2)], [('st', s2)], scale=1.0 / D, bias=EPS)
        act(st[s2][:, 2:3], st[s2][:, 1:2], AF.Exp, [('st', s2)], [('st', s2)], scale=-0.5)
        ts('dve', xn[s2][:], xb[s2][:], st[s2][:, 2:3], None, ALU.mult, None, [('xb', s2), ('st', s2)], [('xn', s2)])
        b = nps()
        pb = ps[b][:].bitcast(BF16)
        for k in range(8):
            tr(pb[:, k * 128:(k + 1) * 128], xn[s2][:, k * 128:(k + 1) * 128], idb[:], [('xn', s2), 'idb'], [('ps', b)], inc=(k == 7))
        ai = 0 if i >= 2 else 2
        for k in range(8):
            dst = hT[:, k, i * 128:(i + 1) * 128]
            if k % 2 == 0:
                act(dst, pb[:, k * 128:(k + 1) * 128], AF.Identity, [('ps', b), 'fv'], [('hT', i)],
                    scale=fv[:, ai, k:k + 1], bias=fv[:, ai + 1, k:k + 1])
            else:
                ts('dve', dst, pb[:, k * 128:(k + 1) * 128], fv[:, ai, k:k + 1], fv[:, ai + 1, k:k + 1], ALU.mult, ALU.add,
                   [('ps', b), 'fv'], [('hT', i)])
        if i >= 1:
            p1_tile(i - 1)
    p1_tile(NTA - 1)

    if stop_after == 'A':
        for i in range(NTA):
            S.out_toks.append(S.lastw[('sc_tm', i)])
        S.emit()
        return nc

    S.barrier()
    top[0] = MARK1
    wfm = sb([128, 8, 4608], BF16, "wfm")
    for k9 in range(9):
        S.dma('pool', wfm[:, :, k9 * 512:(k9 + 1) * 512], w_in[:, 1536 + k9 * 512:1536 + (k9 + 1) * 512].rearrange("(k p) n -> p k n", p=128), writes=['wfm'])
    stg = [sb([128, 512], BF16, "stg") for _ in range(4)]
    sgi = [0]
    cvt = [sb([128, 512], F32, "cvt") for _ in range(4)]
    ustage = [sb([128, 4, 512], BF16, "ustage") for _ in range(2)]
    ub = [sb([128, 512], BF16, "ub") for _ in range(2)]

    def proj(ft_col, tb):
        b = nps()
        for k in range(8):
            mm(ps[b][:], wfm[:, k, ft_col:ft_col + 128], hT[:, k, 256 + tb * 512:256 + (tb + 1) * 512], k == 0, k == 7,
               ['wfm'] + [('hT', 2 + tb * 4 + j) for j in range(4)], [('ps', b)])
        return b

    def conv(b, ft, dst, dkey, s4):
        cv = cvt[s4]
        ck = ('cvt', s4)
        p3 = ps[b][:].rearrange("p (r c) -> p r c", c=64)
        c3 = cv[:].rearrange("p (r c) -> p r c", c=64)
        act(cv[:], ps[b][:], AF.Identity, [('ps', b), 'sm'], [ck], scale=SM('hsw', 12 + ft, 1), bias=SM('hsb', ft, 1))
        stt('dve', c3[:, :, 1:64], p3[:, :, 0:63], SM('hsw', ft, 1), c3[:, :, 1:64], ALU.mult, ALU.add, [('ps', b), 'sm', ck], [ck])
        d3 = dst.rearrange("p (r c) -> p r c", c=64)
        stt('dve', d3[:, :, 0:63], p3[:, :, 1:64], SM('hsw', 24 + ft, 1), c3[:, :, 0:63], ALU.mult, ALU.add, [('ps', b), 'sm', ck], [dkey])
        cp('pool', d3[:, :, 63:64], c3[:, :, 63:64], [ck], [dkey])

    pend = [None]

    def utrans(u2, j, us, tb):
        bt = nps()
        pbt = ps[bt][:].bitcast(BF16)
        for q4 in range(4):
            tr(pbt[:, q4 * 128:(q4 + 1) * 128], u2[:, q4 * 128:(q4 + 1) * 128], idb[:], [('ub', j % 2), 'idb'], [('ps', bt)], inc=(q4 == 3))
        cp('act', us[:, :, j * 128:(j + 1) * 128], pbt[:, 0:512].rearrange("p (a b) -> p a b", b=128), [('ps', bt)], [('ustage', tb % 2)])
        if j == 3:
            S.dma('sp', sc_u[:, tb * 4:(tb + 1) * 4, :], us[:], reads=[('ustage', tb % 2)], writes=[('sc_u', tb)])
    for tb in range(8):
        tsl = slice(tb * 512, (tb + 1) * 512)
        for ft in range(8):
            b = proj((ft) * 128, tb)
            sg = sgi[0] % 4
            sgi[0] += 1
            act(stg[sg][:], ps[b][:], AF.Silu, [('ps', b)], [('stg', sg)])
            dstd = (sc_q if ft < 4 else sc_g)[ft % 4, :, tsl]
            S.dma('sp', dstd, stg[sg][:], reads=[('stg', sg)], writes=[('sc_qg', ft, tb)])
        for ft in range(16):
            b = proj(2560 + ft * 128, tb)
            sg = sgi[0] % 4
            sgi[0] += 1
            act(stg[sg][:], ps[b][:], AF.Sigmoid, [('ps', b)], [('stg', sg)])
            S.dma('sp', sc_gate[ft, :, tsl], stg[sg][:], reads=[('stg', sg)], writes=[('sc_gate', ft, tb)])
        us = ustage[tb % 2]
        for j in range(4):
            b = proj(1024 + j * 128, tb)
            sg = sgi[0] % 4
            sgi[0] += 1
            conv(b, j, stg[sg][:], ('stg', sg), 0)
            S.dma('sp', sc_x0[j, :, tsl], stg[sg][:], reads=[('stg', sg)], writes=[('sc_x0', j, tb)])
            b1 = proj(1024 + (4 + j) * 128, tb)
            conv(b1, 4 + j, cvt[2][:], ('cvt', 2), 1)
            b2 = proj(1024 + (8 + j) * 128, tb)
            conv(b2, 8 + j, cvt[3][:], ('cvt', 3), 1)
            u2 = ub[j % 2]
            tt('pool', u2[:], cvt[2][:], cvt[3][:], ALU.mult, [('cvt', 2), ('cvt', 3)], [('ub', j % 2)])
            if pend[0] is not None:
                pend[0]()
            pend[0] = (lambda u2_=u2, j_=j, us_=us, tb_=tb: utrans(u2_, j_, us_, tb_))
    pend[0]()
    if stop_after == 'P':
        for k_, v_ in list(S.lastw.items()):
            if isinstance(k_, tuple) and str(k_[0]).startswith('sc_'):
                S.out_toks.append(v_)
        S.emit()
        return nc
    S.barrier()
    top[0] = MARK0
    hcat = sb([128, 32, 1024], BF16, "hcat")
    MARKH = top[0]
    zt = sb([33, L], F32, "zt"); w1s = sb([33, 64], F32, "w1s"); w2s = sb([64, 64], F32, "w2s"); w3s = sb([64, 1024], F32, "w3s")
    h1 = sb([64, L], F32, "h1"); h2 = sb([64, L], F32, "h2"); wtmp = sb([64, 512], F32, "wtmp")
    S.dma('sp', zt[:], zT_d, writes=['zt'])
    S.dma('sp', w1s[:], f_w1, writes=['fw'])
    S.dma('sp', w2s[:], f_w2, writes=['fw'])
    S.dma('sp', w3s[:], f_w3, writes=['fw'])

    def sinlayer(src, w, kdim, bcol, dst, skey, dkey):
        for blk in range(8):
            sl = slice(blk * 512, (blk + 1) * 512)
            b = nps()
            mm(ps[b][0:64, :], w[0:kdim, :], src[0:kdim, sl], True, True, [skey, 'fw'], [('ps', b)])
            ts('dve', dst[:, sl], ps[b][0:64, :], sm64[:, bcol:bcol + 1], sm64[:, 2:3], ALU.add, ALU.mult, [('ps', b), 'sm64'], [dkey])
            for _ in range(2):
                S.op('dve', lambda e, d_=dst[:, sl]: e.tensor_single_scalar(out=wtmp[:], in_=d_, scalar=PI, op=ALU.is_gt), [dkey], ['wtmp'])
                stt('dve', dst[:, sl], wtmp[:], -2 * PI, dst[:, sl], ALU.mult, ALU.add, ['wtmp', dkey], [dkey])
                S.op('dve', lambda e, d_=dst[:, sl]: e.tensor_single_scalar(out=wtmp[:], in_=d_, scalar=-PI, op=ALU.is_lt), [dkey], ['wtmp'])
                stt('dve', dst[:, sl], wtmp[:], 2 * PI, dst[:, sl], ALU.mult, ALU.add, ['wtmp', dkey], [dkey])
            act(dst[:, sl], dst[:, sl], AF.Sin, [dkey], [dkey])
    sinlayer(zt, w1s, 33, 0, h1, 'zt', 'h1')
    sinlayer(h1, w2s, 64, 1, h2, 'h1', 'h2')
    dect = [sb([128, 512], F32, "dect") for _ in range(2)]
    ftmp = sb([128, 512], F32, "ftmp")
    ntn = C('ntn')
    for chunk in range(32):
        dt_ = dect[chunk % 2]
        dk = ('dect', chunk % 2)
        act(dt_[:], C('delta'), AF.Exp, ['cst'], [dk], scale=ntn[:, chunk:chunk + 1])
        for half in range(2):
            b = nps()
            mm(ps[b][:], h2[:, chunk * 128:(chunk + 1) * 128], w3s[:, half * 512:(half + 1) * 512], True, True, ['h2', 'fw'], [('ps', b)])
            dst = hcat[:, chunk, half * 512:(half + 1) * 512]
            if chunk == 0:
                tt('dve', ftmp[:], ps[b][:], dt_[:], ALU.mult, [('ps', b), dk], ['ftmp'])
                if half == 0:
                    tt('dve', ftmp[0:1, :], ftmp[0:1, :], hydbc[0:1, :], ALU.add, ['ftmp', 'hydbc'], ['ftmp'])
                else:
                    S.op('dve', lambda e: e.memset(ftmp[0:1, :], 0.0), ['ftmp'], ['ftmp'])
                cp('dve', dst, ftmp[:], ['ftmp'], ['hcat'])
            else:
                tt('dve', dst, ps[b][:], dt_[:], ALU.mult, [('ps', b), dk], ['hcat'])

    S.barrier()
    top[0] = MARKH
    u_tm = sb([128, 32, 512], BF16, "u_tm")
    for q8 in range(8):
        S.dma('sp', u_tm[:, q8 * 4:(q8 + 1) * 4, :], sc_u[:, q8 * 4:(q8 + 1) * 4, :], reads=[('sc_u', q8)], writes=['u_tm'])
    tabs = [sb([128, 32, 256], BF16, "tabf") for _ in range(2)]
    Ast = sb([128, 2, 512], F32, "Ast"); Pst = sb([128, 2, 512], F32, "Pst")
    tq = [sb([128, 512], F32, "tq") for _ in range(6)]
    yst = [sb([128, 512], BF16, "yst") for _ in range(4)]
    SC = 2.0 / NFFT
    ysi = [0]

    def ynext():
        i_ = ysi[0] % 4
        ysi[0] += 1
        return yst[i_], ('yst', i_)
    for fb in range(16):
        tbi, hf_ = fb // 2, fb % 2
        for cs in range(2):
            tab = tabs[cs]
            tk = ('tab', cs)
            S.dma('sp', tab[:], (ctab_d if cs == 0 else stab_d)[tbi][:, :, hf_ * 256:(hf_ + 1) * 256], writes=[tk])
            for m in range(2):
                bb = [nps(), nps(), nps()]
                for chunk in range(32):
                    lhsT = tab[:, chunk, m * 128:(m + 1) * 128]
                    mm(ps[bb[0]][:], lhsT, u_tm[:, chunk, :], chunk == 0, chunk == 31, [tk, 'u_tm'], [('ps', bb[0])])
                    mm(ps[bb[1]][:], lhsT, hcat[:, chunk, 0:512], chunk == 0, chunk == 31, [tk, 'hcat'], [('ps', bb[1])])
                    mm(ps[bb[2]][:], lhsT, hcat[:, chunk, 512:1024], chunk == 0, chunk == 31, [tk, 'hcat'], [('ps', bb[2])])
                ft = fb * 2 + m
                if cs == 0:
                    cp('act', Ast[:, m, :], ps[bb[0]][:], [('ps', bb[0])], [('Ast', m)])
                    cp('act', tq[0][:], ps[bb[2]][:], [('ps', bb[2])], [('tq', 0)])
                    tt('dve', Pst[:, m, :], ps[bb[1]][:], tq[0][:], ALU.add, [('ps', bb[1]), ('tq', 0)], [('Pst', m)])
                else:
                    cp('act', tq[1][:], ps[bb[2]][:], [('ps', bb[2])], [('tq', 1)])
                    tt('dve', tq[2][:], ps[bb[1]][:], tq[1][:], ALU.subtract, [('ps', bb[1]), ('tq', 1)], [('tq', 2)])
                    tt('pool', tq[3][:], Ast[:, m, :], Pst[:, m, :], ALU.mult, [('Ast', m), ('Pst', m)], [('tq', 3)])
                    stt('dve', tq[4][:], ps[bb[0]][:], SC, tq[2][:], ALU.mult, ALU.mult, [('ps', bb[0]), ('tq', 2)], [('tq', 4)])
                    yr, yrk = ynext()
                    stt('dve', yr[:], tq[3][:], SC, tq[4][:], ALU.mult, ALU.subtract, [('tq', 3), ('tq', 4)], [yrk])
                    if ft == 0:
                        ts('dve', yr[0:1, :], yr[0:1, :], 0.5, None, ALU.mult, None, [yrk], [yrk])
                    S.dma('sp', sc_Y[:, ft, :], yr[:], reads=[yrk], writes=[('sc_Y', ft)])
                    tt('pool', tq[3][:], Ast[:, m, :], tq[2][:], ALU.mult, [('Ast', m), ('tq', 2)], [('tq', 3)])
                    stt('dve', tq[5][:], ps[bb[0]][:], SC, Pst[:, m, :], ALU.mult, ALU.mult, [('ps', bb[0]), ('Pst', m)], [('tq', 5)])
                    yq, yqk = ynext()
                    stt('dve', yq[:], tq[3][:], SC, tq[5][:], ALU.mult, ALU.add, [('tq', 3), ('tq', 5)], [yqk])
                    S.dma('sp', sc_Y[:, 32 + ft, :], yq[:], reads=[yqk], writes=[('sc_Y', 32 + ft)])
    bn = [nps(), nps(), nps()]
    for chunk in range(32):
        mm(ps[bn[0]][0:1, :], altb[:, 0:1], u_tm[:, chunk, :], chunk == 0, chunk == 31, ['altb', 'u_tm'], [('ps', bn[0])])
        mm(ps[bn[1]][0:1, :], altb[:, 0:1], hcat[:, chunk, 0:512], chunk == 0, chunk == 31, ['altb', 'hcat'], [('ps', bn[1])])
        mm(ps[bn[2]][0:1, :], altb[:, 0:1], hcat[:, chunk, 512:1024], chunk == 0, chunk == 31, ['altb', 'hcat'], [('ps', bn[2])])
    cp('act', tq[0][0:1, :], ps[bn[2]][0:1, :], [('ps', bn[2])], [('tq', 0)])
    tt('dve', tq[1][0:1, :], ps[bn[1]][0:1, :], tq[0][0:1, :], ALU.add, [('ps', bn[1]), ('tq', 0)], [('tq', 1)])
    stt('dve', ynq[:], ps[bn[0]][0:1, :], 1.0 / NFFT, tq[1][0:1, :], ALU.mult, ALU.mult, [('ps', bn[0]), ('tq', 1)], ['ynq'])

    S.barrier()
    top[0] = MARK0
    Yt = sb([128, 64, 512], BF16, "Yt")
    for q8 in range(8):
        S.dma('sp', Yt[:, q8 * 8:(q8 + 1) * 8, :], sc_Y[:, q8 * 8:(q8 + 1) * 8, :],
              reads=[('sc_Y', f_) for f_ in range(q8 * 8, q8 * 8 + 8)], writes=[('Yt', q8)])
    tabi = [sb([128, 32, 512], BF16, "tabi") for _ in range(2)]
    x0s = [sb([128, 512], BF16, "x0s") for _ in range(2)]
    yo = [sb([128, 512], BF16, "yo") for _ in range(2)]
    for tb in range(8):
        tsl = slice(tb * 512, (tb + 1) * 512)
        S.dma('sp', tabi[0][:], ctab_d[tb], writes=[('tabi', 0)])
        S.dma('sp', tabi[1][:], stab_d[tb], writes=[('tabi', 1)])
        banks = [nps() for _ in range(4)]
        for cs in range(2):
            for ct in range(4):
                for chunk in range(32):
                    mm(ps[banks[ct]][:], Yt[:, cs * 32 + chunk, ct * 128:(ct + 1) * 128], tabi[cs][:, chunk, :],
                       cs == 0 and chunk == 0, False, [('Yt', (cs * 32 + chunk) // 8), ('tabi', cs)], [('ps', banks[ct])], inc=False)
        for ct in range(4):
            mm(ps[banks[ct]][:], ynq[0:1, ct * 128:(ct + 1) * 128], altrowb[0:1, :], False, True, ['ynq', 'altrowb'], [('ps', banks[ct])], inc=True)
            s2 = ct % 2
            S.dma('sp', x0s[s2][:], sc_x0[ct, :, tsl], reads=[('sc_x0', ct, tb)], writes=[('x0s', s2)])
            tt('dve', yo[s2][:], ps[banks[ct]][:], x0s[s2][:], ALU.mult, [('ps', banks[ct]), ('x0s', s2)], [('yo', s2)])
            S.dma('sp', sc_yx0[ct, :, tsl], yo[s2][:], reads=[('yo', s2)], writes=[('sc_yx0', ct, tb)])
    if stop_after == 'H':
        for k_, v_ in list(S.lastw.items()):
            if isinstance(k_, tuple) and str(k_[0]).startswith('sc_'):
                S.out_toks.append(v_)
        S.emit()
        return nc

    S.barrier()
    top[0] = MARK0
    tmb = [sb([128, 1536], F32, "tmb") for _ in range(2)]
    vb = [sb([128, 512], BF16, "vb") for _ in range(2)]
    ebuf = sb([128, 512], F32, "ebuf"); lp = sb([128, 512], F32, "lp")
    lnk = [sb([128, 512], F32, "lnk") for _ in range(2)]
    logf = [sb([128, 512], F32, "logf") for _ in range(2)]
    Ea = [[sb([128, 4, 128], BF16, "Ea") for _ in range(2)] for _ in range(2)]
    Eb = [[sb([128, 4, 128], BF16, "Eb") for _ in range(2)] for _ in range(2)]
    keT = [[sb([128, 512], BF16, "keT") for _ in range(2)] for _ in range(2)]
    kd = [[sb([128, 512], BF16, "kd") for _ in range(2)] for _ in range(2)]
    dec = [[sb([128, 4], F32, "dec") for _ in range(2)] for _ in range(2)]
    Sf = [sb([128, 4, 128], F32, "Sf") for _ in range(2)]
    Sb16 = [sb([128, 4, 128], BF16, "Sb16") for _ in range(2)]
    sbs = sb([128, 32, 512], BF16, "sbs")
    qtb = [sb([128, 4, 128], BF16, "qtb") for _ in range(2)]
    gtb = [sb([128, 4, 128], BF16, "gtb") for _ in range(2)]
    qa = [sb([128, 4, 128], BF16, "qa") for _ in range(2)]
    qe = [sb([128, 4, 128], BF16, "qe") for _ in range(2)]
    scT = [sb([128, 512], BF16, "scT") for _ in range(2)]
    sqb = sb([128, 512], BF16, "sqb"); rs = sb([128, 512], F32, "rs"); otf = sb([128, 512], F32, "otf")
    atb = [sb([128, 512], BF16, "atb") for _ in range(2)]
    flat = lambda a_: a_.rearrange("p a b -> p (a b)")
    for d in range(2):
        S.op('pool', lambda e, t_=Sf[d]: e.memset(flat(t_), 0.0), [], [('Sf', d)])
        S.op('pool', lambda e, t_=Sb16[d]: e.memset(flat(t_), 0.0), [], [('Sb', d)])

    def gate(d, tm, tmk, sl, lite=False):
        zz = tm[:, 512 * (1 + d):512 * (2 + d)]
        act(ebuf[:], zz, AF.Exp, [tmk], ['ebuf'], scale=-1.0)
        act(lp[:], ebuf[:], AF.Ln, ['ebuf'], ['lp'], bias=1.0)
        stt('dve', lnk[d][:], zz, -1.0, c1bc[:], ALU.mult, ALU.add, [tmk, 'c1bc'], [('lnk', d)])
        tt('dve', lnk[d][:], lnk[d][:], lp[:], ALU.subtract, [('lnk', d), 'lp'], [('lnk', d)])
        act(ebuf[:], lnk[d][:], AF.Exp, [('lnk', d)], ['ebuf'])
        act(logf[d][:], ebuf[:], AF.Ln, ['ebuf'], [('logf', d)], scale=-1.0, bias=1.0)
        sfx = 'f' if d == 0 else 'b'
        oi = coffs['Mincl_' + sfx][0]
        Mcat = cst[:, oi:oi + 256]
        nMb = C('nMb_' + sfx)
        Mkd = C('Mkd_' + sfx)
        bkd = nps()
        mm(ps[bkd][:], Mkd, logf[d][:], True, False, [('logf', d), 'cst'], [('ps', bkd)], inc=False)
        mm(ps[bkd][:], ident, lnk[d][:], False, True, [('lnk', d), 'cst'], [('ps', bkd)], inc=True)
        act(kd[d][sl][:], ps[bkd][:], AF.Exp, [('ps', bkd)], [('kd', d, sl)])
        if lite:
            bt_ = nps()
            for h in range(4):
                mm(ps[bt_][:, h:h + 1], logf[d][:, h * 128:(h + 1) * 128], onesf[:, 0:1], True, True, [('logf', d), 'cst'], [('ps', bt_)], inc=(h == 3))
            act(dec[d][sl][:], ps[bt_][:, 0:4], AF.Exp, [('ps', bt_)], [('dec', d, sl)])
            return
        bab = [nps(), nps()]
        for h in range(4):
            mm(ps[bab[h // 2]][:, (h % 2) * 256:(h % 2) * 256 + 256], logf[d][:, h * 128:(h + 1) * 128], Mcat, True, True,
               [('logf', d), 'cst'], [('ps', bab[h // 2])], inc=(h % 2 == 1))
        bke = nps()
        for h in range(4):
            mm(ps[bke][:, h * 128:(h + 1) * 128], lnk[d][:, h * 128:(h + 1) * 128], ident, True, False, [('lnk', d), 'cst'], [('ps', bke)], inc=False)
            mm(ps[bke][:, h * 128:(h + 1) * 128], logf[d][:, h * 128:(h + 1) * 128], nMb, False, True, [('logf', d), 'cst'], [('ps', bke)], inc=(h == 3))
        col = 127 if d == 0 else 0
        for j in range(2):
            v3 = ps[bab[j]][:].rearrange("p (h x) -> p h x", x=256)
            act(Ea[d][sl][:, 2 * j:2 * j + 2, :], v3[:, :, 0:128], AF.Exp, [('ps', bab[j])], [('Ea', d, sl)])
            act(Eb[d][sl][:, 2 * j:2 * j + 2, :], v3[:, :, 128:256], AF.Exp, [('ps', bab[j])], [('Eb', d, sl)])
            for hh in range(2):
                act(dec[d][sl][:, 2 * j + hh:2 * j + hh + 1], ps[bab[j]][:, hh * 256 + col:hh * 256 + col + 1], AF.Exp, [('ps', bab[j])], [('dec', d, sl)])
        act(keT[d][sl][:], ps[bke][:], AF.Exp, [('ps', bke)], [('keT', d, sl)])

    def update(d, vt, vk, sl):
        b = nps()
        for h in range(4):
            mm(ps[b][:, h * 128:(h + 1) * 128], kd[d][sl][:, h * 128:(h + 1) * 128], vt[:, h * 128:(h + 1) * 128], True, True,
               [('kd', d, sl), vk], [('ps', b)], inc=(h == 3))
        for h in range(4):
            stt('dve', Sf[d][:, h, :], Sf[d][:, h, :], dec[d][sl][:, h:h + 1], ps[b][:, h * 128:(h + 1) * 128], ALU.mult, ALU.add,
                [('Sf', d), ('dec', d, sl), ('ps', b)], [('Sf', d)])
        cp('pool', flat(Sb16[d]), flat(Sf[d]), [('Sf', d)], [('Sb', d)])

    def load_tm(i):
        s2 = i % 2
        S.dma('sp', tmb[s2][:], sc_tm[i], reads=[('sc_tm', i)], writes=[('tmb', s2)])
        cp('pool', vb[s2][:], tmb[s2][:, 0:512], [('tmb', s2)], [('vb', s2)])
        return tmb[s2], ('tmb', s2), vb[s2], ('vb', s2)
    gi = [0]

    def lite_step(d, i):
        sl = gi[0] % 2
        gi[0] += 1
        tm, tmk, vt, vk = load_tm(i)
        gate(d, tm, tmk, sl, lite=True)
        update(d, vt, vk, sl)
    for i in (0, 1):
        lite_step(0, i)
    for i in (1, 0):
        lite_step(1, i)
    for n in range(31, -1, -1):
        cp('pool', sbs[:, n, :], flat(Sb16[1]), [('Sb', 1)], [('sbs', n)])
        lite_step(1, n + 2)
    loaded = {}

    def prep(n):
        tm, tmk, vt, vk = load_tm(n + 2)
        loaded[n] = (vt, vk)
        q2 = n % 2
        tsl = slice(n * 128, (n + 1) * 128)
        S.dma('sp', qtb[q2][:], sc_q[:, :, tsl].rearrange("h p t -> p h t"), reads=[('sc_qg', h_, n // 4) for h_ in range(4)], writes=[('qtb', q2)])
        S.dma('sp', gtb[q2][:], sc_g[:, :, tsl].rearrange("h p t -> p h t"), reads=[('sc_qg', 4 + h_, n // 4) for h_ in range(4)], writes=[('gtb', q2)])
        for d in range(2):
            gate(d, tm, tmk, q2)
    prep(0)
    for n in range(32):
        if n + 1 < 32:
            prep(n + 1)
        vt, vk = loaded[n]
        q2 = n % 2
        tsl = slice(n * 128, (n + 1) * 128)
        for d in range(2):
            en = 'dve' if d == 0 else 'pool'
            tt(en, flat(qa[d]), flat(qtb[q2]), flat(Ea[d][q2]), ALU.mult, [('qtb', q2), ('Ea', d, q2)], [('qa', d)])
            tt(en, flat(qe[d]), flat(qtb[q2]), flat(Eb[d][q2]), ALU.mult, [('qtb', q2), ('Eb', d, q2)], [('qe', d)])
            bs = nps()
            for h in range(4):
                mm(ps[bs][:, h * 128:(h + 1) * 128], keT[d][q2][:, h * 128:(h + 1) * 128], qe[d][:, h, :], True, True,
                   [('keT', d, q2), ('qe', d)], [('ps', bs)], inc=(h == 3))
            mk = maskf if d == 0 else maskb
            tt('dve', scT[d][:], ps[bs][:], mk[:], ALU.mult, [('ps', bs), 'maskf', 'maskb'], [('scT', d)])
        bo = nps()
        for h in range(4):
            o_ = ps[bo][:, h * 128:(h + 1) * 128]
            hs = slice(h * 128, (h + 1) * 128)
            mm(o_, vt[:, hs], scT[0][:, hs], True, False, [vk, ('scT', 0)], [('ps', bo)], inc=False)
            mm(o_, vt[:, hs], scT[1][:, hs], False, False, [vk, ('scT', 1)], [('ps', bo)], inc=False)
            mm(o_, Sb16[0][:, h, :], qa[0][:, h, :], False, False, [('Sb', 0), ('qa', 0)], [('ps', bo)], inc=False)
            mm(o_, sbs[:, n, hs], qa[1][:, h, :], False, True, [('sbs', n), ('qa', 1)], [('ps', bo)], inc=(h == 3))
        act(sqb[:], ps[bo][:], AF.Square, [('ps', bo)], ['sqb'])
        bss = nps()
        mm(ps[bss][:], onesb[:], sqb[:], True, True, ['onesb', 'sqb'], [('ps', bss)])
        act(rs[:], ps[bss][:], AF.Ln, [('ps', bss)], ['rs'], scale=1.0 / 128, bias=EPS)
        act(rs[:], rs[:], AF.Exp, ['rs'], ['rs'], scale=-0.5)
        tt('dve', otf[:], ps[bo][:], rs[:], ALU.mult, [('ps', bo), 'rs'], ['otf'])
        stt('dve', atb[q2][:], otf[:], SM('gn'), flat(gtb[q2]), ALU.mult, ALU.mult, ['otf', 'sm', ('gtb', q2)], [('atb', q2)])
        S.dma('sp', sc_a[:, :, tsl].rearrange("h p t -> p h t"), atb[q2][:].rearrange("p (h t) -> p h t", t=128),
              reads=[('atb', q2)], writes=[('sc_a', n)])
        update(0, vt, vk, q2)
    if stop_after == 'G':
        for k_, v_ in list(S.lastw.items()):
            if isinstance(k_, tuple) and str(k_[0]).startswith('sc_'):
                S.out_toks.append(v_)
        S.emit()
        return nc

    S.barrier()
    top[0] = MARK0
    bcv = sb([128, 4, 1024], F32, "bcv")
    for vi in range(4):
        for k in range(8):
            bcast(bcv[:, vi, k * 128:(k + 1) * 128], fv[:, 4 + vi, k:k + 1], 'bcv', 'fv')
    wpa = sb([128, 4, 1024], BF16, "wpa"); wpb = sb([128, 4, 1024], BF16, "wpb"); wout = sb([128, 8, 1024], BF16, "wout")
    S.dma('pool', wpa[:], w_pa.rearrange("(k p) n -> p k n", p=128), writes=['wpa'])
    S.dma('pool', wpb[:], w_pb.rearrange("(k p) n -> p k n", p=128), writes=['wpb'])
    for k2 in range(2):
        S.dma('pool', wout[:, :, k2 * 512:(k2 + 1) * 512], w_out[:, k2 * 512:(k2 + 1) * 512].rearrange("(k p) n -> p k n", p=128), writes=['wout'])
    aTb = [sb([128, 4, 512], BF16, "aTb") for _ in range(2)]
    yxb = [sb([128, 4, 512], BF16, "yxb") for _ in range(2)]
    ggb = [sb([128, 16, 512], BF16, "ggb") for _ in range(2)]
    merged = sb([128, 8, 512], BF16, "merged")
    m1 = [sb([128, 512], F32, "m1") for _ in range(2)]
    xt = [sb([128, 1024], F32, "xt") for _ in range(2)]
    xl = [sb([128, 1024], F32, "xl") for _ in range(2)]
    tmpf = [sb([128, 1024], F32, "tmpf") for _ in range(2)]
    hlb = [sb([128, 1024], BF16, "hlb") for _ in range(2)]
    hls = [sb([128, 8, 512], BF16, "hls") for _ in range(2)]
    junk2 = [sb([128, 1024], BF16, "junk2") for _ in range(2)]

    def interleave(gens):
        gens = list(gens)
        while gens:
            for g_ in list(gens):
                try:
                    next(g_)
                except StopIteration:
                    gens.remove(g_)
    st2 = [sb([128, 8], F32, "st2") for _ in range(2)]
    for tb in range(8):
        tsl = slice(tb * 512, (tb + 1) * 512)
        b2 = tb % 2
        S.dma('sp', aTb[b2][:], sc_a[:, :, tsl].rearrange("h p t -> p h t"), reads=[('sc_a', tb * 4 + j_) for j_ in range(4)], writes=[('aTb', b2)])
        S.dma('sp', yxb[b2][:], sc_yx0[:, :, tsl].rearrange("h p t -> p h t"), reads=[('sc_yx0', c_, tb) for c_ in range(4)], writes=[('yxb', b2)])
        S.dma('sp', ggb[b2][:], sc_gate[:, :, tsl].rearrange("f p t -> p f t"), reads=[('sc_gate', f_, tb) for f_ in range(16)], writes=[('ggb', b2)])
        for dm in range(8):
            bA = nps()
            for k in range(4):
                mm(ps[bA][:], wpa[:, k, dm * 128:(dm + 1) * 128], aTb[b2][:, k, :], k == 0, k == 3, ['wpa', ('aTb', b2)], [('ps', bA)])
            bB = nps()
            for k in range(4):
                mm(ps[bB][:], wpb[:, k, dm * 128:(dm + 1) * 128], yxb[b2][:, k, :], k == 0, k == 3, ['wpb', ('yxb', b2)], [('ps', bB)])
            tt('dve', m1[0][:], ps[bA][:], ggb[b2][:, dm, :], ALU.mult, [('ps', bA), ('ggb', b2)], [('m1', 0)])
            tt('dve', m1[1][:], ps[bB][:], ggb[b2][:, 8 + dm, :], ALU.mult, [('ps', bB), ('ggb', b2)], [('m1', 1)])
            tt('pool', merged[:, dm, :], m1[0][:], m1[1][:], ALU.add, [('m1', 0), ('m1', 1)], [('merged', dm)])
        hs_ = hls[b2]

        def t1_tile(t4, tb=tb, hs_=hs_, b2=b2):
            n = tb * 4 + t4
            s2 = n % 2
            tf = tmpf[s2]
            tfk = ('tmpf', s2)
            S.dma('sp', xt[s2][:], x[n * 128:(n + 1) * 128, :], writes=[('xt', s2)])
            S.op('pool', lambda e, t_=st2[s2]: e.memset(t_[:, 0:2], 0.0), [], [('st2', s2)])
            bh = [nps(), nps()]
            for half in range(2):
                for k in range(8):
                    mm(ps[bh[half]][:], merged[:, k, t4 * 128:(t4 + 1) * 128], wout[:, k, half * 512:(half + 1) * 512], k == 0, k == 7,
                       [('merged', k), 'wout'], [('ps', bh[half])])
                act(junk2[s2][:, 0:512], ps[bh[half]][:], AF.Square, [('ps', bh[half]), ('st2', s2)], [('junk2', s2), ('st2', s2)], accum_out=st2[s2][:, half:half + 1])
            yield
            tt('dve', st2[s2][:, 2:3], st2[s2][:, 0:1], st2[s2][:, 1:2], ALU.add, [('st2', s2)], [('st2', s2)])
            yield
            act(st2[s2][:, 3:4], st2[s2][:, 2:3], AF.Ln, [('st2', s2)], [('st2', s2)], scale=1.0 / D, bias=EPS)
            act(st2[s2][:, 4:5], st2[s2][:, 3:4], AF.Exp, [('st2', s2)], [('st2', s2)], scale=-0.5)
            yield
            for half in range(2):
                hsl = slice(half * 512, (half + 1) * 512)
                stt('dve', tf[:, hsl], ps[bh[half]][:], st2[s2][:, 4:5], bcv[:, 0, hsl], ALU.mult, ALU.mult, [('ps', bh[half]), ('st2', s2), 'bcv'], [tfk])
            yield
            tt('pool', xl[s2][:], tf[:], xt[s2][:], ALU.add, [tfk, ('xt', s2)], [('xl', s2)])
            S.dma('sp', sc_xlat[n * 128:(n + 1) * 128, :], xl[s2][:], reads=[('xl', s2)], writes=[('sc_xlat', n)])
            S.op('pool', lambda e, t_=st2[s2]: e.memset(t_[:, 5:6], 0.0), [], [('st2', s2)])
            yield
            act(junk2[s2][:], xl[s2][:], AF.Square, [('xl', s2), ('st2', s2)], [('junk2', s2), ('st2', s2)], accum_out=st2[s2][:, 5:6])
            act(st2[s2][:, 6:7], st2[s2][:, 5:6], AF.Ln, [('st2', s2)], [('st2', s2)], scale=1.0 / D, bias=EPS)
            act(st2[s2][:, 7:8], st2[s2][:, 6:7], AF.Exp, [('st2', s2)], [('st2', s2)], scale=-0.5)
            yield
            stt('dve', tf[:], xl[s2][:], st2[s2][:, 7:8], bcv[:, 1, :], ALU.mult, ALU.mult, [('xl', s2), ('st2', s2), 'bcv'], [tfk])
            yield
            tt('pool', hlb[s2][:], tf[:], bcv[:, 2, :], ALU.add, [tfk, 'bcv'], [('hlb', s2)])
            yield
            bt = nps()
            pbt = ps[bt][:].bitcast(BF16)
            for k in range(8):
                tr(pbt[:, k * 128:(k + 1) * 128], hlb[s2][:, k * 128:(k + 1) * 128], idb[:], [('hlb', s2), 'idb'], [('ps', bt)], inc=(k == 7))
            yield
            cp('act', hs_[:, :, t4 * 128:(t4 + 1) * 128], pbt[:, 0:1024].rearrange("p (a b) -> p a b", b=128), [('ps', bt)], [('hls', b2)])
        interleave([t1_tile(0), t1_tile(1)])
        interleave([t1_tile(2), t1_tile(3)])
        S.dma('sp', sc_hl[:, :, tsl].rearrange("k p t -> p k t"), hs_[:], reads=[('hls', b2)], writes=[('sc_hl', tb)])

    S.barrier()
    top[0] = MARK0
    g5bc = sb([128, 1024], F32, "g5bc")
    for k in range(8):
        bcast(g5bc[:, k * 128:(k + 1) * 128], fv[:, 7, k:k + 1], 'g5bc', 'fv')
    w1 = sb([128, 8, 4096], BF16, "w1"); w2 = sb([128, 32, 1024], BF16, "w2")
    for k8 in range(8):
        S.dma('pool', w1[:, :, k8 * 512:(k8 + 1) * 512], w_mlp1[:, k8 * 512:(k8 + 1) * 512].rearrange("(k p) n -> p k n", p=128), writes=[('w1', k8)])
    for k8 in range(8):
        S.dma('pool', w2[:, k8 * 4:(k8 + 1) * 4, :], w_mlp2[k8 * 512:(k8 + 1) * 512, :].rearrange("(k p) n -> p k n", p=128), writes=[('w2', k8)])
    hlt = [sb([128, 8, 256], BF16, "hlt") for _ in range(2)]
    hid = sb([128, 32, 256], BF16, "hid")
    rr = [sb([128, 256], F32, "rr") for _ in range(2)]
    xl2 = [sb([128, 1024], F32, "xl2") for _ in range(2)]
    ot = [sb([128, 1024], F32, "ot") for _ in range(2)]
    junk3 = [sb([128, 512], BF16, "junk3") for _ in range(2)]
    st3 = [sb([128, 8], F32, "st3") for _ in range(2)]
    for tb in range(16):
        b2 = tb % 2
        tsl = slice(tb * 256, (tb + 1) * 256)
        S.dma('sp', hlt[b2][:], sc_hl[:, :, tsl].rearrange("k p t -> p k t"), reads=[('sc_hl', tb // 2)], writes=[('hlt', b2)])
        for ff in range(32):
            b = nps()
            for k in range(8):
                mm(ps[b][:, 0:256], w1[:, k, ff * 128:(ff + 1) * 128], hlt[b2][:, k, :], k == 0, k == 7, [('w1', ff // 4), ('hlt', b2)], [('ps', b)])
            r2 = ff % 2
            act(rr[r2][:], ps[b][:, 0:256], AF.Relu, [('ps', b)], [('rr', r2)])
            tt('pool' if ff % 4 else 'dve', hid[:, ff, :], rr[r2][:], rr[r2][:], ALU.mult, [('rr', r2)], [('hid', ff)])
        def t2_tile(t2, tb=tb):
            n = tb * 2 + t2
            s2 = n % 2
            S.dma('sp', xl2[s2][:], sc_xlat[n * 128:(n + 1) * 128, :], reads=[('sc_xlat', n)], writes=[('xl2', s2)])
            S.op('pool', lambda e, t_=st3[s2]: e.memset(t_[:, 0:2], 0.0), [], [('st3', s2)])
            bh = [nps(), nps()]
            for half in range(2):
                for k in range(32):
                    mm(ps[bh[half]][:], hid[:, k, t2 * 128:(t2 + 1) * 128], w2[:, k, half * 512:(half + 1) * 512], k == 0, k == 31,
                       [('hid', k), ('w2', k // 4)], [('ps', bh[half])])
                act(junk3[s2][:], ps[bh[half]][:], AF.Square, [('ps', bh[half]), ('st3', s2)], [('junk3', s2), ('st3', s2)], accum_out=st3[s2][:, half:half + 1])
            yield
            tt('dve', st3[s2][:, 2:3], st3[s2][:, 0:1], st3[s2][:, 1:2], ALU.add, [('st3', s2)], [('st3', s2)])
            yield
            act(st3[s2][:, 3:4], st3[s2][:, 2:3], AF.Ln, [('st3', s2)], [('st3', s2)], scale=1.0 / D, bias=EPS)
            act(st3[s2][:, 4:5], st3[s2][:, 3:4], AF.Exp, [('st3', s2)], [('st3', s2)], scale=-0.5)
            yield
            for half in range(2):
                hsl = slice(half * 512, (half + 1) * 512)
                stt('dve', ot[s2][:, hsl], ps[bh[half]][:], st3[s2][:, 4:5], g5bc[:, hsl], ALU.mult, ALU.mult, [('ps', bh[half]), ('st3', s2), 'g5bc'], [('ot', s2)])
            yield
            tt('pool', ot[s2][:], ot[s2][:], xl2[s2][:], ALU.add, [('ot', s2), ('xl2', s2)], [('ot', s2)])
            S.dma('sp', out[n * 128:(n + 1) * 128, :], ot[s2][:], reads=[('ot', s2)], is_out=True)
        interleave([t2_tile(0), t2_tile(1)])
    S.emit()
    return nc


def make_in_maps(inputs):
    carr, coffs, zT, ctab, stab = get_hc()
    g = lambda k: np.ascontiguousarray(np.asarray(inputs[k], dtype=np.float32))
    maps = []
    for b in range(8):
        m = {
            "x": g('x')[b], "ctx": g('ctx')[b],
            "cvec": np.ascontiguousarray(np.stack([g('c')[b], g('c_ctx')], axis=0)),
            "w_ada": g('w_ada')[0], "b_ada": g('b_ada')[0], "norm_w": g('norm_w')[0], "w_in": g('w_in')[0],
            "lb_param": g('lb_param'), "g_norm": g('g_norm')[0], "hy_short_w": g('hy_short_w')[0],
            "hy_short_b": g('hy_short_b')[0], "f_w1": g('f_w1')[0], "f_b1": g('f_b1')[0], "f_w2": g('f_w2')[0],
            "f_b2": g('f_b2')[0], "f_w3": g('f_w3')[0], "f_freq": g('f_freq')[0], "hy_d": g('hy_d')[0],
            "w_pa": g('w_pa')[0], "w_pb": g('w_pb')[0], "w_out": g('w_out')[0], "w_mlp1": g('w_mlp1')[0],
            "w_mlp2": g('w_mlp2')[0], "consts": carr, "zT": zT, "ctab": ctab, "stab": stab,
        }
        maps.append(m)
    return maps


def kernel(**inputs):
    nc = build_program()
    maps = make_in_maps(inputs)
    res = run_bass_kernel_spmd(nc, maps, core_ids=list(range(8)))
    return np.stack([np.asarray(r["out"], dtype=np.float32) for r in res.results], axis=0)
```

```python
import contextlib
import numpy as np
import ml_dtypes
import concourse.bass as bass
import concourse.mybir as mybir
from concourse.bass_utils import run_bass_kernel_spmd

F32 = mybir.dt.float32
BF16 = mybir.dt.bfloat16
AF = mybir.ActivationFunctionType
ALU = mybir.AluOpType

D = 1024
L = 4096
CTX = 256
NT = 32
NTA = 34
EPS = 1e-6
NFFT = 8192
PI = float(np.pi)


class Sched:
    ENG = ('pe', 'act', 'dve', 'pool', 'sp')

    def __init__(self, nc, ndma=12, nswd=6):
        self.nc = nc
        self.ops = {e: [] for e in self.ENG}
        self.cnt = {e: 0 for e in self.ENG}
        self.known = {e: {} for e in self.ENG}
        self.lastw = {}
        self.readers = {}
        self.ndma = ndma + nswd
        self.nhw = ndma
        self.nswd = nswd
        self.dma_n = 0
        self.swd_n = 0
        self.dma_cnt = [0] * (ndma + nswd)
        self.out_toks = []

    def _need(self, eng, tok, waits, same_ok):
        if tok is None:
            return
        sem, val = tok
        if sem == eng and same_ok:
            return
        if self.known[eng].get(sem, 0) >= val:
            return
        if sem == eng:
            assert val <= self.cnt[eng], "same-engine wait on un-inc'd op"
        self.known[eng][sem] = val
        waits.append((sem, val))

    def _deps(self, eng, reads, writes):
        waits = []
        for k in reads:
            self._need(eng, self.lastw.get(k), waits, False)
        for k in writes:
            self._need(eng, self.lastw.get(k), waits, True)
            for s, v in self.readers.get(k, {}).items():
                self._need(eng, (s, v), waits, True)
        return waits

    def _commit(self, tok, reads, writes):
        for k in reads:
            d = self.readers.setdefault(k, {})
            if d.get(tok[0], 0) < tok[1]:
                d[tok[0]] = tok[1]
        for k in writes:
            self.lastw[k] = tok
            self.readers[k] = {}

    def op(self, eng, fn, reads=(), writes=(), inc=True):
        waits = self._deps(eng, reads, writes)
        if inc:
            self.cnt[eng] += 1
            tok = (eng, self.cnt[eng])
        else:
            tok = (eng, self.cnt[eng] + 1)
        self._commit(tok, reads, writes)
        self.ops[eng].append((waits, fn, eng if inc else None, 1))
        return tok

    def dma(self, q, out, in_, reads=(), writes=(), is_out=False, **kw):
        if q == 'pool':
            slot = self.nhw + self.swd_n % self.nswd
            self.swd_n += 1
        else:
            slot = self.dma_n % self.nhw
            self.dma_n += 1
        sem = 'dma%d' % slot
        waits = self._deps(q, reads, writes)
        prev = self.dma_cnt[slot]
        if prev > 0:
            self._need(q, (sem, prev), waits, False)
        self.dma_cnt[slot] += 16
        tok = (sem, self.dma_cnt[slot])
        self._commit(tok, reads, writes)
        self.ops[q].append((waits, lambda e: e.dma_start(out=out, in_=in_, **kw), sem, 16))
        if is_out:
            self.out_toks.append(tok)
        return tok

    def barrier(self):
        for e in self.ENG:
            waits = []
            for f in self.ENG[:4]:
                if f != e and self.cnt[f] > 0:
                    self._need(e, (f, self.cnt[f]), waits, False)
            for s in range(self.ndma):
                if self.dma_cnt[s] > 0:
                    self._need(e, ('dma%d' % s, self.dma_cnt[s]), waits, False)
            self.ops[e].append((waits, None, None, 0))

    def emit(self):
        nc = self.nc
        waits = []
        for tok in self.out_toks:
            self._need('sp', tok, waits, False)
        self.ops['sp'].append((waits, None, None, 0))
        for e in ('pe', 'act', 'dve', 'pool'):
            if self.ops[e]:
                last = [o for o in self.ops[e] if o[1] is not None][-1]
                assert last[2] is not None, "last op on %s must inc" % e
        names = list(self.ENG[:4]) + ['dma%d' % i for i in range(self.ndma)]
        with contextlib.ExitStack() as st:
            sems = {n: st.enter_context(nc.semaphore(n)) for n in names}
            block = st.enter_context(nc.Block())

            def run(ename):
                def body(e):
                    for waits, fn, incsem, incv in self.ops[ename]:
                        for s, v in waits:
                            e.wait_ge(sems[s], v)
                        if fn is not None:
                            ins = fn(e)
                            if incsem is not None:
                                ins.then_inc(sems[incsem], incv)
                return body
            block.tensor(run('pe'))
            block.scalar(run('act'))
            block.vector(run('dve'))
            block.gpsimd(run('pool'))
            block.sync(run('sp'))


def host_consts():
    p = np.arange(128)
    s = p[:, None]
    t = p[None, :]
    c = {}
    c['idf'] = (s == t)
    c['ones'] = np.ones((128, 128))
    c['Mincl_f'] = (s <= t)
    c['Mb_f'] = (s <= t).astype(np.float64) - (s <= 63)
    c['nMb_f'] = -c['Mb_f']
    c['Mkd_f'] = (s > t)
    c['Mincl_b'] = (s >= t)
    c['Mb_b'] = (s >= t).astype(np.float64) - (s >= 64)
    c['nMb_b'] = -c['Mb_b']
    c['Mkd_b'] = (s < t)
    c['mask_f'] = np.tile((s <= t), (1, 4))
    c['mask_b'] = np.tile((s >= t), (1, 4))
    deltas = np.abs(np.linspace(np.log(1e-2) / 1.5, np.log(1e-2) / 0.3, 512, dtype=np.float32)).astype(np.float64)
    c['delta'] = np.tile(deltas[None, :], (128, 1))
    tt = (np.arange(L).reshape(32, 128).T).astype(np.float64)
    c['ntn'] = -(tt / (L - 1))
    c['alt'] = np.tile(((-1.0) ** p)[:, None], (1, 2))
    c['altrow'] = np.tile(((-1.0) ** np.arange(512))[None, :], (128, 1))
    offs = {}
    cols = []
    o = 0
    for k, v in c.items():
        v = np.asarray(v, dtype=np.float32)
        offs[k] = (o, v.shape[1])
        cols.append(v)
        o += v.shape[1]
    arr = np.concatenate(cols, axis=1).astype(np.float32)
    pos = np.arange(L, dtype=np.float32)
    tn = pos / np.float32(L - 1)
    w = np.float32(2.0 * np.pi) * pos / np.float32(L)
    bands = np.linspace(1e-4, 15, 16, dtype=np.float32)
    ang = w[:, None] * bands[None, :]
    z = np.concatenate([tn[:, None], np.cos(ang), -np.sin(ang)], axis=-1).astype(np.float32)
    zT = np.ascontiguousarray(z.T)
    a = (np.arange(L, dtype=np.int64).reshape(32, 128).T)[None, :, :, None]
    b = np.arange(L, dtype=np.int64).reshape(8, 1, 1, 512)
    m = (a * b) % NFFT
    angm = (2.0 * np.pi / NFFT) * m
    ctab = np.cos(angm).astype(ml_dtypes.bfloat16)
    stab = np.sin(angm).astype(ml_dtypes.bfloat16)
    return arr, offs, zT, ctab, stab


_HC = None


def get_hc():
    global _HC
    if _HC is None:
        _HC = host_consts()
    return _HC


def build_program(debug=False, stop_after=None):
    carr, coffs, _, _, _ = get_hc()
    NCC = carr.shape[1]
    nc = bass.Bass("TRN2", target_bir_lowering=False)
    S = Sched(nc)

    def din(name, shape, dt=F32):
        return nc.dram_tensor(name, list(shape), dt, kind="ExternalInput").ap()

    def dscr(name, shape, dt):
        return nc.dram_tensor(name, list(shape), dt, kind=("ExternalOutput" if debug else "Internal")).ap()

    x = din("x", [L, D]); ctx = din("ctx", [CTX, D]); cvec = din("cvec", [2, D])
    w_ada = din("w_ada", [D, 6 * D]); b_ada = din("b_ada", [6 * D]); norm_w = din("norm_w", [4, D])
    w_in = din("w_in", [D, 6144]); lb_param = din("lb_param", [2, 512]); g_norm = din("g_norm", [128])
    hy_short_w = din("hy_short_w", [3, 1536]); hy_short_b = din("hy_short_b", [1536])
    f_w1 = din("f_w1", [33, 64]); f_b1 = din("f_b1", [64]); f_w2 = din("f_w2", [64, 64]); f_b2 = din("f_b2", [64])
    f_w3 = din("f_w3", [64, 1024]); f_freq = din("f_freq", [64]); hy_d = din("hy_d", [512])
    w_pa = din("w_pa", [512, D]); w_pb = din("w_pb", [512, D]); w_out = din("w_out", [D, D])
    w_mlp1 = din("w_mlp1", [D, 4 * D]); w_mlp2 = din("w_mlp2", [4 * D, D])
    consts_d = din("consts", [128, NCC]); zT_d = din("zT", [33, L])
    ctab_d = din("ctab", [8, 128, 32, 512], BF16); stab_d = din("stab", [8, 128, 32, 512], BF16)
    out = nc.dram_tensor("out", [L, D], F32, kind="ExternalOutput").ap()

    sc_tm = dscr("sc_tm", [NTA, 128, 1536], F32)
    sc_q = dscr("sc_q", [4, 128, L], BF16)
    sc_g = dscr("sc_g", [4, 128, L], BF16)
    sc_x0 = dscr("sc_x0", [4, 128, L], BF16)
    sc_u = dscr("sc_u", [128, 32, 512], BF16)
    sc_gate = dscr("sc_gate", [16, 128, L], BF16)
    sc_Y = dscr("sc_Y", [128, 64, 512], BF16)
    sc_yx0 = dscr("sc_yx0", [4, 128, L], BF16)
    sc_a = dscr("sc_a", [4, 128, L], BF16)
    sc_xlat = dscr("sc_xlat", [L, D], F32)
    sc_hl = dscr("sc_hl", [8, 128, L], BF16)

    ARENA = 102 * 1024
    arena = nc.alloc_sbuf_tensor("arena", [128, ARENA], BF16)
    top = [0]

    def sb(shape, dt, name=None):
        rows = shape[0]
        n = int(np.prod(shape[1:]))
        n16 = n * (2 if dt == F32 else 1)
        off = top[0]
        top[0] += (n16 + 15) // 16 * 16
        assert top[0] <= ARENA, ("SBUF arena overflow", name, top[0])
        ap = arena[0:rows, off:off + n16]
        if dt == F32:
            ap = ap.bitcast(F32)
        if len(shape) == 3:
            ap = ap.rearrange("p (a b) -> p a b", b=shape[2])
        return ap

    ps = [nc.alloc_psum_tensor("ps%d" % i, [128, 512], F32) for i in range(8)]
    psi = [0]

    def nps():
        i = psi[0] % 8
        psi[0] += 1
        return i

    def mm(o, lhsT, rhs, start, stop, reads, writes, inc=None):
        if inc is None:
            inc = stop
        S.op('pe', lambda e: e.matmul(o, lhsT, rhs, start=start, stop=stop), reads, writes, inc)

    def tr(o, i, ident, reads, writes, inc=True):
        S.op('pe', lambda e: e.transpose(o, i, ident), reads, writes, inc)

    def act(o, i, func, reads, writes, **kw):
        S.op('act', lambda e: e.activation(out=o, in_=i, func=func, **kw), reads, writes)

    def tt(eng, o, a, b, op, reads, writes):
        S.op(eng, lambda e: e.tensor_tensor(out=o, in0=a, in1=b, op=op), reads, writes)

    def ts(eng, o, a, s1, s2, op0, op1, reads, writes):
        if s2 is None:
            S.op(eng, lambda e: e.tensor_scalar(out=o, in0=a, scalar1=s1, scalar2=None, op0=op0), reads, writes)
        else:
            S.op(eng, lambda e: e.tensor_scalar(out=o, in0=a, scalar1=s1, scalar2=s2, op0=op0, op1=op1), reads, writes)

    def stt(eng, o, a, sc, b, op0, op1, reads, writes):
        S.op(eng, lambda e: e.scalar_tensor_tensor(out=o, in0=a, scalar=sc, in1=b, op0=op0, op1=op1), reads, writes)

    def interleave(gens):
        gens = list(gens)
        while gens:
            for g_ in list(gens):
                try:
                    next(g_)
                except StopIteration:
                    gens.remove(g_)

    def cp(eng, o, i, reads, writes):
        if eng == 'act':
            act(o, i, AF.Copy, reads, writes)
        else:
            S.op(eng, lambda e: e.tensor_copy(o, i), reads, writes)

    cst = sb([128, NCC], F32, "cst")
    S.dma('sp', cst[:], consts_d, writes=['cst'])

    def C(name, rows=128):
        o, n = coffs[name]
        return cst[0:rows, o:o + n]
    idb = sb([128, 128], BF16, "idb"); onesb = sb([128, 128], BF16, "onesb")
    maskf = sb([128, 512], BF16, "maskf"); maskb = sb([128, 512], BF16, "maskb")
    altb = sb([128, 2], BF16, "altb"); altrowb = sb([1, 512], BF16, "altrowb")
    cp('dve', idb[:], C('idf'), ['cst'], ['idb'])
    cp('dve', onesb[:], C('ones'), ['cst'], ['onesb'])
    cp('dve', maskf[:], C('mask_f'), ['cst'], ['maskf'])
    cp('dve', maskb[:], C('mask_b'), ['cst'], ['maskb'])
    cp('dve', altb[:], C('alt'), ['cst'], ['altb'])
    cp('dve', altrowb[:], C('altrow', 1), ['cst'], ['altrowb'])
    ident = C('idf')

    NSL = dict(allow_slow_non_contiguous=True)
    sm = sb([128, 160], F32, "sm")
    so = {}
    o = [0]

    def smcol(name, n):
        so[name] = o[0]
        o[0] += n
        return sm[:, so[name]:so[name] + n]
    S.dma('sp', smcol('bada', 48), b_ada.rearrange("(t p) -> p t", p=128), writes=['sm'], **NSL)
    S.dma('sp', smcol('nw', 32), norm_w.rearrange("r (t p) -> p (r t)", p=128), writes=['sm'], **NSL)
    S.dma('sp', smcol('c', 8), cvec[0, :].rearrange("(t p) -> p t", p=128), writes=['sm'], **NSL)
    S.dma('sp', smcol('cc', 8), cvec[1, :].rearrange("(t p) -> p t", p=128), writes=['sm'], **NSL)
    S.dma('sp', smcol('lb', 8), lb_param.rearrange("r (t p) -> p (r t)", p=128), writes=['sm'], **NSL)
    S.dma('sp', smcol('gn', 1), g_norm.rearrange("(t p) -> p t", p=128), writes=['sm'], **NSL)
    S.dma('sp', smcol('hsw', 36), hy_short_w.rearrange("r (t p) -> p (r t)", p=128), writes=['sm'], **NSL)
    S.dma('sp', smcol('hsb', 12), hy_short_b.rearrange("(t p) -> p t", p=128), writes=['sm'], **NSL)
    S.dma('sp', smcol('hyd', 4), hy_d.rearrange("(t p) -> p t", p=128), writes=['sm'], **NSL)
    sm64 = sb([64, 4], F32, "sm64")
    S.dma('sp', sm64[:, 0:1], f_b1.rearrange("(p t) -> p t", t=1), writes=['sm64'], **NSL)
    S.dma('sp', sm64[:, 1:2], f_b2.rearrange("(p t) -> p t", t=1), writes=['sm64'], **NSL)
    S.dma('sp', sm64[:, 2:3], f_freq.rearrange("(p t) -> p t", t=1), writes=['sm64'], **NSL)

    def SM(name, i=0, n=1):
        return sm[:, so[name] + i: so[name] + i + n]

    cc2 = sb([128, 8, 2], F32, "cc2")
    cp('dve', cc2[:, :, 0], SM('c', 0, 8), ['sm'], ['cc2'])
    cp('dve', cc2[:, :, 1], SM('cc', 0, 8), ['sm'], ['cc2'])
    e2 = sb([128, 16], F32, "e2")
    cc2f = cc2[:].rearrange("p a b -> p (a b)")
    act(e2[:], cc2f, AF.Exp, ['cc2'], ['e2'], scale=-1.0)
    ts('dve', e2[:], e2[:], 1.0, None, ALU.add, None, ['e2'], ['e2'])
    S.op('dve', lambda e: e.reciprocal(e2[:], e2[:]), ['e2'], ['e2'])
    scb = sb([128, 8, 2], BF16, "scb")
    tt('dve', scb[:].rearrange("p a b -> p (a b)"), cc2f, e2[:], ALU.mult, ['cc2', 'e2'], ['scb'])
    mod = sb([128, 48, 2], F32, "mod")
    fv = sb([128, 8, 8], F32, "fv")
    c1f = sb([128, 4], F32, "c1f")
    c1bc = sb([128, 512], F32, "c1bc")
    hydbc = sb([128, 512], F32, "hydbc")
    diag = [sb([128, 128], F32, "diag") for _ in range(2)]
    ynq = sb([1, 512], BF16, "ynq")
    MARK0 = top[0]
    wad = [sb([128, 8, 512], BF16, "wad") for _ in range(2)]
    def mod_piece(pc):
        wb = wad[pc % 2]
        S.dma('pool', wb[:], w_ada[:, pc * 512:(pc + 1) * 512].rearrange("(k p) n -> p k n", p=128), writes=[('wad', pc % 2)])
        bmp = nps()
        for j in range(4):
            for k in range(8):
                mm(ps[bmp][:, 2 * j:2 * j + 2], wb[:, k, j * 128:(j + 1) * 128], scb[:, k, :], k == 0, k == 7,
                   [('wad', pc % 2), 'scb'], [('ps', bmp)], inc=(k == 7))
        mk_ = ('mod', pc // 2)
        cp('dve', mod[:, pc * 4:pc * 4 + 4, :].rearrange("p a b -> p (a b)"), ps[bmp][:, 0:8], [('ps', bmp)], [mk_])
        for j in range(2):
            tt('dve', mod[:, pc * 4:pc * 4 + 4, j], mod[:, pc * 4:pc * 4 + 4, j], SM('bada', pc * 4, 4), ALU.add, [mk_, 'sm'], [mk_])

    def nwv(r):
        return SM('nw', 8 * r, 8)
    for pc in range(4):
        mod_piece(pc)
    stt('dve', fv[:, 0, :], mod[:, 0:8, 0], 1.0, nwv(0), ALU.add, ALU.mult, [('mod', 0), 'sm'], [('fv', 0)])
    cp('dve', fv[:, 1, :], mod[:, 8:16, 0], [('mod', 1)], [('fv', 0)])
    stt('dve', fv[:, 2, :], mod[:, 0:8, 1], 1.0, nwv(0), ALU.add, ALU.mult, [('mod', 0), 'sm'], [('fv', 0)])
    cp('dve', fv[:, 3, :], mod[:, 8:16, 1], [('mod', 1)], [('fv', 0)])

    def mod_rest_finish():
        tt('dve', fv[:, 4, :], mod[:, 16:24, 0], nwv(1), ALU.mult, [('mod', 2), 'sm'], ['fv'])
        stt('dve', fv[:, 5, :], mod[:, 24:32, 0], 1.0, nwv(2), ALU.add, ALU.mult, [('mod', 3), 'sm'], ['fv'])
        cp('dve', fv[:, 6, :], mod[:, 32:40, 0], [('mod', 4)], ['fv'])
        tt('dve', fv[:, 7, :], mod[:, 40:48, 0], nwv(3), ALU.mult, [('mod', 5), 'sm'], ['fv'])
    tt('dve', c1f[:], SM('lb', 0, 4), SM('lb', 4, 4), ALU.subtract, ['sm'], ['c1f'])
    act(c1f[:], c1f[:], AF.Exp, ['c1f'], ['c1f'])
    act(c1f[:], c1f[:], AF.Ln, ['c1f'], ['c1f'], bias=1.0)
    ts('dve', c1f[:], c1f[:], -1.0, None, ALU.mult, None, ['c1f'], ['c1f'])
    onesf = C('ones')
    di = [0]

    def bcast(dst, col, dkey, rkey):
        d = diag[di[0] % 2]
        dk = ('diag', di[0] % 2)
        di[0] += 1
        ts('dve', d[:], ident, col, None, ALU.mult, None, ['cst', rkey], [dk])
        b = nps()
        mm(ps[b][:, 0:128], onesf, d[:], True, True, ['cst', dk], [('ps', b)])
        cp('act', dst, ps[b][:, 0:128], [('ps', b)], [dkey])
    for k in range(4):
        bcast(c1bc[:, k * 128:(k + 1) * 128], c1f[:, k:k + 1], 'c1bc', 'c1f')
        bcast(hydbc[:, k * 128:(k + 1) * 128], SM('hyd', k, 1), 'hydbc', 'sm')
    hT = sb([128, 8, NTA * 128], BF16, "hT")
    MARK1 = top[0]
    xb = [sb([128, 1024], F32, "xb") for _ in range(3)]
    junk = sb([128, 1024], BF16, "junk")
    xn = [sb([128, 1024], BF16, "xn") for _ in range(3)]
    st = [sb([128, 4], F32, "st") for _ in range(3)]
    wtm = sb([128, 8, 1536], BF16, "wtm")
    for k3 in range(3):
        S.dma('pool', wtm[:, :, k3 * 512:(k3 + 1) * 512], w_in[:, k3 * 512:(k3 + 1) * 512].rearrange("(k p) n -> p k n", p=128), writes=['wtm'])
    tms = [sb([128, 1536], F32, "tms") for _ in range(2)]

    def p1_tile(i):
        s2 = i % 2
        for k3 in range(3):
            b = nps()
            for k in range(8):
                mm(ps[b][:], hT[:, k, i * 128:(i + 1) * 128], wtm[:, k, k3 * 512:(k3 + 1) * 512], k == 0, k == 7,
                   [('hT', i), 'wtm'], [('ps', b)])
            cp('act' if k3 != 1 else 'dve', tms[s2][:, k3 * 512:(k3 + 1) * 512], ps[b][:], [('ps', b)], [('tms', s2)])
        S.dma('sp', sc_tm[i], tms[s2][:], reads=[('tms', s2)], writes=[('sc_tm', i)])
    def a_s1(i):
        s3 = i % 3
        src = ctx[i * 128:(i + 1) * 128, :] if i < 2 else x[(i - 2) * 128:(i - 1) * 128, :]
        S.dma('sp', xb[s3][:], src, writes=[('xb', s3)])
        S.op('pool', lambda e, t_=st[s3]: e.memset(t_[:, 0:1], 0.0), [], [('st', s3)])
        act(junk[:], xb[s3][:], AF.Square, [('xb', s3), ('st', s3)], ['junk', ('st', s3)], accum_out=st[s3][:, 0:1])
        act(st[s3][:, 1:2], st[s3][:, 0:1], AF.Ln, [('st', s3)], [('st', s3)], scale=1.0 / D, bias=EPS)
        act(st[s3][:, 2:3], st[s3][:, 1:2], AF.Exp, [('st', s3)], [('st', s3)], scale=-0.5)
        ts('dve', xn[s3][:], xb[s3][:], st[s3][:, 2:3], None, ALU.mult, None, [('xb', s3), ('st', s3)], [('xn', s3)])

    def a_s2(i):
        s3 = i % 3
        b = nps()
        pb = ps[b][:].bitcast(BF16)
        for k in range(8):
            tr(pb[:, k * 128:(k + 1) * 128], xn[s3][:, k * 128:(k + 1) * 128], idb[:], [('xn', s3), 'idb'], [('ps', b)], inc=(k == 7))
        ai = 0 if i >= 2 else 2
        for k in range(8):
            dst = hT[:, k, i * 128:(i + 1) * 128]
            if k % 2 == 0:
                act(dst, pb[:, k * 128:(k + 1) * 128], AF.Identity, [('ps', b), ('fv', 0)], [('hT', i)],
                    scale=fv[:, ai, k:k + 1], bias=fv[:, ai + 1, k:k + 1])
            else:
                ts('dve', dst, pb[:, k * 128:(k + 1) * 128], fv[:, ai, k:k + 1], fv[:, ai + 1, k:k + 1], ALU.mult, ALU.add,
                   [('ps', b), ('fv', 0)], [('hT', i)])
    npc = [4]
    for it in range(NTA + 2):
        if it < NTA:
            a_s1(it)
        if 0 <= it - 1 < NTA:
            a_s2(it - 1)
        if 0 <= it - 2 < NTA:
            p1_tile(it - 2)
        if it % 4 == 3 and npc[0] < 12:
            mod_piece(npc[0])
            npc[0] += 1
    while npc[0] < 12:
        mod_piece(npc[0])
        npc[0] += 1
    mod_rest_finish()

    if stop_after == 'A':
        for i in range(NTA):
            S.out_toks.append(S.lastw[('sc_tm', i)])
        S.emit()
        return nc

    S.barrier()
    top[0] = MARK1
    wfm = sb([128, 8, 4608], BF16, "wfm")
    for k9 in range(9):
        S.dma('pool', wfm[:, :, k9 * 512:(k9 + 1) * 512], w_in[:, 1536 + k9 * 512:1536 + (k9 + 1) * 512].rearrange("(k p) n -> p k n", p=128), writes=['wfm'])
    stg = [sb([128, 512], BF16, "stg") for _ in range(4)]
    sgi = [0]
    cvt = [sb([128, 512], F32, "cvt") for _ in range(4)]
    ustage = [sb([128, 4, 512], BF16, "ustage") for _ in range(2)]
    ub = [sb([128, 512], BF16, "ub") for _ in range(2)]

    def proj(ft_col, tb):
        b = nps()
        for k in range(8):
            mm(ps[b][:], wfm[:, k, ft_col:ft_col + 128], hT[:, k, 256 + tb * 512:256 + (tb + 1) * 512], k == 0, k == 7,
               ['wfm'] + [('hT', 2 + tb * 4 + j) for j in range(4)], [('ps', b)])
        return b

    def conv(b, ft, dst, dkey, s4):
        cv = cvt[s4]
        ck = ('cvt', s4)
        p3 = ps[b][:].rearrange("p (r c) -> p r c", c=64)
        c3 = cv[:].rearrange("p (r c) -> p r c", c=64)
        act(cv[:], ps[b][:], AF.Identity, [('ps', b), 'sm'], [ck], scale=SM('hsw', 12 + ft, 1), bias=SM('hsb', ft, 1))
        stt('dve', c3[:, :, 1:64], p3[:, :, 0:63], SM('hsw', ft, 1), c3[:, :, 1:64], ALU.mult, ALU.add, [('ps', b), 'sm', ck], [ck])
        d3 = dst.rearrange("p (r c) -> p r c", c=64)
        stt('dve', d3[:, :, 0:63], p3[:, :, 1:64], SM('hsw', 24 + ft, 1), c3[:, :, 0:63], ALU.mult, ALU.add, [('ps', b), 'sm', ck], [dkey])
        cp('pool', d3[:, :, 63:64], c3[:, :, 63:64], [ck], [dkey])

    pend = [None]

    def utrans(u2, j, us, tb):
        bt = nps()
        pbt = ps[bt][:].bitcast(BF16)
        for q4 in range(4):
            tr(pbt[:, q4 * 128:(q4 + 1) * 128], u2[:, q4 * 128:(q4 + 1) * 128], idb[:], [('ub', j % 2), 'idb'], [('ps', bt)], inc=(q4 == 3))
        cp('act', us[:, :, j * 128:(j + 1) * 128], pbt[:, 0:512].rearrange("p (a b) -> p a b", b=128), [('ps', bt)], [('ustage', tb % 2)])
        if j == 3:
            S.dma('sp', sc_u[:, tb * 4:(tb + 1) * 4, :], us[:], reads=[('ustage', tb % 2)], writes=[('sc_u', tb)])
    for tb in range(8):
        tsl = slice(tb * 512, (tb + 1) * 512)
        for ft in range(8):
            b = proj((ft) * 128, tb)
            sg = sgi[0] % 4
            sgi[0] += 1
            act(stg[sg][:], ps[b][:], AF.Silu, [('ps', b)], [('stg', sg)])
            dstd = (sc_q if ft < 4 else sc_g)[ft % 4, :, tsl]
            S.dma('sp', dstd, stg[sg][:], reads=[('stg', sg)], writes=[('sc_qg', ft, tb)])
        for ft in range(16):
            b = proj(2560 + ft * 128, tb)
            sg = sgi[0] % 4
            sgi[0] += 1
            act(stg[sg][:], ps[b][:], AF.Sigmoid, [('ps', b)], [('stg', sg)])
            S.dma('sp', sc_gate[ft, :, tsl], stg[sg][:], reads=[('stg', sg)], writes=[('sc_gate', ft, tb)])
        us = ustage[tb % 2]
        for j in range(4):
            b = proj(1024 + j * 128, tb)
            sg = sgi[0] % 4
            sgi[0] += 1
            conv(b, j, stg[sg][:], ('stg', sg), 0)
            S.dma('sp', sc_x0[j, :, tsl], stg[sg][:], reads=[('stg', sg)], writes=[('sc_x0', j, tb)])
            b1 = proj(1024 + (4 + j) * 128, tb)
            conv(b1, 4 + j, cvt[2][:], ('cvt', 2), 1)
            b2 = proj(1024 + (8 + j) * 128, tb)
            conv(b2, 8 + j, cvt[3][:], ('cvt', 3), 1)
            u2 = ub[j % 2]
            tt('pool', u2[:], cvt[2][:], cvt[3][:], ALU.mult, [('cvt', 2), ('cvt', 3)], [('ub', j % 2)])
            if pend[0] is not None:
                pend[0]()
            pend[0] = (lambda u2_=u2, j_=j, us_=us, tb_=tb: utrans(u2_, j_, us_, tb_))
    pend[0]()
    if stop_after == 'P':
        for k_, v_ in list(S.lastw.items()):
            if isinstance(k_, tuple) and str(k_[0]).startswith('sc_'):
                S.out_toks.append(v_)
        S.emit()
        return nc
    S.barrier()
    top[0] = MARK0
    hcat = sb([128, 32, 1024], BF16, "hcat")
    MARKH = top[0]
    zt = sb([33, L], F32, "zt"); w1s = sb([33, 64], F32, "w1s"); w2s = sb([64, 64], F32, "w2s"); w3s = sb([64, 1024], F32, "w3s")
    h1 = sb([64, L], F32, "h1"); h2 = sb([64, L], F32, "h2"); wtmp = sb([64, 512], F32, "wtmp")
    S.dma('sp', zt[:], zT_d, writes=['zt'])
    S.dma('sp', w1s[:], f_w1, writes=['fw'])
    S.dma('sp', w2s[:], f_w2, writes=['fw'])
    S.dma('sp', w3s[:], f_w3, writes=['fw'])

    def sinlayer(src, w, kdim, bcol, dst, skey, dkey):
        for blk in range(8):
            sl = slice(blk * 512, (blk + 1) * 512)
            b = nps()
            mm(ps[b][0:64, :], w[0:kdim, :], src[0:kdim, sl], True, True, [skey, 'fw'], [('ps', b)])
            ts('dve', dst[:, sl], ps[b][0:64, :], sm64[:, bcol:bcol + 1], sm64[:, 2:3], ALU.add, ALU.mult, [('ps', b), 'sm64'], [dkey])
            for _ in range(2):
                S.op('dve', lambda e, d_=dst[:, sl]: e.tensor_single_scalar(out=wtmp[:], in_=d_, scalar=PI, op=ALU.is_gt), [dkey], ['wtmp'])
                stt('dve', dst[:, sl], wtmp[:], -2 * PI, dst[:, sl], ALU.mult, ALU.add, ['wtmp', dkey], [dkey])
                S.op('dve', lambda e, d_=dst[:, sl]: e.tensor_single_scalar(out=wtmp[:], in_=d_, scalar=-PI, op=ALU.is_lt), [dkey], ['wtmp'])
                stt('dve', dst[:, sl], wtmp[:], 2 * PI, dst[:, sl], ALU.mult, ALU.add, ['wtmp', dkey], [dkey])
            act(dst[:, sl], dst[:, sl], AF.Sin, [dkey], [dkey])
    sinlayer(zt, w1s, 33, 0, h1, 'zt', 'h1')
    sinlayer(h1, w2s, 64, 1, h2, 'h1', 'h2')
    dect = [sb([128, 512], F32, "dect") for _ in range(2)]
    ftmp = sb([128, 512], F32, "ftmp")
    ntn = C('ntn')
    for chunk in range(32):
        dt_ = dect[chunk % 2]
        dk = ('dect', chunk % 2)
        act(dt_[:], C('delta'), AF.Exp, ['cst'], [dk], scale=ntn[:, chunk:chunk + 1])
        for half in range(2):
            b = nps()
            mm(ps[b][:], h2[:, chunk * 128:(chunk + 1) * 128], w3s[:, half * 512:(half + 1) * 512], True, True, ['h2', 'fw'], [('ps', b)])
            dst = hcat[:, chunk, half * 512:(half + 1) * 512]
            if chunk == 0:
                tt('dve', ftmp[:], ps[b][:], dt_[:], ALU.mult, [('ps', b), dk], ['ftmp'])
                if half == 0:
                    tt('dve', ftmp[0:1, :], ftmp[0:1, :], hydbc[0:1, :], ALU.add, ['ftmp', 'hydbc'], ['ftmp'])
                else:
                    S.op('dve', lambda e: e.memset(ftmp[0:1, :], 0.0), ['ftmp'], ['ftmp'])
                cp('dve', dst, ftmp[:], ['ftmp'], ['hcat'])
            else:
                tt('dve', dst, ps[b][:], dt_[:], ALU.mult, [('ps', b), dk], ['hcat'])

    S.barrier()
    top[0] = MARKH
    u_tm = sb([128, 32, 512], BF16, "u_tm")
    for q8 in range(8):
        S.dma('sp', u_tm[:, q8 * 4:(q8 + 1) * 4, :], sc_u[:, q8 * 4:(q8 + 1) * 4, :], reads=[('sc_u', q8)], writes=['u_tm'])
    tabs = [sb([128, 32, 256], BF16, "tabf") for _ in range(2)]
    Ast = sb([128, 2, 512], F32, "Ast"); Pst = sb([128, 2, 512], F32, "Pst")
    tq = [sb([128, 512], F32, "tq") for _ in range(6)]
    yst = [sb([128, 512], BF16, "yst") for _ in range(4)]
    SC = 2.0 / NFFT
    ysi = [0]

    def ynext():
        i_ = ysi[0] % 4
        ysi[0] += 1
        return yst[i_], ('yst', i_)
    for fb in range(16):
        tbi, hf_ = fb // 2, fb % 2
        for cs in range(2):
            tab = tabs[cs]
            tk = ('tab', cs)
            S.dma('sp', tab[:], (ctab_d if cs == 0 else stab_d)[tbi][:, :, hf_ * 256:(hf_ + 1) * 256], writes=[tk])
            for m in range(2):
                bb = [nps(), nps(), nps()]
                for chunk in range(32):
                    lhsT = tab[:, chunk, m * 128:(m + 1) * 128]
                    mm(ps[bb[0]][:], lhsT, u_tm[:, chunk, :], chunk == 0, chunk == 31, [tk, 'u_tm'], [('ps', bb[0])])
                    mm(ps[bb[1]][:], lhsT, hcat[:, chunk, 0:512], chunk == 0, chunk == 31, [tk, 'hcat'], [('ps', bb[1])])
                    mm(ps[bb[2]][:], lhsT, hcat[:, chunk, 512:1024], chunk == 0, chunk == 31, [tk, 'hcat'], [('ps', bb[2])])
                ft = fb * 2 + m
                if cs == 0:
                    cp('act', Ast[:, m, :], ps[bb[0]][:], [('ps', bb[0])], [('Ast', m)])
                    cp('act', tq[0][:], ps[bb[2]][:], [('ps', bb[2])], [('tq', 0)])
                    tt('dve', Pst[:, m, :], ps[bb[1]][:], tq[0][:], ALU.add, [('ps', bb[1]), ('tq', 0)], [('Pst', m)])
                else:
                    cp('act', tq[1][:], ps[bb[2]][:], [('ps', bb[2])], [('tq', 1)])
                    tt('dve', tq[2][:], ps[bb[1]][:], tq[1][:], ALU.subtract, [('ps', bb[1]), ('tq', 1)], [('tq', 2)])
                    tt('pool', tq[3][:], Ast[:, m, :], Pst[:, m, :], ALU.mult, [('Ast', m), ('Pst', m)], [('tq', 3)])
                    stt('dve', tq[4][:], ps[bb[0]][:], SC, tq[2][:], ALU.mult, ALU.mult, [('ps', bb[0]), ('tq', 2)], [('tq', 4)])
                    yr, yrk = ynext()
                    stt('dve', yr[:], tq[3][:], SC, tq[4][:], ALU.mult, ALU.subtract, [('tq', 3), ('tq', 4)], [yrk])
                    if ft == 0:
                        ts('dve', yr[0:1, :], yr[0:1, :], 0.5, None, ALU.mult, None, [yrk], [yrk])
                    S.dma('sp', sc_Y[:, ft, :], yr[:], reads=[yrk], writes=[('sc_Y', ft)])
                    tt('pool', tq[3][:], Ast[:, m, :], tq[2][:], ALU.mult, [('Ast', m), ('tq', 2)], [('tq', 3)])
                    stt('dve', tq[5][:], ps[bb[0]][:], SC, Pst[:, m, :], ALU.mult, ALU.mult, [('ps', bb[0]), ('Pst', m)], [('tq', 5)])
                    yq, yqk = ynext()
                    stt('dve', yq[:], tq[3][:], SC, tq[5][:], ALU.mult, ALU.add, [('tq', 3), ('tq', 5)], [yqk])
                    S.dma('sp', sc_Y[:, 32 + ft, :], yq[:], reads=[yqk], writes=[('sc_Y', 32 + ft)])
    bn = [nps(), nps(), nps()]
    for chunk in range(32):
        mm(ps[bn[0]][0:1, :], altb[:, 0:1], u_tm[:, chunk, :], chunk == 0, chunk == 31, ['altb', 'u_tm'], [('ps', bn[0])])
        mm(ps[bn[1]][0:1, :], altb[:, 0:1], hcat[:, chunk, 0:512], chunk == 0, chunk == 31, ['altb', 'hcat'], [('ps', bn[1])])
        mm(ps[bn[2]][0:1, :], altb[:, 0:1], hcat[:, chunk, 512:1024], chunk == 0, chunk == 31, ['altb', 'hcat'], [('ps', bn[2])])
    cp('act', tq[0][0:1, :], ps[bn[2]][0:1, :], [('ps', bn[2])], [('tq', 0)])
    tt('dve', tq[1][0:1, :], ps[bn[1]][0:1, :], tq[0][0:1, :], ALU.add, [('ps', bn[1]), ('tq', 0)], [('tq', 1)])
    stt('dve', ynq[:], ps[bn[0]][0:1, :], 1.0 / NFFT, tq[1][0:1, :], ALU.mult, ALU.mult, [('ps', bn[0]), ('tq', 1)], ['ynq'])

    S.barrier()
    top[0] = MARK0
    Yt = sb([128, 64, 512], BF16, "Yt")
    for q8 in range(8):
        S.dma('sp', Yt[:, q8 * 8:(q8 + 1) * 8, :], sc_Y[:, q8 * 8:(q8 + 1) * 8, :],
              reads=[('sc_Y', f_) for f_ in range(q8 * 8, q8 * 8 + 8)], writes=[('Yt', q8)])
    tabi = [sb([128, 32, 512], BF16, "tabi") for _ in range(2)]
    x0s = [sb([128, 512], BF16, "x0s") for _ in range(4)]
    yo = [sb([128, 512], BF16, "yo") for _ in range(2)]
    S.dma('sp', tabi[0][:], ctab_d[0], writes=[('tabi', 0)])
    S.dma('sp', tabi[1][:], stab_d[0], writes=[('tabi', 1)])
    for tb in range(8):
        tsl = slice(tb * 512, (tb + 1) * 512)
        for ct in range(4):
            S.dma('act', x0s[ct][:], sc_x0[ct, :, tsl], reads=[('sc_x0', ct, tb)], writes=[('x0s', ct)])
        banks = [nps() for _ in range(4)]
        for cs in range(2):
            for ct in range(4):
                for chunk in range(32):
                    mm(ps[banks[ct]][:], Yt[:, cs * 32 + chunk, ct * 128:(ct + 1) * 128], tabi[cs][:, chunk, :],
                       cs == 0 and chunk == 0, False, [('Yt', (cs * 32 + chunk) // 8), ('tabi', cs)], [('ps', banks[ct])], inc=(chunk == 31))
            if tb + 1 < 8:
                S.dma('sp', tabi[cs][:], (ctab_d if cs == 0 else stab_d)[tb + 1], writes=[('tabi', cs)])
        for ct in range(4):
            mm(ps[banks[ct]][:], ynq[0:1, ct * 128:(ct + 1) * 128], altrowb[0:1, :], False, True, ['ynq', 'altrowb'], [('ps', banks[ct])], inc=True)
            s2 = ct % 2
            tt('dve', yo[s2][:], ps[banks[ct]][:], x0s[ct][:], ALU.mult, [('ps', banks[ct]), ('x0s', ct)], [('yo', s2)])
            S.dma('act', sc_yx0[ct, :, tsl], yo[s2][:], reads=[('yo', s2)], writes=[('sc_yx0', ct, tb)])
    if stop_after == 'H':
        for k_, v_ in list(S.lastw.items()):
            if isinstance(k_, tuple) and str(k_[0]).startswith('sc_'):
                S.out_toks.append(v_)
        S.emit()
        return nc

    S.barrier()
    top[0] = MARK0
    tmb = [sb([128, 1536], F32, "tmb") for _ in range(2)]
    vb = [sb([128, 512], BF16, "vb") for _ in range(2)]
    ebuf = [sb([128, 512], F32, "ebuf") for _ in range(2)]
    lp = [sb([128, 512], F32, "lp") for _ in range(2)]
    kbuf = [sb([128, 512], F32, "kbuf") for _ in range(2)]
    lnk = [sb([128, 512], F32, "lnk") for _ in range(2)]
    logf = [sb([128, 512], F32, "logf") for _ in range(2)]
    Ea = [[sb([128, 4, 128], BF16, "Ea") for _ in range(2)] for _ in range(2)]
    Eb = [[sb([128, 4, 128], BF16, "Eb") for _ in range(2)] for _ in range(2)]
    keT = [[sb([128, 512], BF16, "keT") for _ in range(2)] for _ in range(2)]
    kd = [[sb([128, 512], BF16, "kd") for _ in range(2)] for _ in range(2)]
    dec = [[sb([128, 4], F32, "dec") for _ in range(2)] for _ in range(2)]
    Sf = [sb([128, 4, 128], F32, "Sf") for _ in range(2)]
    Sb16 = [sb([128, 4, 128], BF16, "Sb16") for _ in range(2)]
    sbs = sb([128, 32, 512], BF16, "sbs")
    qtb = [sb([128, 4, 128], BF16, "qtb") for _ in range(2)]
    gtb = [sb([128, 4, 128], BF16, "gtb") for _ in range(2)]
    qa = [sb([128, 4, 128], BF16, "qa") for _ in range(2)]
    qe = [sb([128, 4, 128], BF16, "qe") for _ in range(2)]
    scT = [sb([128, 512], BF16, "scT") for _ in range(2)]
    sqb = sb([128, 512], BF16, "sqb"); rs = sb([128, 512], F32, "rs"); otf = sb([128, 512], F32, "otf")
    atb = [sb([128, 512], BF16, "atb") for _ in range(2)]
    flat = lambda a_: a_.rearrange("p a b -> p (a b)")
    for d in range(2):
        S.op('pool', lambda e, t_=Sf[d]: e.memset(flat(t_), 0.0), [], [('Sf', d)])
        S.op('pool', lambda e, t_=Sb16[d]: e.memset(flat(t_), 0.0), [], [('Sb', d)])

    def gate(d, tm, tmk, sl, lite=False):
        zz = tm[:, 512 * (1 + d):512 * (2 + d)]
        eb_, lp_, kb_ = ebuf[d], lp[d], kbuf[d]
        act(eb_[:], zz, AF.Exp, [tmk], [('ebuf', d)])
        act(lp_[:], eb_[:], AF.Ln, [('ebuf', d)], [('lp', d)], bias=1.0)
        yield
        stt('dve', lnk[d][:], lp_[:], -1.0, c1bc[:], ALU.mult, ALU.add, [('lp', d), 'c1bc'], [('lnk', d)])
        yield
        act(kb_[:], lnk[d][:], AF.Exp, [('lnk', d)], [('kbuf', d)])
        act(logf[d][:], kb_[:], AF.Ln, [('kbuf', d)], [('logf', d)], scale=-1.0, bias=1.0)
        yield
        sfx = 'f' if d == 0 else 'b'
        oi = coffs['Mincl_' + sfx][0]
        Mcat = cst[:, oi:oi + 256]
        nMb = C('nMb_' + sfx)
        Mkd = C('Mkd_' + sfx)
        bkd = nps()
        mm(ps[bkd][:], Mkd, logf[d][:], True, True, [('logf', d), 'cst'], [('ps', bkd)], inc=True)
        if lite:
            bt_ = nps()
            for h in range(4):
                mm(ps[bt_][:, h:h + 1], logf[d][:, h * 128:(h + 1) * 128], onesf[:, 0:1], True, True, [('logf', d), 'cst'], [('ps', bt_)], inc=(h == 3))
        else:
            bab = [nps(), nps()]
            for h in range(4):
                mm(ps[bab[h // 2]][:, (h % 2) * 256:(h % 2) * 256 + 256], logf[d][:, h * 128:(h + 1) * 128], Mcat, True, True,
                   [('logf', d), 'cst'], [('ps', bab[h // 2])], inc=(h % 2 == 1))
            bke = nps()
            for h in range(4):
                mm(ps[bke][:, h * 128:(h + 1) * 128], lnk[d][:, h * 128:(h + 1) * 128], ident, True, False, [('lnk', d), 'cst'], [('ps', bke)], inc=False)
                mm(ps[bke][:, h * 128:(h + 1) * 128], logf[d][:, h * 128:(h + 1) * 128], nMb, False, True, [('logf', d), 'cst'], [('ps', bke)], inc=(h == 3))
        yield
        act(eb_[:], ps[bkd][:], AF.Exp, [('ps', bkd)], [('ebuf', d)])
        if lite:
            act(dec[d][sl][:], ps[bt_][:, 0:4], AF.Exp, [('ps', bt_)], [('dec', d, sl)])
            yield
            tt('dve', kd[d][sl][:], eb_[:], kb_[:], ALU.mult, [('ebuf', d), ('kbuf', d)], [('kd', d, sl)])
            return
        col = 127 if d == 0 else 0
        for j in range(2):
            v3 = ps[bab[j]][:].rearrange("p (h x) -> p h x", x=256)
            act(Ea[d][sl][:, 2 * j:2 * j + 2, :], v3[:, :, 0:128], AF.Exp, [('ps', bab[j])], [('Ea', d, sl)])
            act(Eb[d][sl][:, 2 * j:2 * j + 2, :], v3[:, :, 128:256], AF.Exp, [('ps', bab[j])], [('Eb', d, sl)])
            for hh in range(2):
                act(dec[d][sl][:, 2 * j + hh:2 * j + hh + 1], ps[bab[j]][:, hh * 256 + col:hh * 256 + col + 1], AF.Exp, [('ps', bab[j])], [('dec', d, sl)])
        act(keT[d][sl][:], ps[bke][:], AF.Exp, [('ps', bke)], [('keT', d, sl)])
        yield
        tt('dve', kd[d][sl][:], eb_[:], kb_[:], ALU.mult, [('ebuf', d), ('kbuf', d)], [('kd', d, sl)])

    def update(d, vt, vk, sl):
        b = nps()
        for h in range(4):
            mm(ps[b][:, h * 128:(h + 1) * 128], kd[d][sl][:, h * 128:(h + 1) * 128], vt[:, h * 128:(h + 1) * 128], True, True,
               [('kd', d, sl), vk], [('ps', b)], inc=(h == 3))
        for h in range(4):
            stt('dve', Sf[d][:, h, :], Sf[d][:, h, :], dec[d][sl][:, h:h + 1], ps[b][:, h * 128:(h + 1) * 128], ALU.mult, ALU.add,
                [('Sf', d), ('dec', d, sl), ('ps', b)], [('Sf', d)])
        cp('act', flat(Sb16[d]), flat(Sf[d]), [('Sf', d)], [('Sb', d)])

    def load_tm(i):
        s2 = i % 2
        S.dma('sp', tmb[s2][:], sc_tm[i], reads=[('sc_tm', i)], writes=[('tmb', s2)])
        cp('dve', vb[s2][:], tmb[s2][:, 0:512], [('tmb', s2)], [('vb', s2)])
        return tmb[s2], ('tmb', s2), vb[s2], ('vb', s2)
    gi = [0]

    steps = [(0, 0), (0, 1), (1, 1), (1, 0)] + [(1, n_ + 2) for n_ in range(31, -1, -1)]

    def lite_gate(k_):
        d_, i_ = steps[k_]
        sl = k_ % 2
        tm, tmk, vt, vk = load_tm(i_)
        for _ in gate(d_, tm, tmk, sl, lite=True):
            pass
        return (d_, vt, vk, sl, i_)
    pendg = lite_gate(0)
    for k_ in range(len(steps)):
        cur = pendg
        if k_ + 1 < len(steps):
            pendg = lite_gate(k_ + 1)
        d_, vt, vk, sl, i_ = cur
        if d_ == 1 and i_ >= 2:
            cp('act', sbs[:, i_ - 2, :], flat(Sb16[1]), [('Sb', 1)], [('sbs', i_ - 2)])
        update(d_, vt, vk, sl)
    loaded = {}

    def prep(n):
        tm, tmk, vt, vk = load_tm(n + 2)
        loaded[n] = (vt, vk)
        q2 = n % 2
        tsl = slice(n * 128, (n + 1) * 128)
        S.dma('sp', qtb[q2][:], sc_q[:, :, tsl].rearrange("h p t -> p h t"), reads=[('sc_qg', h_, n // 4) for h_ in range(4)], writes=[('qtb', q2)])
        S.dma('sp', gtb[q2][:], sc_g[:, :, tsl].rearrange("h p t -> p h t"), reads=[('sc_qg', 4 + h_, n // 4) for h_ in range(4)], writes=[('gtb', q2)])
        interleave([gate(0, tm, tmk, q2), gate(1, tm, tmk, q2)])
    prep(0)
    for n in range(32):
        if n + 1 < 32:
            prep(n + 1)
        vt, vk = loaded[n]
        q2 = n % 2
        tsl = slice(n * 128, (n + 1) * 128)
        for d in range(2):
            en = 'dve'
            tt(en, flat(qa[d]), flat(qtb[q2]), flat(Ea[d][q2]), ALU.mult, [('qtb', q2), ('Ea', d, q2)], [('qa', d)])
            tt(en, flat(qe[d]), flat(qtb[q2]), flat(Eb[d][q2]), ALU.mult, [('qtb', q2), ('Eb', d, q2)], [('qe', d)])
            bs = nps()
            for h in range(4):
                mm(ps[bs][:, h * 128:(h + 1) * 128], keT[d][q2][:, h * 128:(h + 1) * 128], qe[d][:, h, :], True, True,
                   [('keT', d, q2), ('qe', d)], [('ps', bs)], inc=(h == 3))
            mk = maskf if d == 0 else maskb
            tt('dve', scT[d][:], ps[bs][:], mk[:], ALU.mult, [('ps', bs), 'maskf', 'maskb'], [('scT', d)])
        bo = nps()
        for h in range(4):
            o_ = ps[bo][:, h * 128:(h + 1) * 128]
            hs = slice(h * 128, (h + 1) * 128)
            mm(o_, vt[:, hs], scT[0][:, hs], True, False, [vk, ('scT', 0)], [('ps', bo)], inc=False)
            mm(o_, vt[:, hs], scT[1][:, hs], False, False, [vk, ('scT', 1)], [('ps', bo)], inc=False)
            mm(o_, Sb16[0][:, h, :], qa[0][:, h, :], False, False, [('Sb', 0), ('qa', 0)], [('ps', bo)], inc=False)
            mm(o_, sbs[:, n, hs], qa[1][:, h, :], False, True, [('sbs', n), ('qa', 1)], [('ps', bo)], inc=(h == 3))
        act(sqb[:], ps[bo][:], AF.Square, [('ps', bo)], ['sqb'])
        bss = nps()
        mm(ps[bss][:], onesb[:], sqb[:], True, True, ['onesb', 'sqb'], [('ps', bss)])
        act(rs[:], ps[bss][:], AF.Ln, [('ps', bss)], ['rs'], scale=1.0 / 128, bias=EPS)
        act(rs[:], rs[:], AF.Exp, ['rs'], ['rs'], scale=-0.5)
        tt('dve', otf[:], ps[bo][:], rs[:], ALU.mult, [('ps', bo), 'rs'], ['otf'])
        stt('dve', atb[q2][:], otf[:], SM('gn'), flat(gtb[q2]), ALU.mult, ALU.mult, ['otf', 'sm', ('gtb', q2)], [('atb', q2)])
        S.dma('sp', sc_a[:, :, tsl].rearrange("h p t -> p h t"), atb[q2][:].rearrange("p (h t) -> p h t", t=128),
              reads=[('atb', q2)], writes=[('sc_a', n)])
        update(0, vt, vk, q2)
    if stop_after == 'G':
        for k_, v_ in list(S.lastw.items()):
            if isinstance(k_, tuple) and str(k_[0]).startswith('sc_'):
                S.out_toks.append(v_)
        S.emit()
        return nc

    S.barrier()
    top[0] = MARK0
    bcv = sb([128, 4, 1024], F32, "bcv")
    for vi in range(4):
        for k in range(8):
            bcast(bcv[:, vi, k * 128:(k + 1) * 128], fv[:, 4 + vi, k:k + 1], 'bcv', 'fv')
    wpa = sb([128, 4, 1024], BF16, "wpa"); wpb = sb([128, 4, 1024], BF16, "wpb"); wout = sb([128, 8, 1024], BF16, "wout")
    S.dma('pool', wpa[:], w_pa.rearrange("(k p) n -> p k n", p=128), writes=['wpa'])
    S.dma('pool', wpb[:], w_pb.rearrange("(k p) n -> p k n", p=128), writes=['wpb'])
    for k2 in range(2):
        S.dma('pool', wout[:, :, k2 * 512:(k2 + 1) * 512], w_out[:, k2 * 512:(k2 + 1) * 512].rearrange("(k p) n -> p k n", p=128), writes=['wout'])
    aTb = [sb([128, 4, 512], BF16, "aTb") for _ in range(2)]
    yxb = [sb([128, 4, 512], BF16, "yxb") for _ in range(2)]
    ggb = [sb([128, 16, 512], BF16, "ggb") for _ in range(2)]
    merged = sb([128, 8, 512], BF16, "merged")
    m1 = [sb([128, 512], F32, "m1") for _ in range(2)]
    xt = [sb([128, 1024], F32, "xt") for _ in range(2)]
    xl = [sb([128, 1024], F32, "xl") for _ in range(2)]
    tmpf = [sb([128, 1024], F32, "tmpf") for _ in range(2)]
    hlb = [sb([128, 1024], BF16, "hlb") for _ in range(2)]
    hls = [sb([128, 8, 512], BF16, "hls") for _ in range(2)]
    junk2 = [sb([128, 1024], BF16, "junk2") for _ in range(2)]

    def interleave(gens):
        gens = list(gens)
        while gens:
            for g_ in list(gens):
                try:
                    next(g_)
                except StopIteration:
                    gens.remove(g_)
    st2 = [sb([128, 8], F32, "st2") for _ in range(2)]
    for tb in range(8):
        tsl = slice(tb * 512, (tb + 1) * 512)
        b2 = tb % 2
        S.dma('sp', aTb[b2][:], sc_a[:, :, tsl].rearrange("h p t -> p h t"), reads=[('sc_a', tb * 4 + j_) for j_ in range(4)], writes=[('aTb', b2)])
        S.dma('sp', yxb[b2][:], sc_yx0[:, :, tsl].rearrange("h p t -> p h t"), reads=[('sc_yx0', c_, tb) for c_ in range(4)], writes=[('yxb', b2)])
        S.dma('sp', ggb[b2][:], sc_gate[:, :, tsl].rearrange("f p t -> p f t"), reads=[('sc_gate', f_, tb) for f_ in range(16)], writes=[('ggb', b2)])
        for dm in range(8):
            bA = nps()
            for k in range(4):
                mm(ps[bA][:], wpa[:, k, dm * 128:(dm + 1) * 128], aTb[b2][:, k, :], k == 0, k == 3, ['wpa', ('aTb', b2)], [('ps', bA)])
            bB = nps()
            for k in range(4):
                mm(ps[bB][:], wpb[:, k, dm * 128:(dm + 1) * 128], yxb[b2][:, k, :], k == 0, k == 3, ['wpb', ('yxb', b2)], [('ps', bB)])
            tt('dve', m1[0][:], ps[bA][:], ggb[b2][:, dm, :], ALU.mult, [('ps', bA), ('ggb', b2)], [('m1', 0)])
            tt('dve', m1[1][:], ps[bB][:], ggb[b2][:, 8 + dm, :], ALU.mult, [('ps', bB), ('ggb', b2)], [('m1', 1)])
            tt('pool', merged[:, dm, :], m1[0][:], m1[1][:], ALU.add, [('m1', 0), ('m1', 1)], [('merged', dm)])
        hs_ = hls[b2]

        def t1_tile(t4, tb=tb, hs_=hs_, b2=b2):
            n = tb * 4 + t4
            s2 = n % 2
            tf = tmpf[s2]
            tfk = ('tmpf', s2)
            S.dma('sp', xt[s2][:], x[n * 128:(n + 1) * 128, :], writes=[('xt', s2)])
            S.op('pool', lambda e, t_=st2[s2]: e.memset(t_[:, 0:2], 0.0), [], [('st2', s2)])
            bh = [nps(), nps()]
            for half in range(2):
                for k in range(8):
                    mm(ps[bh[half]][:], merged[:, k, t4 * 128:(t4 + 1) * 128], wout[:, k, half * 512:(half + 1) * 512], k == 0, k == 7,
                       [('merged', k), 'wout'], [('ps', bh[half])])
                act(junk2[s2][:, 0:512], ps[bh[half]][:], AF.Square, [('ps', bh[half]), ('st2', s2)], [('junk2', s2), ('st2', s2)], accum_out=st2[s2][:, half:half + 1])
            yield
            tt('dve', st2[s2][:, 2:3], st2[s2][:, 0:1], st2[s2][:, 1:2], ALU.add, [('st2', s2)], [('st2', s2)])
            yield
            act(st2[s2][:, 3:4], st2[s2][:, 2:3], AF.Ln, [('st2', s2)], [('st2', s2)], scale=1.0 / D, bias=EPS)
            act(st2[s2][:, 4:5], st2[s2][:, 3:4], AF.Exp, [('st2', s2)], [('st2', s2)], scale=-0.5)
            yield
            for half in range(2):
                hsl = slice(half * 512, (half + 1) * 512)
                stt('dve', tf[:, hsl], ps[bh[half]][:], st2[s2][:, 4:5], bcv[:, 0, hsl], ALU.mult, ALU.mult, [('ps', bh[half]), ('st2', s2), 'bcv'], [tfk])
            yield
            tt('pool', xl[s2][:], tf[:], xt[s2][:], ALU.add, [tfk, ('xt', s2)], [('xl', s2)])
            S.dma('sp', sc_xlat[n * 128:(n + 1) * 128, :], xl[s2][:], reads=[('xl', s2)], writes=[('sc_xlat', n)])
            S.op('pool', lambda e, t_=st2[s2]: e.memset(t_[:, 5:6], 0.0), [], [('st2', s2)])
            yield
            act(junk2[s2][:], xl[s2][:], AF.Square, [('xl', s2), ('st2', s2)], [('junk2', s2), ('st2', s2)], accum_out=st2[s2][:, 5:6])
            act(st2[s2][:, 6:7], st2[s2][:, 5:6], AF.Ln, [('st2', s2)], [('st2', s2)], scale=1.0 / D, bias=EPS)
            act(st2[s2][:, 7:8], st2[s2][:, 6:7], AF.Exp, [('st2', s2)], [('st2', s2)], scale=-0.5)
            yield
            stt('dve', tf[:], xl[s2][:], st2[s2][:, 7:8], bcv[:, 1, :], ALU.mult, ALU.mult, [('xl', s2), ('st2', s2), 'bcv'], [tfk])
            yield
            tt('pool', hlb[s2][:], tf[:], bcv[:, 2, :], ALU.add, [tfk, 'bcv'], [('hlb', s2)])
            yield
            bt = nps()
            pbt = ps[bt][:].bitcast(BF16)
            for k in range(8):
                tr(pbt[:, k * 128:(k + 1) * 128], hlb[s2][:, k * 128:(k + 1) * 128], idb[:], [('hlb', s2), 'idb'], [('ps', bt)], inc=(k == 7))
            yield
            cp('act', hs_[:, :, t4 * 128:(t4 + 1) * 128], pbt[:, 0:1024].rearrange("p (a b) -> p a b", b=128), [('ps', bt)], [('hls', b2)])
        interleave([t1_tile(0), t1_tile(1)])
        interleave([t1_tile(2), t1_tile(3)])
        S.dma('sp', sc_hl[:, :, tsl].rearrange("k p t -> p k t"), hs_[:], reads=[('hls', b2)], writes=[('sc_hl', tb)])

    S.barrier()
    top[0] = MARK0
    g5bc = sb([128, 1024], F32, "g5bc")
    for k in range(8):
        bcast(g5bc[:, k * 128:(k + 1) * 128], fv[:, 7, k:k + 1], 'g5bc', 'fv')
    w1 = sb([128, 8, 4096], BF16, "w1"); w2 = sb([128, 32, 1024], BF16, "w2")
    for k8 in range(8):
        S.dma('pool', w1[:, :, k8 * 512:(k8 + 1) * 512], w_mlp1[:, k8 * 512:(k8 + 1) * 512].rearrange("(k p) n -> p k n", p=128), writes=[('w1', k8)])
    for k8 in range(8):
        S.dma('pool', w2[:, k8 * 4:(k8 + 1) * 4, :], w_mlp2[k8 * 512:(k8 + 1) * 512, :].rearrange("(k p) n -> p k n", p=128), writes=[('w2', k8)])
    hlt = [sb([128, 8, 256], BF16, "hlt") for _ in range(2)]
    hid = sb([128, 32, 256], BF16, "hid")
    rr = [sb([128, 256], F32, "rr") for _ in range(2)]
    xl2 = [sb([128, 1024], F32, "xl2") for _ in range(2)]
    ot = [sb([128, 1024], F32, "ot") for _ in range(2)]
    junk3 = [sb([128, 512], BF16, "junk3") for _ in range(2)]
    st3 = [sb([128, 8], F32, "st3") for _ in range(2)]
    for tb in range(16):
        b2 = tb % 2
        tsl = slice(tb * 256, (tb + 1) * 256)
        S.dma('sp', hlt[b2][:], sc_hl[:, :, tsl].rearrange("k p t -> p k t"), reads=[('sc_hl', tb // 2)], writes=[('hlt', b2)])
        for ff in range(32):
            b = nps()
            for k in range(8):
                mm(ps[b][:, 0:256], w1[:, k, ff * 128:(ff + 1) * 128], hlt[b2][:, k, :], k == 0, k == 7, [('w1', ff // 4), ('hlt', b2)], [('ps', b)])
            r2 = ff % 2
            act(rr[r2][:], ps[b][:, 0:256], AF.Relu, [('ps', b)], [('rr', r2)])
            tt('pool' if ff % 4 else 'dve', hid[:, ff, :], rr[r2][:], rr[r2][:], ALU.mult, [('rr', r2)], [('hid', ff)])
        def t2_tile(t2, tb=tb):
            n = tb * 2 + t2
            s2 = n % 2
            S.dma('sp', xl2[s2][:], sc_xlat[n * 128:(n + 1) * 128, :], reads=[('sc_xlat', n)], writes=[('xl2', s2)])
            S.op('pool', lambda e, t_=st3[s2]: e.memset(t_[:, 0:2], 0.0), [], [('st3', s2)])
            bh = [nps(), nps()]
            for half in range(2):
                for k in range(32):
                    mm(ps[bh[half]][:], hid[:, k, t2 * 128:(t2 + 1) * 128], w2[:, k, half * 512:(half + 1) * 512], k == 0, k == 31,
                       [('hid', k), ('w2', k // 4)], [('ps', bh[half])])
                act(junk3[s2][:], ps[bh[half]][:], AF.Square, [('ps', bh[half]), ('st3', s2)], [('junk3', s2), ('st3', s2)], accum_out=st3[s2][:, half:half + 1])
            yield
            tt('dve', st3[s2][:, 2:3], st3[s2][:, 0:1], st3[s2][:, 1:2], ALU.add, [('st3', s2)], [('st3', s2)])
            yield
            act(st3[s2][:, 3:4], st3[s2][:, 2:3], AF.Ln, [('st3', s2)], [('st3', s2)], scale=1.0 / D, bias=EPS)
            act(st3[s2][:, 4:5], st3[s2][:, 3:4], AF.Exp, [('st3', s2)], [('st3', s2)], scale=-0.5)
            yield
            for half in range(2):
                hsl = slice(half * 512, (half + 1) * 512)
                stt('dve', ot[s2][:, hsl], ps[bh[half]][:], st3[s2][:, 4:5], g5bc[:, hsl], ALU.mult, ALU.mult, [('ps', bh[half]), ('st3', s2), 'g5bc'], [('ot', s2)])
            yield
            tt('pool', ot[s2][:], ot[s2][:], xl2[s2][:], ALU.add, [('ot', s2), ('xl2', s2)], [('ot', s2)])
            S.dma('sp', out[n * 128:(n + 1) * 128, :], ot[s2][:], reads=[('ot', s2)], is_out=True)
        interleave([t2_tile(0), t2_tile(1)])
    S.emit()
    return nc


def make_in_maps(inputs):
    carr, coffs, zT, ctab, stab = get_hc()
    g = lambda k: np.ascontiguousarray(np.asarray(inputs[k], dtype=np.float32))
    maps = []
    for b in range(8):
        m = {
            "x": g('x')[b], "ctx": g('ctx')[b],
            "cvec": np.ascontiguousarray(np.stack([g('c')[b], g('c_ctx')], axis=0)),
            "w_ada": g('w_ada')[0], "b_ada": g('b_ada')[0], "norm_w": g('norm_w')[0], "w_in": g('w_in')[0],
            "lb_param": g('lb_param'), "g_norm": g('g_norm')[0], "hy_short_w": g('hy_short_w')[0],
            "hy_short_b": g('hy_short_b')[0], "f_w1": g('f_w1')[0], "f_b1": g('f_b1')[0], "f_w2": g('f_w2')[0],
            "f_b2": g('f_b2')[0], "f_w3": g('f_w3')[0], "f_freq": g('f_freq')[0], "hy_d": g('hy_d')[0],
            "w_pa": g('w_pa')[0], "w_pb": g('w_pb')[0], "w_out": g('w_out')[0], "w_mlp1": g('w_mlp1')[0],
            "w_mlp2": g('w_mlp2')[0], "consts": carr, "zT": zT, "ctab": ctab, "stab": stab,
        }
        maps.append(m)
    return maps


def kernel(**inputs):
    nc = build_program()
    maps = make_in_maps(inputs)
    res = run_bass_kernel_spmd(nc, maps, core_ids=list(range(8)))
    return np.stack([np.asarray(r["out"], dtype=np.float32) for r in res.results], axis=0)
```

```python
import contextlib
import numpy as np
import ml_dtypes
import concourse.bass as bass
import concourse.mybir as mybir
from concourse.bass_utils import run_bass_kernel_spmd

F32 = mybir.dt.float32
BF16 = mybir.dt.bfloat16
AF = mybir.ActivationFunctionType
ALU = mybir.AluOpType

D = 1024
L = 4096
CTX = 256
NT = 32
NTA = 34
EPS = 1e-6
NFFT = 8192
PI = float(np.pi)


class Sched:
    ENG = ('pe', 'act', 'dve', 'pool', 'sp')

    def __init__(self, nc, ndma=12, nswd=6):
        self.nc = nc
        self.ops = {e: [] for e in self.ENG}
        self.cnt = {e: 0 for e in self.ENG}
        self.known = {e: {} for e in self.ENG}
        self.lastw = {}
        self.readers = {}
        self.ndma = ndma + nswd
        self.nhw = ndma
        self.nswd = nswd
        self.dma_n = 0
        self.swd_n = 0
        self.dma_cnt = [0] * (ndma + nswd)
        self.out_toks = []

    def _need(self, eng, tok, waits, same_ok):
        if tok is None:
            return
        sem, val = tok
        if sem == eng and same_ok:
            return
        if self.known[eng].get(sem, 0) >= val:
            return
        if sem == eng:
            assert val <= self.cnt[eng], "same-engine wait on un-inc'd op"
        self.known[eng][sem] = val
        waits.append((sem, val))

    def _deps(self, eng, reads, writes):
        waits = []
        for k in reads:
            self._need(eng, self.lastw.get(k), waits, False)
            if isinstance(k, tuple) and k[0] == 'ps':
                for s, v in self.readers.get(k, {}).items():
                    self._need(eng, (s, v), waits, True)
        for k in writes:
            self._need(eng, self.lastw.get(k), waits, True)
            for s, v in self.readers.get(k, {}).items():
                self._need(eng, (s, v), waits, True)
        return waits

    def _commit(self, tok, reads, writes):
        for k in reads:
            d = self.readers.setdefault(k, {})
            if d.get(tok[0], 0) < tok[1]:
                d[tok[0]] = tok[1]
        for k in writes:
            self.lastw[k] = tok
            self.readers[k] = {}

    def op(self, eng, fn, reads=(), writes=(), inc=True):
        waits = self._deps(eng, reads, writes)
        if inc:
            self.cnt[eng] += 1
            tok = (eng, self.cnt[eng])
        else:
            tok = (eng, self.cnt[eng] + 1)
        self._commit(tok, reads, writes)
        self.ops[eng].append((waits, fn, eng if inc else None, 1))
        return tok

    def dma(self, q, out, in_, reads=(), writes=(), is_out=False, **kw):
        if q == 'pool':
            slot = self.nhw + self.swd_n % self.nswd
            self.swd_n += 1
        else:
            slot = self.dma_n % self.nhw
            self.dma_n += 1
        sem = 'dma%d' % slot
        waits = self._deps(q, reads, writes)
        prev = self.dma_cnt[slot]
        if prev > 0:
            self._need(q, (sem, prev), waits, False)
        self.dma_cnt[slot] += 16
        tok = (sem, self.dma_cnt[slot])
        self._commit(tok, reads, writes)
        self.ops[q].append((waits, lambda e: e.dma_start(out=out, in_=in_, **kw), sem, 16))
        if is_out:
            self.out_toks.append(tok)
        return tok

    def barrier(self):
        for e in self.ENG:
            waits = []
            for f in self.ENG[:4]:
                if f != e and self.cnt[f] > 0:
                    self._need(e, (f, self.cnt[f]), waits, False)
            for s in range(self.ndma):
                if self.dma_cnt[s] > 0:
                    self._need(e, ('dma%d' % s, self.dma_cnt[s]), waits, False)
            self.ops[e].append((waits, None, None, 0))

    def emit(self):
        nc = self.nc
        waits = []
        for tok in self.out_toks:
            self._need('sp', tok, waits, False)
        self.ops['sp'].append((waits, None, None, 0))
        for e in ('pe', 'act', 'dve', 'pool'):
            if self.ops[e]:
                last = [o for o in self.ops[e] if o[1] is not None][-1]
                assert last[2] is not None, "last op on %s must inc" % e
        names = list(self.ENG[:4]) + ['dma%d' % i for i in range(self.ndma)]
        with contextlib.ExitStack() as st:
            sems = {n: st.enter_context(nc.semaphore(n)) for n in names}
            block = st.enter_context(nc.Block())

            def run(ename):
                def body(e):
                    for waits, fn, incsem, incv in self.ops[ename]:
                        for s, v in waits:
                            e.wait_ge(sems[s], v)
                        if fn is not None:
                            ins = fn(e)
                            if incsem is not None:
                                ins.then_inc(sems[incsem], incv)
                return body
            block.tensor(run('pe'))
            block.scalar(run('act'))
            block.vector(run('dve'))
            block.gpsimd(run('pool'))
            block.sync(run('sp'))


def host_consts():
    p = np.arange(128)
    s = p[:, None]
    t = p[None, :]
    c = {}
    c['idf'] = (s == t)
    c['ones'] = np.ones((128, 128))
    c['Mincl_f'] = (s <= t)
    c['Mb_f'] = (s <= t).astype(np.float64) - (s <= 63)
    c['nMb_f'] = -c['Mb_f']
    c['Mkd_f'] = (s > t)
    c['Mincl_b'] = (s >= t)
    c['Mb_b'] = (s >= t).astype(np.float64) - (s >= 64)
    c['nMb_b'] = -c['Mb_b']
    c['Mkd_b'] = (s < t)
    c['mask_f'] = np.tile((s <= t), (1, 4))
    c['mask_b'] = np.tile((s >= t), (1, 4))
    deltas = np.abs(np.linspace(np.log(1e-2) / 1.5, np.log(1e-2) / 0.3, 512, dtype=np.float32)).astype(np.float64)
    c['delta'] = np.tile(deltas[None, :], (128, 1))
    tt = (np.arange(L).reshape(32, 128).T).astype(np.float64)
    c['ntn'] = -(tt / (L - 1))
    c['alt'] = np.tile(((-1.0) ** p)[:, None], (1, 2))
    c['altrow'] = np.tile(((-1.0) ** np.arange(512))[None, :], (128, 1))
    offs = {}
    cols = []
    o = 0
    for k, v in c.items():
        v = np.asarray(v, dtype=np.float32)
        offs[k] = (o, v.shape[1])
        cols.append(v)
        o += v.shape[1]
    arr = np.concatenate(cols, axis=1).astype(np.float32)
    pos = np.arange(L, dtype=np.float32)
    tn = pos / np.float32(L - 1)
    w = np.float32(2.0 * np.pi) * pos / np.float32(L)
    bands = np.linspace(1e-4, 15, 16, dtype=np.float32)
    ang = w[:, None] * bands[None, :]
    z = np.concatenate([tn[:, None], np.cos(ang), -np.sin(ang)], axis=-1).astype(np.float32)
    zT = np.ascontiguousarray(z.T)
    a = (np.arange(L, dtype=np.int64).reshape(32, 128).T)[None, :, :, None]
    b = np.arange(L, dtype=np.int64).reshape(8, 1, 1, 512)
    m = (a * b) % NFFT
    angm = (2.0 * np.pi / NFFT) * m
    ctab = np.cos(angm).astype(ml_dtypes.bfloat16)
    stab = np.sin(angm).astype(ml_dtypes.bfloat16)
    return arr, offs, zT, ctab, stab


_HC = None


def get_hc():
    global _HC
    if _HC is None:
        _HC = host_consts()
    return _HC


def build_program(debug=False, stop_after=None):
    carr, coffs, _, _, _ = get_hc()
    NCC = carr.shape[1]
    nc = bass.Bass("TRN2", target_bir_lowering=False)
    S = Sched(nc)

    def din(name, shape, dt=F32):
        return nc.dram_tensor(name, list(shape), dt, kind="ExternalInput").ap()

    def dscr(name, shape, dt):
        return nc.dram_tensor(name, list(shape), dt, kind=("ExternalOutput" if debug else "Internal")).ap()

    x = din("x", [L, D]); ctx = din("ctx", [CTX, D]); cvec = din("cvec", [2, D])
    w_ada = din("w_ada", [D, 6 * D]); b_ada = din("b_ada", [6 * D]); norm_w = din("norm_w", [4, D])
    w_in = din("w_in", [D, 6144]); lb_param = din("lb_param", [2, 512]); g_norm = din("g_norm", [128])
    hy_short_w = din("hy_short_w", [3, 1536]); hy_short_b = din("hy_short_b", [1536])
    f_w1 = din("f_w1", [33, 64]); f_b1 = din("f_b1", [64]); f_w2 = din("f_w2", [64, 64]); f_b2 = din("f_b2", [64])
    f_w3 = din("f_w3", [64, 1024]); f_freq = din("f_freq", [64]); hy_d = din("hy_d", [512])
    w_pa = din("w_pa", [512, D]); w_pb = din("w_pb", [512, D]); w_out = din("w_out", [D, D])
    w_mlp1 = din("w_mlp1", [D, 4 * D]); w_mlp2 = din("w_mlp2", [4 * D, D])
    consts_d = din("consts", [128, NCC]); zT_d = din("zT", [33, L])
    ctab_d = din("ctab", [8, 128, 32, 512], BF16); stab_d = din("stab", [8, 128, 32, 512], BF16)
    out = nc.dram_tensor("out", [L, D], F32, kind="ExternalOutput").ap()

    sc_tm = dscr("sc_tm", [NTA, 128, 1536], F32)
    sc_q = dscr("sc_q", [4, 128, L], BF16)
    sc_g = dscr("sc_g", [4, 128, L], BF16)
    sc_x0 = dscr("sc_x0", [4, 128, L], BF16)
    sc_u = dscr("sc_u", [128, 32, 512], BF16)
    sc_gate = dscr("sc_gate", [16, 128, L], BF16)
    sc_Y = dscr("sc_Y", [128, 64, 512], BF16)
    sc_yx0 = dscr("sc_yx0", [4, 128, L], BF16)
    sc_a = dscr("sc_a", [4, 128, L], BF16)
    sc_xlat = dscr("sc_xlat", [L, D], F32)
    sc_hl = dscr("sc_hl", [8, 128, L], BF16)

    ARENA = 103 * 1024
    arena = nc.alloc_sbuf_tensor("arena", [128, ARENA], BF16)
    top = [0]

    def sb(shape, dt, name=None):
        rows = shape[0]
        n = int(np.prod(shape[1:]))
        n16 = n * (2 if dt == F32 else 1)
        off = top[0]
        top[0] += (n16 + 15) // 16 * 16
        assert top[0] <= ARENA, ("SBUF arena overflow", name, top[0])
        ap = arena[0:rows, off:off + n16]
        if dt == F32:
            ap = ap.bitcast(F32)
        if len(shape) == 3:
            ap = ap.rearrange("p (a b) -> p a b", b=shape[2])
        return ap

    ps = [nc.alloc_psum_tensor("ps%d" % i, [128, 512], F32) for i in range(8)]
    psi = [0]

    def nps():
        i = psi[0] % 8
        psi[0] += 1
        return i

    def mm(o, lhsT, rhs, start, stop, reads, writes, inc=None):
        if inc is None:
            inc = stop
        S.op('pe', lambda e: e.matmul(o, lhsT, rhs, start=start, stop=stop), reads, writes, inc)

    def tr(o, i, ident, reads, writes, inc=True):
        S.op('pe', lambda e: e.transpose(o, i, ident), reads, writes, inc)

    def act(o, i, func, reads, writes, **kw):
        S.op('act', lambda e: e.activation(out=o, in_=i, func=func, **kw), reads, writes)

    def tt(eng, o, a, b, op, reads, writes):
        S.op(eng, lambda e: e.tensor_tensor(out=o, in0=a, in1=b, op=op), reads, writes)

    def ts(eng, o, a, s1, s2, op0, op1, reads, writes):
        if s2 is None:
            S.op(eng, lambda e: e.tensor_scalar(out=o, in0=a, scalar1=s1, scalar2=None, op0=op0), reads, writes)
        else:
            S.op(eng, lambda e: e.tensor_scalar(out=o, in0=a, scalar1=s1, scalar2=s2, op0=op0, op1=op1), reads, writes)

    def stt(eng, o, a, sc, b, op0, op1, reads, writes):
        S.op(eng, lambda e: e.scalar_tensor_tensor(out=o, in0=a, scalar=sc, in1=b, op0=op0, op1=op1), reads, writes)

    def interleave(gens):
        gens = list(gens)
        while gens:
            for g_ in list(gens):
                try:
                    next(g_)
                except StopIteration:
                    gens.remove(g_)

    def cp(eng, o, i, reads, writes):
        if eng == 'act':
            act(o, i, AF.Copy, reads, writes)
        else:
            S.op(eng, lambda e: e.tensor_copy(o, i), reads, writes)

    cst = sb([128, NCC], F32, "cst")
    S.dma('sp', cst[:], consts_d, writes=['cst'])

    def C(name, rows=128):
        o, n = coffs[name]
        return cst[0:rows, o:o + n]
    idb = sb([128, 128], BF16, "idb"); onesb = sb([128, 128], BF16, "onesb")
    maskf = sb([128, 512], BF16, "maskf"); maskb = sb([128, 512], BF16, "maskb")
    altb = sb([128, 2], BF16, "altb"); altrowb = sb([1, 512], BF16, "altrowb")
    cp('dve', idb[:], C('idf'), ['cst'], ['idb'])
    cp('dve', onesb[:], C('ones'), ['cst'], ['onesb'])
    cp('dve', maskf[:], C('mask_f'), ['cst'], ['maskf'])
    cp('dve', maskb[:], C('mask_b'), ['cst'], ['maskb'])
    cp('dve', altb[:], C('alt'), ['cst'], ['altb'])
    cp('dve', altrowb[:], C('altrow', 1), ['cst'], ['altrowb'])
    ident = C('idf')

    NSL = dict(allow_slow_non_contiguous=True)
    sm = sb([128, 160], F32, "sm")
    so = {}
    o = [0]

    def smcol(name, n):
        so[name] = o[0]
        o[0] += n
        return sm[:, so[name]:so[name] + n]
    S.dma('sp', smcol('bada', 48), b_ada.rearrange("(t p) -> p t", p=128), writes=['sm'], **NSL)
    S.dma('sp', smcol('nw', 32), norm_w.rearrange("r (t p) -> p (r t)", p=128), writes=['sm'], **NSL)
    S.dma('sp', smcol('c', 8), cvec[0, :].rearrange("(t p) -> p t", p=128), writes=['sm'], **NSL)
    S.dma('sp', smcol('cc', 8), cvec[1, :].rearrange("(t p) -> p t", p=128), writes=['sm'], **NSL)
    S.dma('sp', smcol('lb', 8), lb_param.rearrange("r (t p) -> p (r t)", p=128), writes=['sm'], **NSL)
    S.dma('sp', smcol('gn', 1), g_norm.rearrange("(t p) -> p t", p=128), writes=['sm'], **NSL)
    S.dma('sp', smcol('hsw', 36), hy_short_w.rearrange("r (t p) -> p (r t)", p=128), writes=['sm'], **NSL)
    S.dma('sp', smcol('hsb', 12), hy_short_b.rearrange("(t p) -> p t", p=128), writes=['sm'], **NSL)
    S.dma('sp', smcol('hyd', 4), hy_d.rearrange("(t p) -> p t", p=128), writes=['sm'], **NSL)
    sm64 = sb([64, 4], F32, "sm64")
    S.dma('sp', sm64[:, 0:1], f_b1.rearrange("(p t) -> p t", t=1), writes=['sm64'], **NSL)
    S.dma('sp', sm64[:, 1:2], f_b2.rearrange("(p t) -> p t", t=1), writes=['sm64'], **NSL)
    S.dma('sp', sm64[:, 2:3], f_freq.rearrange("(p t) -> p t", t=1), writes=['sm64'], **NSL)

    def SM(name, i=0, n=1):
        return sm[:, so[name] + i: so[name] + i + n]

    cc2 = sb([128, 8, 2], F32, "cc2")
    cp('dve', cc2[:, :, 0], SM('c', 0, 8), ['sm'], ['cc2'])
    cp('dve', cc2[:, :, 1], SM('cc', 0, 8), ['sm'], ['cc2'])
    e2 = sb([128, 16], F32, "e2")
    cc2f = cc2[:].rearrange("p a b -> p (a b)")
    act(e2[:], cc2f, AF.Exp, ['cc2'], ['e2'], scale=-1.0)
    ts('dve', e2[:], e2[:], 1.0, None, ALU.add, None, ['e2'], ['e2'])
    S.op('dve', lambda e: e.reciprocal(e2[:], e2[:]), ['e2'], ['e2'])
    scb = sb([128, 8, 2], BF16, "scb")
    tt('dve', scb[:].rearrange("p a b -> p (a b)"), cc2f, e2[:], ALU.mult, ['cc2', 'e2'], ['scb'])
    mod = sb([128, 48, 2], F32, "mod")
    fv = sb([128, 8, 8], F32, "fv")
    c1f = sb([128, 4], F32, "c1f")
    c1bc = sb([128, 512], F32, "c1bc")
    hydbc = sb([128, 512], F32, "hydbc")
    diag = [sb([128, 128], F32, "diag") for _ in range(2)]
    ynq = sb([1, 512], BF16, "ynq")
    MARK0 = top[0]
    wad = [sb([128, 8, 512], BF16, "wad") for _ in range(2)]
    def mod_dma(pc):
        S.dma('pool', wad[pc % 2][:], w_ada[:, pc * 512:(pc + 1) * 512].rearrange("(k p) n -> p k n", p=128), writes=[('wad', pc % 2)])

    def mod_piece(pc, dma=True):
        wb = wad[pc % 2]
        if dma:
            mod_dma(pc)
        bmp = nps()
        for j in range(4):
            for k in range(8):
                mm(ps[bmp][:, 2 * j:2 * j + 2], wb[:, k, j * 128:(j + 1) * 128], scb[:, k, :], k == 0, k == 7,
                   [('wad', pc % 2), 'scb'], [('ps', bmp)], inc=(k == 7))
        mk_ = ('mod', pc // 2)
        cp('dve', mod[:, pc * 4:pc * 4 + 4, :].rearrange("p a b -> p (a b)"), ps[bmp][:, 0:8], [('ps', bmp)], [mk_])
        for j in range(2):
            tt('dve', mod[:, pc * 4:pc * 4 + 4, j], mod[:, pc * 4:pc * 4 + 4, j], SM('bada', pc * 4, 4), ALU.add, [mk_, 'sm'], [mk_])

    def nwv(r):
        return SM('nw', 8 * r, 8)
    for pc in range(4):
        mod_piece(pc)
    stt('dve', fv[:, 0, :], mod[:, 0:8, 0], 1.0, nwv(0), ALU.add, ALU.mult, [('mod', 0), 'sm'], [('fv', 0)])
    cp('dve', fv[:, 1, :], mod[:, 8:16, 0], [('mod', 1)], [('fv', 0)])
    stt('dve', fv[:, 2, :], mod[:, 0:8, 1], 1.0, nwv(0), ALU.add, ALU.mult, [('mod', 0), 'sm'], [('fv', 0)])
    cp('dve', fv[:, 3, :], mod[:, 8:16, 1], [('mod', 1)], [('fv', 0)])

    def mod_rest_finish():
        tt('dve', fv[:, 4, :], mod[:, 16:24, 0], nwv(1), ALU.mult, [('mod', 2), 'sm'], ['fv'])
        stt('dve', fv[:, 5, :], mod[:, 24:32, 0], 1.0, nwv(2), ALU.add, ALU.mult, [('mod', 3), 'sm'], ['fv'])
        cp('dve', fv[:, 6, :], mod[:, 32:40, 0], [('mod', 4)], ['fv'])
        tt('dve', fv[:, 7, :], mod[:, 40:48, 0], nwv(3), ALU.mult, [('mod', 5), 'sm'], ['fv'])
    tt('dve', c1f[:], SM('lb', 0, 4), SM('lb', 4, 4), ALU.subtract, ['sm'], ['c1f'])
    act(c1f[:], c1f[:], AF.Exp, ['c1f'], ['c1f'])
    act(c1f[:], c1f[:], AF.Ln, ['c1f'], ['c1f'], bias=1.0)
    ts('dve', c1f[:], c1f[:], -1.0, None, ALU.mult, None, ['c1f'], ['c1f'])
    onesf = C('ones')
    di = [0]

    def bcast(dst, col, dkey, rkey):
        d = diag[di[0] % 2]
        dk = ('diag', di[0] % 2)
        di[0] += 1
        ts('dve', d[:], ident, col, None, ALU.mult, None, ['cst', rkey], [dk])
        b = nps()
        mm(ps[b][:, 0:128], onesf, d[:], True, True, ['cst', dk], [('ps', b)])
        cp('act', dst, ps[b][:, 0:128], [('ps', b)], [dkey])
    for k in range(4):
        bcast(c1bc[:, k * 128:(k + 1) * 128], c1f[:, k:k + 1], 'c1bc', 'c1f')
        bcast(hydbc[:, k * 128:(k + 1) * 128], SM('hyd', k, 1), 'hydbc', 'sm')
    hT = sb([128, 8, NTA * 128], BF16, "hT")
    MARK1 = top[0]
    xb = [sb([128, 1024], F32, "xb") for _ in range(3)]
    junk = sb([128, 1024], BF16, "junk")
    xn = [sb([128, 1024], BF16, "xn") for _ in range(3)]
    st = [sb([128, 4], F32, "st") for _ in range(3)]
    wtm = sb([128, 8, 1536], BF16, "wtm")
    for k3 in range(3):
        S.dma('pool', wtm[:, :, k3 * 512:(k3 + 1) * 512], w_in[:, k3 * 512:(k3 + 1) * 512].rearrange("(k p) n -> p k n", p=128), writes=['wtm'])
    tms = [sb([128, 1536], F32, "tms") for _ in range(2)]

    def p1_tile(i):
        s2 = i % 2
        for k3 in range(3):
            b = nps()
            for k in range(8):
                mm(ps[b][:], hT[:, k, i * 128:(i + 1) * 128], wtm[:, k, k3 * 512:(k3 + 1) * 512], k == 0, k == 7,
                   [('hT', i, 0), ('hT', i, 1), 'wtm'], [('ps', b)])
            cp('act' if k3 != 1 else 'dve', tms[s2][:, k3 * 512:(k3 + 1) * 512], ps[b][:], [('ps', b)], [('tms', s2)])
        S.dma('sp', sc_tm[i], tms[s2][:], reads=[('tms', s2)], writes=[('sc_tm', i)])
    def a_s1(i):
        s3 = i % 3
        src = ctx[i * 128:(i + 1) * 128, :] if i < 2 else x[(i - 2) * 128:(i - 1) * 128, :]
        S.dma('sp', xb[s3][:], src, writes=[('xb', s3)])
        S.op('pool', lambda e, t_=st[s3]: e.memset(t_[:, 0:1], 0.0), [], [('st', s3)])
        act(junk[:], xb[s3][:], AF.Square, [('xb', s3), ('st', s3)], ['junk', ('st', s3)], accum_out=st[s3][:, 0:1])
        act(st[s3][:, 1:2], st[s3][:, 0:1], AF.Ln, [('st', s3)], [('st', s3)], scale=1.0 / D, bias=EPS)
        act(st[s3][:, 2:3], st[s3][:, 1:2], AF.Exp, [('st', s3)], [('st', s3)], scale=-0.5)
        ts('dve', xn[s3][:], xb[s3][:], st[s3][:, 2:3], None, ALU.mult, None, [('xb', s3), ('st', s3)], [('xn', s3)])

    def a_s2(i):
        s3 = i % 3
        bb_ = [nps(), nps()]
        pbs = [ps[bb_[0]][:].bitcast(BF16), ps[bb_[1]][:].bitcast(BF16)]
        for k in range(8):
            tr(pbs[k % 2][:, (k // 2) * 128:(k // 2 + 1) * 128], xn[s3][:, k * 128:(k + 1) * 128], idb[:], [('xn', s3), 'idb'], [('ps', bb_[k % 2])], inc=(k >= 6))
        ai = 0 if i >= 2 else 2
        for k in range(8):
            dst = hT[:, k, i * 128:(i + 1) * 128]
            src_ = pbs[k % 2][:, (k // 2) * 128:(k // 2 + 1) * 128]
            if k % 2 == 0:
                act(dst, src_, AF.Identity, [('ps', bb_[0]), ('fv', 0)], [('hT', i, 0)],
                    scale=fv[:, ai, k:k + 1], bias=fv[:, ai + 1, k:k + 1])
            else:
                ts('dve', dst, src_, fv[:, ai, k:k + 1], fv[:, ai + 1, k:k + 1], ALU.mult, ALU.add,
                   [('ps', bb_[1]), ('fv', 0)], [('hT', i, 1)])
    npc = [4]
    for it in range(NTA + 2):
        if it < NTA:
            a_s1(it)
        if 0 <= it - 1 < NTA:
            a_s2(it - 1)
        if 0 <= it - 2 < NTA:
            p1_tile(it - 2)

    if stop_after == 'A':
        for i in range(NTA):
            S.out_toks.append(S.lastw[('sc_tm', i)])
        S.emit()
        return nc

    S.barrier()
    top[0] = MARK1
    wfm = sb([128, 8, 4608], BF16, "wfm")
    pool_order = [('w', 0), ('w', 1), ('w', 2), ('w', 3), ('w', 4), ('m', 4), ('m', 5), ('w', 5), ('w', 6), ('w', 7), ('w', 8)]

    def wfm_dma(k9):
        S.dma('pool', wfm[:, :, k9 * 512:(k9 + 1) * 512], w_in[:, 1536 + k9 * 512:1536 + (k9 + 1) * 512].rearrange("(k p) n -> p k n", p=128), writes=[('wfm', k9)])
    for kind_, idx_ in pool_order:
        if kind_ == 'w':
            wfm_dma(idx_)
        else:
            mod_dma(idx_)
    stg = [sb([128, 512], BF16, "stg") for _ in range(4)]
    sgi = [0]
    cvt = [sb([128, 512], F32, "cvt") for _ in range(7)]
    ustg = [sb([128, 4, 128], BF16, "ustg") for _ in range(4)]
    ub = [sb([128, 512], BF16, "ub") for _ in range(2)]

    def proj(ft_col, tb):
        b = nps()
        for k in range(8):
            mm(ps[b][:], wfm[:, k, ft_col:ft_col + 128], hT[:, k, 256 + tb * 512:256 + (tb + 1) * 512], k == 0, k == 7,
               [('wfm', ft_col // 512)] + [('hT', 2 + tb * 4 + j, e_) for j in range(4) for e_ in range(2)], [('ps', b)])
        return b

    def conv(b, ft, dst, dkey, s4):
        cv = cvt[s4]
        ck = ('cvt', s4)
        p3 = ps[b][:].rearrange("p (r c) -> p r c", c=64)
        c3 = cv[:].rearrange("p (r c) -> p r c", c=64)
        act(cv[:], ps[b][:], AF.Identity, [('ps', b), 'sm'], [ck], scale=SM('hsw', 12 + ft, 1), bias=SM('hsb', ft, 1))
        stt('dve', c3[:, :, 1:64], p3[:, :, 0:63], SM('hsw', ft, 1), c3[:, :, 1:64], ALU.mult, ALU.add, [('ps', b), 'sm', ck], [ck])
        d3 = dst.rearrange("p (r c) -> p r c", c=64)
        stt('dve', d3[:, :, 0:63], p3[:, :, 1:64], SM('hsw', 24 + ft, 1), c3[:, :, 0:63], ALU.mult, ALU.add, [('ps', b), 'sm', ck], [dkey])
        cp('pool', d3[:, :, 63:64], c3[:, :, 63:64], [ck], [dkey])

    pend = [None]
    usi = [0]

    def utrans(u2, j, tb):
        bt = nps()
        pbt = ps[bt][:].bitcast(BF16)
        for q4 in range(4):
            tr(pbt[:, q4 * 128:(q4 + 1) * 128], u2[:, q4 * 128:(q4 + 1) * 128], idb[:], [('ub', (j * 8 + tb) % 2), 'idb'], [('ps', bt)], inc=(q4 == 3))
        u4 = usi[0] % 4
        usi[0] += 1
        cp('act', ustg[u4][:], pbt[:, 0:512].rearrange("p (a b) -> p a b", b=128), [('ps', bt)], [('ustg', u4)])
        S.dma('sp', sc_u[:, tb * 4:(tb + 1) * 4, j * 128:(j + 1) * 128], ustg[u4][:], reads=[('ustg', u4)], writes=[('sc_u', tb, j)])

    def nstg():
        sg = sgi[0] % 4
        sgi[0] += 1
        return sg
    for ft in range(8):
        for tb in range(8):
            tsl = slice(tb * 512, (tb + 1) * 512)
            b = proj(ft * 128, tb)
            sg = nstg()
            act(stg[sg][:], ps[b][:], AF.Silu, [('ps', b)], [('stg', sg)])
            dstd = (sc_q if ft < 4 else sc_g)[ft % 4, :, tsl]
            S.dma('sp', dstd, stg[sg][:], reads=[('stg', sg)], writes=[('sc_qg', ft, tb)])
    for j in range(4):
        for tb in range(8):
            tsl = slice(tb * 512, (tb + 1) * 512)
            b = proj(1024 + j * 128, tb)
            sg = nstg()
            conv(b, j, stg[sg][:], ('stg', sg), 0)
            S.dma('sp', sc_x0[j, :, tsl], stg[sg][:], reads=[('stg', sg)], writes=[('sc_x0', j, tb)])
    mod_piece(4, dma=False)
    mod_piece(5, dma=False)
    mod_dma(6)
    mod_dma(7)
    for j in range(4):
        for tb in range(8):
            pp = (j * 8 + tb) % 2
            b1 = proj(1024 + (4 + j) * 128, tb)
            conv(b1, 4 + j, cvt[3 + pp][:], ('cvt', 3 + pp), 1)
            b2 = proj(1024 + (8 + j) * 128, tb)
            conv(b2, 8 + j, cvt[5 + pp][:], ('cvt', 5 + pp), 2)
            u2 = ub[pp]
            tt('pool', u2[:], cvt[3 + pp][:], cvt[5 + pp][:], ALU.mult, [('cvt', 3 + pp), ('cvt', 5 + pp)], [('ub', pp)])
            if pend[0] is not None:
                pend[0]()
            pend[0] = (lambda u2_=u2, j_=j, tb_=tb: utrans(u2_, j_, tb_))
        if j == 1:
            mod_piece(6, dma=False)
            mod_piece(7, dma=False)
            mod_dma(8)
            mod_dma(9)
        if j == 3:
            mod_piece(8, dma=False)
            mod_piece(9, dma=False)
            mod_dma(10)
            mod_dma(11)
    pend[0]()
    for ft in range(16):
        for tb in range(8):
            tsl = slice(tb * 512, (tb + 1) * 512)
            b = proj(2560 + ft * 128, tb)
            sg = nstg()
            act(stg[sg][:], ps[b][:], AF.Sigmoid, [('ps', b)], [('stg', sg)])
            S.dma('sp', sc_gate[ft, :, tsl], stg[sg][:], reads=[('stg', sg)], writes=[('sc_gate', ft, tb)])
        if ft == 4:
            mod_piece(10, dma=False)
            mod_piece(11, dma=False)
    mod_rest_finish()
    if stop_after == 'P':
        for k_, v_ in list(S.lastw.items()):
            if isinstance(k_, tuple) and str(k_[0]).startswith('sc_'):
                S.out_toks.append(v_)
        S.emit()
        return nc
    S.barrier()
    top[0] = MARK0
    hcat = sb([128, 32, 1024], BF16, "hcat")
    MARKH = top[0]
    zt = sb([33, L], F32, "zt"); w1s = sb([33, 64], F32, "w1s"); w2s = sb([64, 64], F32, "w2s"); w3s = sb([64, 1024], F32, "w3s")
    h1 = sb([64, L], F32, "h1"); h2 = sb([64, L], F32, "h2"); wtmp = sb([64, 512], F32, "wtmp")
    S.dma('sp', zt[:], zT_d, writes=['zt'])
    S.dma('sp', w1s[:], f_w1, writes=['fw'])
    S.dma('sp', w2s[:], f_w2, writes=['fw'])
    S.dma('sp', w3s[:], f_w3, writes=['fw'])

    def sinlayer(src, w, kdim, bcol, dst, skey, dkey):
        for blk in range(8):
            sl = slice(blk * 512, (blk + 1) * 512)
            b = nps()
            mm(ps[b][0:64, :], w[0:kdim, :], src[0:kdim, sl], True, True, [skey, 'fw'], [('ps', b)])
            ts('dve', dst[:, sl], ps[b][0:64, :], sm64[:, bcol:bcol + 1], sm64[:, 2:3], ALU.add, ALU.mult, [('ps', b), 'sm64'], [dkey])
            for _ in range(1):
                S.op('dve', lambda e, d_=dst[:, sl]: e.tensor_single_scalar(out=wtmp[:], in_=d_, scalar=PI, op=ALU.is_gt), [dkey], ['wtmp'])
                stt('dve', dst[:, sl], wtmp[:], -2 * PI, dst[:, sl], ALU.mult, ALU.add, ['wtmp', dkey], [dkey])
                S.op('dve', lambda e, d_=dst[:, sl]: e.tensor_single_scalar(out=wtmp[:], in_=d_, scalar=-PI, op=ALU.is_lt), [dkey], ['wtmp'])
                stt('dve', dst[:, sl], wtmp[:], 2 * PI, dst[:, sl], ALU.mult, ALU.add, ['wtmp', dkey], [dkey])
            act(dst[:, sl], dst[:, sl], AF.Sin, [dkey], [dkey])
    sinlayer(zt, w1s, 33, 0, h1, 'zt', 'h1')
    sinlayer(h1, w2s, 64, 1, h2, 'h1', 'h2')
    dect = [sb([128, 512], F32, "dect") for _ in range(2)]
    ftmp = sb([128, 512], F32, "ftmp")
    ntn = C('ntn')
    for chunk in range(32):
        dt_ = dect[chunk % 2]
        dk = ('dect', chunk % 2)
        act(dt_[:], C('delta'), AF.Exp, ['cst'], [dk], scale=ntn[:, chunk:chunk + 1])
        for half in range(2):
            b = nps()
            mm(ps[b][:], h2[:, chunk * 128:(chunk + 1) * 128], w3s[:, half * 512:(half + 1) * 512], True, True, ['h2', 'fw'], [('ps', b)])
            dst = hcat[:, chunk, half * 512:(half + 1) * 512]
            if chunk == 0:
                tt('dve', ftmp[:], ps[b][:], dt_[:], ALU.mult, [('ps', b), dk], ['ftmp'])
                if half == 0:
                    tt('dve', ftmp[0:1, :], ftmp[0:1, :], hydbc[0:1, :], ALU.add, ['ftmp', 'hydbc'], ['ftmp'])
                else:
                    S.op('dve', lambda e: e.memset(ftmp[0:1, :], 0.0), ['ftmp'], ['ftmp'])
                cp('dve', dst, ftmp[:], ['ftmp'], ['hcat'])
            else:
                tt('dve', dst, ps[b][:], dt_[:], ALU.mult, [('ps', b), dk], ['hcat'])

    S.barrier()
    top[0] = MARKH
    u_tm = sb([128, 32, 512], BF16, "u_tm")
    for q8 in range(8):
        S.dma('sp', u_tm[:, q8 * 4:(q8 + 1) * 4, :], sc_u[:, q8 * 4:(q8 + 1) * 4, :], reads=[('sc_u', q8, j_) for j_ in range(4)], writes=['u_tm'])
    tabs = [sb([128, 32, 256], BF16, "tabf") for _ in range(2)]
    Ast = sb([128, 2, 512], F32, "Ast"); Pst = sb([128, 2, 512], F32, "Pst")
    tq = [sb([128, 512], F32, "tq") for _ in range(6)]
    yst = [sb([128, 512], BF16, "yst") for _ in range(4)]
    SC = 2.0 / NFFT
    ysi = [0]

    def ynext():
        i_ = ysi[0] % 4
        ysi[0] += 1
        return yst[i_], ('yst', i_)
    for fb in range(16):
        tbi, hf_ = fb // 2, fb % 2
        for cs in range(2):
            tab = tabs[cs]
            tk = ('tab', cs)
            S.dma('sp', tab[:], (ctab_d if cs == 0 else stab_d)[tbi][:, :, hf_ * 256:(hf_ + 1) * 256], writes=[tk])
            for m in range(2):
                bb = [nps(), nps(), nps()]
                for chunk in range(32):
                    lhsT = tab[:, chunk, m * 128:(m + 1) * 128]
                    mm(ps[bb[0]][:], lhsT, u_tm[:, chunk, :], chunk == 0, chunk == 31, [tk, 'u_tm'], [('ps', bb[0])])
                    mm(ps[bb[1]][:], lhsT, hcat[:, chunk, 0:512], chunk == 0, chunk == 31, [tk, 'hcat'], [('ps', bb[1])])
                    mm(ps[bb[2]][:], lhsT, hcat[:, chunk, 512:1024], chunk == 0, chunk == 31, [tk, 'hcat'], [('ps', bb[2])])
                ft = fb * 2 + m
                if cs == 0:
                    cp('act', Ast[:, m, :], ps[bb[0]][:], [('ps', bb[0])], [('Ast', m)])
                    cp('act', tq[0][:], ps[bb[2]][:], [('ps', bb[2])], [('tq', 0)])
                    tt('dve', Pst[:, m, :], ps[bb[1]][:], tq[0][:], ALU.add, [('ps', bb[1]), ('tq', 0)], [('Pst', m)])
                else:
                    cp('act', tq[1][:], ps[bb[2]][:], [('ps', bb[2])], [('tq', 1)])
                    tt('dve', tq[2][:], ps[bb[1]][:], tq[1][:], ALU.subtract, [('ps', bb[1]), ('tq', 1)], [('tq', 2)])
                    tt('pool', tq[3][:], Ast[:, m, :], Pst[:, m, :], ALU.mult, [('Ast', m), ('Pst', m)], [('tq', 3)])
                    stt('dve', tq[4][:], ps[bb[0]][:], SC, tq[2][:], ALU.mult, ALU.mult, [('ps', bb[0]), ('tq', 2)], [('tq', 4)])
                    yr, yrk = ynext()
                    stt('dve', yr[:], tq[3][:], SC, tq[4][:], ALU.mult, ALU.subtract, [('tq', 3), ('tq', 4)], [yrk])
                    if ft == 0:
                        ts('dve', yr[0:1, :], yr[0:1, :], 0.5, None, ALU.mult, None, [yrk], [yrk])
                    S.dma('sp', sc_Y[:, ft, :], yr[:], reads=[yrk], writes=[('sc_Y', ft)])
                    tt('pool', tq[3][:], Ast[:, m, :], tq[2][:], ALU.mult, [('Ast', m), ('tq', 2)], [('tq', 3)])
                    stt('dve', tq[5][:], ps[bb[0]][:], SC, Pst[:, m, :], ALU.mult, ALU.mult, [('ps', bb[0]), ('Pst', m)], [('tq', 5)])
                    yq, yqk = ynext()
                    stt('dve', yq[:], tq[3][:], SC, tq[5][:], ALU.mult, ALU.add, [('tq', 3), ('tq', 5)], [yqk])
                    S.dma('sp', sc_Y[:, 32 + ft, :], yq[:], reads=[yqk], writes=[('sc_Y', 32 + ft)])
    bn = [nps(), nps(), nps()]
    for chunk in range(32):
        mm(ps[bn[0]][0:1, :], altb[:, 0:1], u_tm[:, chunk, :], chunk == 0, chunk == 31, ['altb', 'u_tm'], [('ps', bn[0])])
        mm(ps[bn[1]][0:1, :], altb[:, 0:1], hcat[:, chunk, 0:512], chunk == 0, chunk == 31, ['altb', 'hcat'], [('ps', bn[1])])
        mm(ps[bn[2]][0:1, :], altb[:, 0:1], hcat[:, chunk, 512:1024], chunk == 0, chunk == 31, ['altb', 'hcat'], [('ps', bn[2])])
    cp('act', tq[0][0:1, :], ps[bn[2]][0:1, :], [('ps', bn[2])], [('tq', 0)])
    tt('dve', tq[1][0:1, :], ps[bn[1]][0:1, :], tq[0][0:1, :], ALU.add, [('ps', bn[1]), ('tq', 0)], [('tq', 1)])
    stt('dve', ynq[:], ps[bn[0]][0:1, :], 1.0 / NFFT, tq[1][0:1, :], ALU.mult, ALU.mult, [('ps', bn[0]), ('tq', 1)], ['ynq'])

    S.barrier()
    top[0] = MARK0
    Yt = sb([128, 64, 512], BF16, "Yt")
    for q8 in range(8):
        S.dma('sp', Yt[:, q8 * 8:(q8 + 1) * 8, :], sc_Y[:, q8 * 8:(q8 + 1) * 8, :],
              reads=[('sc_Y', f_) for f_ in range(q8 * 8, q8 * 8 + 8)], writes=[('Yt', q8)])
    tabi = [sb([128, 32, 512], BF16, "tabi") for _ in range(2)]
    x0s = [sb([128, 512], BF16, "x0s") for _ in range(4)]
    yo = [sb([128, 512], BF16, "yo") for _ in range(2)]
    S.dma('sp', tabi[0][:], ctab_d[0], writes=[('tabi', 0)])
    S.dma('sp', tabi[1][:], stab_d[0], writes=[('tabi', 1)])
    for tb in range(8):
        tsl = slice(tb * 512, (tb + 1) * 512)
        for ct in range(4):
            S.dma('act', x0s[ct][:], sc_x0[ct, :, tsl], reads=[('sc_x0', ct, tb)], writes=[('x0s', ct)])
        banks = [nps() for _ in range(4)]
        for cs in range(2):
            for ct in range(4):
                for chunk in range(32):
                    mm(ps[banks[ct]][:], Yt[:, cs * 32 + chunk, ct * 128:(ct + 1) * 128], tabi[cs][:, chunk, :],
                       cs == 0 and chunk == 0, False, [('Yt', (cs * 32 + chunk) // 8), ('tabi', cs)], [('ps', banks[ct])], inc=(chunk == 31))
            if tb + 1 < 8:
                S.dma('sp', tabi[cs][:], (ctab_d if cs == 0 else stab_d)[tb + 1], writes=[('tabi', cs)])
        for ct in range(4):
            mm(ps[banks[ct]][:], ynq[0:1, ct * 128:(ct + 1) * 128], altrowb[0:1, :], False, True, ['ynq', 'altrowb'], [('ps', banks[ct])], inc=True)
            s2 = ct % 2
            tt('dve', yo[s2][:], ps[banks[ct]][:], x0s[ct][:], ALU.mult, [('ps', banks[ct]), ('x0s', ct)], [('yo', s2)])
            S.dma('act', sc_yx0[ct, :, tsl], yo[s2][:], reads=[('yo', s2)], writes=[('sc_yx0', ct, tb)])
    if stop_after == 'H':
        for k_, v_ in list(S.lastw.items()):
            if isinstance(k_, tuple) and str(k_[0]).startswith('sc_'):
                S.out_toks.append(v_)
        S.emit()
        return nc

    S.barrier()
    top[0] = MARK0
    tmb = [sb([128, 1536], F32, "tmb") for _ in range(3)]
    vb = [sb([128, 512], BF16, "vb") for _ in range(3)]
    ebuf = [sb([128, 512], F32, "ebuf") for _ in range(2)]
    lp = [sb([128, 512], F32, "lp") for _ in range(2)]
    kbuf = [sb([128, 512], F32, "kbuf") for _ in range(2)]
    lnk = [sb([128, 512], F32, "lnk") for _ in range(2)]
    logf = [sb([128, 512], F32, "logf") for _ in range(2)]
    Ea = [[sb([128, 4, 128], BF16, "Ea") for _ in range(2)] for _ in range(2)]
    Eb = [[sb([128, 4, 128], BF16, "Eb") for _ in range(2)] for _ in range(2)]
    keT = [[sb([128, 512], BF16, "keT") for _ in range(2)] for _ in range(2)]
    kd = [[sb([128, 512], BF16, "kd") for _ in range(2)] for _ in range(2)]
    dec = [[sb([128, 4], F32, "dec") for _ in range(2)] for _ in range(2)]
    Sf = [sb([128, 4, 128], F32, "Sf") for _ in range(2)]
    Sb16 = [sb([128, 4, 128], BF16, "Sb16") for _ in range(2)]
    sbs = sb([128, 32, 512], BF16, "sbs")
    qtb = [sb([128, 4, 128], BF16, "qtb") for _ in range(3)]
    gtb = [sb([128, 4, 128], BF16, "gtb") for _ in range(3)]
    qa = [sb([128, 4, 128], BF16, "qa") for _ in range(2)]
    qe = [sb([128, 4, 128], BF16, "qe") for _ in range(2)]
    scT = [sb([128, 512], BF16, "scT") for _ in range(2)]
    sqb = sb([128, 512], BF16, "sqb"); rs = sb([128, 512], F32, "rs"); otf = sb([128, 512], F32, "otf")
    atb = [sb([128, 512], BF16, "atb") for _ in range(2)]
    flat = lambda a_: a_.rearrange("p a b -> p (a b)")
    for d in range(2):
        S.op('pool', lambda e, t_=Sf[d]: e.memset(flat(t_), 0.0), [], [('Sf', d)])
        S.op('pool', lambda e, t_=Sb16[d]: e.memset(flat(t_), 0.0), [], [('Sb', d)])

    gate_alloc = [nps]

    def gate(d, tm, tmk, sl, lite=False):
        zz = tm[:, 512 * (1 + d):512 * (2 + d)]
        eb_, lp_, kb_ = ebuf[d], lp[d], kbuf[d]
        act(eb_[:], zz, AF.Exp, [tmk], [('ebuf', d)])
        act(lp_[:], eb_[:], AF.Ln, [('ebuf', d)], [('lp', d)], bias=1.0)
        yield
        stt('dve', lnk[d][:], lp_[:], -1.0, c1bc[:], ALU.mult, ALU.add, [('lp', d), 'c1bc'], [('lnk', d)])
        yield
        act(kb_[:], lnk[d][:], AF.Exp, [('lnk', d)], [('kbuf', d)])
        act(logf[d][:], kb_[:], AF.Ln, [('kbuf', d)], [('logf', d)], scale=-1.0, bias=1.0)
        yield
        sfx = 'f' if d == 0 else 'b'
        oi = coffs['Mincl_' + sfx][0]
        Mcat = cst[:, oi:oi + 256]
        nMb = C('nMb_' + sfx)
        Mkd = C('Mkd_' + sfx)
        bkd = gate_alloc[0]()
        mm(ps[bkd][:], Mkd, logf[d][:], True, True, [('logf', d), 'cst'], [('ps', bkd)], inc=True)
        if lite:
            bt_ = gate_alloc[0]()
            for h in range(4):
                mm(ps[bt_][:, h:h + 1], logf[d][:, h * 128:(h + 1) * 128], onesf[:, 0:1], True, True, [('logf', d), 'cst'], [('ps', bt_)], inc=(h == 3))
        else:
            bab = [gate_alloc[0](), gate_alloc[0]()]
            for h in range(4):
                mm(ps[bab[h // 2]][:, (h % 2) * 256:(h % 2) * 256 + 256], logf[d][:, h * 128:(h + 1) * 128], Mcat, True, True,
                   [('logf', d), 'cst'], [('ps', bab[h // 2])], inc=(h % 2 == 1))
            bke = gate_alloc[0]()
            for h in range(4):
                mm(ps[bke][:, h * 128:(h + 1) * 128], lnk[d][:, h * 128:(h + 1) * 128], ident, True, False, [('lnk', d), 'cst'], [('ps', bke)], inc=False)
                mm(ps[bke][:, h * 128:(h + 1) * 128], logf[d][:, h * 128:(h + 1) * 128], nMb, False, True, [('logf', d), 'cst'], [('ps', bke)], inc=(h == 3))
        yield
        act(eb_[:], ps[bkd][:], AF.Exp, [('ps', bkd)], [('ebuf', d)])
        if lite:
            act(dec[d][sl][:], ps[bt_][:, 0:4], AF.Exp, [('ps', bt_)], [('dec', d, sl)])
            yield
            tt('dve', kd[d][sl][:], eb_[:], kb_[:], ALU.mult, [('ebuf', d), ('kbuf', d)], [('kd', d, sl)])
            return
        col = 127 if d == 0 else 0
        for j in range(2):
            v3 = ps[bab[j]][:].rearrange("p (h x) -> p h x", x=256)
            act(Ea[d][sl][:, 2 * j:2 * j + 2, :], v3[:, :, 0:128], AF.Exp, [('ps', bab[j])], [('Ea', d, sl)])
            act(Eb[d][sl][:, 2 * j:2 * j + 2, :], v3[:, :, 128:256], AF.Exp, [('ps', bab[j])], [('Eb', d, sl)])
            for hh in range(2):
                act(dec[d][sl][:, 2 * j + hh:2 * j + hh + 1], ps[bab[j]][:, hh * 256 + col:hh * 256 + col + 1], AF.Exp, [('ps', bab[j])], [('dec', d, sl)])
        act(keT[d][sl][:], ps[bke][:], AF.Exp, [('ps', bke)], [('keT', d, sl)])
        yield
        tt('dve', kd[d][sl][:], eb_[:], kb_[:], ALU.mult, [('ebuf', d), ('kbuf', d)], [('kd', d, sl)])

    def update(d, vt, vk, sl, bank=None):
        b = nps() if bank is None else bank
        for h in range(4):
            mm(ps[b][:, h * 128:(h + 1) * 128], kd[d][sl][:, h * 128:(h + 1) * 128], vt[:, h * 128:(h + 1) * 128], True, True,
               [('kd', d, sl), vk], [('ps', b)], inc=(h == 3))
        for h in range(4):
            stt('dve', Sf[d][:, h, :], Sf[d][:, h, :], dec[d][sl][:, h:h + 1], ps[b][:, h * 128:(h + 1) * 128], ALU.mult, ALU.add,
                [('Sf', d), ('dec', d, sl), ('ps', b)], [('Sf', d)])
        cp('act', flat(Sb16[d]), flat(Sf[d]), [('Sf', d)], [('Sb', d)])

    ldi = [0]
    ldmap = {}

    def issue_load(key, i):
        s3 = ldi[0] % 3
        ldi[0] += 1
        S.dma('sp', tmb[s3][:], sc_tm[i], reads=[('sc_tm', i)], writes=[('tmb', s3)])
        ldmap[key] = s3

    def load_tm(key):
        s3 = ldmap[key]
        cp('dve', vb[s3][:], tmb[s3][:, 0:512], [('tmb', s3)], [('vb', s3)])
        return tmb[s3], ('tmb', s3), vb[s3], ('vb', s3)
    gi = [0]

    steps = [(0, 0), (0, 1), (1, 1), (1, 0)] + [(1, n_ + 2) for n_ in range(31, -1, -1)]

    def lite_gate(k_):
        d_, i_ = steps[k_]
        sl = k_ % 2
        if k_ + 1 < len(steps):
            issue_load(('g1', k_ + 1), steps[k_ + 1][1])
        tm, tmk, vt, vk = load_tm(('g1', k_))
        for _ in gate(d_, tm, tmk, sl, lite=True):
            pass
        return (d_, vt, vk, sl, i_)
    issue_load(('g1', 0), steps[0][1])
    pendg = lite_gate(0)
    for k_ in range(len(steps)):
        cur = pendg
        if k_ + 1 < len(steps):
            pendg = lite_gate(k_ + 1)
        d_, vt, vk, sl, i_ = cur
        if d_ == 1 and i_ >= 2:
            cp('act', sbs[:, i_ - 2, :], flat(Sb16[1]), [('Sb', 1)], [('sbs', i_ - 2)])
        update(d_, vt, vk, sl)
    loaded = {}
    qslot = {}

    def issue_g2(n):
        issue_load(('g2', n), n + 2)
        q3 = n % 3
        qslot[n] = q3
        tsl = slice(n * 128, (n + 1) * 128)
        S.dma('sp', qtb[q3][:], sc_q[:, :, tsl].rearrange("h p t -> p h t"), reads=[('sc_qg', h_, n // 4) for h_ in range(4)], writes=[('qtb', q3)])
        S.dma('sp', gtb[q3][:], sc_g[:, :, tsl].rearrange("h p t -> p h t"), reads=[('sc_qg', 4 + h_, n // 4) for h_ in range(4)], writes=[('gtb', q3)])

    def prep_gens(n):
        if n + 1 < 32:
            issue_g2(n + 1)
        tm, tmk, vt, vk = load_tm(('g2', n))
        loaded[n] = (vt, vk)
        return [gate(0, tm, tmk, n % 2), gate(1, tm, tmk, n % 2)]

    def core(n):
        vt, vk = loaded[n]
        q2 = n % 2
        q3 = qslot[n]
        tsl = slice(n * 128, (n + 1) * 128)
        bsl = []
        for d in range(2):
            tt('dve', flat(qa[d]), flat(qtb[q3]), flat(Ea[d][q2]), ALU.mult, [('qtb', q3), ('Ea', d, q2)], [('qa', d)])
            tt('dve', flat(qe[d]), flat(qtb[q3]), flat(Eb[d][q2]), ALU.mult, [('qtb', q3), ('Eb', d, q2)], [('qe', d)])
        yield
        for d in range(2):
            bs = 4 + d
            bsl.append(bs)
            for h in range(4):
                mm(ps[bs][:, h * 128:(h + 1) * 128], keT[d][q2][:, h * 128:(h + 1) * 128], qe[d][:, h, :], True, True,
                   [('keT', d, q2), ('qe', d)], [('ps', bs)], inc=(h == 3))
        yield
        for d in range(2):
            mk = maskf if d == 0 else maskb
            tt('dve', scT[d][:], ps[bsl[d]][:], mk[:], ALU.mult, [('ps', bsl[d]), 'maskf', 'maskb'], [('scT', d)])
        yield
        bo = 6
        for h in range(4):
            o_ = ps[bo][:, h * 128:(h + 1) * 128]
            hs = slice(h * 128, (h + 1) * 128)
            mm(o_, vt[:, hs], scT[0][:, hs], True, False, [vk, ('scT', 0)], [('ps', bo)], inc=False)
            mm(o_, vt[:, hs], scT[1][:, hs], False, False, [vk, ('scT', 1)], [('ps', bo)], inc=False)
            mm(o_, Sb16[0][:, h, :], qa[0][:, h, :], False, False, [('Sb', 0), ('qa', 0)], [('ps', bo)], inc=False)
            mm(o_, sbs[:, n, hs], qa[1][:, h, :], False, True, [('sbs', n), ('qa', 1)], [('ps', bo)], inc=(h == 3))
        update(0, vt, vk, q2, bank=7)
        yield
        act(sqb[:], ps[bo][:], AF.Square, [('ps', bo)], ['sqb'])
        yield
        bss = 4
        mm(ps[bss][:], onesb[:], sqb[:], True, True, ['onesb', 'sqb'], [('ps', bss)])
        yield
        act(rs[:], ps[bss][:], AF.Ln, [('ps', bss)], ['rs'], scale=1.0 / 128, bias=EPS)
        act(rs[:], rs[:], AF.Exp, ['rs'], ['rs'], scale=-0.5)
        yield
        tt('dve', otf[:], ps[bo][:], rs[:], ALU.mult, [('ps', bo), 'rs'], ['otf'])
        yield
        stt('dve', atb[q2][:], otf[:], SM('gn'), flat(gtb[q3]), ALU.mult, ALU.mult, ['otf', 'sm', ('gtb', q3)], [('atb', q2)])
        S.dma('sp', sc_a[:, :, tsl].rearrange("h p t -> p h t"), atb[q2][:].rearrange("p (h t) -> p h t", t=128),
              reads=[('atb', q2)], writes=[('sc_a', n)])
    g2i = [0]

    def gps():
        i_ = g2i[0] % 4
        g2i[0] += 1
        return i_

    def chain(gs):
        for g_ in gs:
            yield from g_
    gate_alloc[0] = gps
    issue_g2(0)
    interleave([chain(prep_gens(0))])
    for n in range(32):
        gens = [core(n)]
        if n + 1 < 32:
            gens = [chain(prep_gens(n + 1))] + gens
        interleave(gens)
    gate_alloc[0] = nps
    if stop_after == 'G':
        for k_, v_ in list(S.lastw.items()):
            if isinstance(k_, tuple) and str(k_[0]).startswith('sc_'):
                S.out_toks.append(v_)
        S.emit()
        return nc

    S.barrier()
    top[0] = MARK0
    bcv = sb([128, 4, 1024], F32, "bcv")
    for vi in range(4):
        for k in range(8):
            bcast(bcv[:, vi, k * 128:(k + 1) * 128], fv[:, 4 + vi, k:k + 1], 'bcv', 'fv')
    wpa = sb([128, 4, 1024], BF16, "wpa"); wpb = sb([128, 4, 1024], BF16, "wpb"); wout = sb([128, 8, 1024], BF16, "wout")
    S.dma('pool', wpa[:], w_pa.rearrange("(k p) n -> p k n", p=128), writes=['wpa'])
    S.dma('pool', wpb[:], w_pb.rearrange("(k p) n -> p k n", p=128), writes=['wpb'])
    for k2 in range(2):
        S.dma('pool', wout[:, :, k2 * 512:(k2 + 1) * 512], w_out[:, k2 * 512:(k2 + 1) * 512].rearrange("(k p) n -> p k n", p=128), writes=['wout'])
    aTb = [sb([128, 4, 512], BF16, "aTb") for _ in range(2)]
    yxb = [sb([128, 4, 512], BF16, "yxb") for _ in range(2)]
    ggb = [sb([128, 16, 512], BF16, "ggb") for _ in range(2)]
    merged = sb([128, 8, 512], BF16, "merged")
    m1 = [sb([128, 512], F32, "m1") for _ in range(2)]
    xt = [sb([128, 1024], F32, "xt") for _ in range(2)]
    xl = [sb([128, 1024], F32, "xl") for _ in range(2)]
    tmpf = [sb([128, 1024], F32, "tmpf") for _ in range(2)]
    hlb = [sb([128, 1024], BF16, "hlb") for _ in range(2)]
    hls = [sb([128, 8, 512], BF16, "hls") for _ in range(2)]
    junk2 = [sb([128, 1024], BF16, "junk2") for _ in range(2)]

    def interleave(gens):
        gens = list(gens)
        while gens:
            for g_ in list(gens):
                try:
                    next(g_)
                except StopIteration:
                    gens.remove(g_)
    st2 = [sb([128, 8], F32, "st2") for _ in range(2)]
    for tb in range(8):
        tsl = slice(tb * 512, (tb + 1) * 512)
        b2 = tb % 2
        S.dma('sp', aTb[b2][:], sc_a[:, :, tsl].rearrange("h p t -> p h t"), reads=[('sc_a', tb * 4 + j_) for j_ in range(4)], writes=[('aTb', b2)])
        S.dma('sp', yxb[b2][:], sc_yx0[:, :, tsl].rearrange("h p t -> p h t"), reads=[('sc_yx0', c_, tb) for c_ in range(4)], writes=[('yxb', b2)])
        S.dma('sp', ggb[b2][:], sc_gate[:, :, tsl].rearrange("f p t -> p f t"), reads=[('sc_gate', f_, tb) for f_ in range(16)], writes=[('ggb', b2)])
        for dm in range(8):
            bA = nps()
            for k in range(4):
                mm(ps[bA][:], wpa[:, k, dm * 128:(dm + 1) * 128], aTb[b2][:, k, :], k == 0, k == 3, ['wpa', ('aTb', b2)], [('ps', bA)])
            bB = nps()
            for k in range(4):
                mm(ps[bB][:], wpb[:, k, dm * 128:(dm + 1) * 128], yxb[b2][:, k, :], k == 0, k == 3, ['wpb', ('yxb', b2)], [('ps', bB)])
            tt('dve', m1[0][:], ps[bA][:], ggb[b2][:, dm, :], ALU.mult, [('ps', bA), ('ggb', b2)], [('m1', 0)])
            tt('dve', m1[1][:], ps[bB][:], ggb[b2][:, 8 + dm, :], ALU.mult, [('ps', bB), ('ggb', b2)], [('m1', 1)])
            tt('pool', merged[:, dm, :], m1[0][:], m1[1][:], ALU.add, [('m1', 0), ('m1', 1)], [('merged', dm)])
        hs_ = hls[b2]

        def t1_tile(t4, tb=tb, hs_=hs_, b2=b2):
            n = tb * 4 + t4
            s2 = n % 2
            tf = tmpf[s2]
            tfk = ('tmpf', s2)
            S.dma('sp', xt[s2][:], x[n * 128:(n + 1) * 128, :], writes=[('xt', s2)])
            S.op('pool', lambda e, t_=st2[s2]: e.memset(t_[:, 0:2], 0.0), [], [('st2', s2)])
            bh = [nps(), nps()]
            for half in range(2):
                for k in range(8):
                    mm(ps[bh[half]][:], merged[:, k, t4 * 128:(t4 + 1) * 128], wout[:, k, half * 512:(half + 1) * 512], k == 0, k == 7,
                       [('merged', k), 'wout'], [('ps', bh[half])])
                act(junk2[s2][:, 0:512], ps[bh[half]][:], AF.Square, [('ps', bh[half]), ('st2', s2)], [('junk2', s2), ('st2', s2)], accum_out=st2[s2][:, half:half + 1])
            yield
            tt('dve', st2[s2][:, 2:3], st2[s2][:, 0:1], st2[s2][:, 1:2], ALU.add, [('st2', s2)], [('st2', s2)])
            yield
            act(st2[s2][:, 3:4], st2[s2][:, 2:3], AF.Ln, [('st2', s2)], [('st2', s2)], scale=1.0 / D, bias=EPS)
            act(st2[s2][:, 4:5], st2[s2][:, 3:4], AF.Exp, [('st2', s2)], [('st2', s2)], scale=-0.5)
            yield
            for half in range(2):
                hsl = slice(half * 512, (half + 1) * 512)
                stt('dve', tf[:, hsl], ps[bh[half]][:], st2[s2][:, 4:5], bcv[:, 0, hsl], ALU.mult, ALU.mult, [('ps', bh[half]), ('st2', s2), 'bcv'], [tfk])
            yield
            tt('dve', xl[s2][:], tf[:], xt[s2][:], ALU.add, [tfk, ('xt', s2)], [('xl', s2)])
            S.dma('sp', sc_xlat[n * 128:(n + 1) * 128, :], xl[s2][:], reads=[('xl', s2)], writes=[('sc_xlat', n)])
            S.op('pool', lambda e, t_=st2[s2]: e.memset(t_[:, 5:6], 0.0), [], [('st2', s2)])
            yield
            act(junk2[s2][:], xl[s2][:], AF.Square, [('xl', s2), ('st2', s2)], [('junk2', s2), ('st2', s2)], accum_out=st2[s2][:, 5:6])
            act(st2[s2][:, 6:7], st2[s2][:, 5:6], AF.Ln, [('st2', s2)], [('st2', s2)], scale=1.0 / D, bias=EPS)
            act(st2[s2][:, 7:8], st2[s2][:, 6:7], AF.Exp, [('st2', s2)], [('st2', s2)], scale=-0.5)
            yield
            ts('dve', hlb[s2][:], xl[s2][:], st2[s2][:, 7:8], None, ALU.mult, None, [('xl', s2), ('st2', s2)], [('hlb', s2)])
            yield
            bt2 = [nps(), nps()]
            pbts = [ps[bt2[0]][:].bitcast(BF16), ps[bt2[1]][:].bitcast(BF16)]
            for k in range(8):
                tr(pbts[k % 2][:, (k // 2) * 128:(k // 2 + 1) * 128], hlb[s2][:, k * 128:(k + 1) * 128], idb[:], [('hlb', s2), 'idb'], [('ps', bt2[k % 2])], inc=(k >= 6))
            yield
            for k in range(8):
                dst = hs_[:, k, t4 * 128:(t4 + 1) * 128]
                src_ = pbts[k % 2][:, (k // 2) * 128:(k // 2 + 1) * 128]
                if k % 2 == 0:
                    act(dst, src_, AF.Identity, [('ps', bt2[0]), 'fv'], [('hls', b2, 0)],
                        scale=fv[:, 5, k:k + 1], bias=fv[:, 6, k:k + 1])
                else:
                    ts('dve', dst, src_, fv[:, 5, k:k + 1], fv[:, 6, k:k + 1], ALU.mult, ALU.add,
                       [('ps', bt2[1]), 'fv'], [('hls', b2, 1)])
        interleave([t1_tile(0), t1_tile(1)])
        interleave([t1_tile(2), t1_tile(3)])
        S.dma('sp', sc_hl[:, :, tsl].rearrange("k p t -> p k t"), hs_[:], reads=[('hls', b2, 0), ('hls', b2, 1)], writes=[('sc_hl', tb)])

    S.barrier()
    top[0] = MARK0
    g5bc = sb([128, 1024], F32, "g5bc")
    for k in range(8):
        bcast(g5bc[:, k * 128:(k + 1) * 128], fv[:, 7, k:k + 1], 'g5bc', 'fv')
    w1 = sb([128, 8, 4096], BF16, "w1"); w2 = sb([128, 32, 1024], BF16, "w2")
    for k8 in range(8):
        S.dma('pool', w1[:, :, k8 * 512:(k8 + 1) * 512], w_mlp1[:, k8 * 512:(k8 + 1) * 512].rearrange("(k p) n -> p k n", p=128), writes=[('w1', k8)])
    for k8 in range(8):
        S.dma('pool', w2[:, k8 * 4:(k8 + 1) * 4, :], w_mlp2[k8 * 512:(k8 + 1) * 512, :].rearrange("(k p) n -> p k n", p=128), writes=[('w2', k8)])
    hlt = [sb([128, 8, 256], BF16, "hlt") for _ in range(2)]
    hid = sb([128, 32, 256], BF16, "hid")
    rr = [sb([128, 256], F32, "rr") for _ in range(2)]
    xl2 = [sb([128, 1024], F32, "xl2") for _ in range(2)]
    ot = [sb([128, 1024], F32, "ot") for _ in range(2)]
    junk3 = [sb([128, 512], BF16, "junk3") for _ in range(2)]
    st3 = [sb([128, 8], F32, "st3") for _ in range(2)]
    for tb in range(16):
        b2 = tb % 2
        tsl = slice(tb * 256, (tb + 1) * 256)
        S.dma('sp', hlt[b2][:], sc_hl[:, :, tsl].rearrange("k p t -> p k t"), reads=[('sc_hl', tb // 2)], writes=[('hlt', b2)])
        for ff in range(32):
            b = nps()
            for k in range(8):
                mm(ps[b][:, 0:256], w1[:, k, ff * 128:(ff + 1) * 128], hlt[b2][:, k, :], k == 0, k == 7, [('w1', ff // 4), ('hlt', b2)], [('ps', b)])
            r2 = ff % 2
            act(rr[r2][:], ps[b][:, 0:256], AF.Relu, [('ps', b)], [('rr', r2)])
            tt('pool' if ff % 4 else 'dve', hid[:, ff, :], rr[r2][:], rr[r2][:], ALU.mult, [('rr', r2)], [('hid', ff)])
        def t2_tile(t2, tb=tb):
            n = tb * 2 + t2
            s2 = n % 2
            S.dma('sp', xl2[s2][:], sc_xlat[n * 128:(n + 1) * 128, :], reads=[('sc_xlat', n)], writes=[('xl2', s2)])
            S.op('pool', lambda e, t_=st3[s2]: e.memset(t_[:, 0:2], 0.0), [], [('st3', s2)])
            bh = [nps(), nps()]
            for half in range(2):
                for k in range(32):
                    mm(ps[bh[half]][:], hid[:, k, t2 * 128:(t2 + 1) * 128], w2[:, k, half * 512:(half + 1) * 512], k == 0, k == 31,
                       [('hid', k), ('w2', k // 4)], [('ps', bh[half])])
                act(junk3[s2][:], ps[bh[half]][:], AF.Square, [('ps', bh[half]), ('st3', s2)], [('junk3', s2), ('st3', s2)], accum_out=st3[s2][:, half:half + 1])
            yield
            tt('dve', st3[s2][:, 2:3], st3[s2][:, 0:1], st3[s2][:, 1:2], ALU.add, [('st3', s2)], [('st3', s2)])
            yield
            act(st3[s2][:, 3:4], st3[s2][:, 2:3], AF.Ln, [('st3', s2)], [('st3', s2)], scale=1.0 / D, bias=EPS)
            act(st3[s2][:, 4:5], st3[s2][:, 3:4], AF.Exp, [('st3', s2)], [('st3', s2)], scale=-0.5)
            yield
            for half in range(2):
                hsl = slice(half * 512, (half + 1) * 512)
                stt('dve', ot[s2][:, hsl], ps[bh[half]][:], st3[s2][:, 4:5], g5bc[:, hsl], ALU.mult, ALU.mult, [('ps', bh[half]), ('st3', s2), 'g5bc'], [('ot', s2)])
            yield
            tt('pool', ot[s2][:], ot[s2][:], xl2[s2][:], ALU.add, [('ot', s2), ('xl2', s2)], [('ot', s2)])
            S.dma('sp', out[n * 128:(n + 1) * 128, :], ot[s2][:], reads=[('ot', s2)], is_out=True)
        interleave([t2_tile(0), t2_tile(1)])
    S.emit()
    return nc


def make_in_maps(inputs):
    carr, coffs, zT, ctab, stab = get_hc()
    g = lambda k: np.ascontiguousarray(np.asarray(inputs[k], dtype=np.float32))
    maps = []
    for b in range(8):
        m = {
            "x": g('x')[b], "ctx": g('ctx')[b],
            "cvec": np.ascontiguousarray(np.stack([g('c')[b], g('c_ctx')], axis=0)),
            "w_ada": g('w_ada')[0], "b_ada": g('b_ada')[0], "norm_w": g('norm_w')[0], "w_in": g('w_in')[0],
            "lb_param": g('lb_param'), "g_norm": g('g_norm')[0], "hy_short_w": g('hy_short_w')[0],
            "hy_short_b": g('hy_short_b')[0], "f_w1": g('f_w1')[0], "f_b1": g('f_b1')[0], "f_w2": g('f_w2')[0],
            "f_b2": g('f_b2')[0], "f_w3": g('f_w3')[0], "f_freq": g('f_freq')[0], "hy_d": g('hy_d')[0],
            "w_pa": g('w_pa')[0], "w_pb": g('w_pb')[0], "w_out": g('w_out')[0], "w_mlp1": g('w_mlp1')[0],
            "w_mlp2": g('w_mlp2')[0], "consts": carr, "zT": zT, "ctab": ctab, "stab": stab,
        }
        maps.append(m)
    return maps


def kernel(**inputs):
    nc = build_program()
    maps = make_in_maps(inputs)
    res = run_bass_kernel_spmd(nc, maps, core_ids=list(range(8)))
    return np.stack([np.asarray(r["out"], dtype=np.float32) for r in res.results], axis=0)
```

```python
import contextlib
import numpy as np
import ml_dtypes
import concourse.bass as bass
import concourse.mybir as mybir
from concourse.bass_utils import run_bass_kernel_spmd

F32 = mybir.dt.float32
BF16 = mybir.dt.bfloat16
AF = mybir.ActivationFunctionType
ALU = mybir.AluOpType

D = 1024
L = 4096
CTX = 256
NT = 32
NTA = 34
EPS = 1e-6
NFFT = 8192
PI = float(np.pi)


class Sched:
    ENG = ('pe', 'act', 'dve', 'pool', 'sp')

    def __init__(self, nc, ndma=12, nswd=6):
        self.nc = nc
        self.ops = {e: [] for e in self.ENG}
        self.cnt = {e: 0 for e in self.ENG}
        self.known = {e: {} for e in self.ENG}
        self.lastw = {}
        self.readers = {}
        self.ndma = ndma + nswd
        self.nhw = ndma
        self.nswd = nswd
        self.dma_n = 0
        self.swd_n = 0
        self.dma_cnt = [0] * (ndma + nswd)
        self.out_toks = []

    def _need(self, eng, tok, waits, same_ok):
        if tok is None:
            return
        sem, val = tok
        if sem == eng and same_ok:
            return
        if self.known[eng].get(sem, 0) >= val:
            return
        if sem == eng:
            assert val <= self.cnt[eng], "same-engine wait on un-inc'd op"
        self.known[eng][sem] = val
        waits.append((sem, val))

    def _deps(self, eng, reads, writes):
        waits = []
        for k in reads:
            self._need(eng, self.lastw.get(k), waits, False)
            if isinstance(k, tuple) and k[0] == 'ps':
                for s, v in self.readers.get(k, {}).items():
                    self._need(eng, (s, v), waits, True)
        for k in writes:
            self._need(eng, self.lastw.get(k), waits, True)
            for s, v in self.readers.get(k, {}).items():
                self._need(eng, (s, v), waits, True)
        return waits

    def _commit(self, tok, reads, writes):
        for k in reads:
            d = self.readers.setdefault(k, {})
            if d.get(tok[0], 0) < tok[1]:
                d[tok[0]] = tok[1]
        for k in writes:
            self.lastw[k] = tok
            self.readers[k] = {}

    def op(self, eng, fn, reads=(), writes=(), inc=True):
        waits = self._deps(eng, reads, writes)
        if inc:
            self.cnt[eng] += 1
            tok = (eng, self.cnt[eng])
        else:
            tok = (eng, self.cnt[eng] + 1)
        self._commit(tok, reads, writes)
        self.ops[eng].append((waits, fn, eng if inc else None, 1))
        return tok

    def dma(self, q, out, in_, reads=(), writes=(), is_out=False, **kw):
        if q == 'pool':
            slot = self.nhw + self.swd_n % self.nswd
            self.swd_n += 1
        else:
            slot = self.dma_n % self.nhw
            self.dma_n += 1
        sem = 'dma%d' % slot
        waits = self._deps(q, reads, writes)
        prev = self.dma_cnt[slot]
        if prev > 0:
            self._need(q, (sem, prev), waits, False)
        self.dma_cnt[slot] += 16
        tok = (sem, self.dma_cnt[slot])
        self._commit(tok, reads, writes)
        self.ops[q].append((waits, lambda e: e.dma_start(out=out, in_=in_, **kw), sem, 16))
        if is_out:
            self.out_toks.append(tok)
        return tok

    def barrier(self):
        for e in self.ENG:
            waits = []
            for f in self.ENG[:4]:
                if f != e and self.cnt[f] > 0:
                    self._need(e, (f, self.cnt[f]), waits, False)
            for s in range(self.ndma):
                if self.dma_cnt[s] > 0:
                    self._need(e, ('dma%d' % s, self.dma_cnt[s]), waits, False)
            self.ops[e].append((waits, None, None, 0))

    def emit(self):
        nc = self.nc
        waits = []
        for tok in self.out_toks:
            self._need('sp', tok, waits, False)
        self.ops['sp'].append((waits, None, None, 0))
        for e in ('pe', 'act', 'dve', 'pool'):
            if self.ops[e]:
                last = [o for o in self.ops[e] if o[1] is not None][-1]
                assert last[2] is not None, "last op on %s must inc" % e
        names = list(self.ENG[:4]) + ['dma%d' % i for i in range(self.ndma)]
        with contextlib.ExitStack() as st:
            sems = {n: st.enter_context(nc.semaphore(n)) for n in names}
            block = st.enter_context(nc.Block())

            def run(ename):
                def body(e):
                    for waits, fn, incsem, incv in self.ops[ename]:
                        for s, v in waits:
                            e.wait_ge(sems[s], v)
                        if fn is not None:
                            ins = fn(e)
                            if incsem is not None:
                                ins.then_inc(sems[incsem], incv)
                return body
            block.tensor(run('pe'))
            block.scalar(run('act'))
            block.vector(run('dve'))
            block.gpsimd(run('pool'))
            block.sync(run('sp'))


def host_consts():
    p = np.arange(128)
    s = p[:, None]
    t = p[None, :]
    c = {}
    c['idf'] = (s == t)
    c['ones'] = np.ones((128, 128))
    c['Mincl_f'] = (s <= t)
    c['Mb_f'] = (s <= t).astype(np.float64) - (s <= 63)
    c['nMb_f'] = -c['Mb_f']
    c['Mkd_f'] = (s > t)
    c['Mincl_b'] = (s >= t)
    c['Mb_b'] = (s >= t).astype(np.float64) - (s >= 64)
    c['nMb_b'] = -c['Mb_b']
    c['Mkd_b'] = (s < t)
    c['mask_f'] = np.tile((s <= t), (1, 4))
    c['mask_b'] = np.tile((s >= t), (1, 4))
    deltas = np.abs(np.linspace(np.log(1e-2) / 1.5, np.log(1e-2) / 0.3, 512, dtype=np.float32)).astype(np.float64)
    c['delta'] = np.tile(deltas[None, :], (128, 1))
    tt = (np.arange(L).reshape(32, 128).T).astype(np.float64)
    c['ntn'] = -(tt / (L - 1))
    c['alt'] = np.tile(((-1.0) ** p)[:, None], (1, 2))
    c['altrow'] = np.tile(((-1.0) ** np.arange(512))[None, :], (128, 1))
    offs = {}
    cols = []
    o = 0
    for k, v in c.items():
        v = np.asarray(v, dtype=np.float32)
        offs[k] = (o, v.shape[1])
        cols.append(v)
        o += v.shape[1]
    arr = np.concatenate(cols, axis=1).astype(np.float32)
    pos = np.arange(L, dtype=np.float32)
    tn = pos / np.float32(L - 1)
    w = np.float32(2.0 * np.pi) * pos / np.float32(L)
    bands = np.linspace(1e-4, 15, 16, dtype=np.float32)
    ang = w[:, None] * bands[None, :]
    z = np.concatenate([tn[:, None], np.cos(ang), -np.sin(ang)], axis=-1).astype(np.float32)
    zT = np.ascontiguousarray(z.T)
    a = (np.arange(L, dtype=np.int64).reshape(32, 128).T)[None, :, :, None]
    b = np.arange(L, dtype=np.int64).reshape(8, 1, 1, 512)
    m = (a * b) % NFFT
    angm = (2.0 * np.pi / NFFT) * m
    ctab = np.cos(angm).astype(ml_dtypes.bfloat16)
    stab = np.sin(angm).astype(ml_dtypes.bfloat16)
    return arr, offs, zT, ctab, stab


_HC = None


def get_hc():
    global _HC
    if _HC is None:
        _HC = host_consts()
    return _HC


def build_program(debug=False, stop_after=None):
    carr, coffs, _, _, _ = get_hc()
    NCC = carr.shape[1]
    nc = bass.Bass("TRN2", target_bir_lowering=False)
    S = Sched(nc)

    def din(name, shape, dt=F32):
        return nc.dram_tensor(name, list(shape), dt, kind="ExternalInput").ap()

    def dscr(name, shape, dt):
        return nc.dram_tensor(name, list(shape), dt, kind=("ExternalOutput" if debug else "Internal")).ap()

    x = din("x", [L, D]); ctx = din("ctx", [CTX, D]); cvec = din("cvec", [2, D])
    w_ada = din("w_ada", [D, 6 * D]); b_ada = din("b_ada", [6 * D]); norm_w = din("norm_w", [4, D])
    w_in = din("w_in", [D, 6144]); lb_param = din("lb_param", [2, 512]); g_norm = din("g_norm", [128])
    hy_short_w = din("hy_short_w", [3, 1536]); hy_short_b = din("hy_short_b", [1536])
    f_w1 = din("f_w1", [33, 64]); f_b1 = din("f_b1", [64]); f_w2 = din("f_w2", [64, 64]); f_b2 = din("f_b2", [64])
    f_w3 = din("f_w3", [64, 1024]); f_freq = din("f_freq", [64]); hy_d = din("hy_d", [512])
    w_pa = din("w_pa", [512, D]); w_pb = din("w_pb", [512, D]); w_out = din("w_out", [D, D])
    w_mlp1 = din("w_mlp1", [D, 4 * D]); w_mlp2 = din("w_mlp2", [4 * D, D])
    consts_d = din("consts", [128, NCC]); zT_d = din("zT", [33, L])
    ctab_d = din("ctab", [8, 128, 32, 512], BF16); stab_d = din("stab", [8, 128, 32, 512], BF16)
    out = nc.dram_tensor("out", [L, D], F32, kind="ExternalOutput").ap()

    sc_tm = dscr("sc_tm", [NTA, 128, 1536], F32)
    sc_q = dscr("sc_q", [4, 128, L], BF16)
    sc_g = dscr("sc_g", [4, 128, L], BF16)
    sc_x0 = dscr("sc_x0", [4, 128, L], BF16)
    sc_u = dscr("sc_u", [128, 32, 512], BF16)
    sc_gate = dscr("sc_gate", [16, 128, L], BF16)
    sc_Y = dscr("sc_Y", [128, 64, 512], BF16)
    sc_yx0 = dscr("sc_yx0", [4, 128, L], BF16)
    sc_a = dscr("sc_a", [4, 128, L], BF16)
    sc_xlat = dscr("sc_xlat", [L, D], F32)
    sc_hl = dscr("sc_hl", [8, 128, L], BF16)

    ARENA = 103 * 1024
    arena = nc.alloc_sbuf_tensor("arena", [128, ARENA], BF16)
    top = [0]

    def sb(shape, dt, name=None):
        rows = shape[0]
        n = int(np.prod(shape[1:]))
        n16 = n * (2 if dt == F32 else 1)
        off = top[0]
        top[0] += (n16 + 15) // 16 * 16
        assert top[0] <= ARENA, ("SBUF arena overflow", name, top[0])
        ap = arena[0:rows, off:off + n16]
        if dt == F32:
            ap = ap.bitcast(F32)
        if len(shape) == 3:
            ap = ap.rearrange("p (a b) -> p a b", b=shape[2])
        return ap

    ps = [nc.alloc_psum_tensor("ps%d" % i, [128, 512], F32) for i in range(8)]
    psi = [0]

    def nps():
        i = psi[0] % 8
        psi[0] += 1
        return i

    def mm(o, lhsT, rhs, start, stop, reads, writes, inc=None):
        if inc is None:
            inc = stop
        S.op('pe', lambda e: e.matmul(o, lhsT, rhs, start=start, stop=stop), reads, writes, inc)

    def tr(o, i, ident, reads, writes, inc=True):
        S.op('pe', lambda e: e.transpose(o, i, ident), reads, writes, inc)

    def act(o, i, func, reads, writes, **kw):
        S.op('act', lambda e: e.activation(out=o, in_=i, func=func, **kw), reads, writes)

    def tt(eng, o, a, b, op, reads, writes):
        S.op(eng, lambda e: e.tensor_tensor(out=o, in0=a, in1=b, op=op), reads, writes)

    def ts(eng, o, a, s1, s2, op0, op1, reads, writes):
        if s2 is None:
            S.op(eng, lambda e: e.tensor_scalar(out=o, in0=a, scalar1=s1, scalar2=None, op0=op0), reads, writes)
        else:
            S.op(eng, lambda e: e.tensor_scalar(out=o, in0=a, scalar1=s1, scalar2=s2, op0=op0, op1=op1), reads, writes)

    def stt(eng, o, a, sc, b, op0, op1, reads, writes):
        S.op(eng, lambda e: e.scalar_tensor_tensor(out=o, in0=a, scalar=sc, in1=b, op0=op0, op1=op1), reads, writes)

    def interleave(gens):
        gens = list(gens)
        while gens:
            for g_ in list(gens):
                try:
                    next(g_)
                except StopIteration:
                    gens.remove(g_)

    def cp(eng, o, i, reads, writes):
        if eng == 'act':
            act(o, i, AF.Copy, reads, writes)
        else:
            S.op(eng, lambda e: e.tensor_copy(o, i), reads, writes)

    cst = sb([128, NCC], F32, "cst")
    S.dma('sp', cst[:], consts_d, writes=['cst'])

    def C(name, rows=128):
        o, n = coffs[name]
        return cst[0:rows, o:o + n]
    idb = sb([128, 128], BF16, "idb"); onesb = sb([128, 128], BF16, "onesb")
    maskf = sb([128, 512], BF16, "maskf"); maskb = sb([128, 512], BF16, "maskb")
    altb = sb([128, 2], BF16, "altb"); altrowb = sb([1, 512], BF16, "altrowb")
    cp('dve', idb[:], C('idf'), ['cst'], ['idb'])
    cp('dve', onesb[:], C('ones'), ['cst'], ['onesb'])
    cp('dve', maskf[:], C('mask_f'), ['cst'], ['maskf'])
    cp('dve', maskb[:], C('mask_b'), ['cst'], ['maskb'])
    cp('dve', altb[:], C('alt'), ['cst'], ['altb'])
    cp('dve', altrowb[:], C('altrow', 1), ['cst'], ['altrowb'])
    ident = C('idf')

    NSL = dict(allow_slow_non_contiguous=True)
    sm = sb([128, 160], F32, "sm")
    so = {}
    o = [0]

    def smcol(name, n):
        so[name] = o[0]
        o[0] += n
        return sm[:, so[name]:so[name] + n]
    S.dma('sp', smcol('bada', 48), b_ada.rearrange("(t p) -> p t", p=128), writes=['sm'], **NSL)
    S.dma('sp', smcol('nw', 32), norm_w.rearrange("r (t p) -> p (r t)", p=128), writes=['sm'], **NSL)
    S.dma('sp', smcol('c', 8), cvec[0, :].rearrange("(t p) -> p t", p=128), writes=['sm'], **NSL)
    S.dma('sp', smcol('cc', 8), cvec[1, :].rearrange("(t p) -> p t", p=128), writes=['sm'], **NSL)
    S.dma('sp', smcol('lb', 8), lb_param.rearrange("r (t p) -> p (r t)", p=128), writes=['sm'], **NSL)
    S.dma('sp', smcol('gn', 1), g_norm.rearrange("(t p) -> p t", p=128), writes=['sm'], **NSL)
    S.dma('sp', smcol('hsw', 36), hy_short_w.rearrange("r (t p) -> p (r t)", p=128), writes=['sm'], **NSL)
    S.dma('sp', smcol('hsb', 12), hy_short_b.rearrange("(t p) -> p t", p=128), writes=['sm'], **NSL)
    S.dma('sp', smcol('hyd', 4), hy_d.rearrange("(t p) -> p t", p=128), writes=['sm'], **NSL)
    sm64 = sb([64, 4], F32, "sm64")
    S.dma('sp', sm64[:, 0:1], f_b1.rearrange("(p t) -> p t", t=1), writes=['sm64'], **NSL)
    S.dma('sp', sm64[:, 1:2], f_b2.rearrange("(p t) -> p t", t=1), writes=['sm64'], **NSL)
    S.dma('sp', sm64[:, 2:3], f_freq.rearrange("(p t) -> p t", t=1), writes=['sm64'], **NSL)

    def SM(name, i=0, n=1):
        return sm[:, so[name] + i: so[name] + i + n]

    cc2 = sb([128, 8, 2], F32, "cc2")
    cp('dve', cc2[:, :, 0], SM('c', 0, 8), ['sm'], ['cc2'])
    cp('dve', cc2[:, :, 1], SM('cc', 0, 8), ['sm'], ['cc2'])
    e2 = sb([128, 16], F32, "e2")
    cc2f = cc2[:].rearrange("p a b -> p (a b)")
    act(e2[:], cc2f, AF.Exp, ['cc2'], ['e2'], scale=-1.0)
    ts('dve', e2[:], e2[:], 1.0, None, ALU.add, None, ['e2'], ['e2'])
    S.op('dve', lambda e: e.reciprocal(e2[:], e2[:]), ['e2'], ['e2'])
    scb = sb([128, 8, 2], BF16, "scb")
    tt('dve', scb[:].rearrange("p a b -> p (a b)"), cc2f, e2[:], ALU.mult, ['cc2', 'e2'], ['scb'])
    mod = sb([128, 48, 2], F32, "mod")
    fv = sb([128, 8, 8], F32, "fv")
    c1f = sb([128, 4], F32, "c1f")
    c1bc = sb([128, 512], F32, "c1bc")
    hydbc = sb([128, 512], F32, "hydbc")
    diag = [sb([128, 128], F32, "diag") for _ in range(2)]
    ynq = sb([1, 512], BF16, "ynq")
    MARK0 = top[0]
    wad = [sb([128, 8, 512], BF16, "wad") for _ in range(2)]
    def mod_dma(pc):
        S.dma('pool', wad[pc % 2][:], w_ada[:, pc * 512:(pc + 1) * 512].rearrange("(k p) n -> p k n", p=128), writes=[('wad', pc % 2)])

    def mod_piece(pc, dma=True):
        wb = wad[pc % 2]
        if dma:
            mod_dma(pc)
        bmp = nps()
        for j in range(4):
            for k in range(8):
                mm(ps[bmp][:, 2 * j:2 * j + 2], wb[:, k, j * 128:(j + 1) * 128], scb[:, k, :], k == 0, k == 7,
                   [('wad', pc % 2), 'scb'], [('ps', bmp)], inc=(k == 7))
        mk_ = ('mod', pc // 2)
        cp('dve', mod[:, pc * 4:pc * 4 + 4, :].rearrange("p a b -> p (a b)"), ps[bmp][:, 0:8], [('ps', bmp)], [mk_])
        for j in range(2):
            tt('dve', mod[:, pc * 4:pc * 4 + 4, j], mod[:, pc * 4:pc * 4 + 4, j], SM('bada', pc * 4, 4), ALU.add, [mk_, 'sm'], [mk_])

    def nwv(r):
        return SM('nw', 8 * r, 8)
    for pc in range(4):
        mod_piece(pc)
    stt('dve', fv[:, 0, :], mod[:, 0:8, 0], 1.0, nwv(0), ALU.add, ALU.mult, [('mod', 0), 'sm'], [('fv', 0)])
    cp('dve', fv[:, 1, :], mod[:, 8:16, 0], [('mod', 1)], [('fv', 0)])
    stt('dve', fv[:, 2, :], mod[:, 0:8, 1], 1.0, nwv(0), ALU.add, ALU.mult, [('mod', 0), 'sm'], [('fv', 0)])
    cp('dve', fv[:, 3, :], mod[:, 8:16, 1], [('mod', 1)], [('fv', 0)])

    def mod_rest_finish():
        tt('dve', fv[:, 4, :], mod[:, 16:24, 0], nwv(1), ALU.mult, [('mod', 2), 'sm'], ['fv'])
        stt('dve', fv[:, 5, :], mod[:, 24:32, 0], 1.0, nwv(2), ALU.add, ALU.mult, [('mod', 3), 'sm'], ['fv'])
        cp('dve', fv[:, 6, :], mod[:, 32:40, 0], [('mod', 4)], ['fv'])
        tt('dve', fv[:, 7, :], mod[:, 40:48, 0], nwv(3), ALU.mult, [('mod', 5), 'sm'], ['fv'])
    tt('dve', c1f[:], SM('lb', 0, 4), SM('lb', 4, 4), ALU.subtract, ['sm'], ['c1f'])
    act(c1f[:], c1f[:], AF.Exp, ['c1f'], ['c1f'])
    act(c1f[:], c1f[:], AF.Ln, ['c1f'], ['c1f'], bias=1.0)
    ts('dve', c1f[:], c1f[:], -1.0, None, ALU.mult, None, ['c1f'], ['c1f'])
    onesf = C('ones')
    di = [0]

    def bcast(dst, col, dkey, rkey):
        d = diag[di[0] % 2]
        dk = ('diag', di[0] % 2)
        di[0] += 1
        ts('dve', d[:], ident, col, None, ALU.mult, None, ['cst', rkey], [dk])
        b = nps()
        mm(ps[b][:, 0:128], onesf, d[:], True, True, ['cst', dk], [('ps', b)])
        cp('act', dst, ps[b][:, 0:128], [('ps', b)], [dkey])
    for k in range(4):
        bcast(c1bc[:, k * 128:(k + 1) * 128], c1f[:, k:k + 1], 'c1bc', 'c1f')
        bcast(hydbc[:, k * 128:(k + 1) * 128], SM('hyd', k, 1), 'hydbc', 'sm')
    hT = sb([128, 8, NTA * 128], BF16, "hT")
    MARK1 = top[0]
    xb = [sb([128, 1024], F32, "xb") for _ in range(3)]
    junk = sb([128, 1024], BF16, "junk")
    xn = [sb([128, 1024], BF16, "xn") for _ in range(3)]
    st = [sb([128, 4], F32, "st") for _ in range(3)]
    wtm = sb([128, 8, 1536], BF16, "wtm")
    for k3 in range(3):
        S.dma('pool', wtm[:, :, k3 * 512:(k3 + 1) * 512], w_in[:, k3 * 512:(k3 + 1) * 512].rearrange("(k p) n -> p k n", p=128), writes=['wtm'])
    tms = [sb([128, 1536], F32, "tms") for _ in range(2)]

    def p1_tile(i):
        s2 = i % 2
        for k3 in range(3):
            b = nps()
            for k in range(8):
                mm(ps[b][:], hT[:, k, i * 128:(i + 1) * 128], wtm[:, k, k3 * 512:(k3 + 1) * 512], k == 0, k == 7,
                   [('hT', i, 0), ('hT', i, 1), 'wtm'], [('ps', b)])
            cp('act' if k3 != 1 else 'dve', tms[s2][:, k3 * 512:(k3 + 1) * 512], ps[b][:], [('ps', b)], [('tms', s2)])
        S.dma('sp', sc_tm[i], tms[s2][:], reads=[('tms', s2)], writes=[('sc_tm', i)])
    def a_s1(i):
        s3 = i % 3
        src = ctx[i * 128:(i + 1) * 128, :] if i < 2 else x[(i - 2) * 128:(i - 1) * 128, :]
        S.dma('sp', xb[s3][:], src, writes=[('xb', s3)])
        S.op('pool', lambda e, t_=st[s3]: e.memset(t_[:, 0:1], 0.0), [], [('st', s3)])
        act(junk[:], xb[s3][:], AF.Square, [('xb', s3), ('st', s3)], ['junk', ('st', s3)], accum_out=st[s3][:, 0:1])
        act(st[s3][:, 1:2], st[s3][:, 0:1], AF.Ln, [('st', s3)], [('st', s3)], scale=1.0 / D, bias=EPS)
        act(st[s3][:, 2:3], st[s3][:, 1:2], AF.Exp, [('st', s3)], [('st', s3)], scale=-0.5)
        ts('dve', xn[s3][:], xb[s3][:], st[s3][:, 2:3], None, ALU.mult, None, [('xb', s3), ('st', s3)], [('xn', s3)])

    def a_s2(i):
        s3 = i % 3
        bb_ = [nps(), nps()]
        pbs = [ps[bb_[0]][:].bitcast(BF16), ps[bb_[1]][:].bitcast(BF16)]
        for k in range(8):
            tr(pbs[k % 2][:, (k // 2) * 128:(k // 2 + 1) * 128], xn[s3][:, k * 128:(k + 1) * 128], idb[:], [('xn', s3), 'idb'], [('ps', bb_[k % 2])], inc=(k >= 6))
        ai = 0 if i >= 2 else 2
        for k in range(8):
            dst = hT[:, k, i * 128:(i + 1) * 128]
            src_ = pbs[k % 2][:, (k // 2) * 128:(k // 2 + 1) * 128]
            if k % 2 == 0:
                act(dst, src_, AF.Identity, [('ps', bb_[0]), ('fv', 0)], [('hT', i, 0)],
                    scale=fv[:, ai, k:k + 1], bias=fv[:, ai + 1, k:k + 1])
            else:
                ts('dve', dst, src_, fv[:, ai, k:k + 1], fv[:, ai + 1, k:k + 1], ALU.mult, ALU.add,
                   [('ps', bb_[1]), ('fv', 0)], [('hT', i, 1)])
    npc = [4]
    for it in range(NTA + 2):
        if it < NTA:
            a_s1(it)
        if 0 <= it - 1 < NTA:
            a_s2(it - 1)
        if 0 <= it - 2 < NTA:
            p1_tile(it - 2)

    if stop_after == 'A':
        for i in range(NTA):
            S.out_toks.append(S.lastw[('sc_tm', i)])
        S.emit()
        return nc

    S.barrier()
    top[0] = MARK1
    wfm = sb([128, 8, 4608], BF16, "wfm")
    pool_order = [('w', 0), ('w', 1), ('w', 2), ('w', 3), ('w', 4), ('m', 4), ('m', 5), ('w', 5), ('w', 6), ('w', 7), ('w', 8)]

    def wfm_dma(k9):
        S.dma('pool', wfm[:, :, k9 * 512:(k9 + 1) * 512], w_in[:, 1536 + k9 * 512:1536 + (k9 + 1) * 512].rearrange("(k p) n -> p k n", p=128), writes=[('wfm', k9)])
    for kind_, idx_ in pool_order:
        if kind_ == 'w':
            wfm_dma(idx_)
        else:
            mod_dma(idx_)
    stg = [sb([128, 512], BF16, "stg") for _ in range(4)]
    sgi = [0]
    cvt = [sb([128, 512], F32, "cvt") for _ in range(7)]
    ustg = [sb([128, 4, 128], BF16, "ustg") for _ in range(4)]
    ub = [sb([128, 512], BF16, "ub") for _ in range(2)]

    def proj(ft_col, tb):
        b = nps()
        for k in range(8):
            mm(ps[b][:], wfm[:, k, ft_col:ft_col + 128], hT[:, k, 256 + tb * 512:256 + (tb + 1) * 512], k == 0, k == 7,
               [('wfm', ft_col // 512)] + [('hT', 2 + tb * 4 + j, e_) for j in range(4) for e_ in range(2)], [('ps', b)])
        return b

    def conv(b, ft, dst, dkey, s4):
        cv = cvt[s4]
        ck = ('cvt', s4)
        p3 = ps[b][:].rearrange("p (r c) -> p r c", c=64)
        c3 = cv[:].rearrange("p (r c) -> p r c", c=64)
        act(cv[:], ps[b][:], AF.Identity, [('ps', b), 'sm'], [ck], scale=SM('hsw', 12 + ft, 1), bias=SM('hsb', ft, 1))
        stt('dve', c3[:, :, 1:64], p3[:, :, 0:63], SM('hsw', ft, 1), c3[:, :, 1:64], ALU.mult, ALU.add, [('ps', b), 'sm', ck], [ck])
        d3 = dst.rearrange("p (r c) -> p r c", c=64)
        stt('dve', d3[:, :, 0:63], p3[:, :, 1:64], SM('hsw', 24 + ft, 1), c3[:, :, 0:63], ALU.mult, ALU.add, [('ps', b), 'sm', ck], [dkey])
        cp('pool', d3[:, :, 63:64], c3[:, :, 63:64], [ck], [dkey])

    pend = [None]
    usi = [0]

    def utrans(u2, j, tb):
        bt = nps()
        pbt = ps[bt][:].bitcast(BF16)
        for q4 in range(4):
            tr(pbt[:, q4 * 128:(q4 + 1) * 128], u2[:, q4 * 128:(q4 + 1) * 128], idb[:], [('ub', (j * 8 + tb) % 2), 'idb'], [('ps', bt)], inc=(q4 == 3))
        u4 = usi[0] % 4
        usi[0] += 1
        cp('act', ustg[u4][:], pbt[:, 0:512].rearrange("p (a b) -> p a b", b=128), [('ps', bt)], [('ustg', u4)])
        S.dma('sp', sc_u[:, tb * 4:(tb + 1) * 4, j * 128:(j + 1) * 128], ustg[u4][:], reads=[('ustg', u4)], writes=[('sc_u', tb, j)])

    def nstg():
        sg = sgi[0] % 4
        sgi[0] += 1
        return sg
    for ft in range(8):
        for tb in range(8):
            tsl = slice(tb * 512, (tb + 1) * 512)
            b = proj(ft * 128, tb)
            sg = nstg()
            act(stg[sg][:], ps[b][:], AF.Silu, [('ps', b)], [('stg', sg)])
            dstd = (sc_q if ft < 4 else sc_g)[ft % 4, :, tsl]
            S.dma('sp', dstd, stg[sg][:], reads=[('stg', sg)], writes=[('sc_qg', ft, tb)])
    for j in range(4):
        for tb in range(8):
            tsl = slice(tb * 512, (tb + 1) * 512)
            b = proj(1024 + j * 128, tb)
            sg = nstg()
            conv(b, j, stg[sg][:], ('stg', sg), 0)
            S.dma('sp', sc_x0[j, :, tsl], stg[sg][:], reads=[('stg', sg)], writes=[('sc_x0', j, tb)])
    mod_piece(4, dma=False)
    mod_piece(5, dma=False)
    mod_dma(6)
    mod_dma(7)
    for j in range(4):
        for tb in range(8):
            pp = (j * 8 + tb) % 2
            b1 = proj(1024 + (4 + j) * 128, tb)
            conv(b1, 4 + j, cvt[3 + pp][:], ('cvt', 3 + pp), 1)
            b2 = proj(1024 + (8 + j) * 128, tb)
            conv(b2, 8 + j, cvt[5 + pp][:], ('cvt', 5 + pp), 2)
            u2 = ub[pp]
            tt('pool', u2[:], cvt[3 + pp][:], cvt[5 + pp][:], ALU.mult, [('cvt', 3 + pp), ('cvt', 5 + pp)], [('ub', pp)])
            if pend[0] is not None:
                pend[0]()
            pend[0] = (lambda u2_=u2, j_=j, tb_=tb: utrans(u2_, j_, tb_))
        if j == 1:
            mod_piece(6, dma=False)
            mod_piece(7, dma=False)
            mod_dma(8)
            mod_dma(9)
        if j == 3:
            mod_piece(8, dma=False)
            mod_piece(9, dma=False)
            mod_dma(10)
            mod_dma(11)
    pend[0]()
    for ft in range(16):
        for tb in range(8):
            tsl = slice(tb * 512, (tb + 1) * 512)
            b = proj(2560 + ft * 128, tb)
            sg = nstg()
            act(stg[sg][:], ps[b][:], AF.Sigmoid, [('ps', b)], [('stg', sg)])
            S.dma('sp', sc_gate[ft, :, tsl], stg[sg][:], reads=[('stg', sg)], writes=[('sc_gate', ft, tb)])
        if ft == 4:
            mod_piece(10, dma=False)
            mod_piece(11, dma=False)
    mod_rest_finish()
    if stop_after == 'P':
        for k_, v_ in list(S.lastw.items()):
            if isinstance(k_, tuple) and str(k_[0]).startswith('sc_'):
                S.out_toks.append(v_)
        S.emit()
        return nc
    S.barrier()
    top[0] = MARK0
    hcat = sb([128, 32, 1024], BF16, "hcat")
    MARKH = top[0]
    zt = sb([33, L], F32, "zt"); w1s = sb([33, 64], F32, "w1s"); w2s = sb([64, 64], F32, "w2s"); w3s = sb([64, 1024], F32, "w3s")
    h1 = sb([64, L], F32, "h1"); h2 = sb([64, L], F32, "h2"); wtmp = sb([64, 512], F32, "wtmp")
    S.dma('sp', zt[:], zT_d, writes=['zt'])
    S.dma('sp', w1s[:], f_w1, writes=['fw'])
    S.dma('sp', w2s[:], f_w2, writes=['fw'])
    S.dma('sp', w3s[:], f_w3, writes=['fw'])

    def sinlayer(src, w, kdim, bcol, dst, skey, dkey):
        for blk in range(8):
            sl = slice(blk * 512, (blk + 1) * 512)
            b = nps()
            mm(ps[b][0:64, :], w[0:kdim, :], src[0:kdim, sl], True, True, [skey, 'fw'], [('ps', b)])
            ts('dve', dst[:, sl], ps[b][0:64, :], sm64[:, bcol:bcol + 1], sm64[:, 2:3], ALU.add, ALU.mult, [('ps', b), 'sm64'], [dkey])
            for _ in range(1):
                S.op('dve', lambda e, d_=dst[:, sl]: e.tensor_single_scalar(out=wtmp[:], in_=d_, scalar=PI, op=ALU.is_gt), [dkey], ['wtmp'])
                stt('dve', dst[:, sl], wtmp[:], -2 * PI, dst[:, sl], ALU.mult, ALU.add, ['wtmp', dkey], [dkey])
                S.op('dve', lambda e, d_=dst[:, sl]: e.tensor_single_scalar(out=wtmp[:], in_=d_, scalar=-PI, op=ALU.is_lt), [dkey], ['wtmp'])
                stt('dve', dst[:, sl], wtmp[:], 2 * PI, dst[:, sl], ALU.mult, ALU.add, ['wtmp', dkey], [dkey])
            act(dst[:, sl], dst[:, sl], AF.Sin, [dkey], [dkey])
    sinlayer(zt, w1s, 33, 0, h1, 'zt', 'h1')
    sinlayer(h1, w2s, 64, 1, h2, 'h1', 'h2')
    dect = [sb([128, 512], F32, "dect") for _ in range(2)]
    ftmp = sb([128, 512], F32, "ftmp")
    ntn = C('ntn')
    for chunk in range(32):
        dt_ = dect[chunk % 2]
        dk = ('dect', chunk % 2)
        act(dt_[:], C('delta'), AF.Exp, ['cst'], [dk], scale=ntn[:, chunk:chunk + 1])
        for half in range(2):
            b = nps()
            mm(ps[b][:], h2[:, chunk * 128:(chunk + 1) * 128], w3s[:, half * 512:(half + 1) * 512], True, True, ['h2', 'fw'], [('ps', b)])
            dst = hcat[:, chunk, half * 512:(half + 1) * 512]
            if chunk == 0:
                tt('dve', ftmp[:], ps[b][:], dt_[:], ALU.mult, [('ps', b), dk], ['ftmp'])
                if half == 0:
                    tt('dve', ftmp[0:1, :], ftmp[0:1, :], hydbc[0:1, :], ALU.add, ['ftmp', 'hydbc'], ['ftmp'])
                else:
                    S.op('dve', lambda e: e.memset(ftmp[0:1, :], 0.0), ['ftmp'], ['ftmp'])
                cp('dve', dst, ftmp[:], ['ftmp'], ['hcat'])
            else:
                tt('dve', dst, ps[b][:], dt_[:], ALU.mult, [('ps', b), dk], ['hcat'])

    S.barrier()
    top[0] = MARKH
    u_tm = sb([128, 32, 512], BF16, "u_tm")
    for q8 in range(8):
        S.dma('sp', u_tm[:, q8 * 4:(q8 + 1) * 4, :], sc_u[:, q8 * 4:(q8 + 1) * 4, :], reads=[('sc_u', q8, j_) for j_ in range(4)], writes=['u_tm'])
    tabs = [sb([128, 32, 256], BF16, "tabf") for _ in range(2)]
    Ast = sb([128, 2, 512], F32, "Ast"); Pst = sb([128, 2, 512], F32, "Pst")
    tq = [sb([128, 512], F32, "tq") for _ in range(6)]
    yst = [sb([128, 512], BF16, "yst") for _ in range(4)]
    SC = 2.0 / NFFT
    ysi = [0]

    def ynext():
        i_ = ysi[0] % 4
        ysi[0] += 1
        return yst[i_], ('yst', i_)
    for fb in range(16):
        tbi, hf_ = fb // 2, fb % 2
        for cs in range(2):
            tab = tabs[cs]
            tk = ('tab', cs)
            S.dma('sp', tab[:], (ctab_d if cs == 0 else stab_d)[tbi][:, :, hf_ * 256:(hf_ + 1) * 256], writes=[tk])
            for m in range(2):
                bb = [nps(), nps(), nps()]
                for chunk in range(32):
                    lhsT = tab[:, chunk, m * 128:(m + 1) * 128]
                    mm(ps[bb[0]][:], lhsT, u_tm[:, chunk, :], chunk == 0, chunk == 31, [tk, 'u_tm'], [('ps', bb[0])])
                    mm(ps[bb[1]][:], lhsT, hcat[:, chunk, 0:512], chunk == 0, chunk == 31, [tk, 'hcat'], [('ps', bb[1])])
                    mm(ps[bb[2]][:], lhsT, hcat[:, chunk, 512:1024], chunk == 0, chunk == 31, [tk, 'hcat'], [('ps', bb[2])])
                ft = fb * 2 + m
                if cs == 0:
                    cp('act', Ast[:, m, :], ps[bb[0]][:], [('ps', bb[0])], [('Ast', m)])
                    cp('act', tq[0][:], ps[bb[2]][:], [('ps', bb[2])], [('tq', 0)])
                    tt('dve', Pst[:, m, :], ps[bb[1]][:], tq[0][:], ALU.add, [('ps', bb[1]), ('tq', 0)], [('Pst', m)])
                else:
                    cp('act', tq[1][:], ps[bb[2]][:], [('ps', bb[2])], [('tq', 1)])
                    tt('dve', tq[2][:], ps[bb[1]][:], tq[1][:], ALU.subtract, [('ps', bb[1]), ('tq', 1)], [('tq', 2)])
                    tt('pool', tq[3][:], Ast[:, m, :], Pst[:, m, :], ALU.mult, [('Ast', m), ('Pst', m)], [('tq', 3)])
                    stt('dve', tq[4][:], ps[bb[0]][:], SC, tq[2][:], ALU.mult, ALU.mult, [('ps', bb[0]), ('tq', 2)], [('tq', 4)])
                    yr, yrk = ynext()
                    stt('dve', yr[:], tq[3][:], SC, tq[4][:], ALU.mult, ALU.subtract, [('tq', 3), ('tq', 4)], [yrk])
                    if ft == 0:
                        ts('dve', yr[0:1, :], yr[0:1, :], 0.5, None, ALU.mult, None, [yrk], [yrk])
                    S.dma('sp', sc_Y[:, ft, :], yr[:], reads=[yrk], writes=[('sc_Y', ft)])
                    tt('pool', tq[3][:], Ast[:, m, :], tq[2][:], ALU.mult, [('Ast', m), ('tq', 2)], [('tq', 3)])
                    stt('dve', tq[5][:], ps[bb[0]][:], SC, Pst[:, m, :], ALU.mult, ALU.mult, [('ps', bb[0]), ('Pst', m)], [('tq', 5)])
                    yq, yqk = ynext()
                    stt('dve', yq[:], tq[3][:], SC, tq[5][:], ALU.mult, ALU.add, [('tq', 3), ('tq', 5)], [yqk])
                    S.dma('sp', sc_Y[:, 32 + ft, :], yq[:], reads=[yqk], writes=[('sc_Y', 32 + ft)])
    bn = [nps(), nps(), nps()]
    for chunk in range(32):
        mm(ps[bn[0]][0:1, :], altb[:, 0:1], u_tm[:, chunk, :], chunk == 0, chunk == 31, ['altb', 'u_tm'], [('ps', bn[0])])
        mm(ps[bn[1]][0:1, :], altb[:, 0:1], hcat[:, chunk, 0:512], chunk == 0, chunk == 31, ['altb', 'hcat'], [('ps', bn[1])])
        mm(ps[bn[2]][0:1, :], altb[:, 0:1], hcat[:, chunk, 512:1024], chunk == 0, chunk == 31, ['altb', 'hcat'], [('ps', bn[2])])
    cp('act', tq[0][0:1, :], ps[bn[2]][0:1, :], [('ps', bn[2])], [('tq', 0)])
    tt('dve', tq[1][0:1, :], ps[bn[1]][0:1, :], tq[0][0:1, :], ALU.add, [('ps', bn[1]), ('tq', 0)], [('tq', 1)])
    stt('dve', ynq[:], ps[bn[0]][0:1, :], 1.0 / NFFT, tq[1][0:1, :], ALU.mult, ALU.mult, [('ps', bn[0]), ('tq', 1)], ['ynq'])

    S.barrier()
    top[0] = MARK0
    Yt = sb([128, 64, 512], BF16, "Yt")
    for q8 in range(8):
        S.dma('sp', Yt[:, q8 * 8:(q8 + 1) * 8, :], sc_Y[:, q8 * 8:(q8 + 1) * 8, :],
              reads=[('sc_Y', f_) for f_ in range(q8 * 8, q8 * 8 + 8)], writes=[('Yt', q8)])
    tabi = [sb([128, 32, 512], BF16, "tabi") for _ in range(2)]
    x0s = [sb([128, 512], BF16, "x0s") for _ in range(4)]
    yo = [sb([128, 512], BF16, "yo") for _ in range(2)]
    S.dma('sp', tabi[0][:], ctab_d[0], writes=[('tabi', 0)])
    S.dma('sp', tabi[1][:], stab_d[0], writes=[('tabi', 1)])
    for tb in range(8):
        tsl = slice(tb * 512, (tb + 1) * 512)
        for ct in range(4):
            S.dma('act', x0s[ct][:], sc_x0[ct, :, tsl], reads=[('sc_x0', ct, tb)], writes=[('x0s', ct)])
        banks = [nps() for _ in range(4)]
        for cs in range(2):
            for ct in range(4):
                for chunk in range(32):
                    mm(ps[banks[ct]][:], Yt[:, cs * 32 + chunk, ct * 128:(ct + 1) * 128], tabi[cs][:, chunk, :],
                       cs == 0 and chunk == 0, False, [('Yt', (cs * 32 + chunk) // 8), ('tabi', cs)], [('ps', banks[ct])], inc=(chunk == 31))
            if tb + 1 < 8:
                S.dma('sp', tabi[cs][:], (ctab_d if cs == 0 else stab_d)[tb + 1], writes=[('tabi', cs)])
        for ct in range(4):
            mm(ps[banks[ct]][:], ynq[0:1, ct * 128:(ct + 1) * 128], altrowb[0:1, :], False, True, ['ynq', 'altrowb'], [('ps', banks[ct])], inc=True)
            s2 = ct % 2
            tt('dve', yo[s2][:], ps[banks[ct]][:], x0s[ct][:], ALU.mult, [('ps', banks[ct]), ('x0s', ct)], [('yo', s2)])
            S.dma('act', sc_yx0[ct, :, tsl], yo[s2][:], reads=[('yo', s2)], writes=[('sc_yx0', ct, tb)])
    if stop_after == 'H':
        for k_, v_ in list(S.lastw.items()):
            if isinstance(k_, tuple) and str(k_[0]).startswith('sc_'):
                S.out_toks.append(v_)
        S.emit()
        return nc

    S.barrier()
    top[0] = MARK0
    tmb = [sb([128, 1536], F32, "tmb") for _ in range(4)]
    vb = [sb([128, 512], BF16, "vb") for _ in range(4)]
    ebuf = [sb([128, 512], F32, "ebuf") for _ in range(2)]
    lp = [sb([128, 512], F32, "lp") for _ in range(2)]
    kbuf = [sb([128, 512], F32, "kbuf") for _ in range(2)]
    lnk = [sb([128, 512], F32, "lnk") for _ in range(2)]
    logf = [sb([128, 512], F32, "logf") for _ in range(2)]
    Ea = [[sb([128, 4, 128], BF16, "Ea") for _ in range(2)] for _ in range(2)]
    Eb = [[sb([128, 4, 128], BF16, "Eb") for _ in range(2)] for _ in range(2)]
    keT = [[sb([128, 512], BF16, "keT") for _ in range(2)] for _ in range(2)]
    kd = [[sb([128, 512], BF16, "kd") for _ in range(2)] for _ in range(2)]
    dec = [[sb([128, 4], F32, "dec") for _ in range(2)] for _ in range(2)]
    Sf = [sb([128, 4, 128], F32, "Sf") for _ in range(2)]
    Sb16 = [sb([128, 4, 128], BF16, "Sb16") for _ in range(2)]
    sbs = sb([128, 32, 512], BF16, "sbs")
    qtb = [sb([128, 4, 128], BF16, "qtb") for _ in range(3)]
    gtb = [sb([128, 4, 128], BF16, "gtb") for _ in range(3)]
    qa = [sb([128, 4, 128], BF16, "qa") for _ in range(2)]
    qe = [sb([128, 4, 128], BF16, "qe") for _ in range(2)]
    scT = [sb([128, 512], BF16, "scT") for _ in range(2)]
    sqb = sb([128, 512], BF16, "sqb"); rs = sb([128, 512], F32, "rs"); otf = sb([128, 512], F32, "otf")
    atb = [sb([128, 512], BF16, "atb") for _ in range(2)]
    flat = lambda a_: a_.rearrange("p a b -> p (a b)")
    for d in range(2):
        S.op('pool', lambda e, t_=Sf[d]: e.memset(flat(t_), 0.0), [], [('Sf', d)])
        S.op('pool', lambda e, t_=Sb16[d]: e.memset(flat(t_), 0.0), [], [('Sb', d)])

    gate_alloc = [nps]

    def gate(d, tm, tmk, sl, lite=False, tix=None):
        zz = tm[:, 512 * (1 + d):512 * (2 + d)]
        if tix is None:
            tix = d
        eb_, lp_, kb_ = ebuf[tix], lp[tix], kbuf[tix]
        act(eb_[:], zz, AF.Exp, [tmk], [('ebuf', tix)])
        act(lp_[:], eb_[:], AF.Ln, [('ebuf', tix)], [('lp', tix)], bias=1.0)
        yield
        stt('dve', lnk[tix][:], lp_[:], -1.0, c1bc[:], ALU.mult, ALU.add, [('lp', tix), 'c1bc'], [('lnk', tix)])
        yield
        act(kb_[:], lnk[tix][:], AF.Exp, [('lnk', tix)], [('kbuf', tix)])
        act(logf[tix][:], kb_[:], AF.Ln, [('kbuf', tix)], [('logf', tix)], scale=-1.0, bias=1.0)
        yield
        sfx = 'f' if d == 0 else 'b'
        oi = coffs['Mincl_' + sfx][0]
        Mcat = cst[:, oi:oi + 256]
        nMb = C('nMb_' + sfx)
        Mkd = C('Mkd_' + sfx)
        bkd = gate_alloc[0]()
        mm(ps[bkd][:], Mkd, logf[tix][:], True, True, [('logf', tix), 'cst'], [('ps', bkd)], inc=True)
        if lite:
            bt_ = gate_alloc[0]()
            for h in range(4):
                mm(ps[bt_][:, h:h + 1], logf[tix][:, h * 128:(h + 1) * 128], onesf[:, 0:1], True, True, [('logf', tix), 'cst'], [('ps', bt_)], inc=(h == 3))
        else:
            bab = [gate_alloc[0](), gate_alloc[0]()]
            for h in range(4):
                mm(ps[bab[h // 2]][:, (h % 2) * 256:(h % 2) * 256 + 256], logf[tix][:, h * 128:(h + 1) * 128], Mcat, True, True,
                   [('logf', tix), 'cst'], [('ps', bab[h // 2])], inc=(h % 2 == 1))
            bke = gate_alloc[0]()
            for h in range(4):
                mm(ps[bke][:, h * 128:(h + 1) * 128], lnk[tix][:, h * 128:(h + 1) * 128], ident, True, False, [('lnk', tix), 'cst'], [('ps', bke)], inc=False)
                mm(ps[bke][:, h * 128:(h + 1) * 128], logf[tix][:, h * 128:(h + 1) * 128], nMb, False, True, [('logf', tix), 'cst'], [('ps', bke)], inc=(h == 3))
        yield
        act(eb_[:], ps[bkd][:], AF.Exp, [('ps', bkd)], [('ebuf', tix)])
        if lite:
            act(dec[d][sl][:], ps[bt_][:, 0:4], AF.Exp, [('ps', bt_)], [('dec', d, sl)])
            yield
            tt('dve', kd[d][sl][:], eb_[:], kb_[:], ALU.mult, [('ebuf', tix), ('kbuf', tix)], [('kd', d, sl)])
            return
        col = 127 if d == 0 else 0
        for j in range(2):
            v3 = ps[bab[j]][:].rearrange("p (h x) -> p h x", x=256)
            act(Ea[d][sl][:, 2 * j:2 * j + 2, :], v3[:, :, 0:128], AF.Exp, [('ps', bab[j])], [('Ea', d, sl)])
            act(Eb[d][sl][:, 2 * j:2 * j + 2, :], v3[:, :, 128:256], AF.Exp, [('ps', bab[j])], [('Eb', d, sl)])
            for hh in range(2):
                act(dec[d][sl][:, 2 * j + hh:2 * j + hh + 1], ps[bab[j]][:, hh * 256 + col:hh * 256 + col + 1], AF.Exp, [('ps', bab[j])], [('dec', d, sl)])
        act(keT[d][sl][:], ps[bke][:], AF.Exp, [('ps', bke)], [('keT', d, sl)])
        yield
        tt('dve', kd[d][sl][:], eb_[:], kb_[:], ALU.mult, [('ebuf', tix), ('kbuf', tix)], [('kd', d, sl)])

    def update(d, vt, vk, sl, bank=None):
        b = nps() if bank is None else bank
        for h in range(4):
            mm(ps[b][:, h * 128:(h + 1) * 128], kd[d][sl][:, h * 128:(h + 1) * 128], vt[:, h * 128:(h + 1) * 128], True, True,
               [('kd', d, sl), vk], [('ps', b)], inc=(h == 3))
        for h in range(4):
            stt('dve', Sf[d][:, h, :], Sf[d][:, h, :], dec[d][sl][:, h:h + 1], ps[b][:, h * 128:(h + 1) * 128], ALU.mult, ALU.add,
                [('Sf', d), ('dec', d, sl), ('ps', b)], [('Sf', d)])
        cp('act', flat(Sb16[d]), flat(Sf[d]), [('Sf', d)], [('Sb', d)])

    ldi = [0]
    ldmap = {}

    def issue_load(key, i):
        s3 = ldi[0] % 4
        ldi[0] += 1
        S.dma('sp', tmb[s3][:], sc_tm[i], reads=[('sc_tm', i)], writes=[('tmb', s3)])
        ldmap[key] = s3

    def load_tm(key):
        s3 = ldmap[key]
        cp('dve', vb[s3][:], tmb[s3][:, 0:512], [('tmb', s3)], [('vb', s3)])
        return tmb[s3], ('tmb', s3), vb[s3], ('vb', s3)
    gi = [0]

    steps = [(0, 0), (0, 1), (1, 1), (1, 0)] + [(1, n_ + 2) for n_ in range(31, -1, -1)]

    info = {}

    def lite_gen(k_):
        d_, i_ = steps[k_]
        tm, tmk, vt, vk = load_tm(('g1', k_))
        info[k_] = (d_, vt, vk, k_ % 2, i_)
        return gate(d_, tm, tmk, k_ % 2, lite=True, tix=k_ % 2)
    for kk in (0, 1):
        issue_load(('g1', kk), steps[kk][1])
    for k0 in range(0, len(steps), 2):
        for kk in (k0 + 2, k0 + 3):
            if kk < len(steps):
                issue_load(('g1', kk), steps[kk][1])
        interleave([lite_gen(k0), lite_gen(k0 + 1)])
        for kk in (k0, k0 + 1):
            d_, vt, vk, sl, i_ = info[kk]
            if d_ == 1 and i_ >= 2:
                cp('act', sbs[:, i_ - 2, :], flat(Sb16[1]), [('Sb', 1)], [('sbs', i_ - 2)])
            update(d_, vt, vk, sl)
    loaded = {}
    qslot = {}

    def issue_g2(n):
        issue_load(('g2', n), n + 2)
        q3 = n % 3
        qslot[n] = q3
        tsl = slice(n * 128, (n + 1) * 128)
        S.dma('sp', qtb[q3][:], sc_q[:, :, tsl].rearrange("h p t -> p h t"), reads=[('sc_qg', h_, n // 4) for h_ in range(4)], writes=[('qtb', q3)])
        S.dma('sp', gtb[q3][:], sc_g[:, :, tsl].rearrange("h p t -> p h t"), reads=[('sc_qg', 4 + h_, n // 4) for h_ in range(4)], writes=[('gtb', q3)])

    def prep_gens(n):
        if n + 1 < 32:
            issue_g2(n + 1)
        tm, tmk, vt, vk = load_tm(('g2', n))
        loaded[n] = (vt, vk)
        return [gate(0, tm, tmk, n % 2), gate(1, tm, tmk, n % 2)]

    def core(n):
        vt, vk = loaded[n]
        q2 = n % 2
        q3 = qslot[n]
        tsl = slice(n * 128, (n + 1) * 128)
        bsl = []
        for d in range(2):
            tt('dve', flat(qa[d]), flat(qtb[q3]), flat(Ea[d][q2]), ALU.mult, [('qtb', q3), ('Ea', d, q2)], [('qa', d)])
            tt('dve', flat(qe[d]), flat(qtb[q3]), flat(Eb[d][q2]), ALU.mult, [('qtb', q3), ('Eb', d, q2)], [('qe', d)])
        yield
        for d in range(2):
            bs = 4 + d
            bsl.append(bs)
            for h in range(4):
                mm(ps[bs][:, h * 128:(h + 1) * 128], keT[d][q2][:, h * 128:(h + 1) * 128], qe[d][:, h, :], True, True,
                   [('keT', d, q2), ('qe', d)], [('ps', bs)], inc=(h == 3))
        yield
        for d in range(2):
            mk = maskf if d == 0 else maskb
            tt('dve', scT[d][:], ps[bsl[d]][:], mk[:], ALU.mult, [('ps', bsl[d]), 'maskf', 'maskb'], [('scT', d)])
        yield
        bo = 6
        for h in range(4):
            o_ = ps[bo][:, h * 128:(h + 1) * 128]
            hs = slice(h * 128, (h + 1) * 128)
            mm(o_, vt[:, hs], scT[0][:, hs], True, False, [vk, ('scT', 0)], [('ps', bo)], inc=False)
            mm(o_, vt[:, hs], scT[1][:, hs], False, False, [vk, ('scT', 1)], [('ps', bo)], inc=False)
            mm(o_, Sb16[0][:, h, :], qa[0][:, h, :], False, False, [('Sb', 0), ('qa', 0)], [('ps', bo)], inc=False)
            mm(o_, sbs[:, n, hs], qa[1][:, h, :], False, True, [('sbs', n), ('qa', 1)], [('ps', bo)], inc=(h == 3))
        update(0, vt, vk, q2, bank=7)
        yield
        act(sqb[:], ps[bo][:], AF.Square, [('ps', bo)], ['sqb'])
        yield
        bss = 4
        mm(ps[bss][:], onesb[:], sqb[:], True, True, ['onesb', 'sqb'], [('ps', bss)])
        yield
        act(rs[:], ps[bss][:], AF.Ln, [('ps', bss)], ['rs'], scale=1.0 / 128, bias=EPS)
        act(rs[:], rs[:], AF.Exp, ['rs'], ['rs'], scale=-0.5)
        yield
        tt('dve', otf[:], ps[bo][:], rs[:], ALU.mult, [('ps', bo), 'rs'], ['otf'])
        yield
        stt('dve', atb[q2][:], otf[:], SM('gn'), flat(gtb[q3]), ALU.mult, ALU.mult, ['otf', 'sm', ('gtb', q3)], [('atb', q2)])
        S.dma('sp', sc_a[:, :, tsl].rearrange("h p t -> p h t"), atb[q2][:].rearrange("p (h t) -> p h t", t=128),
              reads=[('atb', q2)], writes=[('sc_a', n)])
    g2i = [0]

    def gps():
        i_ = g2i[0] % 4
        g2i[0] += 1
        return i_

    def chain(gs):
        for g_ in gs:
            yield from g_
    gate_alloc[0] = gps
    issue_g2(0)
    interleave([chain(prep_gens(0))])
    for n in range(32):
        gens = [core(n)]
        if n + 1 < 32:
            gens = [chain(prep_gens(n + 1))] + gens
        interleave(gens)
    gate_alloc[0] = nps
    if stop_after == 'G':
        for k_, v_ in list(S.lastw.items()):
            if isinstance(k_, tuple) and str(k_[0]).startswith('sc_'):
                S.out_toks.append(v_)
        S.emit()
        return nc

    S.barrier()
    top[0] = MARK0
    bcv = sb([128, 4, 1024], F32, "bcv")
    for vi in range(4):
        for k in range(8):
            bcast(bcv[:, vi, k * 128:(k + 1) * 128], fv[:, 4 + vi, k:k + 1], 'bcv', 'fv')
    wpa = sb([128, 4, 1024], BF16, "wpa"); wpb = sb([128, 4, 1024], BF16, "wpb"); wout = sb([128, 8, 1024], BF16, "wout")
    S.dma('pool', wpa[:], w_pa.rearrange("(k p) n -> p k n", p=128), writes=['wpa'])
    S.dma('pool', wpb[:], w_pb.rearrange("(k p) n -> p k n", p=128), writes=['wpb'])
    for k2 in range(2):
        S.dma('pool', wout[:, :, k2 * 512:(k2 + 1) * 512], w_out[:, k2 * 512:(k2 + 1) * 512].rearrange("(k p) n -> p k n", p=128), writes=['wout'])
    aTb = [sb([128, 4, 512], BF16, "aTb") for _ in range(2)]
    yxb = [sb([128, 4, 512], BF16, "yxb") for _ in range(2)]
    ggb = [sb([128, 16, 512], BF16, "ggb") for _ in range(2)]
    merged = sb([128, 8, 512], BF16, "merged")
    m1 = [sb([128, 512], F32, "m1") for _ in range(2)]
    xt = [sb([128, 1024], F32, "xt") for _ in range(2)]
    xl = [sb([128, 1024], F32, "xl") for _ in range(2)]
    tmpf = [sb([128, 1024], F32, "tmpf") for _ in range(2)]
    hlb = [sb([128, 1024], BF16, "hlb") for _ in range(2)]
    hls = [sb([128, 8, 512], BF16, "hls") for _ in range(2)]
    junk2 = [sb([128, 1024], BF16, "junk2") for _ in range(2)]

    def interleave(gens):
        gens = list(gens)
        while gens:
            for g_ in list(gens):
                try:
                    next(g_)
                except StopIteration:
                    gens.remove(g_)
    st2 = [sb([128, 8], F32, "st2") for _ in range(2)]
    def t1_loads(tb):
        tsl = slice(tb * 512, (tb + 1) * 512)
        b2 = tb % 2
        S.dma('sp', aTb[b2][:], sc_a[:, :, tsl].rearrange("h p t -> p h t"), reads=[('sc_a', tb * 4 + j_) for j_ in range(4)], writes=[('aTb', b2)])
        S.dma('sp', yxb[b2][:], sc_yx0[:, :, tsl].rearrange("h p t -> p h t"), reads=[('sc_yx0', c_, tb) for c_ in range(4)], writes=[('yxb', b2)])
        S.dma('sp', ggb[b2][:], sc_gate[:, :, tsl].rearrange("f p t -> p f t"), reads=[('sc_gate', f_, tb) for f_ in range(16)], writes=[('ggb', b2)])
    for tb in range(8):
        tsl = slice(tb * 512, (tb + 1) * 512)
        b2 = tb % 2
        if tb == 0:
            t1_loads(0)
        if tb + 1 < 8:
            t1_loads(tb + 1)
        for dm in range(8):
            bA = nps()
            for k in range(4):
                mm(ps[bA][:], wpa[:, k, dm * 128:(dm + 1) * 128], aTb[b2][:, k, :], k == 0, k == 3, ['wpa', ('aTb', b2)], [('ps', bA)])
            bB = nps()
            for k in range(4):
                mm(ps[bB][:], wpb[:, k, dm * 128:(dm + 1) * 128], yxb[b2][:, k, :], k == 0, k == 3, ['wpb', ('yxb', b2)], [('ps', bB)])
            tt('dve', m1[0][:], ps[bA][:], ggb[b2][:, dm, :], ALU.mult, [('ps', bA), ('ggb', b2)], [('m1', 0)])
            tt('dve', m1[1][:], ps[bB][:], ggb[b2][:, 8 + dm, :], ALU.mult, [('ps', bB), ('ggb', b2)], [('m1', 1)])
            tt('pool', merged[:, dm, :], m1[0][:], m1[1][:], ALU.add, [('m1', 0), ('m1', 1)], [('merged', dm)])
        hs_ = hls[b2]

        def t1_tile(t4, tb=tb, hs_=hs_, b2=b2):
            n = tb * 4 + t4
            s2 = n % 2
            tf = tmpf[s2]
            tfk = ('tmpf', s2)
            S.dma('sp', xt[s2][:], x[n * 128:(n + 1) * 128, :], writes=[('xt', s2)])
            S.op('pool', lambda e, t_=st2[s2]: e.memset(t_[:, 0:2], 0.0), [], [('st2', s2)])
            bh = [nps(), nps()]
            for half in range(2):
                for k in range(8):
                    mm(ps[bh[half]][:], merged[:, k, t4 * 128:(t4 + 1) * 128], wout[:, k, half * 512:(half + 1) * 512], k == 0, k == 7,
                       [('merged', k), 'wout'], [('ps', bh[half])])
                act(junk2[s2][:, 0:512], ps[bh[half]][:], AF.Square, [('ps', bh[half]), ('st2', s2)], [('junk2', s2), ('st2', s2)], accum_out=st2[s2][:, half:half + 1])
            yield
            tt('dve', st2[s2][:, 2:3], st2[s2][:, 0:1], st2[s2][:, 1:2], ALU.add, [('st2', s2)], [('st2', s2)])
            yield
            act(st2[s2][:, 3:4], st2[s2][:, 2:3], AF.Ln, [('st2', s2)], [('st2', s2)], scale=1.0 / D, bias=EPS)
            act(st2[s2][:, 4:5], st2[s2][:, 3:4], AF.Exp, [('st2', s2)], [('st2', s2)], scale=-0.5)
            yield
            for half in range(2):
                hsl = slice(half * 512, (half + 1) * 512)
                stt('dve', tf[:, hsl], ps[bh[half]][:], st2[s2][:, 4:5], bcv[:, 0, hsl], ALU.mult, ALU.mult, [('ps', bh[half]), ('st2', s2), 'bcv'], [tfk])
            yield
            tt('dve', xl[s2][:], tf[:], xt[s2][:], ALU.add, [tfk, ('xt', s2)], [('xl', s2)])
            S.dma('sp', sc_xlat[n * 128:(n + 1) * 128, :], xl[s2][:], reads=[('xl', s2)], writes=[('sc_xlat', n)])
            S.op('pool', lambda e, t_=st2[s2]: e.memset(t_[:, 5:6], 0.0), [], [('st2', s2)])
            yield
            act(junk2[s2][:], xl[s2][:], AF.Square, [('xl', s2), ('st2', s2)], [('junk2', s2), ('st2', s2)], accum_out=st2[s2][:, 5:6])
            act(st2[s2][:, 6:7], st2[s2][:, 5:6], AF.Ln, [('st2', s2)], [('st2', s2)], scale=1.0 / D, bias=EPS)
            act(st2[s2][:, 7:8], st2[s2][:, 6:7], AF.Exp, [('st2', s2)], [('st2', s2)], scale=-0.5)
            yield
            ts('dve', hlb[s2][:], xl[s2][:], st2[s2][:, 7:8], None, ALU.mult, None, [('xl', s2), ('st2', s2)], [('hlb', s2)])
            yield
            bt2 = [nps(), nps()]
            pbts = [ps[bt2[0]][:].bitcast(BF16), ps[bt2[1]][:].bitcast(BF16)]
            for k in range(8):
                tr(pbts[k % 2][:, (k // 2) * 128:(k // 2 + 1) * 128], hlb[s2][:, k * 128:(k + 1) * 128], idb[:], [('hlb', s2), 'idb'], [('ps', bt2[k % 2])], inc=(k >= 6))
            yield
            for k in range(8):
                dst = hs_[:, k, t4 * 128:(t4 + 1) * 128]
                src_ = pbts[k % 2][:, (k // 2) * 128:(k // 2 + 1) * 128]
                if k % 2 == 0:
                    act(dst, src_, AF.Identity, [('ps', bt2[0]), 'fv'], [('hls', b2, 0)],
                        scale=fv[:, 5, k:k + 1], bias=fv[:, 6, k:k + 1])
                else:
                    ts('dve', dst, src_, fv[:, 5, k:k + 1], fv[:, 6, k:k + 1], ALU.mult, ALU.add,
                       [('ps', bt2[1]), 'fv'], [('hls', b2, 1)])
        interleave([t1_tile(0), t1_tile(1)])
        interleave([t1_tile(2), t1_tile(3)])
        S.dma('sp', sc_hl[:, :, tsl].rearrange("k p t -> p k t"), hs_[:], reads=[('hls', b2, 0), ('hls', b2, 1)], writes=[('sc_hl', tb)])

    S.barrier()
    top[0] = MARK0
    g5bc = sb([128, 1024], F32, "g5bc")
    for k in range(8):
        bcast(g5bc[:, k * 128:(k + 1) * 128], fv[:, 7, k:k + 1], 'g5bc', 'fv')
    w1 = sb([128, 8, 4096], BF16, "w1"); w2 = sb([128, 32, 1024], BF16, "w2")
    for k8 in range(8):
        S.dma('pool', w1[:, :, k8 * 512:(k8 + 1) * 512], w_mlp1[:, k8 * 512:(k8 + 1) * 512].rearrange("(k p) n -> p k n", p=128), writes=[('w1', k8)])
    for k8 in range(8):
        S.dma('pool', w2[:, k8 * 4:(k8 + 1) * 4, :], w_mlp2[k8 * 512:(k8 + 1) * 512, :].rearrange("(k p) n -> p k n", p=128), writes=[('w2', k8)])
    hlt = [sb([128, 8, 256], BF16, "hlt") for _ in range(2)]
    hid = sb([128, 32, 256], BF16, "hid")
    rr = [sb([128, 256], F32, "rr") for _ in range(2)]
    xl2 = [sb([128, 1024], F32, "xl2") for _ in range(2)]
    ot = [sb([128, 1024], F32, "ot") for _ in range(2)]
    junk3 = [sb([128, 512], BF16, "junk3") for _ in range(2)]
    st3 = [sb([128, 8], F32, "st3") for _ in range(2)]
    def t2_load(tb):
        S.dma('sp', hlt[tb % 2][:], sc_hl[:, :, tb * 256:(tb + 1) * 256].rearrange("k p t -> p k t"), reads=[('sc_hl', tb // 2)], writes=[('hlt', tb % 2)])
    for tb in range(16):
        b2 = tb % 2
        tsl = slice(tb * 256, (tb + 1) * 256)
        if tb == 0:
            t2_load(0)
        if tb + 1 < 16:
            t2_load(tb + 1)
        for ff in range(32):
            b = nps()
            for k in range(8):
                mm(ps[b][:, 0:256], w1[:, k, ff * 128:(ff + 1) * 128], hlt[b2][:, k, :], k == 0, k == 7, [('w1', ff // 4), ('hlt', b2)], [('ps', b)])
            r2 = ff % 2
            act(rr[r2][:], ps[b][:, 0:256], AF.Relu, [('ps', b)], [('rr', r2)])
            tt('pool' if ff % 4 else 'dve', hid[:, ff, :], rr[r2][:], rr[r2][:], ALU.mult, [('rr', r2)], [('hid', ff)])
        def t2_tile(t2, tb=tb):
            n = tb * 2 + t2
            s2 = n % 2
            S.dma('sp', xl2[s2][:], sc_xlat[n * 128:(n + 1) * 128, :], reads=[('sc_xlat', n)], writes=[('xl2', s2)])
            S.op('pool', lambda e, t_=st3[s2]: e.memset(t_[:, 0:2], 0.0), [], [('st3', s2)])
            bh = [nps(), nps()]
            for half in range(2):
                for k in range(32):
                    mm(ps[bh[half]][:], hid[:, k, t2 * 128:(t2 + 1) * 128], w2[:, k, half * 512:(half + 1) * 512], k == 0, k == 31,
                       [('hid', k), ('w2', k // 4)], [('ps', bh[half])])
                act(junk3[s2][:], ps[bh[half]][:], AF.Square, [('ps', bh[half]), ('st3', s2)], [('junk3', s2), ('st3', s2)], accum_out=st3[s2][:, half:half + 1])
            yield
            tt('dve', st3[s2][:, 2:3], st3[s2][:, 0:1], st3[s2][:, 1:2], ALU.add, [('st3', s2)], [('st3', s2)])
            yield
            act(st3[s2][:, 3:4], st3[s2][:, 2:3], AF.Ln, [('st3', s2)], [('st3', s2)], scale=1.0 / D, bias=EPS)
            act(st3[s2][:, 4:5], st3[s2][:, 3:4], AF.Exp, [('st3', s2)], [('st3', s2)], scale=-0.5)
            yield
            for half in range(2):
                hsl = slice(half * 512, (half + 1) * 512)
                stt('dve', ot[s2][:, hsl], ps[bh[half]][:], st3[s2][:, 4:5], g5bc[:, hsl], ALU.mult, ALU.mult, [('ps', bh[half]), ('st3', s2), 'g5bc'], [('ot', s2)])
            yield
            tt('pool', ot[s2][:], ot[s2][:], xl2[s2][:], ALU.add, [('ot', s2), ('xl2', s2)], [('ot', s2)])
            S.dma('sp', out[n * 128:(n + 1) * 128, :], ot[s2][:], reads=[('ot', s2)], is_out=True)
        interleave([t2_tile(0), t2_tile(1)])
    S.emit()
    return nc


def make_in_maps(inputs):
    carr, coffs, zT, ctab, stab = get_hc()
    g = lambda k: np.ascontiguousarray(np.asarray(inputs[k], dtype=np.float32))
    maps = []
    for b in range(8):
        m = {
            "x": g('x')[b], "ctx": g('ctx')[b],
            "cvec": np.ascontiguousarray(np.stack([g('c')[b], g('c_ctx')], axis=0)),
            "w_ada": g('w_ada')[0], "b_ada": g('b_ada')[0], "norm_w": g('norm_w')[0], "w_in": g('w_in')[0],
            "lb_param": g('lb_param'), "g_norm": g('g_norm')[0], "hy_short_w": g('hy_short_w')[0],
            "hy_short_b": g('hy_short_b')[0], "f_w1": g('f_w1')[0], "f_b1": g('f_b1')[0], "f_w2": g('f_w2')[0],
            "f_b2": g('f_b2')[0], "f_w3": g('f_w3')[0], "f_freq": g('f_freq')[0], "hy_d": g('hy_d')[0],
            "w_pa": g('w_pa')[0], "w_pb": g('w_pb')[0], "w_out": g('w_out')[0], "w_mlp1": g('w_mlp1')[0],
            "w_mlp2": g('w_mlp2')[0], "consts": carr, "zT": zT, "ctab": ctab, "stab": stab,
        }
        maps.append(m)
    return maps


def kernel(**inputs):
    nc = build_program()
    maps = make_in_maps(inputs)
    res = run_bass_kernel_spmd(nc, maps, core_ids=list(range(8)))
    return np.stack([np.asarray(r["out"], dtype=np.float32) for r in res.results], axis=0)
```

```python
import contextlib
import numpy as np
import ml_dtypes
import concourse.bass as bass
import concourse.mybir as mybir
from concourse.bass_utils import run_bass_kernel_spmd

F32 = mybir.dt.float32
BF16 = mybir.dt.bfloat16
AF = mybir.ActivationFunctionType
ALU = mybir.AluOpType

D = 1024
L = 4096
CTX = 256
NT = 32
NTA = 34
EPS = 1e-6
NFFT = 8192
PI = float(np.pi)


class Sched:
    ENG = ('pe', 'act', 'dve', 'pool', 'sp')

    def __init__(self, nc, ndma=12, nswd=6):
        self.nc = nc
        self.ops = {e: [] for e in self.ENG}
        self.cnt = {e: 0 for e in self.ENG}
        self.known = {e: {} for e in self.ENG}
        self.lastw = {}
        self.readers = {}
        self.ndma = ndma + nswd
        self.nhw = ndma
        self.nswd = nswd
        self.dma_n = 0
        self.swd_n = 0
        self.dma_cnt = [0] * (ndma + nswd)
        self.out_toks = []

    def _need(self, eng, tok, waits, same_ok):
        if tok is None:
            return
        sem, val = tok
        if sem == eng and same_ok:
            return
        if self.known[eng].get(sem, 0) >= val:
            return
        if sem == eng:
            assert val <= self.cnt[eng], "same-engine wait on un-inc'd op"
        self.known[eng][sem] = val
        waits.append((sem, val))

    def _deps(self, eng, reads, writes):
        waits = []
        for k in reads:
            self._need(eng, self.lastw.get(k), waits, False)
            if isinstance(k, tuple) and k[0] == 'ps':
                for s, v in self.readers.get(k, {}).items():
                    self._need(eng, (s, v), waits, True)
        for k in writes:
            self._need(eng, self.lastw.get(k), waits, True)
            for s, v in self.readers.get(k, {}).items():
                self._need(eng, (s, v), waits, True)
        return waits

    def _commit(self, tok, reads, writes):
        for k in reads:
            d = self.readers.setdefault(k, {})
            if d.get(tok[0], 0) < tok[1]:
                d[tok[0]] = tok[1]
        for k in writes:
            self.lastw[k] = tok
            self.readers[k] = {}

    def op(self, eng, fn, reads=(), writes=(), inc=True):
        waits = self._deps(eng, reads, writes)
        if inc:
            self.cnt[eng] += 1
            tok = (eng, self.cnt[eng])
        else:
            tok = (eng, self.cnt[eng] + 1)
        self._commit(tok, reads, writes)
        self.ops[eng].append((waits, fn, eng if inc else None, 1))
        return tok

    def dma(self, q, out, in_, reads=(), writes=(), is_out=False, **kw):
        if q == 'pool':
            slot = self.nhw + self.swd_n % self.nswd
            self.swd_n += 1
        else:
            slot = self.dma_n % self.nhw
            self.dma_n += 1
        sem = 'dma%d' % slot
        waits = self._deps(q, reads, writes)
        prev = self.dma_cnt[slot]
        if prev > 0:
            self._need(q, (sem, prev), waits, False)
        self.dma_cnt[slot] += 16
        tok = (sem, self.dma_cnt[slot])
        self._commit(tok, reads, writes)
        self.ops[q].append((waits, lambda e: e.dma_start(out=out, in_=in_, **kw), sem, 16))
        if is_out:
            self.out_toks.append(tok)
        return tok

    def barrier(self):
        for e in self.ENG:
            waits = []
            for f in self.ENG[:4]:
                if f != e and self.cnt[f] > 0:
                    self._need(e, (f, self.cnt[f]), waits, False)
            for s in range(self.ndma):
                if self.dma_cnt[s] > 0:
                    self._need(e, ('dma%d' % s, self.dma_cnt[s]), waits, False)
            self.ops[e].append((waits, None, None, 0))

    def emit(self):
        nc = self.nc
        waits = []
        for tok in self.out_toks:
            self._need('sp', tok, waits, False)
        self.ops['sp'].append((waits, None, None, 0))
        for e in ('pe', 'act', 'dve', 'pool'):
            if self.ops[e]:
                last = [o for o in self.ops[e] if o[1] is not None][-1]
                assert last[2] is not None, "last op on %s must inc" % e
        names = list(self.ENG[:4]) + ['dma%d' % i for i in range(self.ndma)]
        with contextlib.ExitStack() as st:
            sems = {n: st.enter_context(nc.semaphore(n)) for n in names}
            block = st.enter_context(nc.Block())

            def run(ename):
                def body(e):
                    for waits, fn, incsem, incv in self.ops[ename]:
                        for s, v in waits:
                            e.wait_ge(sems[s], v)
                        if fn is not None:
                            ins = fn(e)
                            if incsem is not None:
                                ins.then_inc(sems[incsem], incv)
                return body
            block.tensor(run('pe'))
            block.scalar(run('act'))
            block.vector(run('dve'))
            block.gpsimd(run('pool'))
            block.sync(run('sp'))


def host_consts():
    p = np.arange(128)
    s = p[:, None]
    t = p[None, :]
    c = {}
    c['idf'] = (s == t)
    c['ones'] = np.ones((128, 128))
    c['Mincl_f'] = (s <= t)
    c['Mb_f'] = (s <= t).astype(np.float64) - (s <= 63)
    c['nMb_f'] = -c['Mb_f']
    c['Mkd_f'] = (s > t)
    c['Mincl_b'] = (s >= t)
    c['Mb_b'] = (s >= t).astype(np.float64) - (s >= 64)
    c['nMb_b'] = -c['Mb_b']
    c['Mkd_b'] = (s < t)
    c['mask_f'] = np.tile((s <= t), (1, 4))
    c['mask_b'] = np.tile((s >= t), (1, 4))
    deltas = np.abs(np.linspace(np.log(1e-2) / 1.5, np.log(1e-2) / 0.3, 512, dtype=np.float32)).astype(np.float64)
    c['delta'] = np.tile(deltas[None, :], (128, 1))
    tt = (np.arange(L).reshape(32, 128).T).astype(np.float64)
    c['ntn'] = -(tt / (L - 1))
    c['alt'] = np.tile(((-1.0) ** p)[:, None], (1, 2))
    c['altrow'] = np.tile(((-1.0) ** np.arange(512))[None, :], (128, 1))
    offs = {}
    cols = []
    o = 0
    for k, v in c.items():
        v = np.asarray(v, dtype=np.float32)
        offs[k] = (o, v.shape[1])
        cols.append(v)
        o += v.shape[1]
    arr = np.concatenate(cols, axis=1).astype(np.float32)
    pos = np.arange(L, dtype=np.float32)
    tn = pos / np.float32(L - 1)
    w = np.float32(2.0 * np.pi) * pos / np.float32(L)
    bands = np.linspace(1e-4, 15, 16, dtype=np.float32)
    ang = w[:, None] * bands[None, :]
    z = np.concatenate([tn[:, None], np.cos(ang), -np.sin(ang)], axis=-1).astype(np.float32)
    zT = np.ascontiguousarray(z.T)
    a = (np.arange(L, dtype=np.int64).reshape(32, 128).T)[None, :, :, None]
    b = np.arange(L, dtype=np.int64).reshape(8, 1, 1, 512)
    m = (a * b) % NFFT
    angm = (2.0 * np.pi / NFFT) * m
    ctab = np.cos(angm).astype(ml_dtypes.bfloat16)
    stab = np.sin(angm).astype(ml_dtypes.bfloat16)
    return arr, offs, zT, ctab, stab


_HC = None


def get_hc():
    global _HC
    if _HC is None:
        _HC = host_consts()
    return _HC


def build_program(debug=False, stop_after=None):
    carr, coffs, _, _, _ = get_hc()
    NCC = carr.shape[1]
    nc = bass.Bass("TRN2", target_bir_lowering=False)
    S = Sched(nc)

    def din(name, shape, dt=F32):
        return nc.dram_tensor(name, list(shape), dt, kind="ExternalInput").ap()

    def dscr(name, shape, dt):
        return nc.dram_tensor(name, list(shape), dt, kind=("ExternalOutput" if debug else "Internal")).ap()

    x = din("x", [L, D]); ctx = din("ctx", [CTX, D]); cvec = din("cvec", [2, D])
    w_ada = din("w_ada", [D, 6 * D]); b_ada = din("b_ada", [6 * D]); norm_w = din("norm_w", [4, D])
    w_in = din("w_in", [D, 6144]); lb_param = din("lb_param", [2, 512]); g_norm = din("g_norm", [128])
    hy_short_w = din("hy_short_w", [3, 1536]); hy_short_b = din("hy_short_b", [1536])
    f_w1 = din("f_w1", [33, 64]); f_b1 = din("f_b1", [64]); f_w2 = din("f_w2", [64, 64]); f_b2 = din("f_b2", [64])
    f_w3 = din("f_w3", [64, 1024]); f_freq = din("f_freq", [64]); hy_d = din("hy_d", [512])
    w_pa = din("w_pa", [512, D]); w_pb = din("w_pb", [512, D]); w_out = din("w_out", [D, D])
    w_mlp1 = din("w_mlp1", [D, 4 * D]); w_mlp2 = din("w_mlp2", [4 * D, D])
    consts_d = din("consts", [128, NCC]); zT_d = din("zT", [33, L])
    ctab_d = din("ctab", [8, 128, 32, 512], BF16); stab_d = din("stab", [8, 128, 32, 512], BF16)
    out = nc.dram_tensor("out", [L, D], F32, kind="ExternalOutput").ap()

    sc_tm = dscr("sc_tm", [NTA, 128, 1536], F32)
    sc_q = dscr("sc_q", [4, 128, L], BF16)
    sc_g = dscr("sc_g", [4, 128, L], BF16)
    sc_x0 = dscr("sc_x0", [4, 128, L], BF16)
    sc_u = dscr("sc_u", [128, 32, 512], BF16)
    sc_gate = dscr("sc_gate", [16, 128, L], BF16)
    sc_Y = dscr("sc_Y", [128, 64, 512], BF16)
    sc_yx0 = dscr("sc_yx0", [4, 128, L], BF16)
    sc_a = dscr("sc_a", [4, 128, L], BF16)
    sc_xlat = dscr("sc_xlat", [L, D], F32)
    sc_hl = dscr("sc_hl", [8, 128, L], BF16)

    ARENA = 103 * 1024
    arena = nc.alloc_sbuf_tensor("arena", [128, ARENA], BF16)
    top = [0]

    def sb(shape, dt, name=None):
        rows = shape[0]
        n = int(np.prod(shape[1:]))
        n16 = n * (2 if dt == F32 else 1)
        off = top[0]
        top[0] += (n16 + 15) // 16 * 16
        assert top[0] <= ARENA, ("SBUF arena overflow", name, top[0])
        ap = arena[0:rows, off:off + n16]
        if dt == F32:
            ap = ap.bitcast(F32)
        if len(shape) == 3:
            ap = ap.rearrange("p (a b) -> p a b", b=shape[2])
        return ap

    ps = [nc.alloc_psum_tensor("ps%d" % i, [128, 512], F32) for i in range(8)]
    psi = [0]

    def nps():
        i = psi[0] % 8
        psi[0] += 1
        return i

    def mm(o, lhsT, rhs, start, stop, reads, writes, inc=None):
        if inc is None:
            inc = stop
        S.op('pe', lambda e: e.matmul(o, lhsT, rhs, start=start, stop=stop), reads, writes, inc)

    def tr(o, i, ident, reads, writes, inc=True):
        S.op('pe', lambda e: e.transpose(o, i, ident), reads, writes, inc)

    def act(o, i, func, reads, writes, **kw):
        S.op('act', lambda e: e.activation(out=o, in_=i, func=func, **kw), reads, writes)

    def tt(eng, o, a, b, op, reads, writes):
        S.op(eng, lambda e: e.tensor_tensor(out=o, in0=a, in1=b, op=op), reads, writes)

    def ts(eng, o, a, s1, s2, op0, op1, reads, writes):
        if s2 is None:
            S.op(eng, lambda e: e.tensor_scalar(out=o, in0=a, scalar1=s1, scalar2=None, op0=op0), reads, writes)
        else:
            S.op(eng, lambda e: e.tensor_scalar(out=o, in0=a, scalar1=s1, scalar2=s2, op0=op0, op1=op1), reads, writes)

    def stt(eng, o, a, sc, b, op0, op1, reads, writes):
        S.op(eng, lambda e: e.scalar_tensor_tensor(out=o, in0=a, scalar=sc, in1=b, op0=op0, op1=op1), reads, writes)

    def interleave(gens):
        gens = list(gens)
        while gens:
            for g_ in list(gens):
                try:
                    next(g_)
                except StopIteration:
                    gens.remove(g_)

    def cp(eng, o, i, reads, writes):
        if eng == 'act':
            act(o, i, AF.Copy, reads, writes)
        else:
            S.op(eng, lambda e: e.tensor_copy(o, i), reads, writes)

    cst = sb([128, NCC], F32, "cst")
    S.dma('sp', cst[:], consts_d, writes=['cst'])

    def C(name, rows=128):
        o, n = coffs[name]
        return cst[0:rows, o:o + n]
    idb = sb([128, 128], BF16, "idb"); onesb = sb([128, 128], BF16, "onesb")
    maskf = sb([128, 512], BF16, "maskf"); maskb = sb([128, 512], BF16, "maskb")
    altb = sb([128, 2], BF16, "altb"); altrowb = sb([1, 512], BF16, "altrowb")
    cp('dve', idb[:], C('idf'), ['cst'], ['idb'])
    cp('dve', onesb[:], C('ones'), ['cst'], ['onesb'])
    cp('dve', maskf[:], C('mask_f'), ['cst'], ['maskf'])
    cp('dve', maskb[:], C('mask_b'), ['cst'], ['maskb'])
    cp('dve', altb[:], C('alt'), ['cst'], ['altb'])
    cp('dve', altrowb[:], C('altrow', 1), ['cst'], ['altrowb'])
    ident = C('idf')

    NSL = dict(allow_slow_non_contiguous=True)
    sm = sb([128, 160], F32, "sm")
    so = {}
    o = [0]

    def smcol(name, n):
        so[name] = o[0]
        o[0] += n
        return sm[:, so[name]:so[name] + n]
    S.dma('sp', smcol('bada', 48), b_ada.rearrange("(t p) -> p t", p=128), writes=['sm'], **NSL)
    S.dma('sp', smcol('nw', 32), norm_w.rearrange("r (t p) -> p (r t)", p=128), writes=['sm'], **NSL)
    S.dma('sp', smcol('c', 8), cvec[0, :].rearrange("(t p) -> p t", p=128), writes=['sm'], **NSL)
    S.dma('sp', smcol('cc', 8), cvec[1, :].rearrange("(t p) -> p t", p=128), writes=['sm'], **NSL)
    S.dma('sp', smcol('lb', 8), lb_param.rearrange("r (t p) -> p (r t)", p=128), writes=['sm'], **NSL)
    S.dma('sp', smcol('gn', 1), g_norm.rearrange("(t p) -> p t", p=128), writes=['sm'], **NSL)
    S.dma('sp', smcol('hsw', 36), hy_short_w.rearrange("r (t p) -> p (r t)", p=128), writes=['sm'], **NSL)
    S.dma('sp', smcol('hsb', 12), hy_short_b.rearrange("(t p) -> p t", p=128), writes=['sm'], **NSL)
    S.dma('sp', smcol('hyd', 4), hy_d.rearrange("(t p) -> p t", p=128), writes=['sm'], **NSL)
    sm64 = sb([64, 4], F32, "sm64")
    S.dma('sp', sm64[:, 0:1], f_b1.rearrange("(p t) -> p t", t=1), writes=['sm64'], **NSL)
    S.dma('sp', sm64[:, 1:2], f_b2.rearrange("(p t) -> p t", t=1), writes=['sm64'], **NSL)
    S.dma('sp', sm64[:, 2:3], f_freq.rearrange("(p t) -> p t", t=1), writes=['sm64'], **NSL)

    def SM(name, i=0, n=1):
        return sm[:, so[name] + i: so[name] + i + n]

    cc2 = sb([128, 8, 2], F32, "cc2")
    cp('dve', cc2[:, :, 0], SM('c', 0, 8), ['sm'], ['cc2'])
    cp('dve', cc2[:, :, 1], SM('cc', 0, 8), ['sm'], ['cc2'])
    e2 = sb([128, 16], F32, "e2")
    cc2f = cc2[:].rearrange("p a b -> p (a b)")
    act(e2[:], cc2f, AF.Exp, ['cc2'], ['e2'], scale=-1.0)
    ts('dve', e2[:], e2[:], 1.0, None, ALU.add, None, ['e2'], ['e2'])
    S.op('dve', lambda e: e.reciprocal(e2[:], e2[:]), ['e2'], ['e2'])
    scb = sb([128, 8, 2], BF16, "scb")
    tt('dve', scb[:].rearrange("p a b -> p (a b)"), cc2f, e2[:], ALU.mult, ['cc2', 'e2'], ['scb'])
    mod = sb([128, 48, 2], F32, "mod")
    fv = sb([128, 8, 8], F32, "fv")
    c1f = sb([128, 4], F32, "c1f")
    c1bc = sb([128, 512], F32, "c1bc")
    hydbc = sb([128, 512], F32, "hydbc")
    diag = [sb([128, 128], F32, "diag") for _ in range(2)]
    ynq = sb([1, 512], BF16, "ynq")
    MARK0 = top[0]
    wad = [sb([128, 8, 512], BF16, "wad") for _ in range(2)]
    def mod_dma(pc):
        S.dma('pool', wad[pc % 2][:], w_ada[:, pc * 512:(pc + 1) * 512].rearrange("(k p) n -> p k n", p=128), writes=[('wad', pc % 2)])

    def mod_piece(pc, dma=True):
        wb = wad[pc % 2]
        if dma:
            mod_dma(pc)
        bmp = nps()
        for j in range(4):
            for k in range(8):
                mm(ps[bmp][:, 2 * j:2 * j + 2], wb[:, k, j * 128:(j + 1) * 128], scb[:, k, :], k == 0, k == 7,
                   [('wad', pc % 2), 'scb'], [('ps', bmp)], inc=(k == 7))
        mk_ = ('mod', pc // 2)
        cp('dve', mod[:, pc * 4:pc * 4 + 4, :].rearrange("p a b -> p (a b)"), ps[bmp][:, 0:8], [('ps', bmp)], [mk_])
        for j in range(2):
            tt('dve', mod[:, pc * 4:pc * 4 + 4, j], mod[:, pc * 4:pc * 4 + 4, j], SM('bada', pc * 4, 4), ALU.add, [mk_, 'sm'], [mk_])

    def nwv(r):
        return SM('nw', 8 * r, 8)
    for pc in range(4):
        mod_piece(pc)
    stt('dve', fv[:, 0, :], mod[:, 0:8, 0], 1.0, nwv(0), ALU.add, ALU.mult, [('mod', 0), 'sm'], [('fv', 0)])
    cp('dve', fv[:, 1, :], mod[:, 8:16, 0], [('mod', 1)], [('fv', 0)])
    stt('dve', fv[:, 2, :], mod[:, 0:8, 1], 1.0, nwv(0), ALU.add, ALU.mult, [('mod', 0), 'sm'], [('fv', 0)])
    cp('dve', fv[:, 3, :], mod[:, 8:16, 1], [('mod', 1)], [('fv', 0)])

    def mod_rest_finish():
        tt('dve', fv[:, 4, :], mod[:, 16:24, 0], nwv(1), ALU.mult, [('mod', 2), 'sm'], ['fv'])
        stt('dve', fv[:, 5, :], mod[:, 24:32, 0], 1.0, nwv(2), ALU.add, ALU.mult, [('mod', 3), 'sm'], ['fv'])
        cp('dve', fv[:, 6, :], mod[:, 32:40, 0], [('mod', 4)], ['fv'])
        tt('dve', fv[:, 7, :], mod[:, 40:48, 0], nwv(3), ALU.mult, [('mod', 5), 'sm'], ['fv'])
    tt('dve', c1f[:], SM('lb', 0, 4), SM('lb', 4, 4), ALU.subtract, ['sm'], ['c1f'])
    act(c1f[:], c1f[:], AF.Exp, ['c1f'], ['c1f'])
    act(c1f[:], c1f[:], AF.Ln, ['c1f'], ['c1f'], bias=1.0)
    ts('dve', c1f[:], c1f[:], -1.0, None, ALU.mult, None, ['c1f'], ['c1f'])
    onesf = C('ones')
    di = [0]

    def bcast(dst, col, dkey, rkey):
        d = diag[di[0] % 2]
        dk = ('diag', di[0] % 2)
        di[0] += 1
        ts('dve', d[:], ident, col, None, ALU.mult, None, ['cst', rkey], [dk])
        b = nps()
        mm(ps[b][:, 0:128], onesf, d[:], True, True, ['cst', dk], [('ps', b)])
        cp('act', dst, ps[b][:, 0:128], [('ps', b)], [dkey])
    for k in range(4):
        bcast(c1bc[:, k * 128:(k + 1) * 128], c1f[:, k:k + 1], 'c1bc', 'c1f')
        bcast(hydbc[:, k * 128:(k + 1) * 128], SM('hyd', k, 1), 'hydbc', 'sm')
    hT = sb([128, 8, NTA * 128], BF16, "hT")
    MARK1 = top[0]
    xb = [sb([128, 1024], F32, "xb") for _ in range(3)]
    junk = sb([128, 1024], BF16, "junk")
    xn = [sb([128, 1024], BF16, "xn") for _ in range(3)]
    st = [sb([128, 4], F32, "st") for _ in range(3)]
    wtm = sb([128, 8, 1536], BF16, "wtm")
    for k3 in range(3):
        S.dma('pool', wtm[:, :, k3 * 512:(k3 + 1) * 512], w_in[:, k3 * 512:(k3 + 1) * 512].rearrange("(k p) n -> p k n", p=128), writes=['wtm'])
    tms = [sb([128, 1536], F32, "tms") for _ in range(2)]

    def p1_tile(i):
        s2 = i % 2
        for k3 in range(3):
            b = nps()
            for k in range(8):
                mm(ps[b][:], hT[:, k, i * 128:(i + 1) * 128], wtm[:, k, k3 * 512:(k3 + 1) * 512], k == 0, k == 7,
                   [('hT', i, 0), ('hT', i, 1), 'wtm'], [('ps', b)])
            cp('act' if k3 != 1 else 'dve', tms[s2][:, k3 * 512:(k3 + 1) * 512], ps[b][:], [('ps', b)], [('tms', s2)])
        S.dma('sp', sc_tm[i], tms[s2][:], reads=[('tms', s2)], writes=[('sc_tm', i)])
    def a_s1(i):
        s3 = i % 3
        src = ctx[i * 128:(i + 1) * 128, :] if i < 2 else x[(i - 2) * 128:(i - 1) * 128, :]
        S.dma('sp', xb[s3][:], src, writes=[('xb', s3)])
        S.op('pool', lambda e, t_=st[s3]: e.memset(t_[:, 0:1], 0.0), [], [('st', s3)])
        act(junk[:], xb[s3][:], AF.Square, [('xb', s3), ('st', s3)], ['junk', ('st', s3)], accum_out=st[s3][:, 0:1])
        act(st[s3][:, 1:2], st[s3][:, 0:1], AF.Ln, [('st', s3)], [('st', s3)], scale=1.0 / D, bias=EPS)
        act(st[s3][:, 2:3], st[s3][:, 1:2], AF.Exp, [('st', s3)], [('st', s3)], scale=-0.5)
        ts('dve', xn[s3][:], xb[s3][:], st[s3][:, 2:3], None, ALU.mult, None, [('xb', s3), ('st', s3)], [('xn', s3)])

    def a_s2(i):
        s3 = i % 3
        bb_ = [nps(), nps()]
        pbs = [ps[bb_[0]][:].bitcast(BF16), ps[bb_[1]][:].bitcast(BF16)]
        for k in range(8):
            tr(pbs[k % 2][:, (k // 2) * 128:(k // 2 + 1) * 128], xn[s3][:, k * 128:(k + 1) * 128], idb[:], [('xn', s3), 'idb'], [('ps', bb_[k % 2])], inc=(k >= 6))
        ai = 0 if i >= 2 else 2
        for k in range(8):
            dst = hT[:, k, i * 128:(i + 1) * 128]
            src_ = pbs[k % 2][:, (k // 2) * 128:(k // 2 + 1) * 128]
            if k % 2 == 0:
                act(dst, src_, AF.Identity, [('ps', bb_[0]), ('fv', 0)], [('hT', i, 0)],
                    scale=fv[:, ai, k:k + 1], bias=fv[:, ai + 1, k:k + 1])
            else:
                ts('dve', dst, src_, fv[:, ai, k:k + 1], fv[:, ai + 1, k:k + 1], ALU.mult, ALU.add,
                   [('ps', bb_[1]), ('fv', 0)], [('hT', i, 1)])
    npc = [4]
    for it in range(NTA + 2):
        if it < NTA:
            a_s1(it)
        if 0 <= it - 1 < NTA:
            a_s2(it - 1)
        if 0 <= it - 2 < NTA:
            p1_tile(it - 2)

    if stop_after == 'A':
        for i in range(NTA):
            S.out_toks.append(S.lastw[('sc_tm', i)])
        S.emit()
        return nc

    S.barrier()
    top[0] = MARK1
    wfm = sb([128, 8, 4608], BF16, "wfm")
    pool_order = [('w', 0), ('w', 1), ('w', 2), ('w', 3), ('w', 4), ('m', 4), ('m', 5), ('w', 5), ('w', 6), ('w', 7), ('w', 8)]

    def wfm_dma(k9):
        S.dma('pool', wfm[:, :, k9 * 512:(k9 + 1) * 512], w_in[:, 1536 + k9 * 512:1536 + (k9 + 1) * 512].rearrange("(k p) n -> p k n", p=128), writes=[('wfm', k9)])
    for kind_, idx_ in pool_order:
        if kind_ == 'w':
            wfm_dma(idx_)
        else:
            mod_dma(idx_)
    stg = [sb([128, 512], BF16, "stg") for _ in range(4)]
    sgi = [0]
    cvt = [sb([128, 512], F32, "cvt") for _ in range(7)]
    ustg = [sb([128, 4, 128], BF16, "ustg") for _ in range(4)]
    ub = [sb([128, 512], BF16, "ub") for _ in range(2)]

    def proj(ft_col, tb):
        b = nps()
        for k in range(8):
            mm(ps[b][:], wfm[:, k, ft_col:ft_col + 128], hT[:, k, 256 + tb * 512:256 + (tb + 1) * 512], k == 0, k == 7,
               [('wfm', ft_col // 512)] + [('hT', 2 + tb * 4 + j, e_) for j in range(4) for e_ in range(2)], [('ps', b)])
        return b

    def conv(b, ft, dst, dkey, s4):
        cv = cvt[s4]
        ck = ('cvt', s4)
        p3 = ps[b][:].rearrange("p (r c) -> p r c", c=64)
        c3 = cv[:].rearrange("p (r c) -> p r c", c=64)
        act(cv[:], ps[b][:], AF.Identity, [('ps', b), 'sm'], [ck], scale=SM('hsw', 12 + ft, 1), bias=SM('hsb', ft, 1))
        stt('dve', c3[:, :, 1:64], p3[:, :, 0:63], SM('hsw', ft, 1), c3[:, :, 1:64], ALU.mult, ALU.add, [('ps', b), 'sm', ck], [ck])
        d3 = dst.rearrange("p (r c) -> p r c", c=64)
        stt('dve', d3[:, :, 0:63], p3[:, :, 1:64], SM('hsw', 24 + ft, 1), c3[:, :, 0:63], ALU.mult, ALU.add, [('ps', b), 'sm', ck], [dkey])
        cp('pool', d3[:, :, 63:64], c3[:, :, 63:64], [ck], [dkey])

    pend = [None]
    usi = [0]

    def utrans(u2, j, tb):
        bt = nps()
        pbt = ps[bt][:].bitcast(BF16)
        for q4 in range(4):
            tr(pbt[:, q4 * 128:(q4 + 1) * 128], u2[:, q4 * 128:(q4 + 1) * 128], idb[:], [('ub', (j * 8 + tb) % 2), 'idb'], [('ps', bt)], inc=(q4 == 3))
        u4 = usi[0] % 4
        usi[0] += 1
        cp('act', ustg[u4][:], pbt[:, 0:512].rearrange("p (a b) -> p a b", b=128), [('ps', bt)], [('ustg', u4)])
        S.dma('sp', sc_u[:, tb * 4:(tb + 1) * 4, j * 128:(j + 1) * 128], ustg[u4][:], reads=[('ustg', u4)], writes=[('sc_u', tb, j)])

    def nstg():
        sg = sgi[0] % 4
        sgi[0] += 1
        return sg
    for ft in range(8):
        for tb in range(8):
            tsl = slice(tb * 512, (tb + 1) * 512)
            b = proj(ft * 128, tb)
            sg = nstg()
            act(stg[sg][:], ps[b][:], AF.Silu, [('ps', b)], [('stg', sg)])
            dstd = (sc_q if ft < 4 else sc_g)[ft % 4, :, tsl]
            S.dma('sp', dstd, stg[sg][:], reads=[('stg', sg)], writes=[('sc_qg', ft, tb)])
    for j in range(4):
        for tb in range(8):
            tsl = slice(tb * 512, (tb + 1) * 512)
            b = proj(1024 + j * 128, tb)
            sg = nstg()
            conv(b, j, stg[sg][:], ('stg', sg), 0)
            S.dma('sp', sc_x0[j, :, tsl], stg[sg][:], reads=[('stg', sg)], writes=[('sc_x0', j, tb)])
    mod_piece(4, dma=False)
    mod_piece(5, dma=False)
    mod_dma(6)
    mod_dma(7)
    for j in range(4):
        for tb in range(8):
            pp = (j * 8 + tb) % 2
            b1 = proj(1024 + (4 + j) * 128, tb)
            conv(b1, 4 + j, cvt[3 + pp][:], ('cvt', 3 + pp), 1)
            b2 = proj(1024 + (8 + j) * 128, tb)
            conv(b2, 8 + j, cvt[5 + pp][:], ('cvt', 5 + pp), 2)
            u2 = ub[pp]
            tt('pool', u2[:], cvt[3 + pp][:], cvt[5 + pp][:], ALU.mult, [('cvt', 3 + pp), ('cvt', 5 + pp)], [('ub', pp)])
            if pend[0] is not None:
                pend[0]()
            pend[0] = (lambda u2_=u2, j_=j, tb_=tb: utrans(u2_, j_, tb_))
        if j == 1:
            mod_piece(6, dma=False)
            mod_piece(7, dma=False)
            mod_dma(8)
            mod_dma(9)
        if j == 3:
            mod_piece(8, dma=False)
            mod_piece(9, dma=False)
            mod_dma(10)
            mod_dma(11)
    pend[0]()
    for ft in range(16):
        for tb in range(8):
            tsl = slice(tb * 512, (tb + 1) * 512)
            b = proj(2560 + ft * 128, tb)
            sg = nstg()
            act(stg[sg][:], ps[b][:], AF.Sigmoid, [('ps', b)], [('stg', sg)])
            S.dma('sp', sc_gate[ft, :, tsl], stg[sg][:], reads=[('stg', sg)], writes=[('sc_gate', ft, tb)])
        if ft == 4:
            mod_piece(10, dma=False)
            mod_piece(11, dma=False)
    mod_rest_finish()
    if stop_after == 'P':
        for k_, v_ in list(S.lastw.items()):
            if isinstance(k_, tuple) and str(k_[0]).startswith('sc_'):
                S.out_toks.append(v_)
        S.emit()
        return nc
    S.barrier()
    top[0] = MARK0
    hcat = sb([128, 32, 1024], BF16, "hcat")
    MARKH = top[0]
    zt = sb([33, L], F32, "zt"); w1s = sb([33, 64], F32, "w1s"); w2s = sb([64, 64], F32, "w2s"); w3s = sb([64, 1024], F32, "w3s")
    h1 = sb([64, L], F32, "h1"); h2 = sb([64, L], F32, "h2"); wtmp = sb([64, 512], F32, "wtmp")
    S.dma('sp', zt[:], zT_d, writes=['zt'])
    S.dma('sp', w1s[:], f_w1, writes=['fw'])
    S.dma('sp', w2s[:], f_w2, writes=['fw'])
    S.dma('sp', w3s[:], f_w3, writes=['fw'])

    def sinlayer(src, w, kdim, bcol, dst, skey, dkey):
        for blk in range(8):
            sl = slice(blk * 512, (blk + 1) * 512)
            b = nps()
            mm(ps[b][0:64, :], w[0:kdim, :], src[0:kdim, sl], True, True, [skey, 'fw'], [('ps', b)])
            ts('dve', dst[:, sl], ps[b][0:64, :], sm64[:, bcol:bcol + 1], sm64[:, 2:3], ALU.add, ALU.mult, [('ps', b), 'sm64'], [dkey])
            for _ in range(1):
                S.op('dve', lambda e, d_=dst[:, sl]: e.tensor_single_scalar(out=wtmp[:], in_=d_, scalar=PI, op=ALU.is_gt), [dkey], ['wtmp'])
                stt('dve', dst[:, sl], wtmp[:], -2 * PI, dst[:, sl], ALU.mult, ALU.add, ['wtmp', dkey], [dkey])
                S.op('dve', lambda e, d_=dst[:, sl]: e.tensor_single_scalar(out=wtmp[:], in_=d_, scalar=-PI, op=ALU.is_lt), [dkey], ['wtmp'])
                stt('dve', dst[:, sl], wtmp[:], 2 * PI, dst[:, sl], ALU.mult, ALU.add, ['wtmp', dkey], [dkey])
            act(dst[:, sl], dst[:, sl], AF.Sin, [dkey], [dkey])
    sinlayer(zt, w1s, 33, 0, h1, 'zt', 'h1')
    sinlayer(h1, w2s, 64, 1, h2, 'h1', 'h2')
    dect = [sb([128, 512], F32, "dect") for _ in range(2)]
    ftmp = sb([128, 512], F32, "ftmp")
    ntn = C('ntn')
    for chunk in range(32):
        dt_ = dect[chunk % 2]
        dk = ('dect', chunk % 2)
        act(dt_[:], C('delta'), AF.Exp, ['cst'], [dk], scale=ntn[:, chunk:chunk + 1])
        for half in range(2):
            b = nps()
            mm(ps[b][:], h2[:, chunk * 128:(chunk + 1) * 128], w3s[:, half * 512:(half + 1) * 512], True, True, ['h2', 'fw'], [('ps', b)])
            dst = hcat[:, chunk, half * 512:(half + 1) * 512]
            if chunk == 0:
                tt('dve', ftmp[:], ps[b][:], dt_[:], ALU.mult, [('ps', b), dk], ['ftmp'])
                if half == 0:
                    tt('dve', ftmp[0:1, :], ftmp[0:1, :], hydbc[0:1, :], ALU.add, ['ftmp', 'hydbc'], ['ftmp'])
                else:
                    S.op('dve', lambda e: e.memset(ftmp[0:1, :], 0.0), ['ftmp'], ['ftmp'])
                cp('dve', dst, ftmp[:], ['ftmp'], ['hcat'])
            else:
                tt('dve', dst, ps[b][:], dt_[:], ALU.mult, [('ps', b), dk], ['hcat'])

    S.barrier()
    top[0] = MARKH
    u_tm = sb([128, 32, 512], BF16, "u_tm")
    for q8 in range(8):
        S.dma('sp', u_tm[:, q8 * 4:(q8 + 1) * 4, :], sc_u[:, q8 * 4:(q8 + 1) * 4, :], reads=[('sc_u', q8, j_) for j_ in range(4)], writes=['u_tm'])
    tabs = [sb([128, 32, 256], BF16, "tabf") for _ in range(2)]
    Ast = sb([128, 2, 512], F32, "Ast"); Pst = sb([128, 2, 512], F32, "Pst")
    tq = [sb([128, 512], F32, "tq") for _ in range(6)]
    yst = [sb([128, 512], BF16, "yst") for _ in range(4)]
    SC = 2.0 / NFFT
    ysi = [0]

    def ynext():
        i_ = ysi[0] % 4
        ysi[0] += 1
        return yst[i_], ('yst', i_)
    for fb in range(16):
        tbi, hf_ = fb // 2, fb % 2
        for cs in range(2):
            tab = tabs[cs]
            tk = ('tab', cs)
            S.dma('sp', tab[:], (ctab_d if cs == 0 else stab_d)[tbi][:, :, hf_ * 256:(hf_ + 1) * 256], writes=[tk])
            for m in range(2):
                bb = [nps(), nps(), nps()]
                for chunk in range(32):
                    lhsT = tab[:, chunk, m * 128:(m + 1) * 128]
                    mm(ps[bb[0]][:], lhsT, u_tm[:, chunk, :], chunk == 0, chunk == 31, [tk, 'u_tm'], [('ps', bb[0])])
                    mm(ps[bb[1]][:], lhsT, hcat[:, chunk, 0:512], chunk == 0, chunk == 31, [tk, 'hcat'], [('ps', bb[1])])
                    mm(ps[bb[2]][:], lhsT, hcat[:, chunk, 512:1024], chunk == 0, chunk == 31, [tk, 'hcat'], [('ps', bb[2])])
                ft = fb * 2 + m
                if cs == 0:
                    cp('act', Ast[:, m, :], ps[bb[0]][:], [('ps', bb[0])], [('Ast', m)])
                    cp('act', tq[0][:], ps[bb[2]][:], [('ps', bb[2])], [('tq', 0)])
                    tt('dve', Pst[:, m, :], ps[bb[1]][:], tq[0][:], ALU.add, [('ps', bb[1]), ('tq', 0)], [('Pst', m)])
                else:
                    cp('act', tq[1][:], ps[bb[2]][:], [('ps', bb[2])], [('tq', 1)])
                    tt('dve', tq[2][:], ps[bb[1]][:], tq[1][:], ALU.subtract, [('ps', bb[1]), ('tq', 1)], [('tq', 2)])
                    tt('pool', tq[3][:], Ast[:, m, :], Pst[:, m, :], ALU.mult, [('Ast', m), ('Pst', m)], [('tq', 3)])
                    stt('dve', tq[4][:], ps[bb[0]][:], SC, tq[2][:], ALU.mult, ALU.mult, [('ps', bb[0]), ('tq', 2)], [('tq', 4)])
                    yr, yrk = ynext()
                    stt('dve', yr[:], tq[3][:], SC, tq[4][:], ALU.mult, ALU.subtract, [('tq', 3), ('tq', 4)], [yrk])
                    if ft == 0:
                        ts('dve', yr[0:1, :], yr[0:1, :], 0.5, None, ALU.mult, None, [yrk], [yrk])
                    S.dma('sp', sc_Y[:, ft, :], yr[:], reads=[yrk], writes=[('sc_Y', ft)])
                    tt('pool', tq[3][:], Ast[:, m, :], tq[2][:], ALU.mult, [('Ast', m), ('tq', 2)], [('tq', 3)])
                    stt('dve', tq[5][:], ps[bb[0]][:], SC, Pst[:, m, :], ALU.mult, ALU.mult, [('ps', bb[0]), ('Pst', m)], [('tq', 5)])
                    yq, yqk = ynext()
                    stt('dve', yq[:], tq[3][:], SC, tq[5][:], ALU.mult, ALU.add, [('tq', 3), ('tq', 5)], [yqk])
                    S.dma('sp', sc_Y[:, 32 + ft, :], yq[:], reads=[yqk], writes=[('sc_Y', 32 + ft)])
    bn = [nps(), nps(), nps()]
    for chunk in range(32):
        mm(ps[bn[0]][0:1, :], altb[:, 0:1], u_tm[:, chunk, :], chunk == 0, chunk == 31, ['altb', 'u_tm'], [('ps', bn[0])])
        mm(ps[bn[1]][0:1, :], altb[:, 0:1], hcat[:, chunk, 0:512], chunk == 0, chunk == 31, ['altb', 'hcat'], [('ps', bn[1])])
        mm(ps[bn[2]][0:1, :], altb[:, 0:1], hcat[:, chunk, 512:1024], chunk == 0, chunk == 31, ['altb', 'hcat'], [('ps', bn[2])])
    cp('act', tq[0][0:1, :], ps[bn[2]][0:1, :], [('ps', bn[2])], [('tq', 0)])
    tt('dve', tq[1][0:1, :], ps[bn[1]][0:1, :], tq[0][0:1, :], ALU.add, [('ps', bn[1]), ('tq', 0)], [('tq', 1)])
    stt('dve', ynq[:], ps[bn[0]][0:1, :], 1.0 / NFFT, tq[1][0:1, :], ALU.mult, ALU.mult, [('ps', bn[0]), ('tq', 1)], ['ynq'])

    S.barrier()
    top[0] = MARK0
    Yt = sb([128, 64, 512], BF16, "Yt")
    for q8 in range(8):
        S.dma('sp', Yt[:, q8 * 8:(q8 + 1) * 8, :], sc_Y[:, q8 * 8:(q8 + 1) * 8, :],
              reads=[('sc_Y', f_) for f_ in range(q8 * 8, q8 * 8 + 8)], writes=[('Yt', q8)])
    tabi = [sb([128, 32, 512], BF16, "tabi") for _ in range(2)]
    x0s = [sb([128, 512], BF16, "x0s") for _ in range(4)]
    yo = [sb([128, 512], BF16, "yo") for _ in range(2)]
    S.dma('sp', tabi[0][:], ctab_d[0], writes=[('tabi', 0)])
    S.dma('sp', tabi[1][:], stab_d[0], writes=[('tabi', 1)])
    for tb in range(8):
        tsl = slice(tb * 512, (tb + 1) * 512)
        for ct in range(4):
            S.dma('act', x0s[ct][:], sc_x0[ct, :, tsl], reads=[('sc_x0', ct, tb)], writes=[('x0s', ct)])
        banks = [nps() for _ in range(4)]
        for cs in range(2):
            for ct in range(4):
                for chunk in range(32):
                    mm(ps[banks[ct]][:], Yt[:, cs * 32 + chunk, ct * 128:(ct + 1) * 128], tabi[cs][:, chunk, :],
                       cs == 0 and chunk == 0, False, [('Yt', (cs * 32 + chunk) // 8), ('tabi', cs)], [('ps', banks[ct])], inc=(chunk == 31))
            if tb + 1 < 8:
                S.dma('sp', tabi[cs][:], (ctab_d if cs == 0 else stab_d)[tb + 1], writes=[('tabi', cs)])
        for ct in range(4):
            mm(ps[banks[ct]][:], ynq[0:1, ct * 128:(ct + 1) * 128], altrowb[0:1, :], False, True, ['ynq', 'altrowb'], [('ps', banks[ct])], inc=True)
            s2 = ct % 2
            tt('dve', yo[s2][:], ps[banks[ct]][:], x0s[ct][:], ALU.mult, [('ps', banks[ct]), ('x0s', ct)], [('yo', s2)])
            S.dma('act', sc_yx0[ct, :, tsl], yo[s2][:], reads=[('yo', s2)], writes=[('sc_yx0', ct, tb)])
    if stop_after == 'H':
        for k_, v_ in list(S.lastw.items()):
            if isinstance(k_, tuple) and str(k_[0]).startswith('sc_'):
                S.out_toks.append(v_)
        S.emit()
        return nc

    S.barrier()
    top[0] = MARK0
    tmb = [sb([128, 1536], F32, "tmb") for _ in range(4)]
    vb = [sb([128, 512], BF16, "vb") for _ in range(4)]
    ebuf = [sb([128, 512], F32, "ebuf") for _ in range(2)]
    lp = [sb([128, 512], F32, "lp") for _ in range(2)]
    kbuf = [sb([128, 512], F32, "kbuf") for _ in range(2)]
    lnk = [sb([128, 512], F32, "lnk") for _ in range(2)]
    logf = [sb([128, 512], F32, "logf") for _ in range(2)]
    Ea = [[sb([128, 4, 128], BF16, "Ea") for _ in range(2)] for _ in range(2)]
    Eb = [[sb([128, 4, 128], BF16, "Eb") for _ in range(2)] for _ in range(2)]
    keT = [[sb([128, 512], BF16, "keT") for _ in range(2)] for _ in range(2)]
    kd = [[sb([128, 512], BF16, "kd") for _ in range(2)] for _ in range(2)]
    dec = [[sb([128, 4], F32, "dec") for _ in range(2)] for _ in range(2)]
    nam = [[sb([128, 4], F32, "nam") for _ in range(2)] for _ in range(2)]
    Sf = [sb([128, 4, 128], F32, "Sf") for _ in range(2)]
    Sb16 = [sb([128, 4, 128], BF16, "Sb16") for _ in range(2)]
    sbs = sb([128, 32, 512], BF16, "sbs")
    qtb = [sb([128, 4, 128], BF16, "qtb") for _ in range(3)]
    gtb = [sb([128, 4, 128], BF16, "gtb") for _ in range(3)]
    qa = [sb([128, 4, 128], BF16, "qa") for _ in range(2)]
    qe = [sb([128, 4, 128], BF16, "qe") for _ in range(2)]
    scT = [sb([128, 512], BF16, "scT") for _ in range(2)]
    sqb = sb([128, 512], BF16, "sqb"); rs = sb([128, 512], F32, "rs"); otf = sb([128, 512], F32, "otf")
    atb = [sb([128, 512], BF16, "atb") for _ in range(2)]
    flat = lambda a_: a_.rearrange("p a b -> p (a b)")
    for d in range(2):
        S.op('pool', lambda e, t_=Sf[d]: e.memset(flat(t_), 0.0), [], [('Sf', d)])
        S.op('pool', lambda e, t_=Sb16[d]: e.memset(flat(t_), 0.0), [], [('Sb', d)])

    gate_alloc = [nps]

    def gate(d, tm, tmk, sl, lite=False, tix=None, banks=None):
        zz = tm[:, 512 * (1 + d):512 * (2 + d)]
        if tix is None:
            tix = d
        eb_, lp_, kb_ = ebuf[tix], lp[tix], kbuf[tix]
        act(eb_[:], zz, AF.Exp, [tmk], [('ebuf', tix)])
        act(lp_[:], eb_[:], AF.Ln, [('ebuf', tix)], [('lp', tix)], bias=1.0)
        yield
        stt('dve', lnk[tix][:], lp_[:], -1.0, c1bc[:], ALU.mult, ALU.add, [('lp', tix), 'c1bc'], [('lnk', tix)])
        yield
        act(kb_[:], lnk[tix][:], AF.Exp, [('lnk', tix)], [('kbuf', tix)])
        act(logf[tix][:], kb_[:], AF.Ln, [('kbuf', tix)], [('logf', tix)], scale=-1.0, bias=1.0)
        yield
        sfx = 'f' if d == 0 else 'b'
        Mincl = C('Mincl_' + sfx)
        nMb = C('nMb_' + sfx)
        Mkd = C('Mkd_' + sfx)
        bkd = gate_alloc[0]() if banks is None else banks[0]
        mm(ps[bkd][:], Mkd, logf[tix][:], True, True, [('logf', tix), 'cst'], [('ps', bkd)], inc=True)
        if lite:
            bt_ = gate_alloc[0]()
            for h in range(4):
                mm(ps[bt_][:, h:h + 1], logf[tix][:, h * 128:(h + 1) * 128], onesf[:, 0:1], True, True, [('logf', tix), 'cst'], [('ps', bt_)], inc=(h == 3))
        else:
            ba = gate_alloc[0]() if banks is None else banks[1]
            for h in range(4):
                mm(ps[ba][:, h * 128:(h + 1) * 128], logf[tix][:, h * 128:(h + 1) * 128], Mincl, True, True,
                   [('logf', tix), 'cst'], [('ps', ba)], inc=(h == 3))
        yield
        act(eb_[:], ps[bkd][:], AF.Exp, [('ps', bkd)], [('ebuf', tix)])
        if lite:
            act(dec[d][sl][:], ps[bt_][:, 0:4], AF.Exp, [('ps', bt_)], [('dec', d, sl)])
            yield
            tt('dve', kd[d][sl][:], eb_[:], kb_[:], ALU.mult, [('ebuf', tix), ('kbuf', tix)], [('kd', d, sl)])
            return
        bke = bkd
        for h in range(4):
            mm(ps[bke][:, h * 128:(h + 1) * 128], lnk[tix][:, h * 128:(h + 1) * 128], ident, True, False, [('lnk', tix), 'cst'], [('ps', bke)], inc=False)
            mm(ps[bke][:, h * 128:(h + 1) * 128], logf[tix][:, h * 128:(h + 1) * 128], nMb, False, True, [('logf', tix), 'cst'], [('ps', bke)], inc=(h == 3))
        col = 127 if d == 0 else 0
        cm = 63 if d == 0 else 64
        act(flat(Ea[d][sl]), ps[ba][:], AF.Exp, [('ps', ba)], [('Ea', d, sl)])
        for h in range(4):
            act(dec[d][sl][:, h:h + 1], ps[ba][:, h * 128 + col:h * 128 + col + 1], AF.Exp, [('ps', ba)], [('dec', d, sl)])
            act(nam[d][sl][:, h:h + 1], ps[ba][:, h * 128 + cm:h * 128 + cm + 1], AF.Exp, [('ps', ba)], [('nam', d, sl)], scale=-1.0)
        yield
        act(keT[d][sl][:], ps[bke][:], AF.Exp, [('ps', bke)], [('keT', d, sl)])
        tt('dve', kd[d][sl][:], eb_[:], kb_[:], ALU.mult, [('ebuf', tix), ('kbuf', tix)], [('kd', d, sl)])

    def update(d, vt, vk, sl, bank=None):
        b = nps() if bank is None else bank
        for h in range(4):
            mm(ps[b][:, h * 128:(h + 1) * 128], kd[d][sl][:, h * 128:(h + 1) * 128], vt[:, h * 128:(h + 1) * 128], True, True,
               [('kd', d, sl), vk], [('ps', b)], inc=(h == 3))
        for h in range(4):
            stt('dve', Sf[d][:, h, :], Sf[d][:, h, :], dec[d][sl][:, h:h + 1], ps[b][:, h * 128:(h + 1) * 128], ALU.mult, ALU.add,
                [('Sf', d), ('dec', d, sl), ('ps', b)], [('Sf', d)])
        cp('act', flat(Sb16[d]), flat(Sf[d]), [('Sf', d)], [('Sb', d)])

    ldi = [0]
    ldmap = {}

    def issue_load(key, i):
        s3 = ldi[0] % 4
        ldi[0] += 1
        S.dma('sp', tmb[s3][:], sc_tm[i], reads=[('sc_tm', i)], writes=[('tmb', s3)])
        ldmap[key] = s3

    def load_tm(key):
        s3 = ldmap[key]
        cp('dve', vb[s3][:], tmb[s3][:, 0:512], [('tmb', s3)], [('vb', s3)])
        return tmb[s3], ('tmb', s3), vb[s3], ('vb', s3)
    gi = [0]

    steps = [(0, 0), (0, 1), (1, 1), (1, 0)] + [(1, n_ + 2) for n_ in range(31, -1, -1)]

    info = {}

    def lite_gen(k_):
        d_, i_ = steps[k_]
        tm, tmk, vt, vk = load_tm(('g1', k_))
        info[k_] = (d_, vt, vk, k_ % 2, i_)
        return gate(d_, tm, tmk, k_ % 2, lite=True, tix=k_ % 2)
    for kk in (0, 1):
        issue_load(('g1', kk), steps[kk][1])
    for k0 in range(0, len(steps), 2):
        for kk in (k0 + 2, k0 + 3):
            if kk < len(steps):
                issue_load(('g1', kk), steps[kk][1])
        interleave([lite_gen(k0), lite_gen(k0 + 1)])
        for kk in (k0, k0 + 1):
            d_, vt, vk, sl, i_ = info[kk]
            if d_ == 1 and i_ >= 2:
                cp('act', sbs[:, i_ - 2, :], flat(Sb16[1]), [('Sb', 1)], [('sbs', i_ - 2)])
            update(d_, vt, vk, sl)
    loaded = {}
    qslot = {}

    def issue_g2(n):
        issue_load(('g2', n), n + 2)
        q3 = n % 3
        qslot[n] = q3
        tsl = slice(n * 128, (n + 1) * 128)
        S.dma('sp', qtb[q3][:], sc_q[:, :, tsl].rearrange("h p t -> p h t"), reads=[('sc_qg', h_, n // 4) for h_ in range(4)], writes=[('qtb', q3)])
        S.dma('sp', gtb[q3][:], sc_g[:, :, tsl].rearrange("h p t -> p h t"), reads=[('sc_qg', 4 + h_, n // 4) for h_ in range(4)], writes=[('gtb', q3)])

    def prep_gens(n):
        if n + 1 < 32:
            issue_g2(n + 1)
        tm, tmk, vt, vk = load_tm(('g2', n))
        loaded[n] = (vt, vk)
        return [gate(0, tm, tmk, n % 2, banks=(0, 1)), gate(1, tm, tmk, n % 2, banks=(2, 3))]

    def core(n):
        vt, vk = loaded[n]
        q2 = n % 2
        q3 = qslot[n]
        tsl = slice(n * 128, (n + 1) * 128)
        bsl = []
        for d in range(2):
            tt('dve', flat(qa[d]), flat(qtb[q3]), flat(Ea[d][q2]), ALU.mult, [('qtb', q3), ('Ea', d, q2)], [('qa', d)])
            for h in range(4):
                ts('dve', qe[d][:, h, :], qa[d][:, h, :], nam[d][q2][:, h:h + 1], None, ALU.mult, None, [('qa', d), ('nam', d, q2)], [('qe', d)])
        yield
        for d in range(2):
            bs = 4 + d
            bsl.append(bs)
            for h in range(4):
                mm(ps[bs][:, h * 128:(h + 1) * 128], keT[d][q2][:, h * 128:(h + 1) * 128], qe[d][:, h, :], True, True,
                   [('keT', d, q2), ('qe', d)], [('ps', bs)], inc=(h == 3))
        yield
        for d in range(2):
            mk = maskf if d == 0 else maskb
            tt('dve', scT[d][:], ps[bsl[d]][:], mk[:], ALU.mult, [('ps', bsl[d]), 'maskf', 'maskb'], [('scT', d)])
        yield
        bo = 6
        for h in range(4):
            o_ = ps[bo][:, h * 128:(h + 1) * 128]
            hs = slice(h * 128, (h + 1) * 128)
            mm(o_, vt[:, hs], scT[0][:, hs], True, False, [vk, ('scT', 0)], [('ps', bo)], inc=False)
            mm(o_, vt[:, hs], scT[1][:, hs], False, False, [vk, ('scT', 1)], [('ps', bo)], inc=False)
            mm(o_, Sb16[0][:, h, :], qa[0][:, h, :], False, False, [('Sb', 0), ('qa', 0)], [('ps', bo)], inc=False)
            mm(o_, sbs[:, n, hs], qa[1][:, h, :], False, True, [('sbs', n), ('qa', 1)], [('ps', bo)], inc=(h == 3))
        update(0, vt, vk, q2, bank=7)
        yield
        act(sqb[:], ps[bo][:], AF.Square, [('ps', bo)], ['sqb'])
        yield
        bss = 4
        mm(ps[bss][:], onesb[:], sqb[:], True, True, ['onesb', 'sqb'], [('ps', bss)])
        yield
        act(rs[:], ps[bss][:], AF.Ln, [('ps', bss)], ['rs'], scale=1.0 / 128, bias=EPS)
        act(rs[:], rs[:], AF.Exp, ['rs'], ['rs'], scale=-0.5)
        yield
        tt('dve', otf[:], ps[bo][:], rs[:], ALU.mult, [('ps', bo), 'rs'], ['otf'])
        yield
        stt('dve', atb[q2][:], otf[:], SM('gn'), flat(gtb[q3]), ALU.mult, ALU.mult, ['otf', 'sm', ('gtb', q3)], [('atb', q2)])
        S.dma('sp', sc_a[:, :, tsl].rearrange("h p t -> p h t"), atb[q2][:].rearrange("p (h t) -> p h t", t=128),
              reads=[('atb', q2)], writes=[('sc_a', n)])
    g2i = [0]

    def gps():
        i_ = g2i[0] % 4
        g2i[0] += 1
        return i_

    def chain(gs):
        for g_ in gs:
            yield from g_
    gate_alloc[0] = gps
    issue_g2(0)
    interleave(prep_gens(0))
    for n in range(32):
        gens = [core(n)]
        if n + 1 < 32:
            gens = prep_gens(n + 1) + gens
        interleave(gens)
    gate_alloc[0] = nps
    if stop_after == 'G':
        for k_, v_ in list(S.lastw.items()):
            if isinstance(k_, tuple) and str(k_[0]).startswith('sc_'):
                S.out_toks.append(v_)
        S.emit()
        return nc

    S.barrier()
    top[0] = MARK0
    bcv = sb([128, 4, 1024], F32, "bcv")
    for vi in range(4):
        for k in range(8):
            bcast(bcv[:, vi, k * 128:(k + 1) * 128], fv[:, 4 + vi, k:k + 1], 'bcv', 'fv')
    wpa = sb([128, 4, 1024], BF16, "wpa"); wpb = sb([128, 4, 1024], BF16, "wpb"); wout = sb([128, 8, 1024], BF16, "wout")
    S.dma('pool', wpa[:], w_pa.rearrange("(k p) n -> p k n", p=128), writes=['wpa'])
    S.dma('pool', wpb[:], w_pb.rearrange("(k p) n -> p k n", p=128), writes=['wpb'])
    for k2 in range(2):
        S.dma('pool', wout[:, :, k2 * 512:(k2 + 1) * 512], w_out[:, k2 * 512:(k2 + 1) * 512].rearrange("(k p) n -> p k n", p=128), writes=['wout'])
    aTb = [sb([128, 4, 512], BF16, "aTb") for _ in range(2)]
    yxb = [sb([128, 4, 512], BF16, "yxb") for _ in range(2)]
    ggb = [sb([128, 16, 512], BF16, "ggb") for _ in range(2)]
    merged = sb([128, 8, 512], BF16, "merged")
    m1 = [sb([128, 512], F32, "m1") for _ in range(2)]
    xt = [sb([128, 1024], F32, "xt") for _ in range(2)]
    xl = [sb([128, 1024], F32, "xl") for _ in range(2)]
    tmpf = [sb([128, 1024], F32, "tmpf") for _ in range(2)]
    hlb = [sb([128, 1024], BF16, "hlb") for _ in range(2)]
    hls = [sb([128, 8, 512], BF16, "hls") for _ in range(2)]
    junk2 = [sb([128, 1024], BF16, "junk2") for _ in range(2)]

    def interleave(gens):
        gens = list(gens)
        while gens:
            for g_ in list(gens):
                try:
                    next(g_)
                except StopIteration:
                    gens.remove(g_)
    st2 = [sb([128, 8], F32, "st2") for _ in range(2)]
    def t1_loads(tb):
        tsl = slice(tb * 512, (tb + 1) * 512)
        b2 = tb % 2
        S.dma('sp', aTb[b2][:], sc_a[:, :, tsl].rearrange("h p t -> p h t"), reads=[('sc_a', tb * 4 + j_) for j_ in range(4)], writes=[('aTb', b2)])
        S.dma('sp', yxb[b2][:], sc_yx0[:, :, tsl].rearrange("h p t -> p h t"), reads=[('sc_yx0', c_, tb) for c_ in range(4)], writes=[('yxb', b2)])
        S.dma('sp', ggb[b2][:], sc_gate[:, :, tsl].rearrange("f p t -> p f t"), reads=[('sc_gate', f_, tb) for f_ in range(16)], writes=[('ggb', b2)])
    for tb in range(8):
        tsl = slice(tb * 512, (tb + 1) * 512)
        b2 = tb % 2
        if tb == 0:
            t1_loads(0)
        if tb + 1 < 8:
            t1_loads(tb + 1)
        for dm in range(8):
            bA = nps()
            for k in range(4):
                mm(ps[bA][:], wpa[:, k, dm * 128:(dm + 1) * 128], aTb[b2][:, k, :], k == 0, k == 3, ['wpa', ('aTb', b2)], [('ps', bA)])
            bB = nps()
            for k in range(4):
                mm(ps[bB][:], wpb[:, k, dm * 128:(dm + 1) * 128], yxb[b2][:, k, :], k == 0, k == 3, ['wpb', ('yxb', b2)], [('ps', bB)])
            tt('dve', m1[0][:], ps[bA][:], ggb[b2][:, dm, :], ALU.mult, [('ps', bA), ('ggb', b2)], [('m1', 0)])
            tt('dve', m1[1][:], ps[bB][:], ggb[b2][:, 8 + dm, :], ALU.mult, [('ps', bB), ('ggb', b2)], [('m1', 1)])
            tt('pool', merged[:, dm, :], m1[0][:], m1[1][:], ALU.add, [('m1', 0), ('m1', 1)], [('merged', dm)])
        hs_ = hls[b2]

        def t1_tile(t4, tb=tb, hs_=hs_, b2=b2):
            n = tb * 4 + t4
            s2 = n % 2
            tf = tmpf[s2]
            tfk = ('tmpf', s2)
            S.dma('sp', xt[s2][:], x[n * 128:(n + 1) * 128, :], writes=[('xt', s2)])
            S.op('pool', lambda e, t_=st2[s2]: e.memset(t_[:, 0:2], 0.0), [], [('st2', s2)])
            bh = [nps(), nps()]
            for half in range(2):
                for k in range(8):
                    mm(ps[bh[half]][:], merged[:, k, t4 * 128:(t4 + 1) * 128], wout[:, k, half * 512:(half + 1) * 512], k == 0, k == 7,
                       [('merged', k), 'wout'], [('ps', bh[half])])
                act(junk2[s2][:, 0:512], ps[bh[half]][:], AF.Square, [('ps', bh[half]), ('st2', s2)], [('junk2', s2), ('st2', s2)], accum_out=st2[s2][:, half:half + 1])
            yield
            tt('dve', st2[s2][:, 2:3], st2[s2][:, 0:1], st2[s2][:, 1:2], ALU.add, [('st2', s2)], [('st2', s2)])
            yield
            act(st2[s2][:, 3:4], st2[s2][:, 2:3], AF.Ln, [('st2', s2)], [('st2', s2)], scale=1.0 / D, bias=EPS)
            act(st2[s2][:, 4:5], st2[s2][:, 3:4], AF.Exp, [('st2', s2)], [('st2', s2)], scale=-0.5)
            yield
            for half in range(2):
                hsl = slice(half * 512, (half + 1) * 512)
                stt('dve', tf[:, hsl], ps[bh[half]][:], st2[s2][:, 4:5], bcv[:, 0, hsl], ALU.mult, ALU.mult, [('ps', bh[half]), ('st2', s2), 'bcv'], [tfk])
            yield
            tt('dve', xl[s2][:], tf[:], xt[s2][:], ALU.add, [tfk, ('xt', s2)], [('xl', s2)])
            S.dma('sp', sc_xlat[n * 128:(n + 1) * 128, :], xl[s2][:], reads=[('xl', s2)], writes=[('sc_xlat', n)])
            S.op('pool', lambda e, t_=st2[s2]: e.memset(t_[:, 5:6], 0.0), [], [('st2', s2)])
            yield
            act(junk2[s2][:], xl[s2][:], AF.Square, [('xl', s2), ('st2', s2)], [('junk2', s2), ('st2', s2)], accum_out=st2[s2][:, 5:6])
            act(st2[s2][:, 6:7], st2[s2][:, 5:6], AF.Ln, [('st2', s2)], [('st2', s2)], scale=1.0 / D, bias=EPS)
            act(st2[s2][:, 7:8], st2[s2][:, 6:7], AF.Exp, [('st2', s2)], [('st2', s2)], scale=-0.5)
            yield
            ts('dve', hlb[s2][:], xl[s2][:], st2[s2][:, 7:8], None, ALU.mult, None, [('xl', s2), ('st2', s2)], [('hlb', s2)])
            yield
            bt2 = [nps(), nps()]
            pbts = [ps[bt2[0]][:].bitcast(BF16), ps[bt2[1]][:].bitcast(BF16)]
            for k in range(8):
                tr(pbts[k % 2][:, (k // 2) * 128:(k // 2 + 1) * 128], hlb[s2][:, k * 128:(k + 1) * 128], idb[:], [('hlb', s2), 'idb'], [('ps', bt2[k % 2])], inc=(k >= 6))
            yield
            for k in range(8):
                dst = hs_[:, k, t4 * 128:(t4 + 1) * 128]
                src_ = pbts[k % 2][:, (k // 2) * 128:(k // 2 + 1) * 128]
                if k % 2 == 0:
                    act(dst, src_, AF.Identity, [('ps', bt2[0]), 'fv'], [('hls', b2, 0)],
                        scale=fv[:, 5, k:k + 1], bias=fv[:, 6, k:k + 1])
                else:
                    ts('dve', dst, src_, fv[:, 5, k:k + 1], fv[:, 6, k:k + 1], ALU.mult, ALU.add,
                       [('ps', bt2[1]), 'fv'], [('hls', b2, 1)])
        interleave([t1_tile(0), t1_tile(1)])
        interleave([t1_tile(2), t1_tile(3)])
        S.dma('sp', sc_hl[:, :, tsl].rearrange("k p t -> p k t"), hs_[:], reads=[('hls', b2, 0), ('hls', b2, 1)], writes=[('sc_hl', tb)])

    S.barrier()
    top[0] = MARK0
    g5bc = sb([128, 1024], F32, "g5bc")
    for k in range(8):
        bcast(g5bc[:, k * 128:(k + 1) * 128], fv[:, 7, k:k + 1], 'g5bc', 'fv')
    w1 = sb([128, 8, 4096], BF16, "w1"); w2 = sb([128, 32, 1024], BF16, "w2")
    for k8 in range(8):
        S.dma('pool', w1[:, :, k8 * 512:(k8 + 1) * 512], w_mlp1[:, k8 * 512:(k8 + 1) * 512].rearrange("(k p) n -> p k n", p=128), writes=[('w1', k8)])
    for k8 in range(8):
        S.dma('pool', w2[:, k8 * 4:(k8 + 1) * 4, :], w_mlp2[k8 * 512:(k8 + 1) * 512, :].rearrange("(k p) n -> p k n", p=128), writes=[('w2', k8)])
    hlt = [sb([128, 8, 256], BF16, "hlt") for _ in range(2)]
    hid = sb([128, 32, 256], BF16, "hid")
    rr = [sb([128, 256], F32, "rr") for _ in range(2)]
    xl2 = [sb([128, 1024], F32, "xl2") for _ in range(2)]
    ot = [sb([128, 1024], F32, "ot") for _ in range(2)]
    junk3 = [sb([128, 512], BF16, "junk3") for _ in range(2)]
    st3 = [sb([128, 8], F32, "st3") for _ in range(2)]
    def t2_load(tb):
        S.dma('sp', hlt[tb % 2][:], sc_hl[:, :, tb * 256:(tb + 1) * 256].rearrange("k p t -> p k t"), reads=[('sc_hl', tb // 2)], writes=[('hlt', tb % 2)])
    for tb in range(16):
        b2 = tb % 2
        tsl = slice(tb * 256, (tb + 1) * 256)
        if tb == 0:
            t2_load(0)
        if tb + 1 < 16:
            t2_load(tb + 1)
        for ff in range(32):
            b = nps()
            for k in range(8):
                mm(ps[b][:, 0:256], w1[:, k, ff * 128:(ff + 1) * 128], hlt[b2][:, k, :], k == 0, k == 7, [('w1', ff // 4), ('hlt', b2)], [('ps', b)])
            r2 = ff % 2
            act(rr[r2][:], ps[b][:, 0:256], AF.Relu, [('ps', b)], [('rr', r2)])
            tt('pool' if ff % 4 else 'dve', hid[:, ff, :], rr[r2][:], rr[r2][:], ALU.mult, [('rr', r2)], [('hid', ff)])
        def t2_tile(t2, tb=tb):
            n = tb * 2 + t2
            s2 = n % 2
            S.dma('sp', xl2[s2][:], sc_xlat[n * 128:(n + 1) * 128, :], reads=[('sc_xlat', n)], writes=[('xl2', s2)])
            S.op('pool', lambda e, t_=st3[s2]: e.memset(t_[:, 0:2], 0.0), [], [('st3', s2)])
            bh = [nps(), nps()]
            for half in range(2):
                for k in range(32):
                    mm(ps[bh[half]][:], hid[:, k, t2 * 128:(t2 + 1) * 128], w2[:, k, half * 512:(half + 1) * 512], k == 0, k == 31,
                       [('hid', k), ('w2', k // 4)], [('ps', bh[half])])
                act(junk3[s2][:], ps[bh[half]][:], AF.Square, [('ps', bh[half]), ('st3', s2)], [('junk3', s2), ('st3', s2)], accum_out=st3[s2][:, half:half + 1])
            yield
            tt('dve', st3[s2][:, 2:3], st3[s2][:, 0:1], st3[s2][:, 1:2], ALU.add, [('st3', s2)], [('st3', s2)])
            yield
            act(st3[s2][:, 3:4], st3[s2][:, 2:3], AF.Ln, [('st3', s2)], [('st3', s2)], scale=1.0 / D, bias=EPS)
            act(st3[s2][:, 4:5], st3[s2][:, 3:4], AF.Exp, [('st3', s2)], [('st3', s2)], scale=-0.5)
            yield
            for half in range(2):
                hsl = slice(half * 512, (half + 1) * 512)
                stt('dve', ot[s2][:, hsl], ps[bh[half]][:], st3[s2][:, 4:5], g5bc[:, hsl], ALU.mult, ALU.mult, [('ps', bh[half]), ('st3', s2), 'g5bc'], [('ot', s2)])
            yield
            tt('pool', ot[s2][:], ot[s2][:], xl2[s2][:], ALU.add, [('ot', s2), ('xl2', s2)], [('ot', s2)])
            S.dma('sp', out[n * 128:(n + 1) * 128, :], ot[s2][:], reads=[('ot', s2)], is_out=True)
        interleave([t2_tile(0), t2_tile(1)])
    S.emit()
    return nc


def make_in_maps(inputs):
    carr, coffs, zT, ctab, stab = get_hc()
    g = lambda k: np.ascontiguousarray(np.asarray(inputs[k], dtype=np.float32))
    maps = []
    for b in range(8):
        m = {
            "x": g('x')[b], "ctx": g('ctx')[b],
            "cvec": np.ascontiguousarray(np.stack([g('c')[b], g('c_ctx')], axis=0)),
            "w_ada": g('w_ada')[0], "b_ada": g('b_ada')[0], "norm_w": g('norm_w')[0], "w_in": g('w_in')[0],
            "lb_param": g('lb_param'), "g_norm": g('g_norm')[0], "hy_short_w": g('hy_short_w')[0],
            "hy_short_b": g('hy_short_b')[0], "f_w1": g('f_w1')[0], "f_b1": g('f_b1')[0], "f_w2": g('f_w2')[0],
            "f_b2": g('f_b2')[0], "f_w3": g('f_w3')[0], "f_freq": g('f_freq')[0], "hy_d": g('hy_d')[0],
            "w_pa": g('w_pa')[0], "w_pb": g('w_pb')[0], "w_out": g('w_out')[0], "w_mlp1": g('w_mlp1')[0],
            "w_mlp2": g('w_mlp2')[0], "consts": carr, "zT": zT, "ctab": ctab, "stab": stab,
        }
        maps.append(m)
    return maps


def kernel(**inputs):
    nc = build_program()
    maps = make_in_maps(inputs)
    res = run_bass_kernel_spmd(nc, maps, core_ids=list(range(8)))
    return np.stack([np.asarray(r["out"], dtype=np.float32) for r in res.results], axis=0)
```
